# Optimizing a Trainium2 kernel written in Bass

```python
import math
import jax, jax.numpy as jnp
from jax import lax
import numpy as np

D_MODEL = 1024
BATCH = 2
SEQ = 8192
DEPTH = 4
DEC_BATCH = 128
DEC_SEQ = 1
PAST_LEN = 8192
PAGE_SIZE = 128

MIX_WIDTH = D_MODEL
HEAD_DIM = 64
ATTN_WIDTH = MIX_WIDTH // 2
N_HEADS = ATTN_WIDTH // HEAD_DIM
N_KV_HEADS = N_HEADS // 4
KV_REP = N_HEADS // N_KV_HEADS
WINDOW = 128
CONV_CH = MIX_WIDTH // 4
CONV_WIDTH = 31
SSM_CH = MIX_WIDTH - ATTN_WIDTH - CONV_CH
SSM_GROUP = 16
SSM_GROUPS = SSM_CH // SSM_GROUP
SSM_STATE = 64
D_FF = -(-(-(-8 * D_MODEL // 3)) // 256) * 256
EPS = 1e-6

Q_END = ATTN_WIDTH
K_END = Q_END + N_KV_HEADS * HEAD_DIM
V_END = K_END + N_KV_HEADS * HEAD_DIM
C_END = V_END + 2 * CONV_CH
IN_COLS = C_END + SSM_CH

kernel_name = "hymba_swa_conformer_s5_decoder_step"


def rmsnorm(x, g):
    xf = x.astype(jnp.float32)
    y = xf * lax.rsqrt(jnp.mean(xf * xf, -1, keepdims=True) + EPS)
    return (y * g.astype(jnp.float32)).astype(x.dtype)


def alibi_slopes():
    m = 2.0 ** (-8.0 * jnp.arange(1, N_HEADS + 1, dtype=jnp.float32) / N_HEADS)
    return m.reshape(N_KV_HEADS, KV_REP, 1, 1)


def sink_attention(q, k, v, dist, valid, sinks):
    scale = 1.0 / math.sqrt(HEAD_DIM)
    s = jnp.einsum('...qgrd,...kgd->...grqk', q.astype(jnp.float32), k.astype(jnp.float32)) * scale
    s = s - alibi_slopes() * dist.astype(jnp.float32)
    s = jnp.where(valid, s, -jnp.inf)
    sink = sinks.astype(jnp.float32).reshape(N_KV_HEADS, KV_REP, 1, 1)
    m = jnp.maximum(jnp.max(s, -1, keepdims=True), sink)
    p = jnp.exp(s - m)
    denom = jnp.sum(p, -1, keepdims=True) + jnp.exp(sink - m)
    o = jnp.einsum('...grqk,...kgd->...qgrd', p / denom, v.astype(jnp.float32))
    return o.astype(q.dtype)


def swa_prompt(q, k, v, sinks):
    n, t = q.shape[:2]
    nb = t // WINDOW
    qb = q.reshape(n, nb, WINDOW, N_KV_HEADS, KV_REP, HEAD_DIM)
    pad = jnp.zeros((n, WINDOW, N_KV_HEADS, HEAD_DIM), k.dtype)
    kp = jnp.concatenate([pad, k], 1).reshape(n, nb + 1, WINDOW, N_KV_HEADS, HEAD_DIM)
    vp = jnp.concatenate([pad, v], 1).reshape(n, nb + 1, WINDOW, N_KV_HEADS, HEAD_DIM)
    kb = jnp.concatenate([kp[:, :-1], kp[:, 1:]], 2)
    vb = jnp.concatenate([vp[:, :-1], vp[:, 1:]], 2)
    a = jnp.arange(WINDOW)[:, None]
    b = jnp.arange(2 * WINDOW)[None, :]
    dist = a - b + WINDOW
    key_pos = jnp.arange(nb)[:, None, None] * WINDOW + b[None] - WINDOW
    valid = (dist >= 0) & (dist < WINDOW) & (key_pos >= 0)
    o = sink_attention(qb, kb, vb, dist, valid[:, None, None], sinks)
    keep = min(WINDOW, t)
    return o.reshape(n, t, ATTN_WIDTH), k[:, t - keep:], v[:, t - keep:]


def make_swa_sample(buf_k, buf_v):
    def attend(q, k, v, sinks):
        n, s_len = q.shape[:2]
        w = buf_k.shape[1]
        kk = jnp.concatenate([buf_k.astype(k.dtype), k], 1)
        vv = jnp.concatenate([buf_v.astype(v.dtype), v], 1)
        dist = jnp.arange(s_len)[:, None] - jnp.arange(-w, s_len)[None, :]
        valid = (dist >= 0) & (dist < WINDOW)
        o = sink_attention(q, kk, vv, dist, valid, sinks)
        return o.reshape(n, s_len, ATTN_WIDTH), kk[:, -w:], vv[:, -w:]
    return attend


def conv_module(ag, ctx, dw_w, dw_b, ln_g, ln_b):
    a, g = jnp.split(ag, 2, -1)
    u = a * jax.nn.sigmoid(g)
    full = jnp.concatenate([ctx.astype(u.dtype), u], 1)
    y = lax.conv_general_dilated(full, dw_w[:, None, :].astype(u.dtype), window_strides=(1,),
                                 padding='VALID', dimension_numbers=('NWC', 'WIO', 'NWC'),
                                 feature_group_count=CONV_CH)
    yf = y.astype(jnp.float32) + dw_b.astype(jnp.float32)
    mu = jnp.mean(yf, -1, keepdims=True)
    var = jnp.mean(jnp.square(yf - mu), -1, keepdims=True)
    yn = (yf - mu) * lax.rsqrt(var + EPS) * ln_g.astype(jnp.float32) + ln_b.astype(jnp.float32)
    return jax.nn.silu(yn).astype(u.dtype), full[:, -(CONV_WIDTH - 1):]


def ssm_module(u, h0_re, h0_im, a_re, a_im, log_dt, b_re, b_im, c_re, c_im, d, glu_w, glu_b):
    n, t, _ = u.shape
    f32 = jnp.float32
    uf = u.astype(f32)
    A = lax.complex(a_re.astype(f32), a_im.astype(f32))
    dt = jnp.exp(log_dt.astype(f32))[:, None]
    abar = jnp.exp(A * dt)
    Bc = lax.complex(b_re.astype(f32), b_im.astype(f32))
    bbar = ((abar - 1.0) / A)[..., None] * Bc
    ug = uf.reshape(n, t, SSM_GROUPS, SSM_GROUP).astype(jnp.complex64)
    bu = jnp.einsum('gpc,ntgc->ntgp', bbar, ug)
    h0 = lax.complex(h0_re.astype(f32), h0_im.astype(f32))
    bu = bu.at[:, 0].add(abar * h0)
    a_seq = jnp.broadcast_to(abar, bu.shape)

    def combine(left, right):
        a1, b1 = left
        a2, b2 = right
        return a1 * a2, a2 * b1 + b2

    _, h = lax.associative_scan(combine, (a_seq, bu), axis=1)
    C = lax.complex(c_re.astype(f32), c_im.astype(f32))
    y = jnp.real(jnp.einsum('gcp,ntgp->ntgc', C, h)).reshape(n, t, SSM_CH) + d.astype(f32) * uf
    y = jax.nn.gelu(y)
    out = y * jax.nn.sigmoid(y @ glu_w.astype(f32) + glu_b.astype(f32))
    h_last = h[:, -1]
    return out.astype(u.dtype), jnp.real(h_last), jnp.imag(h_last)


def trunk_layer(x, attend, conv_ctx, h0_re, h0_im, norm_mix_g, w_in, attn_sinks,
                conv_dw_w, conv_dw_b, conv_ln_g, conv_ln_b,
                ssm_a_re, ssm_a_im, ssm_log_dt, ssm_b_re, ssm_b_im, ssm_c_re, ssm_c_im,
                ssm_d, ssm_glu_w, ssm_glu_b, w_out, norm_ffn_g, w_ff_gate, w_ff_up, w_ff_down):
    n, t, _ = x.shape
    h = rmsnorm(x, norm_mix_g)
    z = h @ w_in
    q = z[..., :Q_END].reshape(n, t, N_KV_HEADS, KV_REP, HEAD_DIM)
    k = z[..., Q_END:K_END].reshape(n, t, N_KV_HEADS, HEAD_DIM)
    v = z[..., K_END:V_END].reshape(n, t, N_KV_HEADS, HEAD_DIM)
    attn_out, k_rows, v_rows = attend(q, k, v, attn_sinks)
    conv_out, new_ctx = conv_module(z[..., V_END:C_END], conv_ctx, conv_dw_w, conv_dw_b, conv_ln_g, conv_ln_b)
    ssm_out, h_re, h_im = ssm_module(z[..., C_END:], h0_re, h0_im, ssm_a_re, ssm_a_im, ssm_log_dt,
                                     ssm_b_re, ssm_b_im, ssm_c_re, ssm_c_im, ssm_d, ssm_glu_w, ssm_glu_b)
    x = x + jnp.concatenate([attn_out, conv_out, ssm_out], -1) @ w_out
    hf = rmsnorm(x, norm_ffn_g)
    x = x + (jax.nn.silu(hf @ w_ff_gate) * (hf @ w_ff_up)) @ w_ff_down
    return x, (k_rows, v_rows, new_ctx, h_re, h_im)


def setup_inputs(seed: int = 0) -> dict:
    key = jax.random.key(seed)
    ks = jax.random.split(key, 40)
    f32 = jnp.float32
    nrm = lambda i, shape, s: jax.random.normal(ks[i], shape, f32) * s
    win = min(WINDOW, PAST_LEN)
    n_idx = jnp.arange(SSM_STATE, dtype=f32)
    log_dt = jax.random.uniform(ks[20], (DEPTH, SSM_GROUPS), f32, math.log(1e-3), math.log(1e-1))
    return {
        "x_prompt": nrm(0, (BATCH, SEQ, D_MODEL), 1.0),
        "x_sample": nrm(1, (DEC_BATCH, DEC_SEQ, D_MODEL), 1.0),
        "cache_swa_k": nrm(2, (DEPTH, DEC_BATCH, win, N_KV_HEADS, HEAD_DIM), 1.0),
        "cache_swa_v": nrm(3, (DEPTH, DEC_BATCH, win, N_KV_HEADS, HEAD_DIM), 1.0),
        "cache_conv": nrm(4, (DEPTH, DEC_BATCH, CONV_WIDTH - 1, CONV_CH), 0.5),
        "state_ssm_re": nrm(5, (DEPTH, DEC_BATCH, SSM_GROUPS, SSM_STATE), 0.2),
        "state_ssm_im": nrm(6, (DEPTH, DEC_BATCH, SSM_GROUPS, SSM_STATE), 0.2),
        "norm_mix_g": 1.0 + nrm(7, (DEPTH, D_MODEL), 0.02),
        "w_in": nrm(8, (DEPTH, D_MODEL, IN_COLS), D_MODEL ** -0.5),
        "attn_sinks": nrm(9, (DEPTH, N_HEADS), 0.5),
        "conv_dw_w": nrm(10, (DEPTH, CONV_WIDTH, CONV_CH), CONV_WIDTH ** -0.5),
        "conv_dw_b": nrm(11, (DEPTH, CONV_CH), 0.01),
        "conv_ln_g": 1.0 + nrm(12, (DEPTH, CONV_CH), 0.02),
        "conv_ln_b": nrm(13, (DEPTH, CONV_CH), 0.01),
        "ssm_a_re": -0.5 + nrm(14, (DEPTH, SSM_GROUPS, SSM_STATE), 0.01),
        "ssm_a_im": math.pi * n_idx + nrm(15, (DEPTH, SSM_GROUPS, SSM_STATE), 0.01),
        "ssm_log_dt": log_dt,
        "ssm_b_re": nrm(16, (DEPTH, SSM_GROUPS, SSM_STATE, SSM_GROUP), (2 * SSM_GROUP) ** -0.5),
        "ssm_b_im": nrm(17, (DEPTH, SSM_GROUPS, SSM_STATE, SSM_GROUP), (2 * SSM_GROUP) ** -0.5),
        "ssm_c_re": nrm(18, (DEPTH, SSM_GROUPS, SSM_GROUP, SSM_STATE), (2 * SSM_STATE) ** -0.5),
        "ssm_c_im": nrm(19, (DEPTH, SSM_GROUPS, SSM_GROUP, SSM_STATE), (2 * SSM_STATE) ** -0.5),
        "ssm_d": nrm(21, (DEPTH, SSM_CH), 1.0),
        "ssm_glu_w": nrm(22, (DEPTH, SSM_CH, SSM_CH), SSM_CH ** -0.5),
        "ssm_glu_b": nrm(23, (DEPTH, SSM_CH), 0.01),
        "w_out": nrm(24, (DEPTH, MIX_WIDTH, D_MODEL), MIX_WIDTH ** -0.5),
        "norm_ffn_g": 1.0 + nrm(25, (DEPTH, D_MODEL), 0.02),
        "w_ff_gate": nrm(26, (DEPTH, D_MODEL, D_FF), D_MODEL ** -0.5),
        "w_ff_up": nrm(27, (DEPTH, D_MODEL, D_FF), D_MODEL ** -0.5),
        "w_ff_down": nrm(28, (DEPTH, D_FF, D_MODEL), D_FF ** -0.5),
        "norm_final_g": 1.0 + nrm(29, (D_MODEL,), 0.02),
    }


def reference(x_prompt, x_sample, cache_swa_k, cache_swa_v, cache_conv, state_ssm_re, state_ssm_im,
              norm_mix_g, w_in, attn_sinks, conv_dw_w, conv_dw_b, conv_ln_g, conv_ln_b,
              ssm_a_re, ssm_a_im, ssm_log_dt, ssm_b_re, ssm_b_im, ssm_c_re, ssm_c_im,
              ssm_d, ssm_glu_w, ssm_glu_b, w_out, norm_ffn_g, w_ff_gate, w_ff_up, w_ff_down,
              norm_final_g):
    n_p = x_prompt.shape[0]
    xp, xs = x_prompt, x_sample
    prompt_states, sample_states = [], []
    for l in range(DEPTH):
        lp = [a[l] for a in (norm_mix_g, w_in, attn_sinks, conv_dw_w, conv_dw_b, conv_ln_g, conv_ln_b,
                             ssm_a_re, ssm_a_im, ssm_log_dt, ssm_b_re, ssm_b_im, ssm_c_re, ssm_c_im,
                             ssm_d, ssm_glu_w, ssm_glu_b, w_out, norm_ffn_g, w_ff_gate, w_ff_up, w_ff_down)]
        zero_ctx = jnp.zeros((n_p, CONV_WIDTH - 1, CONV_CH), xp.dtype)
        zero_h = jnp.zeros((n_p, SSM_GROUPS, SSM_STATE), jnp.float32)
        xp, st_p = trunk_layer(xp, swa_prompt, zero_ctx, zero_h, zero_h, *lp)
        xs, st_s = trunk_layer(xs, make_swa_sample(cache_swa_k[l], cache_swa_v[l]), cache_conv[l],
                               state_ssm_re[l], state_ssm_im[l], *lp)
        prompt_states.append(st_p)
        sample_states.append(st_s)
    y_prompt = rmsnorm(xp, norm_final_g)
    y_sample = rmsnorm(xs, norm_final_g)
    new_swa_k_prompt = jnp.stack([s[0] for s in prompt_states])
    new_swa_v_prompt = jnp.stack([s[1] for s in prompt_states])
    new_conv_prompt = jnp.stack([s[2] for s in prompt_states])
    new_ssm_re_prompt = jnp.stack([s[3] for s in prompt_states])
    new_ssm_im_prompt = jnp.stack([s[4] for s in prompt_states])
    new_swa_k_sample = jnp.stack([s[0] for s in sample_states])
    new_swa_v_sample = jnp.stack([s[1] for s in sample_states])
    new_conv_sample = jnp.stack([s[2] for s in sample_states])
    new_ssm_re_sample = jnp.stack([s[3] for s in sample_states])
    new_ssm_im_sample = jnp.stack([s[4] for s in sample_states])
    return (y_prompt, y_sample,
            new_swa_k_prompt, new_swa_v_prompt, new_conv_prompt, new_ssm_re_prompt, new_ssm_im_prompt,
            new_swa_k_sample, new_swa_v_sample, new_conv_sample, new_ssm_re_sample, new_ssm_im_sample)
```

```python
import math
import os
import numpy as np
from contextlib import ExitStack
import concourse.bass as bass
import concourse.mybir as mybir
from concourse.bass_utils import run_bass_kernel_spmd

F32 = mybir.dt.float32
BF16 = mybir.dt.bfloat16
I32 = mybir.dt.int32
AF = mybir.ActivationFunctionType
ALU = mybir.AluOpType
AX = mybir.AxisListType

D = 1024
SEQ = 8192
T = 512
NS = 16
NTM = T + NS
DFF = 2816
NF = 22
J = 256
NSLOT = 4
PVL = 120
NPV = 4 * PVL + 8
EPS = 1e-6
TWO_PI = 2.0 * math.pi


import types


def _freeze(fn):
    if fn.__closure__ is None:
        return fn
    cells = []
    for c in fn.__closure__:
        try:
            cells.append(types.CellType(c.cell_contents))
        except ValueError:
            cells.append(c)
    return types.FunctionType(fn.__code__, fn.__globals__, fn.__name__, fn.__defaults__, tuple(cells))


class _Stub:
    def __init__(self):
        self.closed = True

    def matmul(self, *a, **kw):
        self.closed = bool(kw.get("stop", True))
        return self

    def transpose(self, *a, **kw):
        self.closed = True
        return self

    def then_inc(self, *a, **kw):
        return self


class Buf:
    __slots__ = ("name", "lw", "rd")

    def __init__(self, name):
        self.name = name
        self.lw = None
        self.rd = {}


class Prog:
    ENG = ["tensor", "vector", "scalar", "gpsimd", "sync"]

    def __init__(self, nc, same_eng_sync=("vector", "scalar", "gpsimd")):
        self.nc = nc
        self.ops = {e: [] for e in self.ENG}
        self.cnt = {}
        self.known = {e: {} for e in self.ENG}
        self.same = set(same_eng_sync)
        self.bufs = {}

    def B(self, *key):
        b = self.bufs.get(key)
        if b is None:
            b = self.bufs[key] = Buf(key)
        return b

    def op(self, eng, fn, r=(), w=(), dma=None, inc=None):
        fn = _freeze(fn)
        if getattr(self, "stopped", False):
            return None
        self.nops = getattr(self, "nops", 0) + 1
        if eng == "tensor" and "KLIMIT" in os.environ:
            st_ = _Stub()
            try:
                fn(st_)
            except Exception:
                pass
            self.open_grp = not st_.closed
        if self.nops >= int(os.environ.get("KLIMIT", "100000000")) and not getattr(self, "open_grp", False):
            self.stopped = True
        ex = [b for b in r if b.name[0] in ("ps", "psb")]
        if ex:
            r = [b for b in r if b.name[0] not in ("ps", "psb")]
            w = list(w) + ex
        deps = {}
        for b in list(r) + list(w):
            if b.lw is not None and deps.get(b.lw[0], 0) < b.lw[1]:
                deps[b.lw[0]] = b.lw[1]
        for b in w:
            for s, v in b.rd.items():
                if deps.get(s, 0) < v:
                    deps[s] = v
        if dma is None:
            sem, step = eng, 1
        else:
            sem, step = "dma_" + dma, (16 if inc is None else inc)
        waits = []
        for s, v in deps.items():
            if s == eng and eng not in self.same:
                continue
            if self.known[eng].get(s, 0) >= v:
                continue
            self.known[eng][s] = v
            waits.append((s, v))
        self.cnt[sem] = self.cnt.get(sem, 0) + step
        tok = (sem, self.cnt[sem])
        self.ops[eng].append((fn, waits, sem, step))
        for b in r:
            if b.rd.get(sem, 0) < tok[1]:
                b.rd[sem] = tok[1]
        for b in w:
            b.lw = tok
            b.rd = {}
        return tok

    def emit(self):
        nc = self.nc
        with ExitStack() as st:
            sems = {s: st.enter_context(nc.semaphore(s)) for s in self.cnt}
            block = st.enter_context(nc.Block())
            final = dict(self.cnt)

            def mk(engname):
                def body(e):
                    for fn, waits, sem, step in self.ops[engname]:
                        for s, v in waits:
                            e.wait_ge(sems[s], v)
                        fn(e).then_inc(sems[sem], step)
                    if engname == "sync":
                        for s, v in final.items():
                            e.wait_ge(sems[s], v)
                return body

            for engname in self.ENG:
                if self.ops[engname] or engname == "sync":
                    getattr(block, engname)(mk(engname))


def build(nch=16, depth=4):
    nc = bass.Bass("TRN2", target_bir_lowering=False)
    L = depth
    TT = nch * T

    def din(name, shape, dt=F32):
        return nc.dram_tensor(name, list(shape), dt, kind="ExternalInput").ap()

    def dout(name, shape, dt=F32):
        return nc.dram_tensor(name, list(shape), dt, kind="ExternalOutput").ap()

    xT = din("xT", [D, TT]); xsT = din("xsT", [D, NS])
    w_in = din("w_in", [L, D, 1536]); w_out = din("w_out", [L, D, D])
    w_g = din("w_g", [L, D, DFF]); w_u = din("w_u", [L, D, DFF]); w_d = din("w_d", [L, DFF, D])
    glu_w = din("glu_w", [L, 256, 256])
    pv_d = din("pv", [128, NPV])
    Bn_d = din("Bn", [L, 128, 8 * 2 * 128]); Cn_d = din("Cn", [L, 128, 8 * 2 * 128])
    ident_d = din("ident", [128, 128]); jj_d = din("jj", [128, J])
    abias_d = din("abias", [128, 8 * 256]); sbias_d = din("sbias", [4, 2 * 128]); sinkc_d = din("sinkc", [4, L * 2])
    ck_d = din("ck", [L, NS, 128, 128]); cv_d = din("cv", [L, NS, 128, 128])
    cconv_d = din("cconv", [L, 128, 2 * NS * 30])
    sre_d = din("sre", [L, 128, 8 * NS]); sim_d = din("sim", [L, 128, 8 * NS])

    yT = dout("yT", [D, TT]); ysT = dout("ysT", [D, NS])
    ok_p = dout("ok_p", [L, 128, 128]); ov_p = dout("ov_p", [L, 128, 128])
    oconv_p = dout("oconv_p", [L, 128, 2 * 30]); ossm_p = dout("ossm_p", [L, 128, 16])
    ok_s = dout("ok_s", [L, NS, 128, 128]); ov_s = dout("ov_s", [L, NS, 128, 128])
    oconv_s = dout("oconv_s", [L, 128, 2 * NS * 30]); ossm_s = dout("ossm_s", [L, 128, 2 * 8 * NS])
    rot_d = nc.dram_tensor("rot_scr", [L, 128, 8 * 2 * J], F32).ap()
    wb_d = nc.dram_tensor("wb_scr", [L, 128, 16 * 128], BF16).ap()
    wc_d = nc.dram_tensor("wc_scr", [L, 128, 16 * 128], BF16).ap()

    P = Prog(nc)
    B = P.B
    with ExitStack() as st:
        def sb(name, shape, dt=F32):
            return st.enter_context(nc.sbuf_tensor("sb_" + name, list(shape), dt))

        def pst(name, shape, dt=F32):
            return st.enter_context(nc.psum_tensor(name, list(shape), dt))

        x = sb("x", [128, 8, NTM]); hb = sb("hb", [128, 8, NTM], BF16)
        qT = sb("qT", [64, 8, NTM], BF16); kT = sb("kT", [64, 2, 128 + NTM], BF16)
        vp = sb("vp", [128, 5, 2, 192], BF16)
        ucv = sb("ucv", [128, 2, 30 + NTM]); acc = sb("acc", [128, 2, NTM]); us = sb("us", [128, 2, NTM], BF16)
        hid = sb("hid", [128, 4, NTM], BF16)
        ring = sb("ring", [128, NSLOT, 4096], BF16)
        pvt = sb("pvt", [128, NPV])
        ident = sb("ident", [128, 128]); identb = sb("identb", [128, 128], BF16)
        ones_f = sb("ones_f", [128, 128]); ones_b = sb("ones_b", [128, 128], BF16)
        jj = sb("jj", [128, J])
        abias = sb("abias", [128, 8, 256]); sbias = sb("sbias", [4, 2, 128])
        WB = sb("WB", [128, 16, 128], BF16); WC = sb("WC", [128, 16, 128], BF16)
        Dd = sb("Dd", [128, L * 2, 128], BF16); gluw = sb("gluw", [128, L * 2, 256], BF16)
        lam = sb("lam", [128, L, 4, 8])
        rot = sb("rot", [128, 8, 2, J])
        hcar = sb("hcar", [128, L, 8, 2])
        khalo = sb("khalo", [64, L, 2, 128], BF16); vhalo = sb("vhalo", [128, L, 2, 192], BF16)
        chalo = sb("chalo", [128, L, 2, 30])
        sq = sb("sq", [128, 512], BF16); rstd = sb("rstd", [128, 512])
        sgb = sb("sgb", [128, 2, 512], BF16)
        sc = sb("sc", [128, 2, 256]); pb = sb("pb", [128, 2, 256], BF16); ptb = sb("ptb", [128, 2, 256], BF16)
        sm = sb("sm", [128, 2, 8])
        t1 = sb("t1", [128, J]); t2 = sb("t2", [128, J]); bre = sb("bre", [128, J]); bim = sb("bim", [128, J])
        wre = sb("wre", [128, J]); wim = sb("wim", [128, J])
        ysm = sb("ysm", [128, 2, 512]); yb = sb("yb", [128, 2, 512], BF16); g1 = sb("g1", [128, 512]); g2 = sb("g2", [128, 512])
        ysq = ysm; mean = sb("mean", [128, 512]); var = sb("var", [128, 512])
        xflat = x[:].rearrange("p k n -> p (k n)")
        rotflat = rot[:].rearrange("p s c j -> p (s c j)")
        big = xflat[:, 0:4096]; big2 = rotflat[:, 8 * J:16 * J]; bigi = sb("bigi", [128, 8 * J], I32)[:]
        hbf = sb("hbf", [128, 16 * J], BF16)
        pl = sb("pl", [128, 16, 8])
        tk = sb("tk", [128, 128]); tkb = sb("tkb", [NS, 2, 128])
        Kb = sb("Kb", [128, 2, 128]); Vb = sb("Vb", [128, 2, 128]); KbT = sb("KbT", [64, 2, 128], BF16)
        Vbp = sb("Vbp", [128, 2, 192], BF16)
        ssc = sb("ssc", [4, 4, 128]); spb = sb("spb", [4, 4, 128], BF16); ssm_ = sb("ssm_", [4, 4, 4]); sinkt = sb("sinkt", [4, L * 2])
        sptb = sb("sptb", [128, 2, NS, 4], BF16)
        cs = sb("cs", [128, 2, NS, 31]); cst = sb("cst", [128, 2, NS, 31])
        h0 = sb("h0", [128, 2, 8, NS]); h1 = sb("h1", [128, 2, 8, NS]); h1b = sb("h1b", [128, 2, 8, NS], BF16)

        ps = [pst("ps%d" % i, [128, 512]) for i in range(7)]
        psb = pst("psb", [128, 1024], BF16)

        Q = lambda fn, r=(), w=(), ch="io": P.op("sync", fn, r, w, dma=ch)
        G = lambda fn, r=(), w=(), ch="w": P.op("gpsimd", fn, r, w, dma=ch)
        V = lambda fn, r=(), w=(): P.op("vector", fn, r, w)
        S = lambda fn, r=(), w=(): P.op("scalar", fn, r, w)
        PE = lambda fn, r=(), w=(): P.op("tensor", fn, r, w)
        GP = lambda fn, r=(), w=(): P.op("gpsimd", fn, r, w)

        def pvc(l, off, n=1):
            return pvt[:, l * PVL + off: l * PVL + off + n]

        OG1, OG2, OCW, OCB, OLG, OLB, OSD, OGB, OARE, OAIM, OLDT, OSINK = 0, 8, 16, 78, 80, 82, 84, 86, 88, 96, 104, 112
        rr = [0]

        def bank():
            rr[0] = (rr[0] + 1) % 4
            return rr[0]

        cB = B("const")
        Q(lambda e: e.dma_start(out=pvt[:], in_=pv_d[:, :]), w=[cB])
        Q(lambda e: e.dma_start(out=ident[:], in_=ident_d[:, :]), w=[cB])
        Q(lambda e: e.dma_start(out=jj[:], in_=jj_d[:, :]), w=[cB])
        Q(lambda e: e.dma_start(out=abias[:].rearrange("p h k -> p (h k)"), in_=abias_d[:, :]), w=[cB])
        Q(lambda e: e.dma_start(out=sbias[:].rearrange("p h k -> p (h k)"), in_=sbias_d[:, :]), w=[cB])
        Q(lambda e: e.dma_start(out=sinkt[:], in_=sinkc_d[:, :]), w=[cB])
        V(lambda e: e.memset(ones_f[:], 1.0), w=[cB])
        V(lambda e: e.memset(ones_b[:], 1.0), w=[cB])
        V(lambda e: e.tensor_copy(out=identb[:], in_=ident[:]), r=[cB], w=[B("identb")])
        V(lambda e: e.memset(vp[:].rearrange("p a g c -> p (a g c)"), 0.0), w=[B("vp", i) for i in range(5)])
        V(lambda e: e.memset(vhalo[:].rearrange("p l g c -> p (l g c)"), 0.0), w=[B("vhalo", l) for l in range(L)])
        V(lambda e: e.memset(Vbp[:].rearrange("p g c -> p (g c)"), 0.0), w=[B("Vbp")])
        V(lambda e: e.memset(hcar[:].rearrange("p l s c -> p (l s c)"), 0.0), w=[B("hcar", l) for l in range(L)])
        V(lambda e: e.memset(chalo[:].rearrange("p l c k -> p (l c k)"), 0.0), w=[B("chalo", l) for l in range(L)])

        for l in range(L):
            pB = B("pl")
            are, aim, ldt = pvc(l, OARE, 8), pvc(l, OAIM, 8), pvc(l, OLDT, 8)
            c = lambda i: pl[:, i, :]
            S(lambda e: e.activation(out=c(0), in_=ldt, func=AF.Exp), r=[cB], w=[pB])
            V(lambda e: e.tensor_tensor(out=c(1), in0=are, in1=c(0), op=ALU.mult), r=[pB], w=[pB])
            V(lambda e: e.tensor_tensor(out=c(2), in0=aim, in1=c(0), op=ALU.mult), r=[pB], w=[pB])
            S(lambda e, l=l: e.activation(out=lam[:, l, 0, :], in_=c(1), func=AF.Exp), r=[pB], w=[B("lam", l)])
            V(lambda e: e.tensor_scalar(out=c(3), in0=c(2), scalar1=1.0 / TWO_PI, scalar2=None, op0=ALU.mult), r=[pB], w=[pB])
            V(lambda e: e.tensor_copy(out=bigi[:, 0:8], in_=c(3)), r=[pB], w=[pB])
            V(lambda e: e.tensor_copy(out=c(4), in_=bigi[:, 0:8]), r=[pB], w=[pB])
            V(lambda e: e.tensor_tensor(out=c(4), in0=c(3), in1=c(4), op=ALU.subtract), r=[pB], w=[pB])
            S(lambda e, l=l: e.activation(out=lam[:, l, 2, :], in_=c(4), func=AF.Sin, scale=6.283185), r=[pB], w=[B("lam", l)])
            V(lambda e: e.tensor_scalar(out=c(3), in0=c(3), scalar1=0.25, scalar2=None, op0=ALU.add), r=[pB], w=[pB])
            V(lambda e: e.tensor_copy(out=bigi[:, 0:8], in_=c(3)), r=[pB], w=[pB])
            V(lambda e: e.tensor_copy(out=c(4), in_=bigi[:, 0:8]), r=[pB], w=[pB])
            V(lambda e: e.tensor_tensor(out=c(4), in0=c(3), in1=c(4), op=ALU.subtract), r=[pB], w=[pB])
            S(lambda e, l=l: e.activation(out=lam[:, l, 1, :], in_=c(4), func=AF.Sin, scale=6.283185), r=[pB], w=[B("lam", l)])
            V(lambda e, l=l: e.tensor_tensor(out=c(5), in0=lam[:, l, 0, :], in1=lam[:, l, 1, :], op=ALU.mult), r=[pB, B("lam", l)], w=[pB])
            V(lambda e, l=l: e.tensor_tensor(out=c(6), in0=lam[:, l, 0, :], in1=lam[:, l, 2, :], op=ALU.mult), r=[pB, B("lam", l)], w=[pB])
            V(lambda e: e.tensor_scalar(out=c(7), in0=c(5), scalar1=-1.0, scalar2=None, op0=ALU.add), r=[pB], w=[pB])
            V(lambda e: e.tensor_tensor(out=c(8), in0=are, in1=are, op=ALU.mult), r=[pB], w=[pB])
            V(lambda e: e.tensor_tensor(out=c(9), in0=aim, in1=aim, op=ALU.mult), r=[pB], w=[pB])
            V(lambda e: e.tensor_tensor(out=c(8), in0=c(8), in1=c(9), op=ALU.add), r=[pB], w=[pB])
            V(lambda e: e.reciprocal(out=c(8), in_=c(8)), r=[pB], w=[pB])
            V(lambda e: e.tensor_tensor(out=c(9), in0=c(7), in1=are, op=ALU.mult), r=[pB], w=[pB])
            V(lambda e: e.tensor_tensor(out=c(10), in0=c(6), in1=aim, op=ALU.mult), r=[pB], w=[pB])
            V(lambda e: e.tensor_tensor(out=c(9), in0=c(9), in1=c(10), op=ALU.add), r=[pB], w=[pB])
            V(lambda e: e.tensor_tensor(out=c(11), in0=c(9), in1=c(8), op=ALU.mult), r=[pB], w=[pB])
            V(lambda e: e.tensor_tensor(out=c(9), in0=c(6), in1=are, op=ALU.mult), r=[pB], w=[pB])
            V(lambda e: e.tensor_tensor(out=c(10), in0=c(7), in1=aim, op=ALU.mult), r=[pB], w=[pB])
            V(lambda e: e.tensor_tensor(out=c(9), in0=c(9), in1=c(10), op=ALU.subtract), r=[pB], w=[pB])
            V(lambda e: e.tensor_tensor(out=c(12), in0=c(9), in1=c(8), op=ALU.mult), r=[pB], w=[pB])
            V(lambda e: e.tensor_scalar(out=c(13), in0=c(12), scalar1=-1.0, scalar2=None, op0=ALU.mult), r=[pB], w=[pB])
            bg = big[:, 0:2048].rearrange("p (s c k) -> p s c k", s=8, c=2)
            Q(lambda e, l=l: e.dma_start(out=big[:, 0:2048], in_=Bn_d[l, :, :]), r=[pB], w=[B("big")])
            for s8 in range(8):
                fre, fim, nfim = pl[:, 11, s8:s8 + 1], pl[:, 12, s8:s8 + 1], pl[:, 13, s8:s8 + 1]
                V(lambda e, s8=s8, fre=fre: e.tensor_scalar(out=t1[:, 0:128], in0=bg[:, s8, 0, :], scalar1=fre, scalar2=None, op0=ALU.mult), r=[pB, B("big")], w=[B("t1")])
                V(lambda e, s8=s8, nfim=nfim: e.scalar_tensor_tensor(out=t1[:, 0:128], in0=bg[:, s8, 1, :], scalar=nfim, in1=t1[:, 0:128], op0=ALU.mult, op1=ALU.add), r=[pB, B("big"), B("t1")], w=[B("t1")])
                V(lambda e, s8=s8, fre=fre: e.tensor_scalar(out=t2[:, 0:128], in0=bg[:, s8, 1, :], scalar1=fre, scalar2=None, op0=ALU.mult), r=[pB, B("big")], w=[B("t2")])
                V(lambda e, s8=s8, fim=fim: e.scalar_tensor_tensor(out=t2[:, 0:128], in0=bg[:, s8, 0, :], scalar=fim, in1=t2[:, 0:128], op0=ALU.mult, op1=ALU.add), r=[pB, B("big"), B("t2")], w=[B("t2")])
                PE(lambda e: e.transpose(out=ps[5][:, 0:128], in_=t1[:, 0:128], identity=ident[:]), r=[B("t1"), cB], w=[B("ps", 5)])
                PE(lambda e: e.transpose(out=ps[5][:, 128:256], in_=t2[:, 0:128], identity=ident[:]), r=[B("t2"), cB], w=[B("ps", 5)])
                S(lambda e, l=l, s8=s8: e.activation(out=WB[:, s8 * 2:s8 * 2 + 2, :], in_=ps[5][:, 0:256].rearrange("p (c k) -> p c k", c=2), func=AF.Copy), r=[B("ps", 5)], w=[B("WB")])
            Q(lambda e, l=l: e.dma_start(out=big[:, 0:2048], in_=Cn_d[l, :, :]), w=[B("big")])
            V(lambda e, l=l: e.tensor_copy(out=WC[:, 0:16, :].rearrange("p (s c) k -> p s c k", c=2)[:, :, 0, :], in_=bg[:, :, 0, :]), r=[B("big")], w=[B("WC")])
            V(lambda e, l=l: e.tensor_scalar(out=WC[:, 0:16, :].rearrange("p (s c) k -> p s c k", c=2)[:, :, 1, :], in0=bg[:, :, 1, :], scalar1=-1.0, scalar2=None, op0=ALU.mult), r=[B("big")], w=[B("WC")])
            for ct in range(2):
                V(lambda e, l=l, ct=ct: e.tensor_scalar(out=Dd[:, l * 2 + ct, :], in0=ident[:], scalar1=pvc(l, OSD + ct), scalar2=None, op0=ALU.mult), r=[cB], w=[B("Dd", l)])
            G(lambda e, l=l: e.dma_start(out=gluw[:, l * 2:l * 2 + 2, :], in_=glu_w[l].rearrange("(c p) n -> p c n", p=128)), w=[B("gluw", l)], ch="misc")
            a8 = big2.rearrange("p (s j) -> p s j", s=8)
            for s8 in range(8):
                V(lambda e, s8=s8: e.tensor_scalar(out=a8[:, s8, :], in0=jj[:], scalar1=pl[:, 2, s8:s8 + 1], scalar2=1.0 / TWO_PI, op0=ALU.mult, op1=ALU.mult), r=[pB, cB], w=[B("big2"), B("rot")])
            rt = big.rearrange("p (s c j) -> p s c j", s=8, c=2)
            for ci, sh in ((1, 0.0), (0, 0.25)):
                if sh:
                    V(lambda e, sh=sh: e.tensor_scalar(out=big2, in0=big2, scalar1=sh, scalar2=None, op0=ALU.add), r=[B("big2")], w=[B("big2"), B("rot")])
                V(lambda e: e.tensor_copy(out=bigi, in_=big2), r=[B("big2"), B("rot")], w=[B("bigi")])
                V(lambda e: e.tensor_copy(out=rot[:].rearrange("p s c j -> p (s c j)")[:, 0:8 * J], in_=bigi), r=[B("bigi")], w=[B("rot")])
                V(lambda e: e.tensor_tensor(out=rot[:].rearrange("p s c j -> p (s c j)")[:, 0:8 * J], in0=big2, in1=rot[:].rearrange("p s c j -> p (s c j)")[:, 0:8 * J], op=ALU.subtract), r=[B("big2"), B("rot")], w=[B("rot")])
                S(lambda e, ci=ci: e.activation(out=rt[:, :, ci, :], in_=rot[:].rearrange("p s c j -> p (s c j)")[:, 0:8 * J].rearrange("p (s j) -> p s j", s=8), func=AF.Sin, scale=6.283185), r=[B("rot")], w=[B("big")])
            Q(lambda e, l=l: e.dma_start(out=rot_d[l, :, :], in_=big), r=[B("big")], w=[B("rot_d", l)])
            Q(lambda e, l=l: e.dma_start(out=wb_d[l, :, :], in_=WB[:].rearrange("p a k -> p (a k)")), r=[B("WB")], w=[B("wb_d", l)])
            Q(lambda e, l=l: e.dma_start(out=wc_d[l, :, :], in_=WC[:].rearrange("p a k -> p (a k)")), r=[B("WC")], w=[B("wc_d", l)])

        print('MARK prologue_end', P.nops, flush=True)
        slot_i = [0]

        def wload(src_ap, shape_str, **kw):
            s = slot_i[0] % NSLOT
            slot_i[0] += 1
            n = 1
            for d_ in src_ap.shape[1:]:
                n *= d_
            dst = ring[:, s, 0:n]
            if shape_str:
                dst = dst.rearrange(shape_str, **kw)
            G(lambda e: e.dma_start(out=dst, in_=src_ap), w=[B("slot", s)])
            return s

        def rmsnorm(l, goff, ccs):
            for (c0, cn) in ccs:
                pb_ = ps[6]
                for k in range(8):
                    S(lambda e, k=k: e.activation(out=sq[:, 0:cn], in_=x[:, k, c0:c0 + cn], func=AF.Square), r=[B("x", c0)], w=[B("sq")])
                    PE(lambda e, k=k: e.matmul(pb_[:, 0:cn], lhsT=ones_b[:], rhs=sq[:, 0:cn], start=(k == 0), stop=(k == 7)), r=[B("sq"), cB], w=[B("ps", 6)])
                S(lambda e: e.activation(out=rstd[:, 0:cn], in_=pb_[:, 0:cn], func=AF.Sqrt, scale=1.0 / D, bias=EPS), r=[B("ps", 6)], w=[B("rstd")])
                V(lambda e: e.reciprocal(out=rstd[:, 0:cn], in_=rstd[:, 0:cn]), r=[B("rstd")], w=[B("rstd")])
                for k in range(8):
                    gk = pvt[:, goff + k: goff + k + 1]
                    V(lambda e, k=k, gk=gk: e.scalar_tensor_tensor(out=hb[:, k, c0:c0 + cn], in0=x[:, k, c0:c0 + cn], scalar=gk, in1=rstd[:, 0:cn], op0=ALU.mult, op1=ALU.mult), r=[B("x", c0), B("rstd"), cB], w=[B("hb", c0)])

        for ch in range(nch):
            first, last = (ch == 0), (ch == nch - 1)
            NT = NTM if first else T
            ccs = [(0, 512)] + ([(T, NS)] if first else [])
            pcs = [(0, 512), (512, 512)]
            for k in range(8):
                Q(lambda e, k=k: e.dma_start(out=x[:, k, 0:T], in_=xT[k * 128:(k + 1) * 128, ch * T:(ch + 1) * T]), w=[B("x", 0), B("big")])
            if first:
                Q(lambda e: e.dma_start(out=x[:, :, T:NTM], in_=xsT.rearrange("(k p) n -> p k n", p=128)), w=[B("x", T), B("big")])
            for l in range(L):
                Q(lambda e, l=l: e.dma_start(out=rot[:].rearrange("p s c j -> p (s c j)"), in_=rot_d[l, :, :]), r=[B("rot_d", l)], w=[B("rot")])
                Q(lambda e, l=l: e.dma_start(out=WB[:].rearrange("p a k -> p (a k)"), in_=wb_d[l, :, :]), r=[B("wb_d", l)], w=[B("WB")])
                Q(lambda e, l=l: e.dma_start(out=WC[:].rearrange("p a k -> p (a k)"), in_=wc_d[l, :, :]), r=[B("wc_d", l)], w=[B("WC")])
                s_in = [wload(w_in[l, :, i * 512:(i + 1) * 512].rearrange("(k p) n -> p k n", p=128), "p (k n) -> p k n", k=8) for i in range(3)]

                def win(o_lo, o_n):
                    s = s_in[o_lo // 512]
                    off = o_lo % 512
                    return lambda k: ring[:, s, k * 512 + off: k * 512 + off + o_n], B("slot", s)

                V(lambda e, l=l: e.tensor_copy(out=kT[:, :, 0:128], in_=khalo[:, l, :, :]), r=[B("khalo", l)], w=[B("kT", "h")])
                V(lambda e, l=l: e.tensor_copy(out=vp[:, 0, :, :], in_=vhalo[:, l, :, :]), r=[B("vhalo", l)], w=[B("vp", 0)])
                V(lambda e, l=l: e.tensor_copy(out=ucv[:, :, 0:30], in_=chalo[:, l, :, :]), r=[B("chalo", l)], w=[B("ucv", "h")])

                print('MARK norm1', P.nops, flush=True)
                rmsnorm(l, l * PVL + OG1, ccs)
                print('MARK win', P.nops, flush=True)
                for (c0, cn) in ccs:
                    def mm8(lw, n_out, pbk):
                        f, sb_ = lw
                        for k in range(8):
                            PE(lambda e, k=k: e.matmul(ps[pbk][0:n_out, 0:cn], lhsT=f(k), rhs=hb[:, k, c0:c0 + cn], start=(k == 0), stop=(k == 7)), r=[sb_, B("hb", c0)], w=[B("ps", pbk)])
                    for h in range(8):
                        pbk = bank(); mm8(win(64 * h, 64), 64, pbk)
                        S(lambda e, h=h, pbk=pbk: e.activation(out=qT[:, h, c0:c0 + cn], in_=ps[pbk][0:64, 0:cn], func=AF.Copy, scale=0.125), r=[B("ps", pbk)], w=[B("qT", c0)])
                    for g in range(2):
                        pbk = bank(); mm8(win(512 + 64 * g, 64), 64, pbk)
                        S(lambda e, g=g, pbk=pbk: e.activation(out=kT[:, g, 128 + c0:128 + c0 + cn], in_=ps[pbk][0:64, 0:cn], func=AF.Copy), r=[B("ps", pbk)], w=[B("kT", c0)])
                    for ct in range(2):
                        pa = bank(); mm8(win(768 + 128 * ct, 128), 128, pa)
                        pg = bank(); mm8(win(1024 + 128 * ct, 128), 128, pg)
                        S(lambda e, pg=pg: e.activation(out=g2[:, 0:cn], in_=ps[pg][:, 0:cn], func=AF.Sigmoid), r=[B("ps", pg)], w=[B("g2")])
                        V(lambda e, ct=ct, pa=pa: e.tensor_tensor(out=ucv[:, ct, 30 + c0:30 + c0 + cn], in0=ps[pa][:, 0:cn], in1=g2[:, 0:cn], op=ALU.mult), r=[B("ps", pa), B("g2")], w=[B("ucv", c0)])
                    for ct in range(2):
                        pbk = bank(); mm8(win(1280 + 128 * ct, 128), 128, pbk)
                        S(lambda e, ct=ct, pbk=pbk: e.activation(out=us[:, ct, c0:c0 + cn], in_=ps[pbk][:, 0:cn], func=AF.Copy), r=[B("ps", pbk)], w=[B("us", c0)])
                fv, sv = win(640, 128)
                fk, sk = win(512, 128)
                for bi in range(4):
                    c0 = bi * 128
                    cc0 = (c0 // 512) * 512
                    pbk = bank()
                    for k in range(8):
                        PE(lambda e, k=k: e.matmul(ps[pbk][:, 0:128], lhsT=hb[:, k, c0:c0 + 128], rhs=fv(k), start=(k == 0), stop=(k == 7)), r=[sv, B("hb", cc0)], w=[B("ps", pbk)])
                    S(lambda e, bi=bi, pbk=pbk: e.activation(out=vp[:, bi + 1, :, 64:128], in_=ps[pbk][:, 0:128].rearrange("p (g d) -> p g d", g=2), func=AF.Copy), r=[B("ps", pbk)], w=[B("vp", bi + 1)])
                    if last and bi == 3:
                        V(lambda e, pbk=pbk: e.tensor_copy(out=tk[:], in_=ps[pbk][:, 0:128]), r=[B("ps", pbk), B("vp", bi + 1)], w=[B("tk")])
                        Q(lambda e, l=l: e.dma_start(out=ov_p[l, :, :], in_=tk[:]), r=[B("tk")], w=[B("ov_p", l)])
                        pbk2 = bank()
                        for k in range(8):
                            PE(lambda e, k=k: e.matmul(ps[pbk2][:, 0:128], lhsT=hb[:, k, c0:c0 + 128], rhs=fk(k), start=(k == 0), stop=(k == 7)), r=[sk, B("hb", cc0)], w=[B("ps", pbk2)])
                        V(lambda e, pbk2=pbk2: e.tensor_copy(out=tk[:], in_=ps[pbk2][:, 0:128]), r=[B("ps", pbk2)], w=[B("tk")])
                        Q(lambda e, l=l: e.dma_start(out=ok_p[l, :, :], in_=tk[:]), r=[B("tk")], w=[B("ok_p", l)])
                if first:
                    pbk = bank()
                    for (f_, s_, off) in ((fk, sk, 0), (fv, sv, 128)):
                        for k in range(8):
                            PE(lambda e, k=k, f_=f_, off=off: e.matmul(ps[pbk][0:NS, off:off + 128], lhsT=hb[:, k, T:NTM], rhs=f_(k), start=(k == 0), stop=(k == 7)), r=[s_, B("hb", T)], w=[B("ps", pbk)])
                    V(lambda e, pbk=pbk: e.tensor_copy(out=tkb[:].rearrange("p a k -> p (a k)"), in_=ps[pbk][0:NS, 0:256]), r=[B("ps", pbk)], w=[B("tkb")])
                    for b in range(NS):
                        Q(lambda e, l=l, b=b: e.dma_start(out=ok_s[l, b:b + 1, 0:127, :].rearrange("b r c -> b (r c)"), in_=ck_d[l, b:b + 1, 1:128, :].rearrange("b r c -> b (r c)")), w=[B("ok_s", l)])
                        Q(lambda e, l=l, b=b: e.dma_start(out=ov_s[l, b:b + 1, 0:127, :].rearrange("b r c -> b (r c)"), in_=cv_d[l, b:b + 1, 1:128, :].rearrange("b r c -> b (r c)")), w=[B("ov_s", l)])
                    Q(lambda e, l=l: e.dma_start(out=ok_s[l, :, 127, :], in_=tkb[:, 0, :]), r=[B("tkb")], w=[B("ok_s", l)])
                    Q(lambda e, l=l: e.dma_start(out=ov_s[l, :, 127, :], in_=tkb[:, 1, :]), r=[B("tkb")], w=[B("ov_s", l)])
                if not last:
                    V(lambda e, l=l: e.tensor_copy(out=khalo[:, l, :, :], in_=kT[:, :, T:T + 128]), r=[B("kT", 0)], w=[B("khalo", l)])
                    V(lambda e, l=l: e.tensor_copy(out=vhalo[:, l, :, :], in_=vp[:, 4, :, :]), r=[B("vp", 4)], w=[B("vhalo", l)])
                    V(lambda e, l=l: e.tensor_copy(out=chalo[:, l, :, :], in_=ucv[:, :, T:T + 30]), r=[B("ucv", 0)], w=[B("chalo", l)])
                else:
                    Q(lambda e, l=l: e.dma_start(out=oconv_p[l, :, :].rearrange("p (c k) -> p c k", c=2), in_=ucv[:, :, T:T + 30]), r=[B("ucv", 0)], w=[B("oconv_p", l)])

                print('MARK attn', P.nops, flush=True)
                for bi in range(4):
                    q0 = bi * 128
                    qcc = (q0 // 512) * 512
                    nokprev = first and bi == 0
                    kread = [B("kT", qcc)] + ([B("kT", "h")] if bi == 0 else [B("kT", ((q0 - 128) // 512) * 512)])
                    for tile in range(4):
                        for r2 in range(2):
                            h = tile * 2 + r2
                            g = h // 4
                            a = h % 2
                            k_lo, k_n = (128, 128) if nokprev else (0, 256)
                            PE(lambda e, h=h, g=g, k_lo=k_lo, k_n=k_n: e.matmul(ps[4][:, k_lo:k_lo + k_n], lhsT=qT[:, h, q0:q0 + 128], rhs=kT[:, g, q0 + k_lo:q0 + k_lo + k_n], start=True, stop=True), r=[B("qT", qcc)] + kread, w=[B("ps", 4)])
                            V(lambda e, h=h, a=a, k_lo=k_lo, k_n=k_n: e.tensor_tensor(out=sc[:, a, k_lo:k_lo + k_n], in0=ps[4][:, k_lo:k_lo + k_n], in1=abias[:, h, k_lo:k_lo + k_n], op=ALU.add), r=[B("ps", 4), cB], w=[B("sc", a)])
                            V(lambda e, a=a, k_lo=k_lo, k_n=k_n: e.reduce_max(out=sm[:, a, 0:1], in_=sc[:, a, k_lo:k_lo + k_n], axis=AX.X), r=[B("sc", a)], w=[B("sm", a)])
                            sinkc = pvc(l, OSINK + h)
                            V(lambda e, a=a, sinkc=sinkc: e.tensor_scalar(out=sm[:, a, 1:2], in0=sm[:, a, 0:1], scalar1=sinkc, scalar2=-1.0, op0=ALU.max, op1=ALU.mult), r=[B("sm", a), cB], w=[B("sm", a)])
                            S(lambda e, a=a, k_lo=k_lo, k_n=k_n: e.activation(out=pb[:, a, k_lo:k_lo + k_n], in_=sc[:, a, k_lo:k_lo + k_n], func=AF.Exp, bias=sm[:, a, 1:2], accum_out=sm[:, a, 2:3]), r=[B("sc", a), B("sm", a)], w=[B("pb", a), B("sm", a)])
                            S(lambda e, a=a, sinkc=sinkc: e.activation(out=sm[:, a, 3:4], in_=sinkc, func=AF.Exp, bias=sm[:, a, 1:2]), r=[B("sm", a), cB], w=[B("sm", a)])
                            V(lambda e, a=a: e.tensor_tensor(out=sm[:, a, 4:5], in0=sm[:, a, 2:3], in1=sm[:, a, 3:4], op=ALU.add), r=[B("sm", a)], w=[B("sm", a)])
                            V(lambda e, a=a: e.reciprocal(out=sm[:, a, 5:6], in_=sm[:, a, 4:5]), r=[B("sm", a)], w=[B("sm", a)])
                            V(lambda e, a=a, k_lo=k_lo, k_n=k_n: e.tensor_scalar(out=pb[:, a, k_lo:k_lo + k_n], in0=pb[:, a, k_lo:k_lo + k_n], scalar1=sm[:, a, 5:6], scalar2=None, op0=ALU.mult), r=[B("sm", a), B("pb", a)], w=[B("pb", a)])
                            for kb in range(2):
                                if nokprev and kb == 0:
                                    continue
                                PE(lambda e, a=a, kb=kb: e.transpose(out=psb[:, a * 256 + kb * 128:a * 256 + kb * 128 + 128], in_=pb[:, a, kb * 128:kb * 128 + 128], identity=identb[:]), r=[B("pb", a), B("identb")], w=[B("psb", 0)])
                            S(lambda e, a=a, k_lo=k_lo, k_n=k_n: e.activation(out=ptb[:, a, k_lo:k_lo + k_n], in_=psb[:, a * 256 + k_lo:a * 256 + k_lo + k_n], func=AF.Copy), r=[B("psb", 0)], w=[B("ptb", a)])
                            kbs = [1] if nokprev else [0, 1]
                            for kb in kbs:
                                lo = 64 if r2 == 0 else 0
                                PE(lambda e, a=a, kb=kb, g=g, lo=lo, r2=r2, kbs=kbs: e.matmul(ps[5][:, 0:128], lhsT=vp[:, bi + kb, g, lo:lo + 128], rhs=ptb[:, a, kb * 128:kb * 128 + 128], start=(r2 == 0 and kb == kbs[0]), stop=(r2 == 1 and kb == 1)), r=[B("vp", bi + kb), B("ptb", a)], w=[B("ps", 5)])
                        S(lambda e, tile=tile: e.activation(out=hb[:, tile, q0:q0 + 128], in_=ps[5][:, 0:128], func=AF.Copy), r=[B("ps", 5)], w=[B("hb", qcc)])

                print('MARK sattn', P.nops, flush=True)
                if first:
                    for g in range(2):
                        sk4 = sinkt[:, l * 2 + g:l * 2 + g + 1]
                        for b4 in range(NS // 4):
                            for bb in range(4):
                                b = b4 * 4 + bb
                                Q(lambda e, b=b, l=l: e.dma_start(out=Kb[:, b % 2, :], in_=ok_s[l, b, :, :]), r=[B("ok_s", l)], w=[B("Kb", b % 2)])
                                PE(lambda e, b=b, g=g: e.transpose(out=ps[4][0:64, 0:128], in_=Kb[:, b % 2, g * 64:(g + 1) * 64], identity=ident[:]), r=[B("Kb", b % 2), cB], w=[B("ps", 4)])
                                S(lambda e, b=b: e.activation(out=KbT[:, b % 2, :], in_=ps[4][0:64, 0:128], func=AF.Copy), r=[B("ps", 4)], w=[B("KbT", b % 2)])
                                PE(lambda e, b=b, g=g, bb=bb: e.matmul(ps[6][0:4, bb * 128:bb * 128 + 128], lhsT=qT[:, 4 * g:4 * g + 4, T + b], rhs=KbT[:, b % 2, :], start=True, stop=True), r=[B("qT", T), B("KbT", b % 2)], w=[B("ps", 6)])
                            V(lambda e, g=g: e.tensor_tensor(out=ssc[:], in0=ps[6][0:4, :].rearrange("p (b k) -> p b k", b=4), in1=sbias[:, g:g + 1, :].to_broadcast([4, 4, 128]), op=ALU.add), r=[B("ps", 6), cB], w=[B("ssc")])
                            V(lambda e: e.reduce_max(out=ssm_[:, 0, :], in_=ssc[:], axis=AX.X), r=[B("ssc")], w=[B("ssm_")])
                            V(lambda e, sk4=sk4: e.tensor_scalar(out=ssm_[:, 0, :], in0=ssm_[:, 0, :], scalar1=sk4, scalar2=None, op0=ALU.max), r=[B("ssm_"), cB], w=[B("ssm_")])
                            V(lambda e: e.tensor_tensor(out=ssc[:], in0=ssc[:], in1=ssm_[:, 0, :].unsqueeze(2).to_broadcast([4, 4, 128]), op=ALU.subtract), r=[B("ssc"), B("ssm_")], w=[B("ssc")])
                            S(lambda e: e.activation(out=ssc[:], in_=ssc[:], func=AF.Exp), r=[B("ssc")], w=[B("ssc")])
                            V(lambda e: e.reduce_sum(out=ssm_[:, 1, :], in_=ssc[:], axis=AX.X), r=[B("ssc")], w=[B("ssm_")])
                            S(lambda e, sk4=sk4: e.activation(out=ssm_[:, 2, :], in_=ssm_[:, 0, :], func=AF.Exp, scale=-1.0, bias=sk4), r=[B("ssm_"), cB], w=[B("ssm_")])
                            V(lambda e: e.tensor_tensor(out=ssm_[:, 1, :], in0=ssm_[:, 1, :], in1=ssm_[:, 2, :], op=ALU.add), r=[B("ssm_")], w=[B("ssm_")])
                            V(lambda e: e.reciprocal(out=ssm_[:, 1, :], in_=ssm_[:, 1, :]), r=[B("ssm_")], w=[B("ssm_")])
                            V(lambda e: e.tensor_tensor(out=spb[:], in0=ssc[:], in1=ssm_[:, 1, :].unsqueeze(2).to_broadcast([4, 4, 128]), op=ALU.mult), r=[B("ssc"), B("ssm_")], w=[B("spb")])
                            for bb in range(4):
                                PE(lambda e, bb=bb: e.transpose(out=psb[:, 512 + bb * 4:512 + bb * 4 + 4], in_=spb[:, bb, :], identity=identb[0:4, 0:4]), r=[B("spb"), B("identb")], w=[B("psb", 0)])
                            S(lambda e, g=g, b4=b4: e.activation(out=sptb[:, g, b4 * 4:b4 * 4 + 4, :], in_=psb[:, 512:512 + 16].rearrange("p (b r) -> p b r", r=4), func=AF.Copy), r=[B("psb", 0)], w=[B("sptb", g)])
                    for b in range(NS):
                        Q(lambda e, b=b, l=l: e.dma_start(out=Vb[:, b % 2, :], in_=ov_s[l, b, :, :]), r=[B("ov_s", l)], w=[B("Vb", b % 2)])
                        V(lambda e, b=b: e.tensor_copy(out=Vbp[:, :, 64:128], in_=Vb[:, b % 2, :].rearrange("p (g d) -> p g d", g=2)), r=[B("Vb", b % 2)], w=[B("Vbp")])
                        for tile in range(4):
                            g = tile // 2
                            for r2 in range(2):
                                rr4 = (tile % 2) * 2 + r2
                                lo = 64 if r2 == 0 else 0
                                PE(lambda e, b=b, g=g, tile=tile, r2=r2, rr4=rr4, lo=lo: e.matmul(ps[5][:, 256 + b * 4 + tile:256 + b * 4 + tile + 1], lhsT=Vbp[:, g, lo:lo + 128], rhs=sptb[:, g, b, rr4:rr4 + 1], start=(r2 == 0), stop=(r2 == 1)), r=[B("Vbp"), B("sptb", g)], w=[B("ps", 5)])
                    S(lambda e: e.activation(out=hb[:, 0:4, T:NTM].rearrange("p t b -> p b t"), in_=ps[5][:, 256:256 + NS * 4].rearrange("p (b t) -> p b t", t=4), func=AF.Copy), r=[B("ps", 5)], w=[B("hb", T)])

                print('MARK conv', P.nops, flush=True)
                if first:
                    Q(lambda e, l=l: e.dma_start(out=cs[:, :, :, 0:30], in_=cconv_d[l, :, :].rearrange("p (c b k) -> p c b k", c=2, b=NS)), w=[B("cs")])
                    V(lambda e: e.tensor_copy(out=cs[:, :, :, 30], in_=ucv[:, :, 30 + T:30 + NTM]), r=[B("ucv", T)], w=[B("cs")])
                    Q(lambda e, l=l: e.dma_start(out=oconv_s[l, :, :].rearrange("p (c b k) -> p c b k", c=2, b=NS), in_=cs[:, :, :, 1:31]), r=[B("cs")], w=[B("oconv_s", l)])
                    for ct in range(2):
                        cw = pvt[:, l * PVL + OCW + ct * 31: l * PVL + OCW + ct * 31 + 31]
                        V(lambda e, ct=ct, cw=cw: e.tensor_tensor(out=cst[:, ct, :, :], in0=cs[:, ct, :, :], in1=cw.unsqueeze(1).to_broadcast([128, NS, 31]), op=ALU.mult), r=[B("cs"), cB], w=[B("cst")])
                        V(lambda e, ct=ct: e.reduce_sum(out=acc[:, ct, T:NTM], in_=cst[:, ct, :, :], axis=AX.X), r=[B("cst")], w=[B("acc", T)])
                        V(lambda e, ct=ct: e.tensor_scalar(out=acc[:, ct, T:NTM], in0=acc[:, ct, T:NTM], scalar1=pvc(l, OCB + ct), scalar2=None, op0=ALU.add), r=[B("acc", T), cB], w=[B("acc", T)])
                for (c0, cn) in ccs:
                    if c0 < T:
                        hr = [B("ucv", c0)] + ([B("ucv", "h")] if c0 == 0 else [B("ucv", c0 - 512)])
                        for ct in range(2):
                            for kk in range(31):
                                wk = pvt[:, l * PVL + OCW + ct * 31 + kk: l * PVL + OCW + ct * 31 + kk + 1]
                                if kk == 0:
                                    V(lambda e, ct=ct, wk=wk: e.tensor_scalar(out=acc[:, ct, c0:c0 + cn], in0=ucv[:, ct, c0:c0 + cn], scalar1=wk, scalar2=pvc(l, OCB + ct), op0=ALU.mult, op1=ALU.add), r=hr + [cB], w=[B("acc", c0)])
                                else:
                                    V(lambda e, ct=ct, wk=wk, kk=kk: e.scalar_tensor_tensor(out=acc[:, ct, c0:c0 + cn], in0=ucv[:, ct, c0 + kk:c0 + kk + cn], scalar=wk, in1=acc[:, ct, c0:c0 + cn], op0=ALU.mult, op1=ALU.add), r=hr + [cB, B("acc", c0)], w=[B("acc", c0)])
                    for ct in range(2):
                        S(lambda e, ct=ct: e.activation(out=ysq[:, ct, 0:cn], in_=acc[:, ct, c0:c0 + cn], func=AF.Square), r=[B("acc", c0)], w=[B("ysm", 0), B("ysm", 1)])
                    for ct in range(2):
                        PE(lambda e, ct=ct: e.matmul(ps[6][:, 0:cn], lhsT=ones_f[:], rhs=acc[:, ct, c0:c0 + cn], start=(ct == 0), stop=(ct == 1)), r=[B("acc", c0), cB], w=[B("ps", 6)])
                    V(lambda e: e.tensor_scalar(out=mean[:, 0:cn], in0=ps[6][:, 0:cn], scalar1=1.0 / 256, scalar2=None, op0=ALU.mult), r=[B("ps", 6)], w=[B("mean")])
                    for ct in range(2):
                        PE(lambda e, ct=ct: e.matmul(ps[6][:, 0:cn], lhsT=ones_f[:], rhs=ysq[:, ct, 0:cn], start=(ct == 0), stop=(ct == 1)), r=[B("ysm", 0), B("ysm", 1), cB], w=[B("ps", 6)])
                    V(lambda e: e.tensor_tensor(out=var[:, 0:cn], in0=mean[:, 0:cn], in1=mean[:, 0:cn], op=ALU.mult), r=[B("mean")], w=[B("var")])
                    V(lambda e: e.scalar_tensor_tensor(out=var[:, 0:cn], in0=ps[6][:, 0:cn], scalar=1.0 / 256, in1=var[:, 0:cn], op0=ALU.mult, op1=ALU.subtract), r=[B("ps", 6), B("var")], w=[B("var")])
                    S(lambda e: e.activation(out=var[:, 0:cn], in_=var[:, 0:cn], func=AF.Sqrt, bias=EPS), r=[B("var")], w=[B("var")])
                    V(lambda e: e.reciprocal(out=var[:, 0:cn], in_=var[:, 0:cn]), r=[B("var")], w=[B("var")])
                    for ct in range(2):
                        V(lambda e, ct=ct: e.tensor_tensor(out=g1[:, 0:cn], in0=acc[:, ct, c0:c0 + cn], in1=mean[:, 0:cn], op=ALU.subtract), r=[B("acc", c0), B("mean")], w=[B("g1")])
                        V(lambda e, ct=ct: e.tensor_tensor(out=g1[:, 0:cn], in0=g1[:, 0:cn], in1=var[:, 0:cn], op=ALU.mult), r=[B("g1"), B("var")], w=[B("g1")])
                        V(lambda e, ct=ct: e.tensor_scalar(out=g1[:, 0:cn], in0=g1[:, 0:cn], scalar1=pvc(l, OLG + ct), scalar2=pvc(l, OLB + ct), op0=ALU.mult, op1=ALU.add), r=[B("g1"), cB], w=[B("g1")])
                        S(lambda e, ct=ct: e.activation(out=hb[:, 4 + ct, c0:c0 + cn], in_=g1[:, 0:cn], func=AF.Silu), r=[B("g1")], w=[B("hb", c0)])

                def ssm_out(c0, cn, hsrc_re, hsrc_im, hoff, hbufs):
                    for ct in range(2):
                        for j4 in range(4):
                            s8 = ct * 4 + j4
                            PE(lambda e, ct=ct, s8=s8, j4=j4: e.matmul(ps[6][:, 0:cn], lhsT=WC[:, s8 * 2, :], rhs=hsrc_re(s8), start=(j4 == 0), stop=False), r=hbufs + [B("WC")], w=[B("ps", 6)])
                            PE(lambda e, ct=ct, s8=s8: e.matmul(ps[6][:, 0:cn], lhsT=WC[:, s8 * 2 + 1, :], rhs=hsrc_im(s8), start=False, stop=False), r=hbufs + [B("WC")], w=[B("ps", 6)])
                        PE(lambda e, ct=ct: e.matmul(ps[6][:, 0:cn], lhsT=Dd[:, l * 2 + ct, :], rhs=us[:, ct, c0:c0 + cn], start=False, stop=True), r=[B("us", (c0 // 512) * 512 if c0 < T else T), B("Dd", l)], w=[B("ps", 6)])
                        V(lambda e, ct=ct: e.tensor_copy(out=ysm[:, ct, 0:cn], in_=ps[6][:, 0:cn]), r=[B("ps", 6)], w=[B("ysm", ct)])
                        V(lambda e, ct=ct: e.tensor_tensor(out=g1[:, 0:cn], in0=ysm[:, ct, 0:cn], in1=ysm[:, ct, 0:cn], op=ALU.mult), r=[B("ysm", ct)], w=[B("g1")])
                        V(lambda e, ct=ct: e.tensor_scalar(out=g1[:, 0:cn], in0=g1[:, 0:cn], scalar1=0.044715, scalar2=1.0, op0=ALU.mult, op1=ALU.add), r=[B("g1")], w=[B("g1")])
                        V(lambda e, ct=ct: e.tensor_tensor(out=g1[:, 0:cn], in0=g1[:, 0:cn], in1=ysm[:, ct, 0:cn], op=ALU.mult), r=[B("g1"), B("ysm", ct)], w=[B("g1")])
                        S(lambda e, ct=ct: e.activation(out=g2[:, 0:cn], in_=g1[:, 0:cn], func=AF.Sigmoid, scale=2.0 * math.sqrt(2.0 / math.pi)), r=[B("g1")], w=[B("g2")])
                        V(lambda e, ct=ct: e.tensor_tensor(out=ysm[:, ct, 0:cn], in0=ysm[:, ct, 0:cn], in1=g2[:, 0:cn], op=ALU.mult), r=[B("g2"), B("ysm", ct)], w=[B("ysm", ct)])
                        V(lambda e, ct=ct: e.tensor_copy(out=yb[:, ct, 0:cn], in_=ysm[:, ct, 0:cn]), r=[B("ysm", ct)], w=[B("yb", ct)])
                    for co in range(2):
                        for ct in range(2):
                            PE(lambda e, ct=ct, co=co: e.matmul(ps[6][:, 0:cn], lhsT=gluw[:, l * 2 + ct, co * 128:(co + 1) * 128], rhs=yb[:, ct, 0:cn], start=(ct == 0), stop=(ct == 1)), r=[B("yb", 0), B("yb", 1), B("gluw", l)], w=[B("ps", 6)])
                        S(lambda e, co=co: e.activation(out=g2[:, 0:cn], in_=ps[6][:, 0:cn], func=AF.Sigmoid, bias=pvc(l, OGB + co)), r=[B("ps", 6), cB], w=[B("g2")])
                        V(lambda e, co=co: e.tensor_tensor(out=hb[:, 6 + co, c0:c0 + cn], in0=ysm[:, co, 0:cn], in1=g2[:, 0:cn], op=ALU.mult), r=[B("g2"), B("ysm", co)], w=[B("hb", (c0 // 512) * 512 if c0 < T else T)])

                print('MARK ssm', P.nops, flush=True)
                for sc_i in range(T // J):
                    c0 = sc_i * J
                    ucc = (c0 // 512) * 512
                    for s8 in range(8):
                        ct, a = s8 // 4, s8 % 2
                        for ci, dst in ((0, 0), (1, J)):
                            PE(lambda e, ci=ci, dst=dst, s8=s8, ct=ct: e.matmul(ps[4][:, dst:dst + J], lhsT=WB[:, s8 * 2 + ci, :], rhs=us[:, ct, c0:c0 + J], start=True, stop=True), r=[B("us", ucc), B("WB")], w=[B("ps", 4)])
                        cosT, sinT = rot[:, s8, 0, :], rot[:, s8, 1, :]
                        pre, pim = ps[4][:, 0:J], ps[4][:, J:2 * J]
                        V(lambda e, cosT=cosT, pre=pre: e.tensor_tensor(out=bre[:], in0=pre, in1=cosT, op=ALU.mult), r=[B("ps", 4), B("rot")], w=[B("bre")])
                        V(lambda e, sinT=sinT, pim=pim: e.tensor_tensor(out=t1[:], in0=pim, in1=sinT, op=ALU.mult), r=[B("ps", 4), B("rot")], w=[B("t1")])
                        V(lambda e: e.tensor_tensor(out=bre[:], in0=bre[:], in1=t1[:], op=ALU.add), r=[B("t1"), B("bre")], w=[B("bre")])
                        V(lambda e, cosT=cosT, pim=pim: e.tensor_tensor(out=bim[:], in0=pim, in1=cosT, op=ALU.mult), r=[B("ps", 4), B("rot")], w=[B("bim")])
                        V(lambda e, sinT=sinT, pre=pre: e.tensor_tensor(out=t2[:], in0=pre, in1=sinT, op=ALU.mult), r=[B("ps", 4), B("rot")], w=[B("t2")])
                        V(lambda e: e.tensor_tensor(out=bim[:], in0=bim[:], in1=t2[:], op=ALU.subtract), r=[B("t2"), B("bim")], w=[B("bim")])
                        rho_b = lam[:, l, 0, s8:s8 + 1].to_broadcast([128, J])
                        V(lambda e, rho_b=rho_b, s8=s8: e.tensor_tensor_scan(out=wre[:], data0=rho_b, data1=bre[:], initial=hcar[:, l, s8, 0:1], op0=ALU.mult, op1=ALU.add), r=[B("bre"), B("lam", l), B("hcar", l)], w=[B("wre")])
                        V(lambda e, rho_b=rho_b, s8=s8: e.tensor_tensor_scan(out=wim[:], data0=rho_b, data1=bim[:], initial=hcar[:, l, s8, 1:2], op0=ALU.mult, op1=ALU.add), r=[B("bim"), B("lam", l), B("hcar", l)], w=[B("wim")])
                        V(lambda e, cosT=cosT: e.tensor_tensor(out=t1[:], in0=wre[:], in1=cosT, op=ALU.mult), r=[B("wre"), B("rot")], w=[B("t1")])
                        V(lambda e, sinT=sinT: e.tensor_tensor(out=t2[:], in0=wim[:], in1=sinT, op=ALU.mult), r=[B("wim"), B("rot")], w=[B("t2")])
                        V(lambda e, s8=s8: e.tensor_tensor(out=hbf[:, s8 * J:(s8 + 1) * J], in0=t1[:], in1=t2[:], op=ALU.subtract), r=[B("t1"), B("t2")], w=[B("hbf")])
                        V(lambda e, sinT=sinT: e.tensor_tensor(out=t1[:], in0=wre[:], in1=sinT, op=ALU.mult), r=[B("wre"), B("rot")], w=[B("t1")])
                        V(lambda e, cosT=cosT: e.tensor_tensor(out=t2[:], in0=wim[:], in1=cosT, op=ALU.mult), r=[B("wim"), B("rot")], w=[B("t2")])
                        V(lambda e, s8=s8: e.tensor_tensor(out=hbf[:, 8 * J + s8 * J:8 * J + (s8 + 1) * J], in0=t1[:], in1=t2[:], op=ALU.add), r=[B("t1"), B("t2")], w=[B("hbf")])
                        cl, sl = rot[:, s8, 0, J - 1:J], rot[:, s8, 1, J - 1:J]
                        V(lambda e, sl=sl: e.tensor_tensor(out=sm[:, 0, 6:7], in0=wim[:, J - 1:J], in1=sl, op=ALU.mult), r=[B("wim"), B("rot")], w=[B("sm", 0)])
                        V(lambda e, s8=s8, cl=cl: e.scalar_tensor_tensor(out=hcar[:, l, s8, 0:1], in0=wre[:, J - 1:J], scalar=cl, in1=sm[:, 0, 6:7], op0=ALU.mult, op1=ALU.subtract), r=[B("wre"), B("rot"), B("sm", 0)], w=[B("hcar", l)])
                        V(lambda e, sl=sl: e.tensor_tensor(out=sm[:, 0, 7:8], in0=wre[:, J - 1:J], in1=sl, op=ALU.mult), r=[B("wre"), B("rot")], w=[B("sm", 0)])
                        V(lambda e, s8=s8, cl=cl: e.scalar_tensor_tensor(out=hcar[:, l, s8, 1:2], in0=wim[:, J - 1:J], scalar=cl, in1=sm[:, 0, 7:8], op0=ALU.mult, op1=ALU.add), r=[B("wim"), B("rot"), B("sm", 0)], w=[B("hcar", l)])
                    hbre = hbf
                    ssm_out(c0, J, lambda s8: hbre[:, s8 * J:(s8 + 1) * J], lambda s8: hbre[:, 8 * J + s8 * J:8 * J + (s8 + 1) * J], 0, [B("hbf")])
                if last:
                    Q(lambda e, l=l: e.dma_start(out=ossm_p[l, :, :].rearrange("p (s c) -> p s c", c=2), in_=hcar[:, l, :, :]), r=[B("hcar", l)], w=[B("ossm_p", l)])
                if first:
                    Q(lambda e, l=l: e.dma_start(out=h0[:, 0, :, :], in_=sre_d[l, :, :].rearrange("p (s b) -> p s b", s=8)), w=[B("h0")])
                    Q(lambda e, l=l: e.dma_start(out=h0[:, 1, :, :], in_=sim_d[l, :, :].rearrange("p (s b) -> p s b", s=8)), w=[B("h0")])
                    for s8 in range(8):
                        ct = s8 // 4
                        for ci in range(2):
                            PE(lambda e, ci=ci, s8=s8, ct=ct: e.matmul(ps[4][:, (s8 * 2 + ci) * NS:(s8 * 2 + ci + 1) * NS], lhsT=WB[:, s8 * 2 + ci, :], rhs=us[:, ct, T:NTM], start=True, stop=True), r=[B("us", T), B("WB")], w=[B("ps", 4)])
                    V(lambda e: e.tensor_copy(out=h1[:].rearrange("p c s b -> p s c b"), in_=ps[4][:, 0:16 * NS].rearrange("p (s c b) -> p s c b", s=8, c=2)), r=[B("ps", 4)], w=[B("h1")])
                    for s8 in range(8):
                        lre, lim = pl[:, 5, s8:s8 + 1], pl[:, 6, s8:s8 + 1]
                        V(lambda e, s8=s8: e.tensor_tensor(out=sm[:, 0, 6:7], in0=lam[:, l, 0, s8:s8 + 1], in1=lam[:, l, 1, s8:s8 + 1], op=ALU.mult), r=[B("lam", l)], w=[B("sm", 0)])
                        V(lambda e, s8=s8: e.tensor_tensor(out=sm[:, 0, 7:8], in0=lam[:, l, 0, s8:s8 + 1], in1=lam[:, l, 2, s8:s8 + 1], op=ALU.mult), r=[B("lam", l)], w=[B("sm", 0)])
                        V(lambda e, s8=s8: e.tensor_scalar(out=sm[:, 1, 7:8], in0=sm[:, 0, 7:8], scalar1=-1.0, scalar2=None, op0=ALU.mult), r=[B("sm", 0)], w=[B("sm", 1)])
                        V(lambda e, s8=s8: e.scalar_tensor_tensor(out=h1[:, 0, s8, :], in0=h0[:, 0, s8, :], scalar=sm[:, 0, 6:7], in1=h1[:, 0, s8, :], op0=ALU.mult, op1=ALU.add), r=[B("h0"), B("sm", 0), B("h1")], w=[B("h1")])
                        V(lambda e, s8=s8: e.scalar_tensor_tensor(out=h1[:, 0, s8, :], in0=h0[:, 1, s8, :], scalar=sm[:, 1, 7:8], in1=h1[:, 0, s8, :], op0=ALU.mult, op1=ALU.add), r=[B("h0"), B("sm", 1), B("h1")], w=[B("h1")])
                        V(lambda e, s8=s8: e.scalar_tensor_tensor(out=h1[:, 1, s8, :], in0=h0[:, 1, s8, :], scalar=sm[:, 0, 6:7], in1=h1[:, 1, s8, :], op0=ALU.mult, op1=ALU.add), r=[B("h0"), B("sm", 0), B("h1")], w=[B("h1")])
                        V(lambda e, s8=s8: e.scalar_tensor_tensor(out=h1[:, 1, s8, :], in0=h0[:, 0, s8, :], scalar=sm[:, 0, 7:8], in1=h1[:, 1, s8, :], op0=ALU.mult, op1=ALU.add), r=[B("h0"), B("sm", 0), B("h1")], w=[B("h1")])
                    Q(lambda e, l=l: e.dma_start(out=ossm_s[l, :, :].rearrange("p (c s b) -> p c s b", c=2, s=8), in_=h1[:]), r=[B("h1")], w=[B("ossm_s", l)])
                    V(lambda e: e.tensor_copy(out=h1b[:], in_=h1[:]), r=[B("h1")], w=[B("h1b")])
                    ssm_out(T, NS, lambda s8: h1b[:, 0, s8, :], lambda s8: h1b[:, 1, s8, :], 0, [B("h1b")])

                print('MARK wout', P.nops, flush=True)
                s_out = [wload(w_out[l, :, i * 512:(i + 1) * 512].rearrange("(k p) n -> p k n", p=128), "p (k n) -> p k n", k=8) for i in range(2)]
                for (c0, cn) in ccs:
                    for dt_ in range(8):
                        s = s_out[dt_ // 4]
                        off = (dt_ % 4) * 128
                        pbk = bank()
                        for k in range(8):
                            PE(lambda e, k=k, s=s, off=off, pbk=pbk: e.matmul(ps[pbk][:, 0:cn], lhsT=ring[:, s, k * 512 + off:k * 512 + off + 128], rhs=hb[:, k, c0:c0 + cn], start=(k == 0), stop=(k == 7)), r=[B("slot", s), B("hb", c0)], w=[B("ps", pbk)])
                        V(lambda e, dt_=dt_, pbk=pbk: e.tensor_tensor(out=x[:, dt_, c0:c0 + cn], in0=ps[pbk][:, 0:cn], in1=x[:, dt_, c0:c0 + cn], op=ALU.add), r=[B("ps", pbk), B("x", c0)], w=[B("x", c0)])
                print('MARK ffn', P.nops, flush=True)
                rmsnorm(l, l * PVL + OG2, ccs)
                for fg in range(6):
                    nf = 4 if fg < 5 else 2
                    sg_ = wload(w_g[l, :, fg * 512:fg * 512 + nf * 128].rearrange("(k p) n -> p k n", p=128), "p (k n) -> p k n", k=8)
                    su_ = wload(w_u[l, :, fg * 512:fg * 512 + nf * 128].rearrange("(k p) n -> p k n", p=128), "p (k n) -> p k n", k=8)
                    sd_ = wload(w_d[l, fg * 512:fg * 512 + nf * 128, :].rearrange("(f p) n -> p f n", p=128), "p (f n) -> p f n", f=nf)
                    W_ = nf * 128
                    for (c0, cn) in ccs:
                        for f in range(nf):
                            pg, pu = bank(), bank()
                            for (s_, pb_) in ((sg_, pg), (su_, pu)):
                                for k in range(8):
                                    PE(lambda e, k=k, s_=s_, pb_=pb_, f=f: e.matmul(ps[pb_][:, 0:cn], lhsT=ring[:, s_, k * W_ + f * 128:k * W_ + f * 128 + 128], rhs=hb[:, k, c0:c0 + cn], start=(k == 0), stop=(k == 7)), r=[B("slot", s_), B("hb", c0)], w=[B("ps", pb_)])
                            S(lambda e, pg=pg, f=f: e.activation(out=sgb[:, f % 2, 0:cn], in_=ps[pg][:, 0:cn], func=AF.Silu), r=[B("ps", pg)], w=[B("sgb", f % 2)])
                            V(lambda e, pu=pu, f=f: e.tensor_tensor(out=hid[:, f, c0:c0 + cn], in0=ps[pu][:, 0:cn], in1=sgb[:, f % 2, 0:cn], op=ALU.mult), r=[B("ps", pu), B("sgb", f % 2)], w=[B("hid", c0)])
                        for dt_ in range(8):
                            pbk = bank()
                            for f in range(nf):
                                PE(lambda e, f=f, dt_=dt_, pbk=pbk: e.matmul(ps[pbk][:, 0:cn], lhsT=ring[:, sd_, f * 1024 + dt_ * 128:f * 1024 + dt_ * 128 + 128], rhs=hid[:, f, c0:c0 + cn], start=(f == 0), stop=(f == nf - 1)), r=[B("slot", sd_), B("hid", c0)], w=[B("ps", pbk)])
                            V(lambda e, dt_=dt_, pbk=pbk: e.tensor_tensor(out=x[:, dt_, c0:c0 + cn], in0=ps[pbk][:, 0:cn], in1=x[:, dt_, c0:c0 + cn], op=ALU.add), r=[B("ps", pbk), B("x", c0)], w=[B("x", c0)])
            print('MARK final', P.nops, flush=True)
            for (c0, cn) in ccs:
                for k in range(8):
                    S(lambda e, k=k: e.activation(out=sq[:, 0:cn], in_=x[:, k, c0:c0 + cn], func=AF.Square), r=[B("x", c0)], w=[B("sq")])
                    PE(lambda e, k=k: e.matmul(ps[6][:, 0:cn], lhsT=ones_b[:], rhs=sq[:, 0:cn], start=(k == 0), stop=(k == 7)), r=[B("sq"), cB], w=[B("ps", 6)])
                S(lambda e: e.activation(out=rstd[:, 0:cn], in_=ps[6][:, 0:cn], func=AF.Sqrt, scale=1.0 / D, bias=EPS), r=[B("ps", 6)], w=[B("rstd")])
                V(lambda e: e.reciprocal(out=rstd[:, 0:cn], in_=rstd[:, 0:cn]), r=[B("rstd")], w=[B("rstd")])
                for k in range(8):
                    gk = pvt[:, 4 * PVL + k: 4 * PVL + k + 1]
                    V(lambda e, k=k, gk=gk: e.scalar_tensor_tensor(out=x[:, k, c0:c0 + cn], in0=x[:, k, c0:c0 + cn], scalar=gk, in1=rstd[:, 0:cn], op0=ALU.mult, op1=ALU.mult), r=[B("x", c0), B("rstd"), cB], w=[B("x", c0)])
                    if c0 < T:
                        Q(lambda e, k=k: e.dma_start(out=yT[k * 128:(k + 1) * 128, ch * T + c0:ch * T + c0 + cn], in_=x[:, k, c0:c0 + cn]), r=[B("x", c0)], w=[B("yT")])
                    else:
                        Q(lambda e, k=k: e.dma_start(out=ysT[k * 128:(k + 1) * 128, :], in_=x[:, k, c0:c0 + cn]), r=[B("x", c0)], w=[B("ysT")])
        print('NOPS', P.nops, {k: len(v) for k, v in P.ops.items()}, flush=True)
        P.emit()
    return nc


def _alibi_tables():
    slopes = 2.0 ** (-8.0 * np.arange(1, 9, dtype=np.float32) / 8)
    qi = np.arange(128)[:, None]
    kj = np.arange(256)[None, :]
    dist = qi - kj + 128
    valid = (dist >= 0) & (dist < 128)
    ab = np.where(valid[None], -slopes[:, None, None] * dist[None].astype(np.float32), -30000.0).astype(np.float32)
    ab = np.ascontiguousarray(ab.transpose(1, 0, 2)).reshape(128, 8 * 256)
    dj = (127 - np.arange(128)).astype(np.float32)
    sbias = (-slopes.reshape(2, 4, 1) * dj[None, None, :]).astype(np.float32)
    sbias = np.ascontiguousarray(sbias.transpose(1, 0, 2)).reshape(4, 2 * 128)
    return ab, sbias


def _fm(v):
    return np.ascontiguousarray(np.asarray(v, np.float32).reshape(-1, 128).T)


_NC_CACHE = {}


def kernel(nch=16, depth=4, **inp):
    f = lambda k: np.asarray(inp[k], np.float32)
    L = depth
    TT = nch * T
    key = (nch, depth)
    if key not in _NC_CACHE:
        _NC_CACHE[key] = build(nch, depth)
    nc = _NC_CACHE[key]
    ab, sbias = _alibi_tables()
    pv = np.zeros((128, NPV), np.float32)
    for l in range(L):
        o = l * PVL
        pv[:, o + 0:o + 8] = _fm(f("norm_mix_g")[l])
        pv[:, o + 8:o + 16] = _fm(f("norm_ffn_g")[l])
        cw = f("conv_dw_w")[l]
        for ct in range(2):
            pv[:, o + 16 + ct * 31:o + 16 + (ct + 1) * 31] = cw[:, ct * 128:(ct + 1) * 128].T
        pv[:, o + 78:o + 80] = _fm(f("conv_dw_b")[l])
        pv[:, o + 80:o + 82] = _fm(f("conv_ln_g")[l])
        pv[:, o + 82:o + 84] = _fm(f("conv_ln_b")[l])
        pv[:, o + 84:o + 86] = _fm(f("ssm_d")[l])
        pv[:, o + 86:o + 88] = _fm(f("ssm_glu_b")[l])
        pv[:, o + 88:o + 96] = _fm(f("ssm_a_re")[l].reshape(-1))
        pv[:, o + 96:o + 104] = _fm(f("ssm_a_im")[l].reshape(-1))
        pv[:, o + 104:o + 112] = _fm(np.repeat(f("ssm_log_dt")[l], 64))
        pv[:, o + 112:o + 120] = f("attn_sinks")[l][None, :]
    pv[:, 4 * PVL:4 * PVL + 8] = _fm(f("norm_final_g"))
    Bn = np.zeros((L, 128, 8, 2, 128), np.float32)
    Cn = np.zeros((L, 128, 8, 2, 128), np.float32)
    for l in range(L):
        for ci, (bk, ck_) in enumerate((("ssm_b_re", "ssm_c_re"), ("ssm_b_im", "ssm_c_im"))):
            bb = f(bk)[l]
            cc = f(ck_)[l]
            for g in range(16):
                st_, p0 = g // 2, (g % 2) * 64
                col = (g % 8) * 16
                Bn[l, p0:p0 + 64, st_, ci, col:col + 16] = bb[g]
                Cn[l, p0:p0 + 64, st_, ci, col:col + 16] = cc[g].T
    Bn = Bn.reshape(L, 128, -1)
    Cn = Cn.reshape(L, 128, -1)
    ident = np.eye(128, dtype=np.float32)
    jjv = np.broadcast_to(np.arange(1, J + 1, dtype=np.float32)[None, :], (128, J)).copy()
    xp = f("x_prompt")
    xs = f("x_sample")[:, 0, :]
    shared = {
        "w_in": f("w_in")[:L], "w_out": f("w_out")[:L], "w_g": f("w_ff_gate")[:L], "w_u": f("w_ff_up")[:L],
        "w_d": f("w_ff_down")[:L], "glu_w": f("ssm_glu_w")[:L], "pv": pv, "Bn": Bn, "Cn": Cn, "ident": ident,
        "jj": jjv, "abias": ab, "sbias": sbias,
        "sinkc": np.ascontiguousarray(f("attn_sinks")[:L].reshape(L, 2, 4).transpose(2, 0, 1).reshape(4, L * 2)),
    }
    in_maps = []
    for c in range(8):
        sq_, b0 = c // 4, c * NS
        m = dict(shared)
        m["xT"] = np.ascontiguousarray(xp[sq_, :TT, :].T)
        m["xsT"] = np.ascontiguousarray(xs[b0:b0 + NS].T)
        m["ck"] = np.ascontiguousarray(f("cache_swa_k")[:L, b0:b0 + NS].reshape(L, NS, 128, 128))
        m["cv"] = np.ascontiguousarray(f("cache_swa_v")[:L, b0:b0 + NS].reshape(L, NS, 128, 128))
        cc_ = f("cache_conv")[:L, b0:b0 + NS]
        m["cconv"] = np.ascontiguousarray(cc_.reshape(L, NS, 30, 2, 128).transpose(0, 4, 3, 1, 2)).reshape(L, 128, -1)
        for nm, kk in (("sre", "state_ssm_re"), ("sim", "state_ssm_im")):
            s_ = f(kk)[:L, b0:b0 + NS].reshape(L, NS, 8, 128)
            m[nm] = np.ascontiguousarray(s_.transpose(0, 3, 2, 1)).reshape(L, 128, -1)
        in_maps.append(m)
    res = run_bass_kernel_spmd(nc, in_maps, core_ids=list(range(8))).results
    y_p = np.stack([res[0]["yT"].T, res[4]["yT"].T]).astype(np.float32)
    y_s = np.concatenate([res[c]["ysT"].T for c in range(8)])[:, None, :].astype(np.float32)
    pc = (res[0], res[4])
    k_p = np.stack([np.stack([r["ok_p"][l].reshape(128, 2, 64) for r in pc]) for l in range(L)])
    v_p = np.stack([np.stack([r["ov_p"][l].reshape(128, 2, 64) for r in pc]) for l in range(L)])
    conv_p = np.stack([np.stack([r["oconv_p"][l].reshape(128, 2, 30).transpose(2, 1, 0).reshape(30, 256) for r in pc]) for l in range(L)])
    ssm_p = [np.stack([np.stack([r["ossm_p"][l].reshape(128, 8, 2)[:, :, ci].T.reshape(16, 64) for r in pc]) for l in range(L)]) for ci in range(2)]
    k_s = np.concatenate([res[c]["ok_s"].reshape(L, NS, 128, 2, 64) for c in range(8)], 1)
    v_s = np.concatenate([res[c]["ov_s"].reshape(L, NS, 128, 2, 64) for c in range(8)], 1)
    conv_s = np.concatenate([res[c]["oconv_s"].reshape(L, 128, 2, NS, 30).transpose(0, 3, 4, 2, 1).reshape(L, NS, 30, 256) for c in range(8)], 1)
    ssm_s = [np.concatenate([res[c]["ossm_s"].reshape(L, 128, 2, 8, NS)[:, :, ci].transpose(0, 3, 2, 1).reshape(L, NS, 16, 64) for c in range(8)], 1) for ci in range(2)]
    outs = (y_p, y_s, k_p, v_p, conv_p, ssm_p[0], ssm_p[1], k_s, v_s, conv_s, ssm_s[0], ssm_s[1])
    return tuple(np.ascontiguousarray(o, dtype=np.float32) for o in outs)
```

```python
import math
import os
import numpy as np
from contextlib import ExitStack
import concourse.bass as bass
import concourse.mybir as mybir
from concourse.bass_utils import run_bass_kernel_spmd

F32 = mybir.dt.float32
BF16 = mybir.dt.bfloat16
I32 = mybir.dt.int32
AF = mybir.ActivationFunctionType
ALU = mybir.AluOpType
AX = mybir.AxisListType

D = 1024
SEQ = 8192
T = 512
NS = 16
NTM = T + NS
DFF = 2816
NF = 22
J = 256
NSLOT = 4
PVL = 120
NPV = 4 * PVL + 8
EPS = 1e-6
TWO_PI = 2.0 * math.pi


import types


def _freeze(fn):
    if fn.__closure__ is None:
        return fn
    cells = []
    for c in fn.__closure__:
        try:
            cells.append(types.CellType(c.cell_contents))
        except ValueError:
            cells.append(c)
    return types.FunctionType(fn.__code__, fn.__globals__, fn.__name__, fn.__defaults__, tuple(cells))


class _Stub:
    def __init__(self):
        self.closed = True

    def matmul(self, *a, **kw):
        self.closed = bool(kw.get("stop", True))
        return self

    def transpose(self, *a, **kw):
        self.closed = True
        return self

    def then_inc(self, *a, **kw):
        return self


class Buf:
    __slots__ = ("name", "lw", "rd")

    def __init__(self, name):
        self.name = name
        self.lw = None
        self.rd = {}


class Prog:
    ENG = ["tensor", "vector", "scalar", "gpsimd", "sync"]

    def __init__(self, nc, same_eng_sync=("vector", "scalar", "gpsimd")):
        self.nc = nc
        self.ops = {e: [] for e in self.ENG}
        self.cnt = {}
        self.known = {e: {} for e in self.ENG}
        self.same = set(same_eng_sync)
        self.bufs = {}
        self.dry = False
        self.dq = {}
        self.dryn = 0
        self.nops = 0

    def B(self, *key):
        b = self.bufs.get(key)
        if b is None:
            b = self.bufs[key] = Buf(key)
        return b

    def op(self, eng, fn, r=(), w=(), dma=None, inc=None):
        if self.dry:
            self.dryn += 1
            return None
        fn = _freeze(fn)
        if getattr(self, "stopped", False):
            return None
        self.nops += 1
        if eng == "tensor" and "KLIMIT" in os.environ:
            st_ = _Stub()
            try:
                fn(st_)
            except Exception:
                pass
            self.open_grp = not st_.closed
        if self.nops >= int(os.environ.get("KLIMIT", "100000000")) and not getattr(self, "open_grp", False):
            self.stopped = True
        ex = [b for b in r if b.name[0] in ("ps", "psb")]
        if ex:
            r = [b for b in r if b.name[0] not in ("ps", "psb")]
            w = list(w) + ex
        deps = {}
        for b in list(r) + list(w):
            if b.lw is not None and deps.get(b.lw[0], 0) < b.lw[1]:
                deps[b.lw[0]] = b.lw[1]
        for b in w:
            for s, v in b.rd.items():
                if deps.get(s, 0) < v:
                    deps[s] = v
        pre = None
        if dma is None:
            sem, step = eng, 1
        else:
            npool = 32 if eng == "sync" else 8
            i = self.dq.get(eng, 0)
            self.dq[eng] = i + 1
            sem, step = "d%s%d" % (eng[0], i % npool), 16
            if i >= npool:
                pre = (sem, 16 * (i // npool))
        waits = []
        if pre is not None and deps.get(pre[0], 0) < pre[1]:
            deps[pre[0]] = pre[1]
        for s, v in deps.items():
            if s == eng and eng not in self.same:
                continue
            if self.known[eng].get(s, 0) >= v:
                continue
            self.known[eng][s] = v
            waits.append((s, v))
        self.cnt[sem] = self.cnt.get(sem, 0) + step
        tok = (sem, self.cnt[sem])
        self.ops[eng].append((fn, waits, sem, step))
        for b in r:
            if b.rd.get(sem, 0) < tok[1]:
                b.rd[sem] = tok[1]
        for b in w:
            b.lw = tok
            b.rd = {}
        return tok

    def emit(self):
        nc = self.nc
        with ExitStack() as st:
            sems = {s: st.enter_context(nc.semaphore(s)) for s in self.cnt}
            block = st.enter_context(nc.Block())
            final = dict(self.cnt)

            def mk(engname):
                def body(e):
                    for fn, waits, sem, step in self.ops[engname]:
                        for s, v in waits:
                            e.wait_ge(sems[s], v)
                        fn(e).then_inc(sems[sem], step)
                    if engname == "sync":
                        for s, v in final.items():
                            e.wait_ge(sems[s], v)
                return body

            for engname in self.ENG:
                if self.ops[engname] or engname == "sync":
                    getattr(block, engname)(mk(engname))


def build(nch=16, depth=4):
    nc = bass.Bass("TRN2", target_bir_lowering=False)
    L = depth
    TT = nch * T

    def din(name, shape, dt=F32):
        return nc.dram_tensor(name, list(shape), dt, kind="ExternalInput").ap()

    def dout(name, shape, dt=F32):
        return nc.dram_tensor(name, list(shape), dt, kind="ExternalOutput").ap()

    xT = din("xT", [D, TT]); xsT = din("xsT", [D, NS])
    w_in = din("w_in", [L, D, 1536]); w_out = din("w_out", [L, D, D])
    w_g = din("w_g", [L, D, DFF]); w_u = din("w_u", [L, D, DFF]); w_d = din("w_d", [L, DFF, D])
    glu_w = din("glu_w", [L, 256, 256])
    pv_d = din("pv", [128, NPV])
    Bn_d = din("Bn", [L, 128, 8 * 2 * 128]); Cn_d = din("Cn", [L, 128, 8 * 2 * 128])
    ident_d = din("ident", [128, 128]); jj_d = din("jj", [128, J])
    abias_d = din("abias", [128, 256]); sbias_d = din("sbias", [4, 2 * 128]); sinkc_d = din("sinkc", [4, L * 2])
    ck_d = din("ck", [L, NS, 128, 128]); cv_d = din("cv", [L, NS, 128, 128])
    cconv_d = din("cconv", [L, 128, 2 * NS * 30])
    sre_d = din("sre", [L, 128, 8 * NS]); sim_d = din("sim", [L, 128, 8 * NS])

    yT = dout("yT", [D, TT]); ysT = dout("ysT", [D, NS])
    ok_p = dout("ok_p", [L, 128, 128]); ov_p = dout("ov_p", [L, 128, 128])
    oconv_p = dout("oconv_p", [L, 128, 2 * 30]); ossm_p = dout("ossm_p", [L, 128, 16])
    ok_s = dout("ok_s", [L, NS, 128, 128]); ov_s = dout("ov_s", [L, NS, 128, 128])
    oconv_s = dout("oconv_s", [L, 128, 2 * NS * 30]); ossm_s = dout("ossm_s", [L, 128, 2 * 8 * NS])
    rot_d = nc.dram_tensor("rot_scr", [L, 128, 8 * 2 * J], F32).ap()
    wb_d = nc.dram_tensor("wb_scr", [L, 128, 16 * 128], BF16).ap()
    wc_d = nc.dram_tensor("wc_scr", [L, 128, 16 * 128], BF16).ap()

    P = Prog(nc)
    B = P.B
    with ExitStack() as st:
        def sb(name, shape, dt=F32):
            return st.enter_context(nc.sbuf_tensor("sb_" + name, list(shape), dt))

        def pst(name, shape, dt=F32):
            return st.enter_context(nc.psum_tensor(name, list(shape), dt))

        x = sb("x", [128, 8, NTM]); hb = sb("hb", [128, 8, NTM], BF16)
        x1 = sb("x1", [128, 8, T]); hb1 = sb("hb1", [128, 8, T], BF16)
        X = [x, x1]; HB = [hb, hb1]
        qT = sb("qT", [64, 8, NTM], BF16); kT = sb("kT", [64, 2, 128 + NTM], BF16)
        vp = sb("vp", [128, 5, 2, 192], BF16)
        ucv = sb("ucv", [128, 2, 30 + NTM]); acc = sb("acc", [128, 2, NTM]); us = sb("us", [128, 2, NTM], BF16)
        hid = sb("hid", [128, 4, NTM], BF16)
        ringM = sb("ringM", [128, 3, 4096], BF16); ringF = sb("ringF", [128, 3, 4096], BF16)
        pvt = sb("pvt", [128, NPV])
        ident = sb("ident", [128, 128]); identb = sb("identb", [128, 128], BF16)
        ones_f = sb("ones_f", [128, 128]); ones_b = sb("ones_b", [128, 128], BF16)
        jj = sb("jj", [128, J])
        abias = sb("abias", [128, 256]); sbias = sb("sbias", [4, 2, 128])
        WB = sb("WB", [128, 16, 128], BF16); WC = sb("WC", [128, 16, 128], BF16)
        Dd = sb("Dd", [128, 2, 128], BF16); gluw = sb("gluw", [128, 2, 256], BF16)
        lam = sb("lam", [128, L, 4, 8])
        rot2 = sb("rot2", [128, 2, 2, J])
        hcar = sb("hcar", [128, L, 8, 2])
        khalo = sb("khalo", [64, L, 2, 128], BF16); vhalo = sb("vhalo", [128, L, 2, 192], BF16)
        chalo = sb("chalo", [128, L, 2, 30])
        sq = sb("sq", [128, 512], BF16); rstd = sb("rstd", [128, 512])
        sq2 = sb("sq2", [128, 512], BF16); rstd2 = sb("rstd2", [128, 512])
        sgb = sb("sgb", [128, 2, 512], BF16)
        sc = sb("sc", [128, 2, 256]); pb = sb("pb", [128, 2, 256], BF16); ptb = sb("ptb", [128, 2, 256], BF16)
        sm = sb("sm", [128, 2, 8])
        t1 = sb("t1", [128, J]); t2 = sb("t2", [128, J]); bre = sb("bre", [128, J]); bim = sb("bim", [128, J])
        wre = sb("wre", [128, J]); wim = sb("wim", [128, J])
        ysm = sb("ysm", [128, 2, 512]); yb = sb("yb", [128, 2, 512], BF16); g1 = sb("g1", [128, 512]); g2 = sb("g2", [128, 512])
        ysq = ysm; cst = ysm[:].rearrange("p c n -> p (c n)")[:, 0:2 * NS * 31].rearrange("p (c b k) -> p c b k", c=2, b=NS); mean = sb("mean", [128, 512]); var = g2
        xflat = x[:].rearrange("p k n -> p (k n)")
        rotflat = x1[:].rearrange("p k n -> p (k n)")[:, 0:16 * J]
        big = xflat[:, 0:4096]; big2 = rotflat[:, 8 * J:16 * J]; bigi = sb("bigi", [128, 512], I32)[:]
        hbf = sb("hbf", [128, 16 * J], BF16)
        pl = sb("pl", [128, 16, 8])
        tk = sb("tk", [128, 128]); tkb = sb("tkb", [NS, 2, 128])
        Kb = sb("Kb", [128, 1, 128]); Vb = sb("Vb", [128, 1, 128]); KbT = sb("KbT", [64, 2, 128], BF16)
        Vbp = sb("Vbp", [128, 2, 192], BF16)
        ssc = sb("ssc", [4, 4, 128]); spb = sb("spb", [4, 4, 128], BF16); ssm_ = sb("ssm_", [4, 4, 4]); sinkt = sb("sinkt", [4, L * 2])
        sptb = sb("sptb", [128, 2, NS, 4], BF16)
        cs = sb("cs", [128, 2, NS, 31])
        h0 = sb("h0", [128, 2, 8, NS]); h1 = sb("h1", [128, 2, 8, NS]); h1b = sb("h1b", [128, 2, 8, NS], BF16)

        ps = [pst("ps%d" % i, [128, 512]) for i in range(7)]
        psb = pst("psb", [128, 1024], BF16)

        Q = lambda fn, r=(), w=(), ch="io": P.op("sync", fn, r, w, dma=ch)
        G = lambda fn, r=(), w=(), ch="w": P.op("gpsimd", fn, r, w, dma=ch)
        V = lambda fn, r=(), w=(): P.op("vector", fn, r, w)
        S = lambda fn, r=(), w=(): P.op("scalar", fn, r, w)
        PE = lambda fn, r=(), w=(): P.op("tensor", fn, r, w)
        GP = lambda fn, r=(), w=(): P.op("gpsimd", fn, r, w)

        def pvc(l, off, n=1):
            return pvt[:, l * PVL + off: l * PVL + off + n]

        OG1, OG2, OCW, OCB, OLG, OLB, OSD, OGB, OARE, OAIM, OLDT, OSINK = 0, 8, 16, 78, 80, 82, 84, 86, 88, 96, 104, 112
        rr = [0]

        def bank():
            rr[0] = (rr[0] + 1) % 3
            return 4 + rr[0]

        fr = [0]

        def fbank():
            fr[0] = (fr[0] + 1) % 4
            return fr[0]

        cB = B("const")
        Q(lambda e: e.dma_start(out=pvt[:], in_=pv_d[:, :]), w=[cB])
        Q(lambda e: e.dma_start(out=ident[:], in_=ident_d[:, :]), w=[cB])
        Q(lambda e: e.dma_start(out=jj[:], in_=jj_d[:, :]), w=[cB])
        Q(lambda e: e.dma_start(out=abias[:], in_=abias_d[:, :]), w=[cB])
        Q(lambda e: e.dma_start(out=sbias[:].rearrange("p h k -> p (h k)"), in_=sbias_d[:, :]), w=[cB])
        Q(lambda e: e.dma_start(out=sinkt[:], in_=sinkc_d[:, :]), w=[cB])
        V(lambda e: e.memset(ones_f[:], 1.0), w=[cB])
        V(lambda e: e.memset(ones_b[:], 1.0), w=[cB])
        V(lambda e: e.tensor_copy(out=identb[:], in_=ident[:]), r=[cB], w=[B("identb")])
        V(lambda e: e.memset(vp[:].rearrange("p a g c -> p (a g c)"), 0.0), w=[B("vp", i) for i in range(5)])
        V(lambda e: e.memset(vhalo[:].rearrange("p l g c -> p (l g c)"), 0.0), w=[B("vhalo", l) for l in range(L)])
        V(lambda e: e.memset(Vbp[:].rearrange("p g c -> p (g c)"), 0.0), w=[B("Vbp")])
        V(lambda e: e.memset(hcar[:].rearrange("p l s c -> p (l s c)"), 0.0), w=[B("hcar", l) for l in range(L)])
        V(lambda e: e.memset(chalo[:].rearrange("p l c k -> p (l c k)"), 0.0), w=[B("chalo", l) for l in range(L)])

        for l in range(L):
            pB = B("pl")
            are, aim, ldt = pvc(l, OARE, 8), pvc(l, OAIM, 8), pvc(l, OLDT, 8)
            c = lambda i: pl[:, i, :]
            S(lambda e: e.activation(out=c(0), in_=ldt, func=AF.Exp), r=[cB], w=[pB])
            V(lambda e: e.tensor_tensor(out=c(1), in0=are, in1=c(0), op=ALU.mult), r=[pB], w=[pB])
            V(lambda e: e.tensor_tensor(out=c(2), in0=aim, in1=c(0), op=ALU.mult), r=[pB], w=[pB])
            S(lambda e, l=l: e.activation(out=lam[:, l, 0, :], in_=c(1), func=AF.Exp), r=[pB], w=[B("lam", l)])
            V(lambda e: e.tensor_scalar(out=c(3), in0=c(2), scalar1=1.0 / TWO_PI, scalar2=None, op0=ALU.mult), r=[pB], w=[pB])
            V(lambda e: e.tensor_copy(out=bigi[:, 0:8], in_=c(3)), r=[pB], w=[pB])
            V(lambda e: e.tensor_copy(out=c(4), in_=bigi[:, 0:8]), r=[pB], w=[pB])
            V(lambda e: e.tensor_tensor(out=c(4), in0=c(3), in1=c(4), op=ALU.subtract), r=[pB], w=[pB])
            S(lambda e, l=l: e.activation(out=lam[:, l, 2, :], in_=c(4), func=AF.Sin, scale=6.283185), r=[pB], w=[B("lam", l)])
            V(lambda e: e.tensor_scalar(out=c(3), in0=c(3), scalar1=0.25, scalar2=None, op0=ALU.add), r=[pB], w=[pB])
            V(lambda e: e.tensor_copy(out=bigi[:, 0:8], in_=c(3)), r=[pB], w=[pB])
            V(lambda e: e.tensor_copy(out=c(4), in_=bigi[:, 0:8]), r=[pB], w=[pB])
            V(lambda e: e.tensor_tensor(out=c(4), in0=c(3), in1=c(4), op=ALU.subtract), r=[pB], w=[pB])
            S(lambda e, l=l: e.activation(out=lam[:, l, 1, :], in_=c(4), func=AF.Sin, scale=6.283185), r=[pB], w=[B("lam", l)])
            V(lambda e, l=l: e.tensor_tensor(out=c(5), in0=lam[:, l, 0, :], in1=lam[:, l, 1, :], op=ALU.mult), r=[pB, B("lam", l)], w=[pB])
            V(lambda e, l=l: e.tensor_tensor(out=c(6), in0=lam[:, l, 0, :], in1=lam[:, l, 2, :], op=ALU.mult), r=[pB, B("lam", l)], w=[pB])
            V(lambda e: e.tensor_scalar(out=c(7), in0=c(5), scalar1=-1.0, scalar2=None, op0=ALU.add), r=[pB], w=[pB])
            V(lambda e: e.tensor_tensor(out=c(8), in0=are, in1=are, op=ALU.mult), r=[pB], w=[pB])
            V(lambda e: e.tensor_tensor(out=c(9), in0=aim, in1=aim, op=ALU.mult), r=[pB], w=[pB])
            V(lambda e: e.tensor_tensor(out=c(8), in0=c(8), in1=c(9), op=ALU.add), r=[pB], w=[pB])
            V(lambda e: e.reciprocal(out=c(8), in_=c(8)), r=[pB], w=[pB])
            V(lambda e: e.tensor_tensor(out=c(9), in0=c(7), in1=are, op=ALU.mult), r=[pB], w=[pB])
            V(lambda e: e.tensor_tensor(out=c(10), in0=c(6), in1=aim, op=ALU.mult), r=[pB], w=[pB])
            V(lambda e: e.tensor_tensor(out=c(9), in0=c(9), in1=c(10), op=ALU.add), r=[pB], w=[pB])
            V(lambda e: e.tensor_tensor(out=c(11), in0=c(9), in1=c(8), op=ALU.mult), r=[pB], w=[pB])
            V(lambda e: e.tensor_tensor(out=c(9), in0=c(6), in1=are, op=ALU.mult), r=[pB], w=[pB])
            V(lambda e: e.tensor_tensor(out=c(10), in0=c(7), in1=aim, op=ALU.mult), r=[pB], w=[pB])
            V(lambda e: e.tensor_tensor(out=c(9), in0=c(9), in1=c(10), op=ALU.subtract), r=[pB], w=[pB])
            V(lambda e: e.tensor_tensor(out=c(12), in0=c(9), in1=c(8), op=ALU.mult), r=[pB], w=[pB])
            V(lambda e: e.tensor_scalar(out=c(13), in0=c(12), scalar1=-1.0, scalar2=None, op0=ALU.mult), r=[pB], w=[pB])
            bg = big[:, 0:2048].rearrange("p (s c k) -> p s c k", s=8, c=2)
            Q(lambda e, l=l: e.dma_start(out=big[:, 0:2048], in_=Bn_d[l, :, :]), r=[pB], w=[B("big")])
            for s8 in range(8):
                fre, fim, nfim = pl[:, 11, s8:s8 + 1], pl[:, 12, s8:s8 + 1], pl[:, 13, s8:s8 + 1]
                V(lambda e, s8=s8, fre=fre: e.tensor_scalar(out=t1[:, 0:128], in0=bg[:, s8, 0, :], scalar1=fre, scalar2=None, op0=ALU.mult), r=[pB, B("big")], w=[B("t1")])
                V(lambda e, s8=s8, nfim=nfim: e.scalar_tensor_tensor(out=t1[:, 0:128], in0=bg[:, s8, 1, :], scalar=nfim, in1=t1[:, 0:128], op0=ALU.mult, op1=ALU.add), r=[pB, B("big"), B("t1")], w=[B("t1")])
                V(lambda e, s8=s8, fre=fre: e.tensor_scalar(out=t2[:, 0:128], in0=bg[:, s8, 1, :], scalar1=fre, scalar2=None, op0=ALU.mult), r=[pB, B("big")], w=[B("t2")])
                V(lambda e, s8=s8, fim=fim: e.scalar_tensor_tensor(out=t2[:, 0:128], in0=bg[:, s8, 0, :], scalar=fim, in1=t2[:, 0:128], op0=ALU.mult, op1=ALU.add), r=[pB, B("big"), B("t2")], w=[B("t2")])
                PE(lambda e: e.transpose(out=ps[5][:, 0:128], in_=t1[:, 0:128], identity=ident[:]), r=[B("t1"), cB], w=[B("ps", 5)])
                PE(lambda e: e.transpose(out=ps[5][:, 128:256], in_=t2[:, 0:128], identity=ident[:]), r=[B("t2"), cB], w=[B("ps", 5)])
                S(lambda e, l=l, s8=s8: e.activation(out=WB[:, s8 * 2:s8 * 2 + 2, :], in_=ps[5][:, 0:256].rearrange("p (c k) -> p c k", c=2), func=AF.Copy), r=[B("ps", 5)], w=[B("WB")])
            Q(lambda e, l=l: e.dma_start(out=big[:, 0:2048], in_=Cn_d[l, :, :]), w=[B("big")])
            V(lambda e, l=l: e.tensor_copy(out=WC[:, 0:16, :].rearrange("p (s c) k -> p s c k", c=2)[:, :, 0, :], in_=bg[:, :, 0, :]), r=[B("big")], w=[B("WC")])
            V(lambda e, l=l: e.tensor_scalar(out=WC[:, 0:16, :].rearrange("p (s c) k -> p s c k", c=2)[:, :, 1, :], in0=bg[:, :, 1, :], scalar1=-1.0, scalar2=None, op0=ALU.mult), r=[B("big")], w=[B("WC")])
            a8 = big2.rearrange("p (s j) -> p s j", s=8)
            for s8 in range(8):
                V(lambda e, s8=s8: e.tensor_scalar(out=a8[:, s8, :], in0=jj[:], scalar1=pl[:, 2, s8:s8 + 1], scalar2=1.0 / TWO_PI, op0=ALU.mult, op1=ALU.mult), r=[pB, cB], w=[B("big2"), B("rot")])
            rt = big.rearrange("p (s c j) -> p s c j", s=8, c=2)
            for ci, sh in ((1, 0.0), (0, 0.25)):
                if sh:
                    V(lambda e, sh=sh: e.tensor_scalar(out=big2, in0=big2, scalar1=sh, scalar2=None, op0=ALU.add), r=[B("big2")], w=[B("big2"), B("rot")])
                for pc in range(8 * J // 512):
                    V(lambda e, pc=pc: e.tensor_copy(out=bigi, in_=big2[:, pc * 512:(pc + 1) * 512]), r=[B("big2"), B("rot")], w=[B("bigi")])
                    V(lambda e, pc=pc: e.tensor_copy(out=rotflat[:, pc * 512:(pc + 1) * 512], in_=bigi), r=[B("bigi")], w=[B("rot")])
                V(lambda e: e.tensor_tensor(out=rotflat[:, 0:8 * J], in0=big2, in1=rotflat[:, 0:8 * J], op=ALU.subtract), r=[B("big2"), B("rot")], w=[B("rot")])
                S(lambda e, ci=ci: e.activation(out=rt[:, :, ci, :], in_=rotflat[:, 0:8 * J].rearrange("p (s j) -> p s j", s=8), func=AF.Sin, scale=6.283185), r=[B("rot")], w=[B("big")])
            Q(lambda e, l=l: e.dma_start(out=rot_d[l, :, :], in_=big), r=[B("big")], w=[B("rot_d", l)])
            Q(lambda e, l=l: e.dma_start(out=wb_d[l, :, :], in_=WB[:].rearrange("p a k -> p (a k)")), r=[B("WB")], w=[B("wb_d", l)])
            Q(lambda e, l=l: e.dma_start(out=wc_d[l, :, :], in_=WC[:].rearrange("p a k -> p (a k)")), r=[B("WC")], w=[B("wc_d", l)])

        print('MARK prologue_end', P.nops, flush=True)
        slot_i = {"M": 0, "F": 0}

        def wload(which, src_ap, shape_str, **kw):
            ring = ringM if which == "M" else ringF
            s = slot_i[which] % 3
            if not P.dry:
                slot_i[which] += 1
            n = 1
            for d_ in src_ap.shape[1:]:
                n *= d_
            dst = ring[:, s, 0:n]
            if shape_str:
                dst = dst.rearrange(shape_str, **kw)
            G(lambda e: e.dma_start(out=dst, in_=src_ap), w=[B("slot" + which, s)])
            return s

        def rmsnorm(x, hb, par, goff, ccs, sq, rstd, pbk, tag):
            for (c0, cn) in ccs:
                pb_ = ps[pbk]
                for k in range(8):
                    S(lambda e, k=k: e.activation(out=sq[:, 0:cn], in_=x[:, k, c0:c0 + cn], func=AF.Square), r=[B("x", par, c0)], w=[B("sq", tag)])
                    PE(lambda e, k=k: e.matmul(pb_[:, 0:cn], lhsT=ones_b[:], rhs=sq[:, 0:cn], start=(k == 0), stop=(k == 7)), r=[B("sq", tag), cB], w=[B("ps", pbk)])
                S(lambda e: e.activation(out=rstd[:, 0:cn], in_=pb_[:, 0:cn], func=AF.Sqrt, scale=1.0 / D, bias=EPS), r=[B("ps", pbk)], w=[B("rstd", tag)])
                V(lambda e: e.reciprocal(out=rstd[:, 0:cn], in_=rstd[:, 0:cn]), r=[B("rstd", tag)], w=[B("rstd", tag)])
                for k in range(8):
                    gk = pvt[:, goff + k: goff + k + 1]
                    V(lambda e, k=k, gk=gk: e.scalar_tensor_tensor(out=hb[:, k, c0:c0 + cn], in0=x[:, k, c0:c0 + cn], scalar=gk, in1=rstd[:, 0:cn], op0=ALU.mult, op1=ALU.mult), r=[B("x", par, c0), B("rstd", tag), cB], w=[B("hb", par, c0)])

        def gen_mix(ch, l):
            par = ch % 2; x = X[par]; hb = HB[par]
            ring, SL = ringM, "slotM"
            first, last = (ch == 0), (ch == nch - 1)
            ccs = [(0, 512)] + ([(T, NS)] if first else [])
            if l == 0:
                for k in range(8):
                    Q(lambda e, k=k: e.dma_start(out=x[:, k, 0:T], in_=xT[k * 128:(k + 1) * 128, ch * T:(ch + 1) * T]), w=[B("x", par, 0), B("big"), B("rot"), B("big2")])
                if first:
                    Q(lambda e: e.dma_start(out=x[:, :, T:NTM], in_=xsT.rearrange("(k p) n -> p k n", p=128)), w=[B("x", par, T), B("big")])
                yield
            if True:
                Q(lambda e, l=l: e.dma_start(out=WB[:].rearrange("p a k -> p (a k)"), in_=wb_d[l, :, :]), r=[B("wb_d", l)], w=[B("WB")])
                Q(lambda e, l=l: e.dma_start(out=WC[:].rearrange("p a k -> p (a k)"), in_=wc_d[l, :, :]), r=[B("wc_d", l)], w=[B("WC")])
                for ct in range(2):
                    V(lambda e, ct=ct: e.tensor_scalar(out=Dd[:, ct, :], in0=ident[:], scalar1=pvc(l, OSD + ct), scalar2=None, op0=ALU.mult), r=[cB], w=[B("Dd")])
                G(lambda e: e.dma_start(out=gluw[:], in_=glu_w[l].rearrange("(c p) n -> p c n", p=128)), w=[B("gluw")])
                s_in = [wload("M", w_in[l, :, i * 512:(i + 1) * 512].rearrange("(k p) n -> p k n", p=128), "p (k n) -> p k n", k=8) for i in range(3)]

                def win(o_lo, o_n):
                    s = s_in[o_lo // 512]
                    off = o_lo % 512
                    return lambda k: ring[:, s, k * 512 + off: k * 512 + off + o_n], B(SL, s)

                V(lambda e, l=l: e.tensor_copy(out=kT[:, :, 0:128], in_=khalo[:, l, :, :]), r=[B("khalo", l)], w=[B("kT", "h")])
                V(lambda e, l=l: e.tensor_copy(out=vp[:, 0, :, :], in_=vhalo[:, l, :, :]), r=[B("vhalo", l)], w=[B("vp", 0)])
                V(lambda e, l=l: e.tensor_copy(out=ucv[:, :, 0:30], in_=chalo[:, l, :, :]), r=[B("chalo", l)], w=[B("ucv", "h")])

                rmsnorm(x, hb, par, l * PVL + OG1, ccs, sq, rstd, 6, "m")
                for (c0, cn) in ccs:
                    def mm8(lw, n_out, pbk):
                        f, sb_ = lw
                        for k in range(8):
                            PE(lambda e, k=k: e.matmul(ps[pbk][0:n_out, 0:cn], lhsT=f(k), rhs=hb[:, k, c0:c0 + cn], start=(k == 0), stop=(k == 7)), r=[sb_, B("hb", par, c0)], w=[B("ps", pbk)])
                    for h in range(8):
                        pbk = bank(); mm8(win(64 * h, 64), 64, pbk)
                        S(lambda e, h=h, pbk=pbk: e.activation(out=qT[:, h, c0:c0 + cn], in_=ps[pbk][0:64, 0:cn], func=AF.Copy, scale=0.125), r=[B("ps", pbk)], w=[B("qT", c0)])
                        yield
                    for g in range(2):
                        pbk = bank(); mm8(win(512 + 64 * g, 64), 64, pbk)
                        S(lambda e, g=g, pbk=pbk: e.activation(out=kT[:, g, 128 + c0:128 + c0 + cn], in_=ps[pbk][0:64, 0:cn], func=AF.Copy), r=[B("ps", pbk)], w=[B("kT", c0)])
                        yield
                    for ct in range(2):
                        pa = bank(); mm8(win(768 + 128 * ct, 128), 128, pa)
                        pg = bank(); mm8(win(1024 + 128 * ct, 128), 128, pg)
                        S(lambda e, pg=pg: e.activation(out=g2[:, 0:cn], in_=ps[pg][:, 0:cn], func=AF.Sigmoid), r=[B("ps", pg)], w=[B("g2")])
                        V(lambda e, ct=ct, pa=pa: e.tensor_tensor(out=ucv[:, ct, 30 + c0:30 + c0 + cn], in0=ps[pa][:, 0:cn], in1=g2[:, 0:cn], op=ALU.mult), r=[B("ps", pa), B("g2")], w=[B("ucv", c0)])
                        yield
                    for ct in range(2):
                        pbk = bank(); mm8(win(1280 + 128 * ct, 128), 128, pbk)
                        S(lambda e, ct=ct, pbk=pbk: e.activation(out=us[:, ct, c0:c0 + cn], in_=ps[pbk][:, 0:cn], func=AF.Copy), r=[B("ps", pbk)], w=[B("us", c0)])
                        yield
                fv, sv = win(640, 128)
                fk, sk = win(512, 128)
                for bi in range(4):
                    c0 = bi * 128
                    cc0 = (c0 // 512) * 512
                    pbk = bank()
                    for k in range(8):
                        PE(lambda e, k=k: e.matmul(ps[pbk][:, 0:128], lhsT=hb[:, k, c0:c0 + 128], rhs=fv(k), start=(k == 0), stop=(k == 7)), r=[sv, B("hb", par, cc0)], w=[B("ps", pbk)])
                    S(lambda e, bi=bi, pbk=pbk: e.activation(out=vp[:, bi + 1, :, 64:128], in_=ps[pbk][:, 0:128].rearrange("p (g d) -> p g d", g=2), func=AF.Copy), r=[B("ps", pbk)], w=[B("vp", bi + 1)])
                    yield
                    if last and bi == 3:
                        V(lambda e, pbk=pbk: e.tensor_copy(out=tk[:], in_=ps[pbk][:, 0:128]), r=[B("ps", pbk), B("vp", bi + 1)], w=[B("tk")])
                        Q(lambda e, l=l: e.dma_start(out=ov_p[l, :, :], in_=tk[:]), r=[B("tk")], w=[B("ov_p", l)])
                        pbk2 = bank()
                        for k in range(8):
                            PE(lambda e, k=k: e.matmul(ps[pbk2][:, 0:128], lhsT=hb[:, k, c0:c0 + 128], rhs=fk(k), start=(k == 0), stop=(k == 7)), r=[sk, B("hb", par, cc0)], w=[B("ps", pbk2)])
                        V(lambda e, pbk2=pbk2: e.tensor_copy(out=tk[:], in_=ps[pbk2][:, 0:128]), r=[B("ps", pbk2)], w=[B("tk")])
                        Q(lambda e, l=l: e.dma_start(out=ok_p[l, :, :], in_=tk[:]), r=[B("tk")], w=[B("ok_p", l)])
                if first:
                    pbk = bank()
                    for (f_, s_, off) in ((fk, sk, 0), (fv, sv, 128)):
                        for k in range(8):
                            PE(lambda e, k=k, f_=f_, off=off: e.matmul(ps[pbk][0:NS, off:off + 128], lhsT=hb[:, k, T:NTM], rhs=f_(k), start=(k == 0), stop=(k == 7)), r=[s_, B("hb", par, T)], w=[B("ps", pbk)])
                    V(lambda e, pbk=pbk: e.tensor_copy(out=tkb[:].rearrange("p a k -> p (a k)"), in_=ps[pbk][0:NS, 0:256]), r=[B("ps", pbk)], w=[B("tkb")])
                    for b in range(NS):
                        Q(lambda e, l=l, b=b: e.dma_start(out=ok_s[l, b:b + 1, 0:127, :].rearrange("b r c -> b (r c)"), in_=ck_d[l, b:b + 1, 1:128, :].rearrange("b r c -> b (r c)")), w=[B("ok_s", l)])
                        Q(lambda e, l=l, b=b: e.dma_start(out=ov_s[l, b:b + 1, 0:127, :].rearrange("b r c -> b (r c)"), in_=cv_d[l, b:b + 1, 1:128, :].rearrange("b r c -> b (r c)")), w=[B("ov_s", l)])
                    Q(lambda e, l=l: e.dma_start(out=ok_s[l, :, 127, :], in_=tkb[:, 0, :]), r=[B("tkb")], w=[B("ok_s", l)])
                    Q(lambda e, l=l: e.dma_start(out=ov_s[l, :, 127, :], in_=tkb[:, 1, :]), r=[B("tkb")], w=[B("ov_s", l)])
                if not last:
                    V(lambda e, l=l: e.tensor_copy(out=khalo[:, l, :, :], in_=kT[:, :, T:T + 128]), r=[B("kT", 0)], w=[B("khalo", l)])
                    V(lambda e, l=l: e.tensor_copy(out=vhalo[:, l, :, :], in_=vp[:, 4, :, :]), r=[B("vp", 4)], w=[B("vhalo", l)])
                    V(lambda e, l=l: e.tensor_copy(out=chalo[:, l, :, :], in_=ucv[:, :, T:T + 30]), r=[B("ucv", 0)], w=[B("chalo", l)])
                else:
                    Q(lambda e, l=l: e.dma_start(out=oconv_p[l, :, :].rearrange("p (c k) -> p c k", c=2), in_=ucv[:, :, T:T + 30]), r=[B("ucv", 0)], w=[B("oconv_p", l)])

                for bi in range(4):
                    q0 = bi * 128
                    qcc = (q0 // 512) * 512
                    nokprev = first and bi == 0
                    kread = [B("kT", qcc)] + ([B("kT", "h")] if bi == 0 else [B("kT", ((q0 - 128) // 512) * 512)])
                    for tile in range(4):
                        for r2 in range(2):
                            h = tile * 2 + r2
                            g = h // 4
                            a = h % 2
                            k_lo, k_n = (128, 128) if nokprev else (0, 256)
                            PE(lambda e, h=h, g=g, k_lo=k_lo, k_n=k_n: e.matmul(ps[4][:, k_lo:k_lo + k_n], lhsT=qT[:, h, q0:q0 + 128], rhs=kT[:, g, q0 + k_lo:q0 + k_lo + k_n], start=True, stop=True), r=[B("qT", qcc)] + kread, w=[B("ps", 4)])
                            V(lambda e, h=h, a=a, k_lo=k_lo, k_n=k_n: e.scalar_tensor_tensor(out=sc[:, a, k_lo:k_lo + k_n], in0=abias[:, k_lo:k_lo + k_n], scalar=float(2.0 ** (-(h + 1))), in1=ps[4][:, k_lo:k_lo + k_n], op0=ALU.mult, op1=ALU.add), r=[B("ps", 4), cB], w=[B("sc", a)])
                            V(lambda e, a=a, k_lo=k_lo, k_n=k_n: e.reduce_max(out=sm[:, a, 0:1], in_=sc[:, a, k_lo:k_lo + k_n], axis=AX.X), r=[B("sc", a)], w=[B("sm", a)])
                            sinkc = pvc(l, OSINK + h)
                            V(lambda e, a=a, sinkc=sinkc: e.tensor_scalar(out=sm[:, a, 1:2], in0=sm[:, a, 0:1], scalar1=sinkc, scalar2=-1.0, op0=ALU.max, op1=ALU.mult), r=[B("sm", a), cB], w=[B("sm", a)])
                            S(lambda e, a=a, k_lo=k_lo, k_n=k_n: e.activation(out=pb[:, a, k_lo:k_lo + k_n], in_=sc[:, a, k_lo:k_lo + k_n], func=AF.Exp, bias=sm[:, a, 1:2], accum_out=sm[:, a, 2:3]), r=[B("sc", a), B("sm", a)], w=[B("pb", a), B("sm", a)])
                            S(lambda e, a=a, sinkc=sinkc: e.activation(out=sm[:, a, 3:4], in_=sinkc, func=AF.Exp, bias=sm[:, a, 1:2]), r=[B("sm", a), cB], w=[B("sm", a)])
                            V(lambda e, a=a: e.tensor_tensor(out=sm[:, a, 4:5], in0=sm[:, a, 2:3], in1=sm[:, a, 3:4], op=ALU.add), r=[B("sm", a)], w=[B("sm", a)])
                            V(lambda e, a=a: e.reciprocal(out=sm[:, a, 5:6], in_=sm[:, a, 4:5]), r=[B("sm", a)], w=[B("sm", a)])
                            V(lambda e, a=a, k_lo=k_lo, k_n=k_n: e.tensor_scalar(out=pb[:, a, k_lo:k_lo + k_n], in0=pb[:, a, k_lo:k_lo + k_n], scalar1=sm[:, a, 5:6], scalar2=None, op0=ALU.mult), r=[B("sm", a), B("pb", a)], w=[B("pb", a)])
                            for kb in range(2):
                                if nokprev and kb == 0:
                                    continue
                                PE(lambda e, a=a, kb=kb: e.transpose(out=psb[:, a * 256 + kb * 128:a * 256 + kb * 128 + 128], in_=pb[:, a, kb * 128:kb * 128 + 128], identity=identb[:]), r=[B("pb", a), B("identb")], w=[B("psb", 0)])
                            S(lambda e, a=a, k_lo=k_lo, k_n=k_n: e.activation(out=ptb[:, a, k_lo:k_lo + k_n], in_=psb[:, a * 256 + k_lo:a * 256 + k_lo + k_n], func=AF.Copy), r=[B("psb", 0)], w=[B("ptb", a)])
                            kbs = [1] if nokprev else [0, 1]
                            for kb in kbs:
                                lo = 64 if r2 == 0 else 0
                                PE(lambda e, a=a, kb=kb, g=g, lo=lo, r2=r2, kbs=kbs: e.matmul(ps[5][:, 0:128], lhsT=vp[:, bi + kb, g, lo:lo + 128], rhs=ptb[:, a, kb * 128:kb * 128 + 128], start=(r2 == 0 and kb == kbs[0]), stop=(r2 == 1 and kb == 1)), r=[B("vp", bi + kb), B("ptb", a)], w=[B("ps", 5)])
                        S(lambda e, tile=tile: e.activation(out=hb[:, tile, q0:q0 + 128], in_=ps[5][:, 0:128], func=AF.Copy), r=[B("ps", 5)], w=[B("hb", par, qcc)])
                        yield

                if first:
                    for g in range(2):
                        sk4 = sinkt[:, l * 2 + g:l * 2 + g + 1]
                        for b4 in range(NS // 4):
                            for bb in range(4):
                                b = b4 * 4 + bb
                                Q(lambda e, b=b, l=l: e.dma_start(out=Kb[:, 0, :], in_=ok_s[l, b, :, :]), r=[B("ok_s", l)], w=[B("Kb", 0)])
                                PE(lambda e, b=b, g=g: e.transpose(out=ps[4][0:64, 0:128], in_=Kb[:, 0, g * 64:(g + 1) * 64], identity=ident[:]), r=[B("Kb", 0), cB], w=[B("ps", 4)])
                                S(lambda e, b=b: e.activation(out=KbT[:, b % 2, :], in_=ps[4][0:64, 0:128], func=AF.Copy), r=[B("ps", 4)], w=[B("KbT", b % 2)])
                                PE(lambda e, b=b, g=g, bb=bb: e.matmul(ps[6][0:4, bb * 128:bb * 128 + 128], lhsT=qT[:, 4 * g:4 * g + 4, T + b], rhs=KbT[:, b % 2, :], start=True, stop=True), r=[B("qT", T), B("KbT", b % 2)], w=[B("ps", 6)])
                            V(lambda e, g=g: e.tensor_tensor(out=ssc[:], in0=ps[6][0:4, :].rearrange("p (b k) -> p b k", b=4), in1=sbias[:, g:g + 1, :].to_broadcast([4, 4, 128]), op=ALU.add), r=[B("ps", 6), cB], w=[B("ssc")])
                            V(lambda e: e.reduce_max(out=ssm_[:, 0, :], in_=ssc[:], axis=AX.X), r=[B("ssc")], w=[B("ssm_")])
                            V(lambda e, sk4=sk4: e.tensor_scalar(out=ssm_[:, 0, :], in0=ssm_[:, 0, :], scalar1=sk4, scalar2=None, op0=ALU.max), r=[B("ssm_"), cB], w=[B("ssm_")])
                            V(lambda e: e.tensor_tensor(out=ssc[:], in0=ssc[:], in1=ssm_[:, 0, :].unsqueeze(2).to_broadcast([4, 4, 128]), op=ALU.subtract), r=[B("ssc"), B("ssm_")], w=[B("ssc")])
                            S(lambda e: e.activation(out=ssc[:], in_=ssc[:], func=AF.Exp), r=[B("ssc")], w=[B("ssc")])
                            V(lambda e: e.reduce_sum(out=ssm_[:, 1, :], in_=ssc[:], axis=AX.X), r=[B("ssc")], w=[B("ssm_")])
                            S(lambda e, sk4=sk4: e.activation(out=ssm_[:, 2, :], in_=ssm_[:, 0, :], func=AF.Exp, scale=-1.0, bias=sk4), r=[B("ssm_"), cB], w=[B("ssm_")])
                            V(lambda e: e.tensor_tensor(out=ssm_[:, 1, :], in0=ssm_[:, 1, :], in1=ssm_[:, 2, :], op=ALU.add), r=[B("ssm_")], w=[B("ssm_")])
                            V(lambda e: e.reciprocal(out=ssm_[:, 1, :], in_=ssm_[:, 1, :]), r=[B("ssm_")], w=[B("ssm_")])
                            V(lambda e: e.tensor_tensor(out=spb[:], in0=ssc[:], in1=ssm_[:, 1, :].unsqueeze(2).to_broadcast([4, 4, 128]), op=ALU.mult), r=[B("ssc"), B("ssm_")], w=[B("spb")])
                            for bb in range(4):
                                PE(lambda e, bb=bb: e.transpose(out=psb[:, 512 + bb * 4:512 + bb * 4 + 4], in_=spb[:, bb, :], identity=identb[0:4, 0:4]), r=[B("spb"), B("identb")], w=[B("psb", 0)])
                            S(lambda e, g=g, b4=b4: e.activation(out=sptb[:, g, b4 * 4:b4 * 4 + 4, :], in_=psb[:, 512:512 + 16].rearrange("p (b r) -> p b r", r=4), func=AF.Copy), r=[B("psb", 0)], w=[B("sptb", g)])
                            yield
                    for b in range(NS):
                        Q(lambda e, b=b, l=l: e.dma_start(out=Vb[:, 0, :], in_=ov_s[l, b, :, :]), r=[B("ov_s", l)], w=[B("Vb", 0)])
                        V(lambda e, b=b: e.tensor_copy(out=Vbp[:, :, 64:128], in_=Vb[:, 0, :].rearrange("p (g d) -> p g d", g=2)), r=[B("Vb", 0)], w=[B("Vbp")])
                        for tile in range(4):
                            g = tile // 2
                            for r2 in range(2):
                                rr4 = (tile % 2) * 2 + r2
                                lo = 64 if r2 == 0 else 0
                                PE(lambda e, b=b, g=g, tile=tile, r2=r2, rr4=rr4, lo=lo: e.matmul(ps[5][:, 256 + b * 4 + tile:256 + b * 4 + tile + 1], lhsT=Vbp[:, g, lo:lo + 128], rhs=sptb[:, g, b, rr4:rr4 + 1], start=(r2 == 0), stop=(r2 == 1)), r=[B("Vbp"), B("sptb", g)], w=[B("ps", 5)])
                    S(lambda e: e.activation(out=hb[:, 0:4, T:NTM].rearrange("p t b -> p b t"), in_=ps[5][:, 256:256 + NS * 4].rearrange("p (b t) -> p b t", t=4), func=AF.Copy), r=[B("ps", 5)], w=[B("hb", par, T)])

                if first:
                    Q(lambda e, l=l: e.dma_start(out=cs[:, :, :, 0:30], in_=cconv_d[l, :, :].rearrange("p (c b k) -> p c b k", c=2, b=NS)), w=[B("cs")])
                    V(lambda e: e.tensor_copy(out=cs[:, :, :, 30], in_=ucv[:, :, 30 + T:30 + NTM]), r=[B("ucv", T)], w=[B("cs")])
                    Q(lambda e, l=l: e.dma_start(out=oconv_s[l, :, :].rearrange("p (c b k) -> p c b k", c=2, b=NS), in_=cs[:, :, :, 1:31]), r=[B("cs")], w=[B("oconv_s", l)])
                    for ct in range(2):
                        cw = pvt[:, l * PVL + OCW + ct * 31: l * PVL + OCW + ct * 31 + 31]
                        V(lambda e, ct=ct, cw=cw: e.tensor_tensor(out=cst[:, ct, :, :], in0=cs[:, ct, :, :], in1=cw.unsqueeze(1).to_broadcast([128, NS, 31]), op=ALU.mult), r=[B("cs"), cB], w=[B("ysm", 0), B("ysm", 1)])
                        V(lambda e, ct=ct: e.reduce_sum(out=acc[:, ct, T:NTM], in_=cst[:, ct, :, :], axis=AX.X), r=[B("ysm", 0), B("ysm", 1)], w=[B("acc", T)])
                        V(lambda e, ct=ct: e.tensor_scalar(out=acc[:, ct, T:NTM], in0=acc[:, ct, T:NTM], scalar1=pvc(l, OCB + ct), scalar2=None, op0=ALU.add), r=[B("acc", T), cB], w=[B("acc", T)])
                for (c0, cn) in ccs:
                    if c0 < T:
                        hr = [B("ucv", c0)] + ([B("ucv", "h")] if c0 == 0 else [B("ucv", c0 - 512)])
                        for ct in range(2):
                            for kk in range(31):
                                wk = pvt[:, l * PVL + OCW + ct * 31 + kk: l * PVL + OCW + ct * 31 + kk + 1]
                                if kk == 0:
                                    V(lambda e, ct=ct, wk=wk: e.tensor_scalar(out=acc[:, ct, c0:c0 + cn], in0=ucv[:, ct, c0:c0 + cn], scalar1=wk, scalar2=pvc(l, OCB + ct), op0=ALU.mult, op1=ALU.add), r=hr + [cB], w=[B("acc", c0)])
                                else:
                                    V(lambda e, ct=ct, wk=wk, kk=kk: e.scalar_tensor_tensor(out=acc[:, ct, c0:c0 + cn], in0=ucv[:, ct, c0 + kk:c0 + kk + cn], scalar=wk, in1=acc[:, ct, c0:c0 + cn], op0=ALU.mult, op1=ALU.add), r=hr + [cB, B("acc", c0)], w=[B("acc", c0)])
                    for ct in range(2):
                        S(lambda e, ct=ct: e.activation(out=ysq[:, ct, 0:cn], in_=acc[:, ct, c0:c0 + cn], func=AF.Square), r=[B("acc", c0)], w=[B("ysm", 0), B("ysm", 1)])
                    for ct in range(2):
                        PE(lambda e, ct=ct: e.matmul(ps[6][:, 0:cn], lhsT=ones_f[:], rhs=acc[:, ct, c0:c0 + cn], start=(ct == 0), stop=(ct == 1)), r=[B("acc", c0), cB], w=[B("ps", 6)])
                    V(lambda e: e.tensor_scalar(out=mean[:, 0:cn], in0=ps[6][:, 0:cn], scalar1=1.0 / 256, scalar2=None, op0=ALU.mult), r=[B("ps", 6)], w=[B("mean")])
                    for ct in range(2):
                        PE(lambda e, ct=ct: e.matmul(ps[6][:, 0:cn], lhsT=ones_f[:], rhs=ysq[:, ct, 0:cn], start=(ct == 0), stop=(ct == 1)), r=[B("ysm", 0), B("ysm", 1), cB], w=[B("ps", 6)])
                    V(lambda e: e.tensor_tensor(out=var[:, 0:cn], in0=mean[:, 0:cn], in1=mean[:, 0:cn], op=ALU.mult), r=[B("mean")], w=[B("g2")])
                    V(lambda e: e.scalar_tensor_tensor(out=var[:, 0:cn], in0=ps[6][:, 0:cn], scalar=1.0 / 256, in1=var[:, 0:cn], op0=ALU.mult, op1=ALU.subtract), r=[B("ps", 6), B("g2")], w=[B("g2")])
                    S(lambda e: e.activation(out=var[:, 0:cn], in_=var[:, 0:cn], func=AF.Sqrt, bias=EPS), r=[B("g2")], w=[B("g2")])
                    V(lambda e: e.reciprocal(out=var[:, 0:cn], in_=var[:, 0:cn]), r=[B("g2")], w=[B("g2")])
                    for ct in range(2):
                        V(lambda e, ct=ct: e.tensor_tensor(out=g1[:, 0:cn], in0=acc[:, ct, c0:c0 + cn], in1=mean[:, 0:cn], op=ALU.subtract), r=[B("acc", c0), B("mean")], w=[B("g1")])
                        V(lambda e, ct=ct: e.tensor_tensor(out=g1[:, 0:cn], in0=g1[:, 0:cn], in1=var[:, 0:cn], op=ALU.mult), r=[B("g1"), B("g2")], w=[B("g1")])
                        V(lambda e, ct=ct: e.tensor_scalar(out=g1[:, 0:cn], in0=g1[:, 0:cn], scalar1=pvc(l, OLG + ct), scalar2=pvc(l, OLB + ct), op0=ALU.mult, op1=ALU.add), r=[B("g1"), cB], w=[B("g1")])
                        S(lambda e, ct=ct: e.activation(out=hb[:, 4 + ct, c0:c0 + cn], in_=g1[:, 0:cn], func=AF.Silu), r=[B("g1")], w=[B("hb", par, c0)])
                        yield

                def ssm_out(c0, cn, hsrc_re, hsrc_im, hoff, hbufs):
                    for ct in range(2):
                        for j4 in range(4):
                            s8 = ct * 4 + j4
                            PE(lambda e, ct=ct, s8=s8, j4=j4: e.matmul(ps[6][:, 0:cn], lhsT=WC[:, s8 * 2, :], rhs=hsrc_re(s8), start=(j4 == 0), stop=False), r=hbufs + [B("WC")], w=[B("ps", 6)])
                            PE(lambda e, ct=ct, s8=s8: e.matmul(ps[6][:, 0:cn], lhsT=WC[:, s8 * 2 + 1, :], rhs=hsrc_im(s8), start=False, stop=False), r=hbufs + [B("WC")], w=[B("ps", 6)])
                        PE(lambda e, ct=ct: e.matmul(ps[6][:, 0:cn], lhsT=Dd[:, ct, :], rhs=us[:, ct, c0:c0 + cn], start=False, stop=True), r=[B("us", (c0 // 512) * 512 if c0 < T else T), B("Dd")], w=[B("ps", 6)])
                        V(lambda e, ct=ct: e.tensor_copy(out=ysm[:, ct, 0:cn], in_=ps[6][:, 0:cn]), r=[B("ps", 6)], w=[B("ysm", ct)])
                        V(lambda e, ct=ct: e.tensor_tensor(out=g1[:, 0:cn], in0=ysm[:, ct, 0:cn], in1=ysm[:, ct, 0:cn], op=ALU.mult), r=[B("ysm", ct)], w=[B("g1")])
                        V(lambda e, ct=ct: e.tensor_scalar(out=g1[:, 0:cn], in0=g1[:, 0:cn], scalar1=0.044715, scalar2=1.0, op0=ALU.mult, op1=ALU.add), r=[B("g1")], w=[B("g1")])
                        V(lambda e, ct=ct: e.tensor_tensor(out=g1[:, 0:cn], in0=g1[:, 0:cn], in1=ysm[:, ct, 0:cn], op=ALU.mult), r=[B("g1"), B("ysm", ct)], w=[B("g1")])
                        S(lambda e, ct=ct: e.activation(out=g2[:, 0:cn], in_=g1[:, 0:cn], func=AF.Sigmoid, scale=2.0 * math.sqrt(2.0 / math.pi)), r=[B("g1")], w=[B("g2")])
                        V(lambda e, ct=ct: e.tensor_tensor(out=ysm[:, ct, 0:cn], in0=ysm[:, ct, 0:cn], in1=g2[:, 0:cn], op=ALU.mult), r=[B("g2"), B("ysm", ct)], w=[B("ysm", ct)])
                        V(lambda e, ct=ct: e.tensor_copy(out=yb[:, ct, 0:cn], in_=ysm[:, ct, 0:cn]), r=[B("ysm", ct)], w=[B("yb", ct)])
                    for co in range(2):
                        for ct in range(2):
                            PE(lambda e, ct=ct, co=co: e.matmul(ps[6][:, 0:cn], lhsT=gluw[:, ct, co * 128:(co + 1) * 128], rhs=yb[:, ct, 0:cn], start=(ct == 0), stop=(ct == 1)), r=[B("yb", 0), B("yb", 1), B("gluw")], w=[B("ps", 6)])
                        S(lambda e, co=co: e.activation(out=g2[:, 0:cn], in_=ps[6][:, 0:cn], func=AF.Sigmoid, bias=pvc(l, OGB + co)), r=[B("ps", 6), cB], w=[B("g2")])
                        V(lambda e, co=co: e.tensor_tensor(out=hb[:, 6 + co, c0:c0 + cn], in0=ysm[:, co, 0:cn], in1=g2[:, 0:cn], op=ALU.mult), r=[B("g2"), B("ysm", co)], w=[B("hb", par, (c0 // 512) * 512 if c0 < T else T)])

                for sc_i in range(T // J):
                    c0 = sc_i * J
                    ucc = (c0 // 512) * 512
                    for s8 in range(8):
                        ct, a = s8 // 4, s8 % 2
                        for ci, dst in ((0, 0), (1, J)):
                            PE(lambda e, ci=ci, dst=dst, s8=s8, ct=ct: e.matmul(ps[4][:, dst:dst + J], lhsT=WB[:, s8 * 2 + ci, :], rhs=us[:, ct, c0:c0 + J], start=True, stop=True), r=[B("us", ucc), B("WB")], w=[B("ps", 4)])
                        ra = s8 % 2
                        Q(lambda e, s8=s8, ra=ra: e.dma_start(out=rot2[:, ra, :, :].rearrange("p c j -> p (c j)"), in_=rot_d[l, :, s8 * 2 * J:(s8 + 1) * 2 * J]), r=[B("rot_d", l)], w=[B("rot2", ra)])
                        cosT, sinT = rot2[:, ra, 0, :], rot2[:, ra, 1, :]
                        pre, pim = ps[4][:, 0:J], ps[4][:, J:2 * J]
                        V(lambda e, cosT=cosT, pre=pre: e.tensor_tensor(out=bre[:], in0=pre, in1=cosT, op=ALU.mult), r=[B("ps", 4), B("rot2", ra)], w=[B("bre")])
                        V(lambda e, sinT=sinT, pim=pim: e.tensor_tensor(out=t1[:], in0=pim, in1=sinT, op=ALU.mult), r=[B("ps", 4), B("rot2", ra)], w=[B("t1")])
                        V(lambda e: e.tensor_tensor(out=bre[:], in0=bre[:], in1=t1[:], op=ALU.add), r=[B("t1"), B("bre")], w=[B("bre")])
                        V(lambda e, cosT=cosT, pim=pim: e.tensor_tensor(out=bim[:], in0=pim, in1=cosT, op=ALU.mult), r=[B("ps", 4), B("rot2", ra)], w=[B("bim")])
                        V(lambda e, sinT=sinT, pre=pre: e.tensor_tensor(out=t2[:], in0=pre, in1=sinT, op=ALU.mult), r=[B("ps", 4), B("rot2", ra)], w=[B("t2")])
                        V(lambda e: e.tensor_tensor(out=bim[:], in0=bim[:], in1=t2[:], op=ALU.subtract), r=[B("t2"), B("bim")], w=[B("bim")])
                        rho_b = lam[:, l, 0, s8:s8 + 1].to_broadcast([128, J])
                        V(lambda e, rho_b=rho_b, s8=s8: e.tensor_tensor_scan(out=wre[:], data0=rho_b, data1=bre[:], initial=hcar[:, l, s8, 0:1], op0=ALU.mult, op1=ALU.add), r=[B("bre"), B("lam", l), B("hcar", l)], w=[B("wre")])
                        V(lambda e, rho_b=rho_b, s8=s8: e.tensor_tensor_scan(out=wim[:], data0=rho_b, data1=bim[:], initial=hcar[:, l, s8, 1:2], op0=ALU.mult, op1=ALU.add), r=[B("bim"), B("lam", l), B("hcar", l)], w=[B("wim")])
                        V(lambda e, cosT=cosT: e.tensor_tensor(out=t1[:], in0=wre[:], in1=cosT, op=ALU.mult), r=[B("wre"), B("rot2", ra)], w=[B("t1")])
                        V(lambda e, sinT=sinT: e.tensor_tensor(out=t2[:], in0=wim[:], in1=sinT, op=ALU.mult), r=[B("wim"), B("rot2", ra)], w=[B("t2")])
                        V(lambda e, s8=s8: e.tensor_tensor(out=hbf[:, s8 * J:(s8 + 1) * J], in0=t1[:], in1=t2[:], op=ALU.subtract), r=[B("t1"), B("t2")], w=[B("hbf")])
                        V(lambda e, sinT=sinT: e.tensor_tensor(out=t1[:], in0=wre[:], in1=sinT, op=ALU.mult), r=[B("wre"), B("rot2", ra)], w=[B("t1")])
                        V(lambda e, cosT=cosT: e.tensor_tensor(out=t2[:], in0=wim[:], in1=cosT, op=ALU.mult), r=[B("wim"), B("rot2", ra)], w=[B("t2")])
                        V(lambda e, s8=s8: e.tensor_tensor(out=hbf[:, 8 * J + s8 * J:8 * J + (s8 + 1) * J], in0=t1[:], in1=t2[:], op=ALU.add), r=[B("t1"), B("t2")], w=[B("hbf")])
                        cl, sl = rot2[:, ra, 0, J - 1:J], rot2[:, ra, 1, J - 1:J]
                        V(lambda e, sl=sl: e.tensor_tensor(out=sm[:, 0, 6:7], in0=wim[:, J - 1:J], in1=sl, op=ALU.mult), r=[B("wim"), B("rot2", ra)], w=[B("sm", 0)])
                        V(lambda e, s8=s8, cl=cl: e.scalar_tensor_tensor(out=hcar[:, l, s8, 0:1], in0=wre[:, J - 1:J], scalar=cl, in1=sm[:, 0, 6:7], op0=ALU.mult, op1=ALU.subtract), r=[B("wre"), B("rot2", ra), B("sm", 0)], w=[B("hcar", l)])
                        V(lambda e, sl=sl: e.tensor_tensor(out=sm[:, 0, 7:8], in0=wre[:, J - 1:J], in1=sl, op=ALU.mult), r=[B("wre"), B("rot2", ra)], w=[B("sm", 0)])
                        V(lambda e, s8=s8, cl=cl: e.scalar_tensor_tensor(out=hcar[:, l, s8, 1:2], in0=wim[:, J - 1:J], scalar=cl, in1=sm[:, 0, 7:8], op0=ALU.mult, op1=ALU.add), r=[B("wim"), B("rot2", ra), B("sm", 0)], w=[B("hcar", l)])
                        yield
                    hbre = hbf
                    ssm_out(c0, J, lambda s8: hbre[:, s8 * J:(s8 + 1) * J], lambda s8: hbre[:, 8 * J + s8 * J:8 * J + (s8 + 1) * J], 0, [B("hbf")])
                    yield
                if last:
                    Q(lambda e, l=l: e.dma_start(out=ossm_p[l, :, :].rearrange("p (s c) -> p s c", c=2), in_=hcar[:, l, :, :]), r=[B("hcar", l)], w=[B("ossm_p", l)])
                if first:
                    Q(lambda e, l=l: e.dma_start(out=h0[:, 0, :, :], in_=sre_d[l, :, :].rearrange("p (s b) -> p s b", s=8)), w=[B("h0")])
                    Q(lambda e, l=l: e.dma_start(out=h0[:, 1, :, :], in_=sim_d[l, :, :].rearrange("p (s b) -> p s b", s=8)), w=[B("h0")])
                    for s8 in range(8):
                        ct = s8 // 4
                        for ci in range(2):
                            PE(lambda e, ci=ci, s8=s8, ct=ct: e.matmul(ps[4][:, (s8 * 2 + ci) * NS:(s8 * 2 + ci + 1) * NS], lhsT=WB[:, s8 * 2 + ci, :], rhs=us[:, ct, T:NTM], start=True, stop=True), r=[B("us", T), B("WB")], w=[B("ps", 4)])
                    V(lambda e: e.tensor_copy(out=h1[:].rearrange("p c s b -> p s c b"), in_=ps[4][:, 0:16 * NS].rearrange("p (s c b) -> p s c b", s=8, c=2)), r=[B("ps", 4)], w=[B("h1")])
                    for s8 in range(8):
                        lre, lim = pl[:, 5, s8:s8 + 1], pl[:, 6, s8:s8 + 1]
                        V(lambda e, s8=s8: e.tensor_tensor(out=sm[:, 0, 6:7], in0=lam[:, l, 0, s8:s8 + 1], in1=lam[:, l, 1, s8:s8 + 1], op=ALU.mult), r=[B("lam", l)], w=[B("sm", 0)])
                        V(lambda e, s8=s8: e.tensor_tensor(out=sm[:, 0, 7:8], in0=lam[:, l, 0, s8:s8 + 1], in1=lam[:, l, 2, s8:s8 + 1], op=ALU.mult), r=[B("lam", l)], w=[B("sm", 0)])
                        V(lambda e, s8=s8: e.tensor_scalar(out=sm[:, 1, 7:8], in0=sm[:, 0, 7:8], scalar1=-1.0, scalar2=None, op0=ALU.mult), r=[B("sm", 0)], w=[B("sm", 1)])
                        V(lambda e, s8=s8: e.scalar_tensor_tensor(out=h1[:, 0, s8, :], in0=h0[:, 0, s8, :], scalar=sm[:, 0, 6:7], in1=h1[:, 0, s8, :], op0=ALU.mult, op1=ALU.add), r=[B("h0"), B("sm", 0), B("h1")], w=[B("h1")])
                        V(lambda e, s8=s8: e.scalar_tensor_tensor(out=h1[:, 0, s8, :], in0=h0[:, 1, s8, :], scalar=sm[:, 1, 7:8], in1=h1[:, 0, s8, :], op0=ALU.mult, op1=ALU.add), r=[B("h0"), B("sm", 1), B("h1")], w=[B("h1")])
                        V(lambda e, s8=s8: e.scalar_tensor_tensor(out=h1[:, 1, s8, :], in0=h0[:, 1, s8, :], scalar=sm[:, 0, 6:7], in1=h1[:, 1, s8, :], op0=ALU.mult, op1=ALU.add), r=[B("h0"), B("sm", 0), B("h1")], w=[B("h1")])
                        V(lambda e, s8=s8: e.scalar_tensor_tensor(out=h1[:, 1, s8, :], in0=h0[:, 0, s8, :], scalar=sm[:, 0, 7:8], in1=h1[:, 1, s8, :], op0=ALU.mult, op1=ALU.add), r=[B("h0"), B("sm", 0), B("h1")], w=[B("h1")])
                    Q(lambda e, l=l: e.dma_start(out=ossm_s[l, :, :].rearrange("p (c s b) -> p c s b", c=2, s=8), in_=h1[:]), r=[B("h1")], w=[B("ossm_s", l)])
                    V(lambda e: e.tensor_copy(out=h1b[:], in_=h1[:]), r=[B("h1")], w=[B("h1b")])
                    ssm_out(T, NS, lambda s8: h1b[:, 0, s8, :], lambda s8: h1b[:, 1, s8, :], 0, [B("h1b")])
                    yield

                s_out = [wload("M", w_out[l, :, i * 512:(i + 1) * 512].rearrange("(k p) n -> p k n", p=128), "p (k n) -> p k n", k=8) for i in range(2)]
                for (c0, cn) in ccs:
                    for dt_ in range(8):
                        s = s_out[dt_ // 4]
                        off = (dt_ % 4) * 128
                        pbk = bank()
                        for k in range(8):
                            PE(lambda e, k=k, s=s, off=off, pbk=pbk: e.matmul(ps[pbk][:, 0:cn], lhsT=ring[:, s, k * 512 + off:k * 512 + off + 128], rhs=hb[:, k, c0:c0 + cn], start=(k == 0), stop=(k == 7)), r=[B(SL, s), B("hb", par, c0)], w=[B("ps", pbk)])
                        V(lambda e, dt_=dt_, pbk=pbk: e.tensor_tensor(out=x[:, dt_, c0:c0 + cn], in0=ps[pbk][:, 0:cn], in1=x[:, dt_, c0:c0 + cn], op=ALU.add), r=[B("ps", pbk), B("x", par, c0)], w=[B("x", par, c0)])
                        yield

        def gen_ffn(ch, l):
            par = ch % 2; x = X[par]; hb = HB[par]
            first, last = (ch == 0), (ch == nch - 1)
            ccs = [(0, 512)] + ([(T, NS)] if first else [])
            sq, rstd = sq2, rstd2
            ring, SL = ringF, "slotF"
            bank = fbank
            if True:
                rmsnorm(x, hb, par, l * PVL + OG2, ccs, sq2, rstd2, fbank(), "f")
                for fg in range(6):
                    nf = 4 if fg < 5 else 2
                    sg_ = wload("F", w_g[l, :, fg * 512:fg * 512 + nf * 128].rearrange("(k p) n -> p k n", p=128), "p (k n) -> p k n", k=8)
                    su_ = wload("F", w_u[l, :, fg * 512:fg * 512 + nf * 128].rearrange("(k p) n -> p k n", p=128), "p (k n) -> p k n", k=8)
                    sd_ = wload("F", w_d[l, fg * 512:fg * 512 + nf * 128, :].rearrange("(f p) n -> p f n", p=128), "p (f n) -> p f n", f=nf)
                    W_ = nf * 128
                    for (c0, cn) in ccs:
                        for f in range(nf):
                            pg, pu = bank(), bank()
                            for (s_, pb_) in ((sg_, pg), (su_, pu)):
                                for k in range(8):
                                    PE(lambda e, k=k, s_=s_, pb_=pb_, f=f: e.matmul(ps[pb_][:, 0:cn], lhsT=ring[:, s_, k * W_ + f * 128:k * W_ + f * 128 + 128], rhs=hb[:, k, c0:c0 + cn], start=(k == 0), stop=(k == 7)), r=[B(SL, s_), B("hb", par, c0)], w=[B("ps", pb_)])
                            S(lambda e, pg=pg, f=f: e.activation(out=sgb[:, f % 2, 0:cn], in_=ps[pg][:, 0:cn], func=AF.Silu), r=[B("ps", pg)], w=[B("sgb", f % 2)])
                            V(lambda e, pu=pu, f=f: e.tensor_tensor(out=hid[:, f, c0:c0 + cn], in0=ps[pu][:, 0:cn], in1=sgb[:, f % 2, 0:cn], op=ALU.mult), r=[B("ps", pu), B("sgb", f % 2)], w=[B("hid", c0)])
                            yield
                        for dt_ in range(8):
                            pbk = bank()
                            for f in range(nf):
                                PE(lambda e, f=f, dt_=dt_, pbk=pbk: e.matmul(ps[pbk][:, 0:cn], lhsT=ring[:, sd_, f * 1024 + dt_ * 128:f * 1024 + dt_ * 128 + 128], rhs=hid[:, f, c0:c0 + cn], start=(f == 0), stop=(f == nf - 1)), r=[B(SL, sd_), B("hid", c0)], w=[B("ps", pbk)])
                            V(lambda e, dt_=dt_, pbk=pbk: e.tensor_tensor(out=x[:, dt_, c0:c0 + cn], in0=ps[pbk][:, 0:cn], in1=x[:, dt_, c0:c0 + cn], op=ALU.add), r=[B("ps", pbk), B("x", par, c0)], w=[B("x", par, c0)])
                            yield
            if l == L - 1:
                for (c0, cn) in ccs:
                    for k in range(8):
                        S(lambda e, k=k: e.activation(out=sq[:, 0:cn], in_=x[:, k, c0:c0 + cn], func=AF.Square), r=[B("x", par, c0)], w=[B("sq", "f")])
                        PE(lambda e, k=k: e.matmul(ps[0][:, 0:cn], lhsT=ones_b[:], rhs=sq[:, 0:cn], start=(k == 0), stop=(k == 7)), r=[B("sq", "f"), cB], w=[B("ps", 0)])
                    S(lambda e: e.activation(out=rstd[:, 0:cn], in_=ps[0][:, 0:cn], func=AF.Sqrt, scale=1.0 / D, bias=EPS), r=[B("ps", 0)], w=[B("rstd", "f")])
                    V(lambda e: e.reciprocal(out=rstd[:, 0:cn], in_=rstd[:, 0:cn]), r=[B("rstd", "f")], w=[B("rstd", "f")])
                    for k in range(8):
                        gk = pvt[:, 4 * PVL + k: 4 * PVL + k + 1]
                        V(lambda e, k=k, gk=gk: e.scalar_tensor_tensor(out=x[:, k, c0:c0 + cn], in0=x[:, k, c0:c0 + cn], scalar=gk, in1=rstd[:, 0:cn], op0=ALU.mult, op1=ALU.mult), r=[B("x", par, c0), B("rstd", "f"), cB], w=[B("x", par, c0)])
                        if c0 < T:
                            Q(lambda e, k=k: e.dma_start(out=yT[k * 128:(k + 1) * 128, ch * T + c0:ch * T + c0 + cn], in_=x[:, k, c0:c0 + cn]), r=[B("x", par, c0)], w=[B("yT")])
                        else:
                            Q(lambda e, k=k: e.dma_start(out=ysT[k * 128:(k + 1) * 128, :], in_=x[:, k, c0:c0 + cn]), r=[B("x", par, c0)], w=[B("ysT")])
            yield

        def count_ops(gen):
            P.dry = True
            n0 = P.dryn
            for _ in gen:
                pass
            P.dry = False
            return P.dryn - n0

        def run2(ga, na, gb, nb):
            ia = ib = 0
            alive_a, alive_b = ga is not None, gb is not None
            while alive_a or alive_b:
                pick_a = alive_a and (not alive_b or ia * nb <= ib * na)
                n0 = P.nops
                if pick_a:
                    try:
                        next(ga)
                    except StopIteration:
                        alive_a = False
                    ia += P.nops - n0
                else:
                    try:
                        next(gb)
                    except StopIteration:
                        alive_b = False
                    ib += P.nops - n0

        streams = []
        for ch in range(nch):
            ph = []
            for l in range(L):
                ph.append(("m", ch, l))
                ph.append(("f", ch, l))
            streams.append(ph)
        steps = []
        t = 0
        start = {}
        for ch in range(nch):
            start[ch] = (ch // 2) * 2 * L + (ch % 2)
        nsteps = max(start[c] + 2 * L for c in range(nch))
        for t in range(nsteps):
            cur = []
            for ch in range(nch):
                p = t - start[ch]
                if 0 <= p < 2 * L:
                    cur.append(streams[ch][p])
            steps.append(cur)
        for cur in steps:
            gens = []
            for (kind, ch, l) in cur:
                mk = (lambda: gen_mix(ch, l)) if kind == "m" else (lambda: gen_ffn(ch, l))
                n = count_ops(mk())
                gens.append((mk(), max(n, 1)))
            if len(gens) == 1:
                run2(gens[0][0], gens[0][1], None, 1)
            else:
                run2(gens[0][0], gens[0][1], gens[1][0], gens[1][1])
        print('NOPS', P.nops, {k: len(v) for k, v in P.ops.items()}, flush=True)
        P.emit()
    return nc


def _alibi_tables():
    slopes = 2.0 ** (-8.0 * np.arange(1, 9, dtype=np.float32) / 8)
    qi = np.arange(128)[:, None]
    kj = np.arange(256)[None, :]
    dist = qi - kj + 128
    valid = (dist >= 0) & (dist < 128)
    ab = np.where(valid, -dist.astype(np.float32), -1.0e7).astype(np.float32)
    dj = (127 - np.arange(128)).astype(np.float32)
    sbias = (-slopes.reshape(2, 4, 1) * dj[None, None, :]).astype(np.float32)
    sbias = np.ascontiguousarray(sbias.transpose(1, 0, 2)).reshape(4, 2 * 128)
    return ab, sbias


def _fm(v):
    return np.ascontiguousarray(np.asarray(v, np.float32).reshape(-1, 128).T)


_NC_CACHE = {}


def kernel(nch=16, depth=4, **inp):
    f = lambda k: np.asarray(inp[k], np.float32)
    L = depth
    TT = nch * T
    key = (nch, depth)
    if key not in _NC_CACHE:
        _NC_CACHE[key] = build(nch, depth)
    nc = _NC_CACHE[key]
    ab, sbias = _alibi_tables()
    pv = np.zeros((128, NPV), np.float32)
    for l in range(L):
        o = l * PVL
        pv[:, o + 0:o + 8] = _fm(f("norm_mix_g")[l])
        pv[:, o + 8:o + 16] = _fm(f("norm_ffn_g")[l])
        cw = f("conv_dw_w")[l]
        for ct in range(2):
            pv[:, o + 16 + ct * 31:o + 16 + (ct + 1) * 31] = cw[:, ct * 128:(ct + 1) * 128].T
        pv[:, o + 78:o + 80] = _fm(f("conv_dw_b")[l])
        pv[:, o + 80:o + 82] = _fm(f("conv_ln_g")[l])
        pv[:, o + 82:o + 84] = _fm(f("conv_ln_b")[l])
        pv[:, o + 84:o + 86] = _fm(f("ssm_d")[l])
        pv[:, o + 86:o + 88] = _fm(f("ssm_glu_b")[l])
        pv[:, o + 88:o + 96] = _fm(f("ssm_a_re")[l].reshape(-1))
        pv[:, o + 96:o + 104] = _fm(f("ssm_a_im")[l].reshape(-1))
        pv[:, o + 104:o + 112] = _fm(np.repeat(f("ssm_log_dt")[l], 64))
        pv[:, o + 112:o + 120] = f("attn_sinks")[l][None, :]
    pv[:, 4 * PVL:4 * PVL + 8] = _fm(f("norm_final_g"))
    Bn = np.zeros((L, 128, 8, 2, 128), np.float32)
    Cn = np.zeros((L, 128, 8, 2, 128), np.float32)
    for l in range(L):
        for ci, (bk, ck_) in enumerate((("ssm_b_re", "ssm_c_re"), ("ssm_b_im", "ssm_c_im"))):
            bb = f(bk)[l]
            cc = f(ck_)[l]
            for g in range(16):
                st_, p0 = g // 2, (g % 2) * 64
                col = (g % 8) * 16
                Bn[l, p0:p0 + 64, st_, ci, col:col + 16] = bb[g]
                Cn[l, p0:p0 + 64, st_, ci, col:col + 16] = cc[g].T
    Bn = Bn.reshape(L, 128, -1)
    Cn = Cn.reshape(L, 128, -1)
    ident = np.eye(128, dtype=np.float32)
    jjv = np.broadcast_to(np.arange(1, J + 1, dtype=np.float32)[None, :], (128, J)).copy()
    xp = f("x_prompt")
    xs = f("x_sample")[:, 0, :]
    shared = {
        "w_in": f("w_in")[:L], "w_out": f("w_out")[:L], "w_g": f("w_ff_gate")[:L], "w_u": f("w_ff_up")[:L],
        "w_d": f("w_ff_down")[:L], "glu_w": f("ssm_glu_w")[:L], "pv": pv, "Bn": Bn, "Cn": Cn, "ident": ident,
        "jj": jjv, "abias": ab, "sbias": sbias,
        "sinkc": np.ascontiguousarray(f("attn_sinks")[:L].reshape(L, 2, 4).transpose(2, 0, 1).reshape(4, L * 2)),
    }
    in_maps = []
    for c in range(8):
        sq_, b0 = c // 4, c * NS
        m = dict(shared)
        m["xT"] = np.ascontiguousarray(xp[sq_, :TT, :].T)
        m["xsT"] = np.ascontiguousarray(xs[b0:b0 + NS].T)
        m["ck"] = np.ascontiguousarray(f("cache_swa_k")[:L, b0:b0 + NS].reshape(L, NS, 128, 128))
        m["cv"] = np.ascontiguousarray(f("cache_swa_v")[:L, b0:b0 + NS].reshape(L, NS, 128, 128))
        cc_ = f("cache_conv")[:L, b0:b0 + NS]
        m["cconv"] = np.ascontiguousarray(cc_.reshape(L, NS, 30, 2, 128).transpose(0, 4, 3, 1, 2)).reshape(L, 128, -1)
        for nm, kk in (("sre", "state_ssm_re"), ("sim", "state_ssm_im")):
            s_ = f(kk)[:L, b0:b0 + NS].reshape(L, NS, 8, 128)
            m[nm] = np.ascontiguousarray(s_.transpose(0, 3, 2, 1)).reshape(L, 128, -1)
        in_maps.append(m)
    res = run_bass_kernel_spmd(nc, in_maps, core_ids=list(range(8))).results
    y_p = np.stack([res[0]["yT"].T, res[4]["yT"].T]).astype(np.float32)
    y_s = np.concatenate([res[c]["ysT"].T for c in range(8)])[:, None, :].astype(np.float32)
    pc = (res[0], res[4])
    k_p = np.stack([np.stack([r["ok_p"][l].reshape(128, 2, 64) for r in pc]) for l in range(L)])
    v_p = np.stack([np.stack([r["ov_p"][l].reshape(128, 2, 64) for r in pc]) for l in range(L)])
    conv_p = np.stack([np.stack([r["oconv_p"][l].reshape(128, 2, 30).transpose(2, 1, 0).reshape(30, 256) for r in pc]) for l in range(L)])
    ssm_p = [np.stack([np.stack([r["ossm_p"][l].reshape(128, 8, 2)[:, :, ci].T.reshape(16, 64) for r in pc]) for l in range(L)]) for ci in range(2)]
    k_s = np.concatenate([res[c]["ok_s"].reshape(L, NS, 128, 2, 64) for c in range(8)], 1)
    v_s = np.concatenate([res[c]["ov_s"].reshape(L, NS, 128, 2, 64) for c in range(8)], 1)
    conv_s = np.concatenate([res[c]["oconv_s"].reshape(L, 128, 2, NS, 30).transpose(0, 3, 4, 2, 1).reshape(L, NS, 30, 256) for c in range(8)], 1)
    ssm_s = [np.concatenate([res[c]["ossm_s"].reshape(L, 128, 2, 8, NS)[:, :, ci].transpose(0, 3, 2, 1).reshape(L, NS, 16, 64) for c in range(8)], 1) for ci in range(2)]
    outs = (y_p, y_s, k_p, v_p, conv_p, ssm_p[0], ssm_p[1], k_s, v_s, conv_s, ssm_s[0], ssm_s[1])
    return tuple(np.ascontiguousarray(o, dtype=np.float32) for o in outs)
```

```python
import math
import os
import numpy as np
from contextlib import ExitStack
import concourse.bass as bass
import concourse.mybir as mybir
from concourse.bass_utils import run_bass_kernel_spmd

F32 = mybir.dt.float32
BF16 = mybir.dt.bfloat16
I32 = mybir.dt.int32
AF = mybir.ActivationFunctionType
ALU = mybir.AluOpType
AX = mybir.AxisListType

D = 1024
SEQ = 8192
T = 512
NS = 16
NTM = T + NS
DFF = 2816
NF = 22
J = 256
NSLOT = 4
PVL = 120
NPV = 4 * PVL + 8
EPS = 1e-6
TWO_PI = 2.0 * math.pi


import types


def _freeze(fn):
    if fn.__closure__ is None:
        return fn
    cells = []
    for c in fn.__closure__:
        try:
            cells.append(types.CellType(c.cell_contents))
        except ValueError:
            cells.append(c)
    return types.FunctionType(fn.__code__, fn.__globals__, fn.__name__, fn.__defaults__, tuple(cells))


class _Stub:
    def __init__(self):
        self.closed = True

    def matmul(self, *a, **kw):
        self.closed = bool(kw.get("stop", True))
        return self

    def transpose(self, *a, **kw):
        self.closed = True
        return self

    def then_inc(self, *a, **kw):
        return self


class Buf:
    __slots__ = ("name", "lw", "rd")

    def __init__(self, name):
        self.name = name
        self.lw = None
        self.rd = {}


class Prog:
    ENG = ["tensor", "vector", "scalar", "gpsimd", "sync"]

    def __init__(self, nc, same_eng_sync=("vector", "scalar", "gpsimd")):
        self.nc = nc
        self.ops = {e: [] for e in self.ENG}
        self.cnt = {}
        self.known = {e: {} for e in self.ENG}
        self.same = set(same_eng_sync)
        self.bufs = {}
        self.dry = False
        self.dq = {}
        self.dryn = 0
        self.nops = 0

    def B(self, *key):
        b = self.bufs.get(key)
        if b is None:
            b = self.bufs[key] = Buf(key)
        return b

    def op(self, eng, fn, r=(), w=(), dma=None, inc=None):
        if self.dry:
            self.dryn += 1
            return None
        fn = _freeze(fn)
        if getattr(self, "stopped", False):
            return None
        self.nops += 1
        if eng == "tensor" and "KLIMIT" in os.environ:
            st_ = _Stub()
            try:
                fn(st_)
            except Exception:
                pass
            self.open_grp = not st_.closed
        if self.nops >= int(os.environ.get("KLIMIT", "100000000")) and not getattr(self, "open_grp", False):
            self.stopped = True
        ex = [b for b in r if b.name[0] in ("ps", "psb")]
        if ex:
            r = [b for b in r if b.name[0] not in ("ps", "psb")]
            w = list(w) + ex
        deps = {}
        for b in list(r) + list(w):
            if b.lw is not None and deps.get(b.lw[0], 0) < b.lw[1]:
                deps[b.lw[0]] = b.lw[1]
        for b in w:
            for s, v in b.rd.items():
                if deps.get(s, 0) < v:
                    deps[s] = v
        pre = None
        if dma is None:
            sem, step = eng, 1
        else:
            npool = 32 if eng == "sync" else 8
            i = self.dq.get(eng, 0)
            self.dq[eng] = i + 1
            sem, step = "d%s%d" % (eng[0], i % npool), 16
            if i >= npool:
                pre = (sem, 16 * (i // npool))
        waits = []
        if pre is not None and deps.get(pre[0], 0) < pre[1]:
            deps[pre[0]] = pre[1]
        for s, v in deps.items():
            if s == eng and eng not in self.same:
                continue
            if self.known[eng].get(s, 0) >= v:
                continue
            self.known[eng][s] = v
            waits.append((s, v))
        self.cnt[sem] = self.cnt.get(sem, 0) + step
        tok = (sem, self.cnt[sem])
        self.ops[eng].append((fn, waits, sem, step))
        for b in r:
            if b.rd.get(sem, 0) < tok[1]:
                b.rd[sem] = tok[1]
        for b in w:
            b.lw = tok
            b.rd = {}
        return tok

    def emit(self):
        nc = self.nc
        with ExitStack() as st:
            sems = {s: st.enter_context(nc.semaphore(s)) for s in self.cnt}
            block = st.enter_context(nc.Block())
            final = dict(self.cnt)

            def mk(engname):
                def body(e):
                    for fn, waits, sem, step in self.ops[engname]:
                        for s, v in waits:
                            e.wait_ge(sems[s], v)
                        fn(e).then_inc(sems[sem], step)
                    if engname == "sync":
                        for s, v in final.items():
                            e.wait_ge(sems[s], v)
                return body

            for engname in self.ENG:
                if self.ops[engname] or engname == "sync":
                    getattr(block, engname)(mk(engname))


def build(nch=16, depth=4):
    nc = bass.Bass("TRN2", target_bir_lowering=False)
    L = depth
    TT = nch * T

    def din(name, shape, dt=F32):
        return nc.dram_tensor(name, list(shape), dt, kind="ExternalInput").ap()

    def dout(name, shape, dt=F32):
        return nc.dram_tensor(name, list(shape), dt, kind="ExternalOutput").ap()

    xT = din("xT", [D, TT]); xsT = din("xsT", [D, NS])
    w_in = din("w_in", [L, D, 1536]); w_out = din("w_out", [L, D, D])
    w_g = din("w_g", [L, D, DFF]); w_u = din("w_u", [L, D, DFF]); w_d = din("w_d", [L, DFF, D])
    glu_w = din("glu_w", [L, 256, 256])
    pv_d = din("pv", [128, NPV])
    Bn_d = din("Bn", [L, 128, 8 * 2 * 128]); Cn_d = din("Cn", [L, 128, 8 * 2 * 128])
    ident_d = din("ident", [128, 128]); jj_d = din("jj", [128, J])
    abias_d = din("abias", [128, 256]); sbias_d = din("sbias", [4, 2 * 128]); sinkc_d = din("sinkc", [4, L * 2])
    ck_d = din("ck", [L, NS, 128, 128]); cv_d = din("cv", [L, NS, 128, 128])
    cconv_d = din("cconv", [L, 128, 2 * NS * 30])
    sre_d = din("sre", [L, 128, 8 * NS]); sim_d = din("sim", [L, 128, 8 * NS])

    yT = dout("yT", [D, TT]); ysT = dout("ysT", [D, NS])
    ok_p = dout("ok_p", [L, 128, 128]); ov_p = dout("ov_p", [L, 128, 128])
    oconv_p = dout("oconv_p", [L, 128, 2 * 30]); ossm_p = dout("ossm_p", [L, 128, 16])
    ok_s = dout("ok_s", [L, NS, 128, 128]); ov_s = dout("ov_s", [L, NS, 128, 128])
    oconv_s = dout("oconv_s", [L, 128, 2 * NS * 30]); ossm_s = dout("ossm_s", [L, 128, 2 * 8 * NS])
    wscr = nc.dram_tensor("w_scr", [L * 23, 128, 4096], BF16).ap()
    rot_d = nc.dram_tensor("rot_scr", [L, 128, 8 * 2 * J], F32).ap()
    wb_d = nc.dram_tensor("wb_scr", [L, 128, 16 * 128], BF16).ap()
    wc_d = nc.dram_tensor("wc_scr", [L, 128, 16 * 128], BF16).ap()

    P = Prog(nc)
    B = P.B
    with ExitStack() as st:
        def sb(name, shape, dt=F32):
            return st.enter_context(nc.sbuf_tensor("sb_" + name, list(shape), dt))

        def pst(name, shape, dt=F32):
            return st.enter_context(nc.psum_tensor(name, list(shape), dt))

        x = sb("x", [128, 8, NTM]); hb = sb("hb", [128, 8, NTM], BF16)
        x1 = sb("x1", [128, 8, T]); hb1 = sb("hb1", [128, 8, T], BF16)
        X = [x, x1]; HB = [hb, hb1]
        qT = sb("qT", [64, 8, NTM], BF16); kT = sb("kT", [64, 2, 128 + NTM], BF16)
        vp = sb("vp", [128, 5, 2, 192], BF16)
        ucv = sb("ucv", [128, 2, 30 + NTM]); acc = sb("acc", [128, 2, NTM]); us = sb("us", [128, 2, NTM], BF16)
        hid = sb("hid", [128, 4, NTM], BF16)
        ringM = sb("ringM", [128, 3, 4096], BF16); ringF = sb("ringF", [128, 3, 4096], BF16)
        pvt = sb("pvt", [128, NPV])
        ident = sb("ident", [128, 128]); identb = sb("identb", [128, 128], BF16)
        ones_f = sb("ones_f", [128, 128]); ones_b = sb("ones_b", [128, 128], BF16)
        jj = sb("jj", [128, J])
        abias = sb("abias", [128, 256]); sbias = sb("sbias", [4, 2, 128])
        WB = sb("WB", [128, 16, 128], BF16); WC = sb("WC", [128, 16, 128], BF16)
        Dd = sb("Dd", [128, 2, 128], BF16); gluw = sb("gluw", [128, 2, 256], BF16)
        lam = sb("lam", [128, L, 4, 8])
        rot2 = sb("rot2", [128, 2, 2, J])
        hcar = sb("hcar", [128, L, 8, 2])
        khalo = sb("khalo", [64, L, 2, 128], BF16); vhalo = sb("vhalo", [128, L, 2, 192], BF16)
        chalo = sb("chalo", [128, L, 2, 30])
        sq = sb("sq", [128, 512], BF16); rstd = sb("rstd", [128, 512])
        sq2 = sb("sq2", [128, 512], BF16); rstd2 = sb("rstd2", [128, 512])
        sgb = sb("sgb", [128, 2, 512], BF16)
        sc = sb("sc", [128, 2, 256]); pb = sb("pb", [128, 2, 256], BF16); ptb = sb("ptb", [128, 2, 256], BF16)
        sm = sb("sm", [128, 2, 8])
        t1 = sb("t1", [128, J]); t2 = sb("t2", [128, J]); bre = sb("bre", [128, J]); bim = sb("bim", [128, J])
        wre = sb("wre", [128, J]); wim = sb("wim", [128, J])
        ysm = sb("ysm", [128, 2, 512]); yb = sb("yb", [128, 2, 512], BF16); g1 = sb("g1", [128, 512]); g2 = sb("g2", [128, 512])
        ysq = ysm; cst = ysm[:].rearrange("p c n -> p (c n)")[:, 0:2 * NS * 31].rearrange("p (c b k) -> p c b k", c=2, b=NS); mean = sb("mean", [128, 512]); var = g2
        xflat = x[:].rearrange("p k n -> p (k n)")
        rotflat = x1[:].rearrange("p k n -> p (k n)")[:, 0:16 * J]
        big = xflat[:, 0:4096]; big2 = rotflat[:, 8 * J:16 * J]; bigi = sb("bigi", [128, 512], I32)[:]
        hbf = sb("hbf", [128, 16 * J], BF16)
        pl = sb("pl", [128, 16, 8])
        tk = sb("tk", [128, 128]); tkb = sb("tkb", [NS, 2, 128])
        Kb = sb("Kb", [128, 1, 128]); Vb = sb("Vb", [128, 1, 128]); KbT = sb("KbT", [64, 2, 128], BF16)
        Vbp = sb("Vbp", [128, 2, 192], BF16)
        ssc = sb("ssc", [4, 4, 128]); spb = sb("spb", [4, 4, 128], BF16); ssm_ = sb("ssm_", [4, 4, 4]); sinkt = sb("sinkt", [4, L * 2])
        sptb = sb("sptb", [128, 2, NS, 4], BF16)
        cs = sb("cs", [128, 2, NS, 31])
        h0 = sb("h0", [128, 2, 8, NS]); h1 = sb("h1", [128, 2, 8, NS]); h1b = sb("h1b", [128, 2, 8, NS], BF16)

        ps = [pst("ps%d" % i, [128, 512]) for i in range(7)]
        psb = pst("psb", [128, 1024], BF16)

        Q = lambda fn, r=(), w=(), ch="io": P.op("sync", fn, r, w, dma=ch)
        G = lambda fn, r=(), w=(), ch="w": P.op("gpsimd", fn, r, w, dma=ch)
        V = lambda fn, r=(), w=(): P.op("vector", fn, r, w)
        S = lambda fn, r=(), w=(): P.op("scalar", fn, r, w)
        PE = lambda fn, r=(), w=(): P.op("tensor", fn, r, w)
        GP = lambda fn, r=(), w=(): P.op("gpsimd", fn, r, w)

        def pvc(l, off, n=1):
            return pvt[:, l * PVL + off: l * PVL + off + n]

        OG1, OG2, OCW, OCB, OLG, OLB, OSD, OGB, OARE, OAIM, OLDT, OSINK = 0, 8, 16, 78, 80, 82, 84, 86, 88, 96, 104, 112
        rr = [0]

        def bank():
            rr[0] = (rr[0] + 1) % 3
            return 4 + rr[0]

        fr = [0]

        def fbank():
            fr[0] = (fr[0] + 1) % 4
            return fr[0]

        cB = B("const")
        Q(lambda e: e.dma_start(out=pvt[:], in_=pv_d[:, :]), w=[cB])
        Q(lambda e: e.dma_start(out=ident[:], in_=ident_d[:, :]), w=[cB])
        Q(lambda e: e.dma_start(out=jj[:], in_=jj_d[:, :]), w=[cB])
        Q(lambda e: e.dma_start(out=abias[:], in_=abias_d[:, :]), w=[cB])
        Q(lambda e: e.dma_start(out=sbias[:].rearrange("p h k -> p (h k)"), in_=sbias_d[:, :]), w=[cB])
        Q(lambda e: e.dma_start(out=sinkt[:], in_=sinkc_d[:, :]), w=[cB])
        V(lambda e: e.memset(ones_f[:], 1.0), w=[cB])
        V(lambda e: e.memset(ones_b[:], 1.0), w=[cB])
        V(lambda e: e.tensor_copy(out=identb[:], in_=ident[:]), r=[cB], w=[B("identb")])
        V(lambda e: e.memset(vp[:].rearrange("p a g c -> p (a g c)"), 0.0), w=[B("vp", i) for i in range(5)])
        V(lambda e: e.memset(vhalo[:].rearrange("p l g c -> p (l g c)"), 0.0), w=[B("vhalo", l) for l in range(L)])
        V(lambda e: e.memset(Vbp[:].rearrange("p g c -> p (g c)"), 0.0), w=[B("Vbp")])
        V(lambda e: e.memset(hcar[:].rearrange("p l s c -> p (l s c)"), 0.0), w=[B("hcar", l) for l in range(L)])
        V(lambda e: e.memset(chalo[:].rearrange("p l c k -> p (l c k)"), 0.0), w=[B("chalo", l) for l in range(L)])

        for l in range(L):
            pB = B("pl")
            are, aim, ldt = pvc(l, OARE, 8), pvc(l, OAIM, 8), pvc(l, OLDT, 8)
            c = lambda i: pl[:, i, :]
            S(lambda e: e.activation(out=c(0), in_=ldt, func=AF.Exp), r=[cB], w=[pB])
            V(lambda e: e.tensor_tensor(out=c(1), in0=are, in1=c(0), op=ALU.mult), r=[pB], w=[pB])
            V(lambda e: e.tensor_tensor(out=c(2), in0=aim, in1=c(0), op=ALU.mult), r=[pB], w=[pB])
            S(lambda e, l=l: e.activation(out=lam[:, l, 0, :], in_=c(1), func=AF.Exp), r=[pB], w=[B("lam", l)])
            V(lambda e: e.tensor_scalar(out=c(3), in0=c(2), scalar1=1.0 / TWO_PI, scalar2=None, op0=ALU.mult), r=[pB], w=[pB])
            V(lambda e: e.tensor_copy(out=bigi[:, 0:8], in_=c(3)), r=[pB], w=[pB])
            V(lambda e: e.tensor_copy(out=c(4), in_=bigi[:, 0:8]), r=[pB], w=[pB])
            V(lambda e: e.tensor_tensor(out=c(4), in0=c(3), in1=c(4), op=ALU.subtract), r=[pB], w=[pB])
            S(lambda e, l=l: e.activation(out=lam[:, l, 2, :], in_=c(4), func=AF.Sin, scale=6.283185), r=[pB], w=[B("lam", l)])
            V(lambda e: e.tensor_scalar(out=c(3), in0=c(3), scalar1=0.25, scalar2=None, op0=ALU.add), r=[pB], w=[pB])
            V(lambda e: e.tensor_copy(out=bigi[:, 0:8], in_=c(3)), r=[pB], w=[pB])
            V(lambda e: e.tensor_copy(out=c(4), in_=bigi[:, 0:8]), r=[pB], w=[pB])
            V(lambda e: e.tensor_tensor(out=c(4), in0=c(3), in1=c(4), op=ALU.subtract), r=[pB], w=[pB])
            S(lambda e, l=l: e.activation(out=lam[:, l, 1, :], in_=c(4), func=AF.Sin, scale=6.283185), r=[pB], w=[B("lam", l)])
            V(lambda e, l=l: e.tensor_tensor(out=c(5), in0=lam[:, l, 0, :], in1=lam[:, l, 1, :], op=ALU.mult), r=[pB, B("lam", l)], w=[pB])
            V(lambda e, l=l: e.tensor_tensor(out=c(6), in0=lam[:, l, 0, :], in1=lam[:, l, 2, :], op=ALU.mult), r=[pB, B("lam", l)], w=[pB])
            V(lambda e: e.tensor_scalar(out=c(7), in0=c(5), scalar1=-1.0, scalar2=None, op0=ALU.add), r=[pB], w=[pB])
            V(lambda e: e.tensor_tensor(out=c(8), in0=are, in1=are, op=ALU.mult), r=[pB], w=[pB])
            V(lambda e: e.tensor_tensor(out=c(9), in0=aim, in1=aim, op=ALU.mult), r=[pB], w=[pB])
            V(lambda e: e.tensor_tensor(out=c(8), in0=c(8), in1=c(9), op=ALU.add), r=[pB], w=[pB])
            V(lambda e: e.reciprocal(out=c(8), in_=c(8)), r=[pB], w=[pB])
            V(lambda e: e.tensor_tensor(out=c(9), in0=c(7), in1=are, op=ALU.mult), r=[pB], w=[pB])
            V(lambda e: e.tensor_tensor(out=c(10), in0=c(6), in1=aim, op=ALU.mult), r=[pB], w=[pB])
            V(lambda e: e.tensor_tensor(out=c(9), in0=c(9), in1=c(10), op=ALU.add), r=[pB], w=[pB])
            V(lambda e: e.tensor_tensor(out=c(11), in0=c(9), in1=c(8), op=ALU.mult), r=[pB], w=[pB])
            V(lambda e: e.tensor_tensor(out=c(9), in0=c(6), in1=are, op=ALU.mult), r=[pB], w=[pB])
            V(lambda e: e.tensor_tensor(out=c(10), in0=c(7), in1=aim, op=ALU.mult), r=[pB], w=[pB])
            V(lambda e: e.tensor_tensor(out=c(9), in0=c(9), in1=c(10), op=ALU.subtract), r=[pB], w=[pB])
            V(lambda e: e.tensor_tensor(out=c(12), in0=c(9), in1=c(8), op=ALU.mult), r=[pB], w=[pB])
            V(lambda e: e.tensor_scalar(out=c(13), in0=c(12), scalar1=-1.0, scalar2=None, op0=ALU.mult), r=[pB], w=[pB])
            bg = big[:, 0:2048].rearrange("p (s c k) -> p s c k", s=8, c=2)
            Q(lambda e, l=l: e.dma_start(out=big[:, 0:2048], in_=Bn_d[l, :, :]), r=[pB], w=[B("big")])
            for s8 in range(8):
                fre, fim, nfim = pl[:, 11, s8:s8 + 1], pl[:, 12, s8:s8 + 1], pl[:, 13, s8:s8 + 1]
                V(lambda e, s8=s8, fre=fre: e.tensor_scalar(out=t1[:, 0:128], in0=bg[:, s8, 0, :], scalar1=fre, scalar2=None, op0=ALU.mult), r=[pB, B("big")], w=[B("t1")])
                V(lambda e, s8=s8, nfim=nfim: e.scalar_tensor_tensor(out=t1[:, 0:128], in0=bg[:, s8, 1, :], scalar=nfim, in1=t1[:, 0:128], op0=ALU.mult, op1=ALU.add), r=[pB, B("big"), B("t1")], w=[B("t1")])
                V(lambda e, s8=s8, fre=fre: e.tensor_scalar(out=t2[:, 0:128], in0=bg[:, s8, 1, :], scalar1=fre, scalar2=None, op0=ALU.mult), r=[pB, B("big")], w=[B("t2")])
                V(lambda e, s8=s8, fim=fim: e.scalar_tensor_tensor(out=t2[:, 0:128], in0=bg[:, s8, 0, :], scalar=fim, in1=t2[:, 0:128], op0=ALU.mult, op1=ALU.add), r=[pB, B("big"), B("t2")], w=[B("t2")])
                PE(lambda e: e.transpose(out=ps[5][:, 0:128], in_=t1[:, 0:128], identity=ident[:]), r=[B("t1"), cB], w=[B("ps", 5)])
                PE(lambda e: e.transpose(out=ps[5][:, 128:256], in_=t2[:, 0:128], identity=ident[:]), r=[B("t2"), cB], w=[B("ps", 5)])
                S(lambda e, l=l, s8=s8: e.activation(out=WB[:, s8 * 2:s8 * 2 + 2, :], in_=ps[5][:, 0:256].rearrange("p (c k) -> p c k", c=2), func=AF.Copy), r=[B("ps", 5)], w=[B("WB")])
            Q(lambda e, l=l: e.dma_start(out=big[:, 0:2048], in_=Cn_d[l, :, :]), w=[B("big")])
            V(lambda e, l=l: e.tensor_copy(out=WC[:, 0:16, :].rearrange("p (s c) k -> p s c k", c=2)[:, :, 0, :], in_=bg[:, :, 0, :]), r=[B("big")], w=[B("WC")])
            V(lambda e, l=l: e.tensor_scalar(out=WC[:, 0:16, :].rearrange("p (s c) k -> p s c k", c=2)[:, :, 1, :], in0=bg[:, :, 1, :], scalar1=-1.0, scalar2=None, op0=ALU.mult), r=[B("big")], w=[B("WC")])
            a8 = big2.rearrange("p (s j) -> p s j", s=8)
            for s8 in range(8):
                V(lambda e, s8=s8: e.tensor_scalar(out=a8[:, s8, :], in0=jj[:], scalar1=pl[:, 2, s8:s8 + 1], scalar2=1.0 / TWO_PI, op0=ALU.mult, op1=ALU.mult), r=[pB, cB], w=[B("big2"), B("rot")])
            rt = big.rearrange("p (s c j) -> p s c j", s=8, c=2)
            for ci, sh in ((1, 0.0), (0, 0.25)):
                if sh:
                    V(lambda e, sh=sh: e.tensor_scalar(out=big2, in0=big2, scalar1=sh, scalar2=None, op0=ALU.add), r=[B("big2")], w=[B("big2"), B("rot")])
                for pc in range(8 * J // 512):
                    V(lambda e, pc=pc: e.tensor_copy(out=bigi, in_=big2[:, pc * 512:(pc + 1) * 512]), r=[B("big2"), B("rot")], w=[B("bigi")])
                    V(lambda e, pc=pc: e.tensor_copy(out=rotflat[:, pc * 512:(pc + 1) * 512], in_=bigi), r=[B("bigi")], w=[B("rot")])
                V(lambda e: e.tensor_tensor(out=rotflat[:, 0:8 * J], in0=big2, in1=rotflat[:, 0:8 * J], op=ALU.subtract), r=[B("big2"), B("rot")], w=[B("rot")])
                S(lambda e, ci=ci: e.activation(out=rt[:, :, ci, :], in_=rotflat[:, 0:8 * J].rearrange("p (s j) -> p s j", s=8), func=AF.Sin, scale=6.283185), r=[B("rot")], w=[B("big")])
            Q(lambda e, l=l: e.dma_start(out=rot_d[l, :, :], in_=big), r=[B("big")], w=[B("rot_d", l)])
            Q(lambda e, l=l: e.dma_start(out=wb_d[l, :, :], in_=WB[:].rearrange("p a k -> p (a k)")), r=[B("WB")], w=[B("wb_d", l)])
            Q(lambda e, l=l: e.dma_start(out=wc_d[l, :, :], in_=WC[:].rearrange("p a k -> p (a k)")), r=[B("WC")], w=[B("wc_d", l)])

        print('MARK prologue_end', P.nops, flush=True)
        slot_i = {"M": 0, "F": 0}

        def wload(which, src_ap, shape_str, wid=None, ch=0, **kw):
            ring = ringM if which == "M" else ringF
            s = slot_i[which] % 3
            if not P.dry:
                slot_i[which] += 1
            n = 1
            for d_ in src_ap.shape[1:]:
                n *= d_
            if ch == 0:
                dst = ring[:, s, 0:n]
                if shape_str:
                    dst = dst.rearrange(shape_str, **kw)
                G(lambda e: e.dma_start(out=dst, in_=src_ap), w=[B("slot" + which, s)])
                if nch > 1:
                    Q(lambda e: e.dma_start(out=wscr[wid, :, 0:n], in_=ring[:, s, 0:n]), r=[B("slot" + which, s)], w=[B("wscr", wid)])
            else:
                G(lambda e: e.dma_start(out=ring[:, s, 0:n], in_=wscr[wid, :, 0:n]), r=[B("wscr", wid)], w=[B("slot" + which, s)])
            return s

        def rmsnorm(x, hb, par, goff, ccs, sq, rstd, pbk, tag):
            for (c0, cn) in ccs:
                pb_ = ps[pbk]
                for k in range(8):
                    S(lambda e, k=k: e.activation(out=sq[:, 0:cn], in_=x[:, k, c0:c0 + cn], func=AF.Square), r=[B("x", par, c0)], w=[B("sq", tag)])
                    PE(lambda e, k=k: e.matmul(pb_[:, 0:cn], lhsT=ones_b[:], rhs=sq[:, 0:cn], start=(k == 0), stop=(k == 7)), r=[B("sq", tag), cB], w=[B("ps", pbk)])
                S(lambda e: e.activation(out=rstd[:, 0:cn], in_=pb_[:, 0:cn], func=AF.Sqrt, scale=1.0 / D, bias=EPS), r=[B("ps", pbk)], w=[B("rstd", tag)])
                V(lambda e: e.reciprocal(out=rstd[:, 0:cn], in_=rstd[:, 0:cn]), r=[B("rstd", tag)], w=[B("rstd", tag)])
                for k in range(8):
                    gk = pvt[:, goff + k: goff + k + 1]
                    V(lambda e, k=k, gk=gk: e.scalar_tensor_tensor(out=hb[:, k, c0:c0 + cn], in0=x[:, k, c0:c0 + cn], scalar=gk, in1=rstd[:, 0:cn], op0=ALU.mult, op1=ALU.mult), r=[B("x", par, c0), B("rstd", tag), cB], w=[B("hb", par, c0)])

        def gen_mix(ch, l):
            par = ch % 2; x = X[par]; hb = HB[par]
            ring, SL = ringM, "slotM"
            first, last = (ch == 0), (ch == nch - 1)
            ccs = [(0, 512)] + ([(T, NS)] if first else [])
            if l == 0:
                for k in range(8):
                    Q(lambda e, k=k: e.dma_start(out=x[:, k, 0:T], in_=xT[k * 128:(k + 1) * 128, ch * T:(ch + 1) * T]), w=[B("x", par, 0), B("big"), B("rot"), B("big2")])
                if first:
                    Q(lambda e: e.dma_start(out=x[:, :, T:NTM], in_=xsT.rearrange("(k p) n -> p k n", p=128)), w=[B("x", par, T), B("big")])
                yield
            if True:
                Q(lambda e, l=l: e.dma_start(out=WB[:].rearrange("p a k -> p (a k)"), in_=wb_d[l, :, :]), r=[B("wb_d", l)], w=[B("WB")])
                Q(lambda e, l=l: e.dma_start(out=WC[:].rearrange("p a k -> p (a k)"), in_=wc_d[l, :, :]), r=[B("wc_d", l)], w=[B("WC")])
                for ct in range(2):
                    V(lambda e, ct=ct: e.tensor_scalar(out=Dd[:, ct, :], in0=ident[:], scalar1=pvc(l, OSD + ct), scalar2=None, op0=ALU.mult), r=[cB], w=[B("Dd")])
                G(lambda e: e.dma_start(out=gluw[:], in_=glu_w[l].rearrange("(c p) n -> p c n", p=128)), w=[B("gluw")])
                s_in = [wload("M", w_in[l, :, i * 512:(i + 1) * 512].rearrange("(k p) n -> p k n", p=128), "p (k n) -> p k n", wid=l * 23 + i, ch=ch, k=8) for i in range(3)]

                def win(o_lo, o_n):
                    s = s_in[o_lo // 512]
                    off = o_lo % 512
                    return lambda k: ring[:, s, k * 512 + off: k * 512 + off + o_n], B(SL, s)

                V(lambda e, l=l: e.tensor_copy(out=kT[:, :, 0:128], in_=khalo[:, l, :, :]), r=[B("khalo", l)], w=[B("kT", "h")])
                V(lambda e, l=l: e.tensor_copy(out=vp[:, 0, :, :], in_=vhalo[:, l, :, :]), r=[B("vhalo", l)], w=[B("vp", 0)])
                V(lambda e, l=l: e.tensor_copy(out=ucv[:, :, 0:30], in_=chalo[:, l, :, :]), r=[B("chalo", l)], w=[B("ucv", "h")])

                rmsnorm(x, hb, par, l * PVL + OG1, ccs, sq, rstd, 6, "m")
                for (c0, cn) in ccs:
                    def mm8(lw, n_out, pbk):
                        f, sb_ = lw
                        for k in range(8):
                            PE(lambda e, k=k: e.matmul(ps[pbk][0:n_out, 0:cn], lhsT=f(k), rhs=hb[:, k, c0:c0 + cn], start=(k == 0), stop=(k == 7)), r=[sb_, B("hb", par, c0)], w=[B("ps", pbk)])
                    for h in range(8):
                        pbk = bank(); mm8(win(64 * h, 64), 64, pbk)
                        S(lambda e, h=h, pbk=pbk: e.activation(out=qT[:, h, c0:c0 + cn], in_=ps[pbk][0:64, 0:cn], func=AF.Copy, scale=0.125), r=[B("ps", pbk)], w=[B("qT", c0)])
                        yield
                    for g in range(2):
                        pbk = bank(); mm8(win(512 + 64 * g, 64), 64, pbk)
                        S(lambda e, g=g, pbk=pbk: e.activation(out=kT[:, g, 128 + c0:128 + c0 + cn], in_=ps[pbk][0:64, 0:cn], func=AF.Copy), r=[B("ps", pbk)], w=[B("kT", c0)])
                        yield
                    for ct in range(2):
                        pa = bank(); mm8(win(768 + 128 * ct, 128), 128, pa)
                        pg = bank(); mm8(win(1024 + 128 * ct, 128), 128, pg)
                        S(lambda e, pg=pg: e.activation(out=g2[:, 0:cn], in_=ps[pg][:, 0:cn], func=AF.Sigmoid), r=[B("ps", pg)], w=[B("g2")])
                        V(lambda e, ct=ct, pa=pa: e.tensor_tensor(out=ucv[:, ct, 30 + c0:30 + c0 + cn], in0=ps[pa][:, 0:cn], in1=g2[:, 0:cn], op=ALU.mult), r=[B("ps", pa), B("g2")], w=[B("ucv", c0)])
                        yield
                    for ct in range(2):
                        pbk = bank(); mm8(win(1280 + 128 * ct, 128), 128, pbk)
                        S(lambda e, ct=ct, pbk=pbk: e.activation(out=us[:, ct, c0:c0 + cn], in_=ps[pbk][:, 0:cn], func=AF.Copy), r=[B("ps", pbk)], w=[B("us", c0)])
                        yield
                fv, sv = win(640, 128)
                fk, sk = win(512, 128)
                for bi in range(4):
                    c0 = bi * 128
                    cc0 = (c0 // 512) * 512
                    pbk = bank()
                    for k in range(8):
                        PE(lambda e, k=k: e.matmul(ps[pbk][:, 0:128], lhsT=hb[:, k, c0:c0 + 128], rhs=fv(k), start=(k == 0), stop=(k == 7)), r=[sv, B("hb", par, cc0)], w=[B("ps", pbk)])
                    S(lambda e, bi=bi, pbk=pbk: e.activation(out=vp[:, bi + 1, :, 64:128], in_=ps[pbk][:, 0:128].rearrange("p (g d) -> p g d", g=2), func=AF.Copy), r=[B("ps", pbk)], w=[B("vp", bi + 1)])
                    yield
                    if last and bi == 3:
                        V(lambda e, pbk=pbk: e.tensor_copy(out=tk[:], in_=ps[pbk][:, 0:128]), r=[B("ps", pbk), B("vp", bi + 1)], w=[B("tk")])
                        Q(lambda e, l=l: e.dma_start(out=ov_p[l, :, :], in_=tk[:]), r=[B("tk")], w=[B("ov_p", l)])
                        pbk2 = bank()
                        for k in range(8):
                            PE(lambda e, k=k: e.matmul(ps[pbk2][:, 0:128], lhsT=hb[:, k, c0:c0 + 128], rhs=fk(k), start=(k == 0), stop=(k == 7)), r=[sk, B("hb", par, cc0)], w=[B("ps", pbk2)])
                        V(lambda e, pbk2=pbk2: e.tensor_copy(out=tk[:], in_=ps[pbk2][:, 0:128]), r=[B("ps", pbk2)], w=[B("tk")])
                        Q(lambda e, l=l: e.dma_start(out=ok_p[l, :, :], in_=tk[:]), r=[B("tk")], w=[B("ok_p", l)])
                if first:
                    pbk = bank()
                    for (f_, s_, off) in ((fk, sk, 0), (fv, sv, 128)):
                        for k in range(8):
                            PE(lambda e, k=k, f_=f_, off=off: e.matmul(ps[pbk][0:NS, off:off + 128], lhsT=hb[:, k, T:NTM], rhs=f_(k), start=(k == 0), stop=(k == 7)), r=[s_, B("hb", par, T)], w=[B("ps", pbk)])
                    V(lambda e, pbk=pbk: e.tensor_copy(out=tkb[:].rearrange("p a k -> p (a k)"), in_=ps[pbk][0:NS, 0:256]), r=[B("ps", pbk)], w=[B("tkb")])
                    for b in range(NS):
                        Q(lambda e, l=l, b=b: e.dma_start(out=ok_s[l, b:b + 1, 0:127, :].rearrange("b r c -> b (r c)"), in_=ck_d[l, b:b + 1, 1:128, :].rearrange("b r c -> b (r c)")), w=[B("ok_s", l)])
                        Q(lambda e, l=l, b=b: e.dma_start(out=ov_s[l, b:b + 1, 0:127, :].rearrange("b r c -> b (r c)"), in_=cv_d[l, b:b + 1, 1:128, :].rearrange("b r c -> b (r c)")), w=[B("ov_s", l)])
                    Q(lambda e, l=l: e.dma_start(out=ok_s[l, :, 127, :], in_=tkb[:, 0, :]), r=[B("tkb")], w=[B("ok_s", l)])
                    Q(lambda e, l=l: e.dma_start(out=ov_s[l, :, 127, :], in_=tkb[:, 1, :]), r=[B("tkb")], w=[B("ov_s", l)])
                if not last:
                    V(lambda e, l=l: e.tensor_copy(out=khalo[:, l, :, :], in_=kT[:, :, T:T + 128]), r=[B("kT", 0)], w=[B("khalo", l)])
                    V(lambda e, l=l: e.tensor_copy(out=vhalo[:, l, :, :], in_=vp[:, 4, :, :]), r=[B("vp", 4)], w=[B("vhalo", l)])
                    V(lambda e, l=l: e.tensor_copy(out=chalo[:, l, :, :], in_=ucv[:, :, T:T + 30]), r=[B("ucv", 0)], w=[B("chalo", l)])
                else:
                    Q(lambda e, l=l: e.dma_start(out=oconv_p[l, :, :].rearrange("p (c k) -> p c k", c=2), in_=ucv[:, :, T:T + 30]), r=[B("ucv", 0)], w=[B("oconv_p", l)])

                for bi in range(4):
                    q0 = bi * 128
                    qcc = (q0 // 512) * 512
                    nokprev = first and bi == 0
                    kread = [B("kT", qcc)] + ([B("kT", "h")] if bi == 0 else [B("kT", ((q0 - 128) // 512) * 512)])
                    for tile in range(4):
                        for r2 in range(2):
                            h = tile * 2 + r2
                            g = h // 4
                            a = h % 2
                            k_lo, k_n = (128, 128) if nokprev else (0, 256)
                            PE(lambda e, h=h, g=g, k_lo=k_lo, k_n=k_n: e.matmul(ps[4][:, k_lo:k_lo + k_n], lhsT=qT[:, h, q0:q0 + 128], rhs=kT[:, g, q0 + k_lo:q0 + k_lo + k_n], start=True, stop=True), r=[B("qT", qcc)] + kread, w=[B("ps", 4)])
                            V(lambda e, h=h, a=a, k_lo=k_lo, k_n=k_n: e.scalar_tensor_tensor(out=sc[:, a, k_lo:k_lo + k_n], in0=abias[:, k_lo:k_lo + k_n], scalar=float(2.0 ** (-(h + 1))), in1=ps[4][:, k_lo:k_lo + k_n], op0=ALU.mult, op1=ALU.add), r=[B("ps", 4), cB], w=[B("sc", a)])
                            V(lambda e, a=a, k_lo=k_lo, k_n=k_n: e.reduce_max(out=sm[:, a, 0:1], in_=sc[:, a, k_lo:k_lo + k_n], axis=AX.X), r=[B("sc", a)], w=[B("sm", a)])
                            sinkc = pvc(l, OSINK + h)
                            V(lambda e, a=a, sinkc=sinkc: e.tensor_scalar(out=sm[:, a, 1:2], in0=sm[:, a, 0:1], scalar1=sinkc, scalar2=-1.0, op0=ALU.max, op1=ALU.mult), r=[B("sm", a), cB], w=[B("sm", a)])
                            S(lambda e, a=a, k_lo=k_lo, k_n=k_n: e.activation(out=pb[:, a, k_lo:k_lo + k_n], in_=sc[:, a, k_lo:k_lo + k_n], func=AF.Exp, bias=sm[:, a, 1:2], accum_out=sm[:, a, 2:3]), r=[B("sc", a), B("sm", a)], w=[B("pb", a), B("sm", a)])
                            S(lambda e, a=a, sinkc=sinkc: e.activation(out=sm[:, a, 3:4], in_=sinkc, func=AF.Exp, bias=sm[:, a, 1:2]), r=[B("sm", a), cB], w=[B("sm", a)])
                            V(lambda e, a=a: e.tensor_tensor(out=sm[:, a, 4:5], in0=sm[:, a, 2:3], in1=sm[:, a, 3:4], op=ALU.add), r=[B("sm", a)], w=[B("sm", a)])
                            V(lambda e, a=a: e.reciprocal(out=sm[:, a, 5:6], in_=sm[:, a, 4:5]), r=[B("sm", a)], w=[B("sm", a)])
                            V(lambda e, a=a, k_lo=k_lo, k_n=k_n: e.tensor_scalar(out=pb[:, a, k_lo:k_lo + k_n], in0=pb[:, a, k_lo:k_lo + k_n], scalar1=sm[:, a, 5:6], scalar2=None, op0=ALU.mult), r=[B("sm", a), B("pb", a)], w=[B("pb", a)])
                            for kb in range(2):
                                if nokprev and kb == 0:
                                    continue
                                PE(lambda e, a=a, kb=kb: e.transpose(out=psb[:, a * 256 + kb * 128:a * 256 + kb * 128 + 128], in_=pb[:, a, kb * 128:kb * 128 + 128], identity=identb[:]), r=[B("pb", a), B("identb")], w=[B("psb", 0)])
                            S(lambda e, a=a, k_lo=k_lo, k_n=k_n: e.activation(out=ptb[:, a, k_lo:k_lo + k_n], in_=psb[:, a * 256 + k_lo:a * 256 + k_lo + k_n], func=AF.Copy), r=[B("psb", 0)], w=[B("ptb", a)])
                            kbs = [1] if nokprev else [0, 1]
                            for kb in kbs:
                                lo = 64 if r2 == 0 else 0
                                PE(lambda e, a=a, kb=kb, g=g, lo=lo, r2=r2, kbs=kbs: e.matmul(ps[5][:, 0:128], lhsT=vp[:, bi + kb, g, lo:lo + 128], rhs=ptb[:, a, kb * 128:kb * 128 + 128], start=(r2 == 0 and kb == kbs[0]), stop=(r2 == 1 and kb == 1)), r=[B("vp", bi + kb), B("ptb", a)], w=[B("ps", 5)])
                        S(lambda e, tile=tile: e.activation(out=hb[:, tile, q0:q0 + 128], in_=ps[5][:, 0:128], func=AF.Copy), r=[B("ps", 5)], w=[B("hb", par, qcc)])
                        yield

                if first:
                    for g in range(2):
                        sk4 = sinkt[:, l * 2 + g:l * 2 + g + 1]
                        for b4 in range(NS // 4):
                            for bb in range(4):
                                b = b4 * 4 + bb
                                Q(lambda e, b=b, l=l: e.dma_start(out=Kb[:, 0, :], in_=ok_s[l, b, :, :]), r=[B("ok_s", l)], w=[B("Kb", 0)])
                                PE(lambda e, b=b, g=g: e.transpose(out=ps[4][0:64, 0:128], in_=Kb[:, 0, g * 64:(g + 1) * 64], identity=ident[:]), r=[B("Kb", 0), cB], w=[B("ps", 4)])
                                S(lambda e, b=b: e.activation(out=KbT[:, b % 2, :], in_=ps[4][0:64, 0:128], func=AF.Copy), r=[B("ps", 4)], w=[B("KbT", b % 2)])
                                PE(lambda e, b=b, g=g, bb=bb: e.matmul(ps[6][0:4, bb * 128:bb * 128 + 128], lhsT=qT[:, 4 * g:4 * g + 4, T + b], rhs=KbT[:, b % 2, :], start=True, stop=True), r=[B("qT", T), B("KbT", b % 2)], w=[B("ps", 6)])
                            V(lambda e, g=g: e.tensor_tensor(out=ssc[:], in0=ps[6][0:4, :].rearrange("p (b k) -> p b k", b=4), in1=sbias[:, g:g + 1, :].to_broadcast([4, 4, 128]), op=ALU.add), r=[B("ps", 6), cB], w=[B("ssc")])
                            V(lambda e: e.reduce_max(out=ssm_[:, 0, :], in_=ssc[:], axis=AX.X), r=[B("ssc")], w=[B("ssm_")])
                            V(lambda e, sk4=sk4: e.tensor_scalar(out=ssm_[:, 0, :], in0=ssm_[:, 0, :], scalar1=sk4, scalar2=None, op0=ALU.max), r=[B("ssm_"), cB], w=[B("ssm_")])
                            V(lambda e: e.tensor_tensor(out=ssc[:], in0=ssc[:], in1=ssm_[:, 0, :].unsqueeze(2).to_broadcast([4, 4, 128]), op=ALU.subtract), r=[B("ssc"), B("ssm_")], w=[B("ssc")])
                            S(lambda e: e.activation(out=ssc[:], in_=ssc[:], func=AF.Exp), r=[B("ssc")], w=[B("ssc")])
                            V(lambda e: e.reduce_sum(out=ssm_[:, 1, :], in_=ssc[:], axis=AX.X), r=[B("ssc")], w=[B("ssm_")])
                            S(lambda e, sk4=sk4: e.activation(out=ssm_[:, 2, :], in_=ssm_[:, 0, :], func=AF.Exp, scale=-1.0, bias=sk4), r=[B("ssm_"), cB], w=[B("ssm_")])
                            V(lambda e: e.tensor_tensor(out=ssm_[:, 1, :], in0=ssm_[:, 1, :], in1=ssm_[:, 2, :], op=ALU.add), r=[B("ssm_")], w=[B("ssm_")])
                            V(lambda e: e.reciprocal(out=ssm_[:, 1, :], in_=ssm_[:, 1, :]), r=[B("ssm_")], w=[B("ssm_")])
                            V(lambda e: e.tensor_tensor(out=spb[:], in0=ssc[:], in1=ssm_[:, 1, :].unsqueeze(2).to_broadcast([4, 4, 128]), op=ALU.mult), r=[B("ssc"), B("ssm_")], w=[B("spb")])
                            for bb in range(4):
                                PE(lambda e, bb=bb: e.transpose(out=psb[:, 512 + bb * 4:512 + bb * 4 + 4], in_=spb[:, bb, :], identity=identb[0:4, 0:4]), r=[B("spb"), B("identb")], w=[B("psb", 0)])
                            S(lambda e, g=g, b4=b4: e.activation(out=sptb[:, g, b4 * 4:b4 * 4 + 4, :], in_=psb[:, 512:512 + 16].rearrange("p (b r) -> p b r", r=4), func=AF.Copy), r=[B("psb", 0)], w=[B("sptb", g)])
                            yield
                    for b in range(NS):
                        Q(lambda e, b=b, l=l: e.dma_start(out=Vb[:, 0, :], in_=ov_s[l, b, :, :]), r=[B("ov_s", l)], w=[B("Vb", 0)])
                        V(lambda e, b=b: e.tensor_copy(out=Vbp[:, :, 64:128], in_=Vb[:, 0, :].rearrange("p (g d) -> p g d", g=2)), r=[B("Vb", 0)], w=[B("Vbp")])
                        for tile in range(4):
                            g = tile // 2
                            for r2 in range(2):
                                rr4 = (tile % 2) * 2 + r2
                                lo = 64 if r2 == 0 else 0
                                PE(lambda e, b=b, g=g, tile=tile, r2=r2, rr4=rr4, lo=lo: e.matmul(ps[5][:, 256 + b * 4 + tile:256 + b * 4 + tile + 1], lhsT=Vbp[:, g, lo:lo + 128], rhs=sptb[:, g, b, rr4:rr4 + 1], start=(r2 == 0), stop=(r2 == 1)), r=[B("Vbp"), B("sptb", g)], w=[B("ps", 5)])
                    S(lambda e: e.activation(out=hb[:, 0:4, T:NTM].rearrange("p t b -> p b t"), in_=ps[5][:, 256:256 + NS * 4].rearrange("p (b t) -> p b t", t=4), func=AF.Copy), r=[B("ps", 5)], w=[B("hb", par, T)])

                if first:
                    Q(lambda e, l=l: e.dma_start(out=cs[:, :, :, 0:30], in_=cconv_d[l, :, :].rearrange("p (c b k) -> p c b k", c=2, b=NS)), w=[B("cs")])
                    V(lambda e: e.tensor_copy(out=cs[:, :, :, 30], in_=ucv[:, :, 30 + T:30 + NTM]), r=[B("ucv", T)], w=[B("cs")])
                    Q(lambda e, l=l: e.dma_start(out=oconv_s[l, :, :].rearrange("p (c b k) -> p c b k", c=2, b=NS), in_=cs[:, :, :, 1:31]), r=[B("cs")], w=[B("oconv_s", l)])
                    for ct in range(2):
                        cw = pvt[:, l * PVL + OCW + ct * 31: l * PVL + OCW + ct * 31 + 31]
                        V(lambda e, ct=ct, cw=cw: e.tensor_tensor(out=cst[:, ct, :, :], in0=cs[:, ct, :, :], in1=cw.unsqueeze(1).to_broadcast([128, NS, 31]), op=ALU.mult), r=[B("cs"), cB], w=[B("ysm", 0), B("ysm", 1)])
                        V(lambda e, ct=ct: e.reduce_sum(out=acc[:, ct, T:NTM], in_=cst[:, ct, :, :], axis=AX.X), r=[B("ysm", 0), B("ysm", 1)], w=[B("acc", T)])
                        V(lambda e, ct=ct: e.tensor_scalar(out=acc[:, ct, T:NTM], in0=acc[:, ct, T:NTM], scalar1=pvc(l, OCB + ct), scalar2=None, op0=ALU.add), r=[B("acc", T), cB], w=[B("acc", T)])
                for (c0, cn) in ccs:
                    if c0 < T:
                        hr = [B("ucv", c0)] + ([B("ucv", "h")] if c0 == 0 else [B("ucv", c0 - 512)])
                        for ct in range(2):
                            for kk in range(31):
                                wk = pvt[:, l * PVL + OCW + ct * 31 + kk: l * PVL + OCW + ct * 31 + kk + 1]
                                if kk == 0:
                                    V(lambda e, ct=ct, wk=wk: e.tensor_scalar(out=acc[:, ct, c0:c0 + cn], in0=ucv[:, ct, c0:c0 + cn], scalar1=wk, scalar2=pvc(l, OCB + ct), op0=ALU.mult, op1=ALU.add), r=hr + [cB], w=[B("acc", c0)])
                                else:
                                    V(lambda e, ct=ct, wk=wk, kk=kk: e.scalar_tensor_tensor(out=acc[:, ct, c0:c0 + cn], in0=ucv[:, ct, c0 + kk:c0 + kk + cn], scalar=wk, in1=acc[:, ct, c0:c0 + cn], op0=ALU.mult, op1=ALU.add), r=hr + [cB, B("acc", c0)], w=[B("acc", c0)])
                    for ct in range(2):
                        S(lambda e, ct=ct: e.activation(out=ysq[:, ct, 0:cn], in_=acc[:, ct, c0:c0 + cn], func=AF.Square), r=[B("acc", c0)], w=[B("ysm", 0), B("ysm", 1)])
                    for ct in range(2):
                        PE(lambda e, ct=ct: e.matmul(ps[6][:, 0:cn], lhsT=ones_f[:], rhs=acc[:, ct, c0:c0 + cn], start=(ct == 0), stop=(ct == 1)), r=[B("acc", c0), cB], w=[B("ps", 6)])
                    V(lambda e: e.tensor_scalar(out=mean[:, 0:cn], in0=ps[6][:, 0:cn], scalar1=1.0 / 256, scalar2=None, op0=ALU.mult), r=[B("ps", 6)], w=[B("mean")])
                    for ct in range(2):
                        PE(lambda e, ct=ct: e.matmul(ps[6][:, 0:cn], lhsT=ones_f[:], rhs=ysq[:, ct, 0:cn], start=(ct == 0), stop=(ct == 1)), r=[B("ysm", 0), B("ysm", 1), cB], w=[B("ps", 6)])
                    V(lambda e: e.tensor_tensor(out=var[:, 0:cn], in0=mean[:, 0:cn], in1=mean[:, 0:cn], op=ALU.mult), r=[B("mean")], w=[B("g2")])
                    V(lambda e: e.scalar_tensor_tensor(out=var[:, 0:cn], in0=ps[6][:, 0:cn], scalar=1.0 / 256, in1=var[:, 0:cn], op0=ALU.mult, op1=ALU.subtract), r=[B("ps", 6), B("g2")], w=[B("g2")])
                    S(lambda e: e.activation(out=var[:, 0:cn], in_=var[:, 0:cn], func=AF.Sqrt, bias=EPS), r=[B("g2")], w=[B("g2")])
                    V(lambda e: e.reciprocal(out=var[:, 0:cn], in_=var[:, 0:cn]), r=[B("g2")], w=[B("g2")])
                    for ct in range(2):
                        V(lambda e, ct=ct: e.tensor_tensor(out=g1[:, 0:cn], in0=acc[:, ct, c0:c0 + cn], in1=mean[:, 0:cn], op=ALU.subtract), r=[B("acc", c0), B("mean")], w=[B("g1")])
                        V(lambda e, ct=ct: e.tensor_tensor(out=g1[:, 0:cn], in0=g1[:, 0:cn], in1=var[:, 0:cn], op=ALU.mult), r=[B("g1"), B("g2")], w=[B("g1")])
                        V(lambda e, ct=ct: e.tensor_scalar(out=g1[:, 0:cn], in0=g1[:, 0:cn], scalar1=pvc(l, OLG + ct), scalar2=pvc(l, OLB + ct), op0=ALU.mult, op1=ALU.add), r=[B("g1"), cB], w=[B("g1")])
                        S(lambda e, ct=ct: e.activation(out=hb[:, 4 + ct, c0:c0 + cn], in_=g1[:, 0:cn], func=AF.Silu), r=[B("g1")], w=[B("hb", par, c0)])
                        yield

                def ssm_out(c0, cn, hsrc_re, hsrc_im, hoff, hbufs):
                    for ct in range(2):
                        for j4 in range(4):
                            s8 = ct * 4 + j4
                            PE(lambda e, ct=ct, s8=s8, j4=j4: e.matmul(ps[6][:, 0:cn], lhsT=WC[:, s8 * 2, :], rhs=hsrc_re(s8), start=(j4 == 0), stop=False), r=hbufs + [B("WC")], w=[B("ps", 6)])
                            PE(lambda e, ct=ct, s8=s8: e.matmul(ps[6][:, 0:cn], lhsT=WC[:, s8 * 2 + 1, :], rhs=hsrc_im(s8), start=False, stop=False), r=hbufs + [B("WC")], w=[B("ps", 6)])
                        PE(lambda e, ct=ct: e.matmul(ps[6][:, 0:cn], lhsT=Dd[:, ct, :], rhs=us[:, ct, c0:c0 + cn], start=False, stop=True), r=[B("us", (c0 // 512) * 512 if c0 < T else T), B("Dd")], w=[B("ps", 6)])
                        V(lambda e, ct=ct: e.tensor_copy(out=ysm[:, ct, 0:cn], in_=ps[6][:, 0:cn]), r=[B("ps", 6)], w=[B("ysm", ct)])
                        V(lambda e, ct=ct: e.tensor_tensor(out=g1[:, 0:cn], in0=ysm[:, ct, 0:cn], in1=ysm[:, ct, 0:cn], op=ALU.mult), r=[B("ysm", ct)], w=[B("g1")])
                        V(lambda e, ct=ct: e.tensor_scalar(out=g1[:, 0:cn], in0=g1[:, 0:cn], scalar1=0.044715, scalar2=1.0, op0=ALU.mult, op1=ALU.add), r=[B("g1")], w=[B("g1")])
                        V(lambda e, ct=ct: e.tensor_tensor(out=g1[:, 0:cn], in0=g1[:, 0:cn], in1=ysm[:, ct, 0:cn], op=ALU.mult), r=[B("g1"), B("ysm", ct)], w=[B("g1")])
                        S(lambda e, ct=ct: e.activation(out=g2[:, 0:cn], in_=g1[:, 0:cn], func=AF.Sigmoid, scale=2.0 * math.sqrt(2.0 / math.pi)), r=[B("g1")], w=[B("g2")])
                        V(lambda e, ct=ct: e.tensor_tensor(out=ysm[:, ct, 0:cn], in0=ysm[:, ct, 0:cn], in1=g2[:, 0:cn], op=ALU.mult), r=[B("g2"), B("ysm", ct)], w=[B("ysm", ct)])
                        V(lambda e, ct=ct: e.tensor_copy(out=yb[:, ct, 0:cn], in_=ysm[:, ct, 0:cn]), r=[B("ysm", ct)], w=[B("yb", ct)])
                    for co in range(2):
                        for ct in range(2):
                            PE(lambda e, ct=ct, co=co: e.matmul(ps[6][:, 0:cn], lhsT=gluw[:, ct, co * 128:(co + 1) * 128], rhs=yb[:, ct, 0:cn], start=(ct == 0), stop=(ct == 1)), r=[B("yb", 0), B("yb", 1), B("gluw")], w=[B("ps", 6)])
                        S(lambda e, co=co: e.activation(out=g2[:, 0:cn], in_=ps[6][:, 0:cn], func=AF.Sigmoid, bias=pvc(l, OGB + co)), r=[B("ps", 6), cB], w=[B("g2")])
                        V(lambda e, co=co: e.tensor_tensor(out=hb[:, 6 + co, c0:c0 + cn], in0=ysm[:, co, 0:cn], in1=g2[:, 0:cn], op=ALU.mult), r=[B("g2"), B("ysm", co)], w=[B("hb", par, (c0 // 512) * 512 if c0 < T else T)])

                for sc_i in range(T // J):
                    c0 = sc_i * J
                    ucc = (c0 // 512) * 512
                    for s8 in range(8):
                        ct, a = s8 // 4, s8 % 2
                        for ci, dst in ((0, 0), (1, J)):
                            PE(lambda e, ci=ci, dst=dst, s8=s8, ct=ct: e.matmul(ps[4][:, dst:dst + J], lhsT=WB[:, s8 * 2 + ci, :], rhs=us[:, ct, c0:c0 + J], start=True, stop=True), r=[B("us", ucc), B("WB")], w=[B("ps", 4)])
                        ra = s8 % 2
                        Q(lambda e, s8=s8, ra=ra: e.dma_start(out=rot2[:, ra, :, :].rearrange("p c j -> p (c j)"), in_=rot_d[l, :, s8 * 2 * J:(s8 + 1) * 2 * J]), r=[B("rot_d", l)], w=[B("rot2", ra)])
                        cosT, sinT = rot2[:, ra, 0, :], rot2[:, ra, 1, :]
                        pre, pim = ps[4][:, 0:J], ps[4][:, J:2 * J]
                        V(lambda e, cosT=cosT, pre=pre: e.tensor_tensor(out=bre[:], in0=pre, in1=cosT, op=ALU.mult), r=[B("ps", 4), B("rot2", ra)], w=[B("bre")])
                        V(lambda e, sinT=sinT, pim=pim: e.tensor_tensor(out=t1[:], in0=pim, in1=sinT, op=ALU.mult), r=[B("ps", 4), B("rot2", ra)], w=[B("t1")])
                        V(lambda e: e.tensor_tensor(out=bre[:], in0=bre[:], in1=t1[:], op=ALU.add), r=[B("t1"), B("bre")], w=[B("bre")])
                        V(lambda e, cosT=cosT, pim=pim: e.tensor_tensor(out=bim[:], in0=pim, in1=cosT, op=ALU.mult), r=[B("ps", 4), B("rot2", ra)], w=[B("bim")])
                        V(lambda e, sinT=sinT, pre=pre: e.tensor_tensor(out=t2[:], in0=pre, in1=sinT, op=ALU.mult), r=[B("ps", 4), B("rot2", ra)], w=[B("t2")])
                        V(lambda e: e.tensor_tensor(out=bim[:], in0=bim[:], in1=t2[:], op=ALU.subtract), r=[B("t2"), B("bim")], w=[B("bim")])
                        rho_b = lam[:, l, 0, s8:s8 + 1].to_broadcast([128, J])
                        V(lambda e, rho_b=rho_b, s8=s8: e.tensor_tensor_scan(out=wre[:], data0=rho_b, data1=bre[:], initial=hcar[:, l, s8, 0:1], op0=ALU.mult, op1=ALU.add), r=[B("bre"), B("lam", l), B("hcar", l)], w=[B("wre")])
                        V(lambda e, rho_b=rho_b, s8=s8: e.tensor_tensor_scan(out=wim[:], data0=rho_b, data1=bim[:], initial=hcar[:, l, s8, 1:2], op0=ALU.mult, op1=ALU.add), r=[B("bim"), B("lam", l), B("hcar", l)], w=[B("wim")])
                        V(lambda e, cosT=cosT: e.tensor_tensor(out=t1[:], in0=wre[:], in1=cosT, op=ALU.mult), r=[B("wre"), B("rot2", ra)], w=[B("t1")])
                        V(lambda e, sinT=sinT: e.tensor_tensor(out=t2[:], in0=wim[:], in1=sinT, op=ALU.mult), r=[B("wim"), B("rot2", ra)], w=[B("t2")])
                        V(lambda e, s8=s8: e.tensor_tensor(out=hbf[:, s8 * J:(s8 + 1) * J], in0=t1[:], in1=t2[:], op=ALU.subtract), r=[B("t1"), B("t2")], w=[B("hbf")])
                        V(lambda e, sinT=sinT: e.tensor_tensor(out=t1[:], in0=wre[:], in1=sinT, op=ALU.mult), r=[B("wre"), B("rot2", ra)], w=[B("t1")])
                        V(lambda e, cosT=cosT: e.tensor_tensor(out=t2[:], in0=wim[:], in1=cosT, op=ALU.mult), r=[B("wim"), B("rot2", ra)], w=[B("t2")])
                        V(lambda e, s8=s8: e.tensor_tensor(out=hbf[:, 8 * J + s8 * J:8 * J + (s8 + 1) * J], in0=t1[:], in1=t2[:], op=ALU.add), r=[B("t1"), B("t2")], w=[B("hbf")])
                        cl, sl = rot2[:, ra, 0, J - 1:J], rot2[:, ra, 1, J - 1:J]
                        V(lambda e, sl=sl: e.tensor_tensor(out=sm[:, 0, 6:7], in0=wim[:, J - 1:J], in1=sl, op=ALU.mult), r=[B("wim"), B("rot2", ra)], w=[B("sm", 0)])
                        V(lambda e, s8=s8, cl=cl: e.scalar_tensor_tensor(out=hcar[:, l, s8, 0:1], in0=wre[:, J - 1:J], scalar=cl, in1=sm[:, 0, 6:7], op0=ALU.mult, op1=ALU.subtract), r=[B("wre"), B("rot2", ra), B("sm", 0)], w=[B("hcar", l)])
                        V(lambda e, sl=sl: e.tensor_tensor(out=sm[:, 0, 7:8], in0=wre[:, J - 1:J], in1=sl, op=ALU.mult), r=[B("wre"), B("rot2", ra)], w=[B("sm", 0)])
                        V(lambda e, s8=s8, cl=cl: e.scalar_tensor_tensor(out=hcar[:, l, s8, 1:2], in0=wim[:, J - 1:J], scalar=cl, in1=sm[:, 0, 7:8], op0=ALU.mult, op1=ALU.add), r=[B("wim"), B("rot2", ra), B("sm", 0)], w=[B("hcar", l)])
                        yield
                    hbre = hbf
                    ssm_out(c0, J, lambda s8: hbre[:, s8 * J:(s8 + 1) * J], lambda s8: hbre[:, 8 * J + s8 * J:8 * J + (s8 + 1) * J], 0, [B("hbf")])
                    yield
                if last:
                    Q(lambda e, l=l: e.dma_start(out=ossm_p[l, :, :].rearrange("p (s c) -> p s c", c=2), in_=hcar[:, l, :, :]), r=[B("hcar", l)], w=[B("ossm_p", l)])
                if first:
                    Q(lambda e, l=l: e.dma_start(out=h0[:, 0, :, :], in_=sre_d[l, :, :].rearrange("p (s b) -> p s b", s=8)), w=[B("h0")])
                    Q(lambda e, l=l: e.dma_start(out=h0[:, 1, :, :], in_=sim_d[l, :, :].rearrange("p (s b) -> p s b", s=8)), w=[B("h0")])
                    for s8 in range(8):
                        ct = s8 // 4
                        for ci in range(2):
                            PE(lambda e, ci=ci, s8=s8, ct=ct: e.matmul(ps[4][:, (s8 * 2 + ci) * NS:(s8 * 2 + ci + 1) * NS], lhsT=WB[:, s8 * 2 + ci, :], rhs=us[:, ct, T:NTM], start=True, stop=True), r=[B("us", T), B("WB")], w=[B("ps", 4)])
                    V(lambda e: e.tensor_copy(out=h1[:].rearrange("p c s b -> p s c b"), in_=ps[4][:, 0:16 * NS].rearrange("p (s c b) -> p s c b", s=8, c=2)), r=[B("ps", 4)], w=[B("h1")])
                    for s8 in range(8):
                        lre, lim = pl[:, 5, s8:s8 + 1], pl[:, 6, s8:s8 + 1]
                        V(lambda e, s8=s8: e.tensor_tensor(out=sm[:, 0, 6:7], in0=lam[:, l, 0, s8:s8 + 1], in1=lam[:, l, 1, s8:s8 + 1], op=ALU.mult), r=[B("lam", l)], w=[B("sm", 0)])
                        V(lambda e, s8=s8: e.tensor_tensor(out=sm[:, 0, 7:8], in0=lam[:, l, 0, s8:s8 + 1], in1=lam[:, l, 2, s8:s8 + 1], op=ALU.mult), r=[B("lam", l)], w=[B("sm", 0)])
                        V(lambda e, s8=s8: e.tensor_scalar(out=sm[:, 1, 7:8], in0=sm[:, 0, 7:8], scalar1=-1.0, scalar2=None, op0=ALU.mult), r=[B("sm", 0)], w=[B("sm", 1)])
                        V(lambda e, s8=s8: e.scalar_tensor_tensor(out=h1[:, 0, s8, :], in0=h0[:, 0, s8, :], scalar=sm[:, 0, 6:7], in1=h1[:, 0, s8, :], op0=ALU.mult, op1=ALU.add), r=[B("h0"), B("sm", 0), B("h1")], w=[B("h1")])
                        V(lambda e, s8=s8: e.scalar_tensor_tensor(out=h1[:, 0, s8, :], in0=h0[:, 1, s8, :], scalar=sm[:, 1, 7:8], in1=h1[:, 0, s8, :], op0=ALU.mult, op1=ALU.add), r=[B("h0"), B("sm", 1), B("h1")], w=[B("h1")])
                        V(lambda e, s8=s8: e.scalar_tensor_tensor(out=h1[:, 1, s8, :], in0=h0[:, 1, s8, :], scalar=sm[:, 0, 6:7], in1=h1[:, 1, s8, :], op0=ALU.mult, op1=ALU.add), r=[B("h0"), B("sm", 0), B("h1")], w=[B("h1")])
                        V(lambda e, s8=s8: e.scalar_tensor_tensor(out=h1[:, 1, s8, :], in0=h0[:, 0, s8, :], scalar=sm[:, 0, 7:8], in1=h1[:, 1, s8, :], op0=ALU.mult, op1=ALU.add), r=[B("h0"), B("sm", 0), B("h1")], w=[B("h1")])
                    Q(lambda e, l=l: e.dma_start(out=ossm_s[l, :, :].rearrange("p (c s b) -> p c s b", c=2, s=8), in_=h1[:]), r=[B("h1")], w=[B("ossm_s", l)])
                    V(lambda e: e.tensor_copy(out=h1b[:], in_=h1[:]), r=[B("h1")], w=[B("h1b")])
                    ssm_out(T, NS, lambda s8: h1b[:, 0, s8, :], lambda s8: h1b[:, 1, s8, :], 0, [B("h1b")])
                    yield

                s_out = [wload("M", w_out[l, :, i * 512:(i + 1) * 512].rearrange("(k p) n -> p k n", p=128), "p (k n) -> p k n", wid=l * 23 + 3 + i, ch=ch, k=8) for i in range(2)]
                for (c0, cn) in ccs:
                    for dt_ in range(8):
                        s = s_out[dt_ // 4]
                        off = (dt_ % 4) * 128
                        pbk = bank()
                        for k in range(8):
                            PE(lambda e, k=k, s=s, off=off, pbk=pbk: e.matmul(ps[pbk][:, 0:cn], lhsT=ring[:, s, k * 512 + off:k * 512 + off + 128], rhs=hb[:, k, c0:c0 + cn], start=(k == 0), stop=(k == 7)), r=[B(SL, s), B("hb", par, c0)], w=[B("ps", pbk)])
                        V(lambda e, dt_=dt_, pbk=pbk: e.tensor_tensor(out=x[:, dt_, c0:c0 + cn], in0=ps[pbk][:, 0:cn], in1=x[:, dt_, c0:c0 + cn], op=ALU.add), r=[B("ps", pbk), B("x", par, c0)], w=[B("x", par, c0)])
                        yield

        def gen_ffn(ch, l):
            par = ch % 2; x = X[par]; hb = HB[par]
            first, last = (ch == 0), (ch == nch - 1)
            ccs = [(0, 512)] + ([(T, NS)] if first else [])
            sq, rstd = sq2, rstd2
            ring, SL = ringF, "slotF"
            bank = fbank
            if True:
                rmsnorm(x, hb, par, l * PVL + OG2, ccs, sq2, rstd2, fbank(), "f")
                for fg in range(6):
                    nf = 4 if fg < 5 else 2
                    sg_ = wload("F", w_g[l, :, fg * 512:fg * 512 + nf * 128].rearrange("(k p) n -> p k n", p=128), "p (k n) -> p k n", wid=l * 23 + 5 + fg * 3, ch=ch, k=8)
                    su_ = wload("F", w_u[l, :, fg * 512:fg * 512 + nf * 128].rearrange("(k p) n -> p k n", p=128), "p (k n) -> p k n", wid=l * 23 + 6 + fg * 3, ch=ch, k=8)
                    sd_ = wload("F", w_d[l, fg * 512:fg * 512 + nf * 128, :].rearrange("(f p) n -> p f n", p=128), "p (f n) -> p f n", wid=l * 23 + 7 + fg * 3, ch=ch, f=nf)
                    W_ = nf * 128
                    for (c0, cn) in ccs:
                        for f in range(nf):
                            pg, pu = bank(), bank()
                            for (s_, pb_) in ((sg_, pg), (su_, pu)):
                                for k in range(8):
                                    PE(lambda e, k=k, s_=s_, pb_=pb_, f=f: e.matmul(ps[pb_][:, 0:cn], lhsT=ring[:, s_, k * W_ + f * 128:k * W_ + f * 128 + 128], rhs=hb[:, k, c0:c0 + cn], start=(k == 0), stop=(k == 7)), r=[B(SL, s_), B("hb", par, c0)], w=[B("ps", pb_)])
                            S(lambda e, pg=pg, f=f: e.activation(out=sgb[:, f % 2, 0:cn], in_=ps[pg][:, 0:cn], func=AF.Silu), r=[B("ps", pg)], w=[B("sgb", f % 2)])
                            V(lambda e, pu=pu, f=f: e.tensor_tensor(out=hid[:, f, c0:c0 + cn], in0=ps[pu][:, 0:cn], in1=sgb[:, f % 2, 0:cn], op=ALU.mult), r=[B("ps", pu), B("sgb", f % 2)], w=[B("hid", c0)])
                            yield
                        for dt_ in range(8):
                            pbk = bank()
                            for f in range(nf):
                                PE(lambda e, f=f, dt_=dt_, pbk=pbk: e.matmul(ps[pbk][:, 0:cn], lhsT=ring[:, sd_, f * 1024 + dt_ * 128:f * 1024 + dt_ * 128 + 128], rhs=hid[:, f, c0:c0 + cn], start=(f == 0), stop=(f == nf - 1)), r=[B(SL, sd_), B("hid", c0)], w=[B("ps", pbk)])
                            V(lambda e, dt_=dt_, pbk=pbk: e.tensor_tensor(out=x[:, dt_, c0:c0 + cn], in0=ps[pbk][:, 0:cn], in1=x[:, dt_, c0:c0 + cn], op=ALU.add), r=[B("ps", pbk), B("x", par, c0)], w=[B("x", par, c0)])
                            yield
            if l == L - 1:
                for (c0, cn) in ccs:
                    for k in range(8):
                        S(lambda e, k=k: e.activation(out=sq[:, 0:cn], in_=x[:, k, c0:c0 + cn], func=AF.Square), r=[B("x", par, c0)], w=[B("sq", "f")])
                        PE(lambda e, k=k: e.matmul(ps[0][:, 0:cn], lhsT=ones_b[:], rhs=sq[:, 0:cn], start=(k == 0), stop=(k == 7)), r=[B("sq", "f"), cB], w=[B("ps", 0)])
                    S(lambda e: e.activation(out=rstd[:, 0:cn], in_=ps[0][:, 0:cn], func=AF.Sqrt, scale=1.0 / D, bias=EPS), r=[B("ps", 0)], w=[B("rstd", "f")])
                    V(lambda e: e.reciprocal(out=rstd[:, 0:cn], in_=rstd[:, 0:cn]), r=[B("rstd", "f")], w=[B("rstd", "f")])
                    for k in range(8):
                        gk = pvt[:, 4 * PVL + k: 4 * PVL + k + 1]
                        V(lambda e, k=k, gk=gk: e.scalar_tensor_tensor(out=x[:, k, c0:c0 + cn], in0=x[:, k, c0:c0 + cn], scalar=gk, in1=rstd[:, 0:cn], op0=ALU.mult, op1=ALU.mult), r=[B("x", par, c0), B("rstd", "f"), cB], w=[B("x", par, c0)])
                        if c0 < T:
                            Q(lambda e, k=k: e.dma_start(out=yT[k * 128:(k + 1) * 128, ch * T + c0:ch * T + c0 + cn], in_=x[:, k, c0:c0 + cn]), r=[B("x", par, c0)], w=[B("yT")])
                        else:
                            Q(lambda e, k=k: e.dma_start(out=ysT[k * 128:(k + 1) * 128, :], in_=x[:, k, c0:c0 + cn]), r=[B("x", par, c0)], w=[B("ysT")])
            yield

        def count_ops(gen):
            P.dry = True
            n0 = P.dryn
            for _ in gen:
                pass
            P.dry = False
            return P.dryn - n0

        def run2(ga, na, gb, nb):
            ia = ib = 0
            alive_a, alive_b = ga is not None, gb is not None
            while alive_a or alive_b:
                pick_a = alive_a and (not alive_b or ia * nb <= ib * na)
                n0 = P.nops
                if pick_a:
                    try:
                        next(ga)
                    except StopIteration:
                        alive_a = False
                    ia += P.nops - n0
                else:
                    try:
                        next(gb)
                    except StopIteration:
                        alive_b = False
                    ib += P.nops - n0

        streams = []
        for ch in range(nch):
            ph = []
            for l in range(L):
                ph.append(("m", ch, l))
                ph.append(("f", ch, l))
            streams.append(ph)
        steps = []
        t = 0
        start = {}
        for ch in range(nch):
            start[ch] = (ch // 2) * 2 * L + (ch % 2)
        nsteps = max(start[c] + 2 * L for c in range(nch))
        for t in range(nsteps):
            cur = []
            for ch in range(nch):
                p = t - start[ch]
                if 0 <= p < 2 * L:
                    cur.append(streams[ch][p])
            steps.append(cur)
        for cur in steps:
            gens = []
            for (kind, ch, l) in cur:
                mk = (lambda: gen_mix(ch, l)) if kind == "m" else (lambda: gen_ffn(ch, l))
                n = count_ops(mk())
                gens.append((mk(), max(n, 1)))
            if len(gens) == 1:
                run2(gens[0][0], gens[0][1], None, 1)
            else:
                run2(gens[0][0], gens[0][1], gens[1][0], gens[1][1])
        print('NOPS', P.nops, {k: len(v) for k, v in P.ops.items()}, flush=True)
        P.emit()
    return nc


def _alibi_tables():
    slopes = 2.0 ** (-8.0 * np.arange(1, 9, dtype=np.float32) / 8)
    qi = np.arange(128)[:, None]
    kj = np.arange(256)[None, :]
    dist = qi - kj + 128
    valid = (dist >= 0) & (dist < 128)
    ab = np.where(valid, -dist.astype(np.float32), -1.0e7).astype(np.float32)
    dj = (127 - np.arange(128)).astype(np.float32)
    sbias = (-slopes.reshape(2, 4, 1) * dj[None, None, :]).astype(np.float32)
    sbias = np.ascontiguousarray(sbias.transpose(1, 0, 2)).reshape(4, 2 * 128)
    return ab, sbias


def _fm(v):
    return np.ascontiguousarray(np.asarray(v, np.float32).reshape(-1, 128).T)


_NC_CACHE = {}


def kernel(nch=16, depth=4, **inp):
    f = lambda k: np.asarray(inp[k], np.float32)
    L = depth
    TT = nch * T
    key = (nch, depth)
    if key not in _NC_CACHE:
        _NC_CACHE[key] = build(nch, depth)
    nc = _NC_CACHE[key]
    ab, sbias = _alibi_tables()
    pv = np.zeros((128, NPV), np.float32)
    for l in range(L):
        o = l * PVL
        pv[:, o + 0:o + 8] = _fm(f("norm_mix_g")[l])
        pv[:, o + 8:o + 16] = _fm(f("norm_ffn_g")[l])
        cw = f("conv_dw_w")[l]
        for ct in range(2):
            pv[:, o + 16 + ct * 31:o + 16 + (ct + 1) * 31] = cw[:, ct * 128:(ct + 1) * 128].T
        pv[:, o + 78:o + 80] = _fm(f("conv_dw_b")[l])
        pv[:, o + 80:o + 82] = _fm(f("conv_ln_g")[l])
        pv[:, o + 82:o + 84] = _fm(f("conv_ln_b")[l])
        pv[:, o + 84:o + 86] = _fm(f("ssm_d")[l])
        pv[:, o + 86:o + 88] = _fm(f("ssm_glu_b")[l])
        pv[:, o + 88:o + 96] = _fm(f("ssm_a_re")[l].reshape(-1))
        pv[:, o + 96:o + 104] = _fm(f("ssm_a_im")[l].reshape(-1))
        pv[:, o + 104:o + 112] = _fm(np.repeat(f("ssm_log_dt")[l], 64))
        pv[:, o + 112:o + 120] = f("attn_sinks")[l][None, :]
    pv[:, 4 * PVL:4 * PVL + 8] = _fm(f("norm_final_g"))
    Bn = np.zeros((L, 128, 8, 2, 128), np.float32)
    Cn = np.zeros((L, 128, 8, 2, 128), np.float32)
    for l in range(L):
        for ci, (bk, ck_) in enumerate((("ssm_b_re", "ssm_c_re"), ("ssm_b_im", "ssm_c_im"))):
            bb = f(bk)[l]
            cc = f(ck_)[l]
            for g in range(16):
                st_, p0 = g // 2, (g % 2) * 64
                col = (g % 8) * 16
                Bn[l, p0:p0 + 64, st_, ci, col:col + 16] = bb[g]
                Cn[l, p0:p0 + 64, st_, ci, col:col + 16] = cc[g].T
    Bn = Bn.reshape(L, 128, -1)
    Cn = Cn.reshape(L, 128, -1)
    ident = np.eye(128, dtype=np.float32)
    jjv = np.broadcast_to(np.arange(1, J + 1, dtype=np.float32)[None, :], (128, J)).copy()
    xp = f("x_prompt")
    xs = f("x_sample")[:, 0, :]
    shared = {
        "w_in": f("w_in")[:L], "w_out": f("w_out")[:L], "w_g": f("w_ff_gate")[:L], "w_u": f("w_ff_up")[:L],
        "w_d": f("w_ff_down")[:L], "glu_w": f("ssm_glu_w")[:L], "pv": pv, "Bn": Bn, "Cn": Cn, "ident": ident,
        "jj": jjv, "abias": ab, "sbias": sbias,
        "sinkc": np.ascontiguousarray(f("attn_sinks")[:L].reshape(L, 2, 4).transpose(2, 0, 1).reshape(4, L * 2)),
    }
    in_maps = []
    for c in range(8):
        sq_, b0 = c // 4, c * NS
        m = dict(shared)
        m["xT"] = np.ascontiguousarray(xp[sq_, :TT, :].T)
        m["xsT"] = np.ascontiguousarray(xs[b0:b0 + NS].T)
        m["ck"] = np.ascontiguousarray(f("cache_swa_k")[:L, b0:b0 + NS].reshape(L, NS, 128, 128))
        m["cv"] = np.ascontiguousarray(f("cache_swa_v")[:L, b0:b0 + NS].reshape(L, NS, 128, 128))
        cc_ = f("cache_conv")[:L, b0:b0 + NS]
        m["cconv"] = np.ascontiguousarray(cc_.reshape(L, NS, 30, 2, 128).transpose(0, 4, 3, 1, 2)).reshape(L, 128, -1)
        for nm, kk in (("sre", "state_ssm_re"), ("sim", "state_ssm_im")):
            s_ = f(kk)[:L, b0:b0 + NS].reshape(L, NS, 8, 128)
            m[nm] = np.ascontiguousarray(s_.transpose(0, 3, 2, 1)).reshape(L, 128, -1)
        in_maps.append(m)
    res = run_bass_kernel_spmd(nc, in_maps, core_ids=list(range(8))).results
    y_p = np.stack([res[0]["yT"].T, res[4]["yT"].T]).astype(np.float32)
    y_s = np.concatenate([res[c]["ysT"].T for c in range(8)])[:, None, :].astype(np.float32)
    pc = (res[0], res[4])
    k_p = np.stack([np.stack([r["ok_p"][l].reshape(128, 2, 64) for r in pc]) for l in range(L)])
    v_p = np.stack([np.stack([r["ov_p"][l].reshape(128, 2, 64) for r in pc]) for l in range(L)])
    conv_p = np.stack([np.stack([r["oconv_p"][l].reshape(128, 2, 30).transpose(2, 1, 0).reshape(30, 256) for r in pc]) for l in range(L)])
    ssm_p = [np.stack([np.stack([r["ossm_p"][l].reshape(128, 8, 2)[:, :, ci].T.reshape(16, 64) for r in pc]) for l in range(L)]) for ci in range(2)]
    k_s = np.concatenate([res[c]["ok_s"].reshape(L, NS, 128, 2, 64) for c in range(8)], 1)
    v_s = np.concatenate([res[c]["ov_s"].reshape(L, NS, 128, 2, 64) for c in range(8)], 1)
    conv_s = np.concatenate([res[c]["oconv_s"].reshape(L, 128, 2, NS, 30).transpose(0, 3, 4, 2, 1).reshape(L, NS, 30, 256) for c in range(8)], 1)
    ssm_s = [np.concatenate([res[c]["ossm_s"].reshape(L, 128, 2, 8, NS)[:, :, ci].transpose(0, 3, 2, 1).reshape(L, NS, 16, 64) for c in range(8)], 1) for ci in range(2)]
    outs = (y_p, y_s, k_p, v_p, conv_p, ssm_p[0], ssm_p[1], k_s, v_s, conv_s, ssm_s[0], ssm_s[1])
    return tuple(np.ascontiguousarray(o, dtype=np.float32) for o in outs)
```

```python
import math
import os
import numpy as np
from contextlib import ExitStack
import concourse.bass as bass
import concourse.mybir as mybir
from concourse.bass_utils import run_bass_kernel_spmd

F32 = mybir.dt.float32
BF16 = mybir.dt.bfloat16
I32 = mybir.dt.int32
AF = mybir.ActivationFunctionType
ALU = mybir.AluOpType
AX = mybir.AxisListType

D = 1024
SEQ = 8192
T = 512
NS = 16
NTM = T + NS
DFF = 2816
NF = 22
J = 256
NSLOT = 4
PVL = 120
NPV = 4 * PVL + 8
EPS = 1e-6
TWO_PI = 2.0 * math.pi


import types


def _freeze(fn):
    if fn.__closure__ is None:
        return fn
    cells = []
    for c in fn.__closure__:
        try:
            cells.append(types.CellType(c.cell_contents))
        except ValueError:
            cells.append(c)
    return types.FunctionType(fn.__code__, fn.__globals__, fn.__name__, fn.__defaults__, tuple(cells))


class _Stub:
    def __init__(self):
        self.closed = True

    def matmul(self, *a, **kw):
        self.closed = bool(kw.get("stop", True))
        return self

    def transpose(self, *a, **kw):
        self.closed = True
        return self

    def then_inc(self, *a, **kw):
        return self


class Buf:
    __slots__ = ("name", "lw", "rd")

    def __init__(self, name):
        self.name = name
        self.lw = None
        self.rd = {}


class Prog:
    ENG = ["tensor", "vector", "scalar", "gpsimd", "sync"]

    def __init__(self, nc, same_eng_sync=("vector", "scalar", "gpsimd")):
        self.nc = nc
        self.ops = {e: [] for e in self.ENG}
        self.cnt = {}
        self.known = {e: {} for e in self.ENG}
        self.same = set(same_eng_sync)
        self.bufs = {}
        self.dry = False
        self.dq = {}
        self.dryn = 0
        self.nops = 0

    def B(self, *key):
        b = self.bufs.get(key)
        if b is None:
            b = self.bufs[key] = Buf(key)
        return b

    def op(self, eng, fn, r=(), w=(), dma=None, inc=None):
        if self.dry:
            self.dryn += 1
            return None
        fn = _freeze(fn)
        if getattr(self, "stopped", False):
            return None
        self.nops += 1
        if eng == "tensor" and "KLIMIT" in os.environ:
            st_ = _Stub()
            try:
                fn(st_)
            except Exception:
                pass
            self.open_grp = not st_.closed
        if self.nops >= int(os.environ.get("KLIMIT", "100000000")) and not getattr(self, "open_grp", False):
            self.stopped = True
        ex = [b for b in r if b.name[0] in ("ps", "psb")]
        if ex:
            r = [b for b in r if b.name[0] not in ("ps", "psb")]
            w = list(w) + ex
        deps = {}
        for b in list(r) + list(w):
            if b.lw is not None and deps.get(b.lw[0], 0) < b.lw[1]:
                deps[b.lw[0]] = b.lw[1]
        for b in w:
            for s, v in b.rd.items():
                if deps.get(s, 0) < v:
                    deps[s] = v
        pre = None
        if dma is None:
            sem, step = eng, 1
        else:
            npool = 32 if eng == "sync" else 8
            i = self.dq.get(eng, 0)
            self.dq[eng] = i + 1
            sem, step = "d%s%d" % (eng[0], i % npool), 16
            if i >= npool:
                pre = (sem, 16 * (i // npool))
        waits = []
        if pre is not None and deps.get(pre[0], 0) < pre[1]:
            deps[pre[0]] = pre[1]
        for s, v in deps.items():
            if s == eng and eng not in self.same:
                continue
            if self.known[eng].get(s, 0) >= v:
                continue
            self.known[eng][s] = v
            waits.append((s, v))
        self.cnt[sem] = self.cnt.get(sem, 0) + step
        tok = (sem, self.cnt[sem])
        self.ops[eng].append((fn, waits, sem, step))
        for b in r:
            if b.rd.get(sem, 0) < tok[1]:
                b.rd[sem] = tok[1]
        for b in w:
            b.lw = tok
            b.rd = {}
        return tok

    def emit(self):
        nc = self.nc
        with ExitStack() as st:
            sems = {s: st.enter_context(nc.semaphore(s)) for s in self.cnt}
            block = st.enter_context(nc.Block())
            final = dict(self.cnt)

            def mk(engname):
                def body(e):
                    for fn, waits, sem, step in self.ops[engname]:
                        for s, v in waits:
                            e.wait_ge(sems[s], v)
                        fn(e).then_inc(sems[sem], step)
                    if engname == "sync":
                        for s, v in final.items():
                            e.wait_ge(sems[s], v)
                return body

            for engname in self.ENG:
                if self.ops[engname] or engname == "sync":
                    getattr(block, engname)(mk(engname))


def build(nch=16, depth=4):
    nc = bass.Bass("TRN2", target_bir_lowering=False)
    L = depth
    TT = nch * T

    def din(name, shape, dt=F32):
        return nc.dram_tensor(name, list(shape), dt, kind="ExternalInput").ap()

    def dout(name, shape, dt=F32):
        return nc.dram_tensor(name, list(shape), dt, kind="ExternalOutput").ap()

    xT = din("xT", [D, TT]); xsT = din("xsT", [D, NS])
    w_in = din("w_in", [L, D, 1536]); w_out = din("w_out", [L, D, D])
    w_g = din("w_g", [L, D, DFF]); w_u = din("w_u", [L, D, DFF]); w_d = din("w_d", [L, DFF, D])
    glu_w = din("glu_w", [L, 256, 256])
    pv_d = din("pv", [128, NPV])
    Bn_d = din("Bn", [L, 128, 8 * 2 * 128]); Cn_d = din("Cn", [L, 128, 8 * 2 * 128])
    ident_d = din("ident", [128, 128]); jj_d = din("jj", [128, J])
    abias_d = din("abias", [128, 256]); sbias_d = din("sbias", [4, 2 * 128]); sinkc_d = din("sinkc", [4, L * 2])
    ck_d = din("ck", [L, NS, 128, 128]); cv_d = din("cv", [L, NS, 128, 128])
    cconv_d = din("cconv", [L, 128, 2 * NS * 30])
    sre_d = din("sre", [L, 128, 8 * NS]); sim_d = din("sim", [L, 128, 8 * NS])

    yT = dout("yT", [D, TT]); ysT = dout("ysT", [D, NS])
    ok_p = dout("ok_p", [L, 128, 128]); ov_p = dout("ov_p", [L, 128, 128])
    oconv_p = dout("oconv_p", [L, 128, 2 * 30]); ossm_p = dout("ossm_p", [L, 128, 16])
    ok_s = dout("ok_s", [L, NS, 128, 128]); ov_s = dout("ov_s", [L, NS, 128, 128])
    oconv_s = dout("oconv_s", [L, 128, 2 * NS * 30]); ossm_s = dout("ossm_s", [L, 128, 2 * 8 * NS])
    wscr = nc.dram_tensor("w_scr", [L * 23, 128, 4096], BF16).ap()
    rot_d = nc.dram_tensor("rot_scr", [L, 128, 8 * 2 * J], F32).ap()
    wb_d = nc.dram_tensor("wb_scr", [L, 128, 16 * 128], BF16).ap()
    wc_d = nc.dram_tensor("wc_scr", [L, 128, 16 * 128], BF16).ap()

    P = Prog(nc)
    B = P.B
    with ExitStack() as st:
        def sb(name, shape, dt=F32):
            return st.enter_context(nc.sbuf_tensor("sb_" + name, list(shape), dt))

        def pst(name, shape, dt=F32):
            return st.enter_context(nc.psum_tensor(name, list(shape), dt))

        x = sb("x", [128, 8, NTM]); hb = sb("hb", [128, 8, NTM], BF16)
        x1 = sb("x1", [128, 8, T]); hb1 = sb("hb1", [128, 8, T], BF16)
        X = [x, x1]; HB = [hb, hb1]
        qT = sb("qT", [64, 8, NTM], BF16); kT = sb("kT", [64, 2, 128 + NTM], BF16)
        vp = sb("vp", [128, 5, 2, 192], BF16)
        ucv = sb("ucv", [128, 2, 30 + NTM]); acc = sb("acc", [128, 2, NTM]); us = sb("us", [128, 2, NTM], BF16)
        hid = sb("hid", [128, 4, NTM], BF16)
        ringM = sb("ringM", [128, 3, 4096], BF16); ringF = sb("ringF", [128, 3, 4096], BF16)
        pvt = sb("pvt", [128, NPV])
        ident = sb("ident", [128, 128]); identb = sb("identb", [128, 128], BF16)
        ones_f = sb("ones_f", [128, 128]); ones_b = sb("ones_b", [128, 128], BF16)
        jj = sb("jj", [128, J])
        abias = sb("abias", [128, 256]); sbias = sb("sbias", [4, 2, 128])
        WB = sb("WB", [128, 16, 128], BF16); WC = sb("WC", [128, 16, 128], BF16)
        Dd = sb("Dd", [128, 2, 128], BF16); gluw = sb("gluw", [128, 2, 256], BF16)
        lam = sb("lam", [128, L, 4, 8])
        rot2 = sb("rot2", [128, 2, 2, J])
        hcar = sb("hcar", [128, L, 8, 2])
        khalo = sb("khalo", [64, L, 2, 128], BF16); vhalo = sb("vhalo", [128, L, 2, 192], BF16)
        chalo = sb("chalo", [128, L, 2, 30])
        sq = sb("sq", [128, 512], BF16); rstd = sb("rstd", [128, 512])
        sq2 = sb("sq2", [128, 512], BF16); rstd2 = sb("rstd2", [128, 512])
        sgb = sb("sgb", [128, 2, 512], BF16)
        sc = sb("sc", [128, 2, 256]); pb = sb("pb", [128, 2, 256], BF16); ptb = sb("ptb", [128, 2, 256], BF16)
        sm = sb("sm", [128, 2, 8])
        t1 = sb("t1", [128, J]); t2 = sb("t2", [128, J]); bre = sb("bre", [128, J]); bim = sb("bim", [128, J])
        wre = sb("wre", [128, J]); wim = sb("wim", [128, J])
        ysm = sb("ysm", [128, 2, 512]); yb = sb("yb", [128, 2, 512], BF16); g1 = sb("g1", [128, 512]); g2 = sb("g2", [128, 512])
        ysq = ysm; cst = ysm[:].rearrange("p c n -> p (c n)")[:, 0:2 * NS * 31].rearrange("p (c b k) -> p c b k", c=2, b=NS); mean = sb("mean", [128, 512]); var = g2
        xflat = x[:].rearrange("p k n -> p (k n)")
        rotflat = x1[:].rearrange("p k n -> p (k n)")[:, 0:16 * J]
        big = xflat[:, 0:4096]; big2 = rotflat[:, 8 * J:16 * J]; bigi = sb("bigi", [128, 512], I32)[:]
        hbf = sb("hbf", [128, 16 * J], BF16)
        pl = sb("pl", [128, 16, 8])
        tk = sb("tk", [128, 128]); tkb = sb("tkb", [NS, 2, 128])
        Kb = sb("Kb", [128, 1, 128]); Vb = sb("Vb", [128, 1, 128]); KbT = sb("KbT", [64, 2, 128], BF16)
        Vbp = sb("Vbp", [128, 2, 192], BF16)
        ssc = sb("ssc", [4, 4, 128]); spb = sb("spb", [4, 4, 128], BF16); ssm_ = sb("ssm_", [4, 4, 4]); sinkt = sb("sinkt", [4, L * 2])
        sptb = sb("sptb", [128, 2, NS, 4], BF16)
        cs = sb("cs", [128, 2, NS, 31])
        h0 = sb("h0", [128, 2, 8, NS]); h1 = sb("h1", [128, 2, 8, NS]); h1b = sb("h1b", [128, 2, 8, NS], BF16)

        ps = [pst("ps%d" % i, [128, 512]) for i in range(7)]
        psb = pst("psb", [128, 1024], BF16)

        Q = lambda fn, r=(), w=(), ch="io": P.op("sync", fn, r, w, dma=ch)
        G = lambda fn, r=(), w=(), ch="w": P.op("gpsimd", fn, r, w, dma=ch)
        V = lambda fn, r=(), w=(): P.op("vector", fn, r, w)
        S = lambda fn, r=(), w=(): P.op("scalar", fn, r, w)
        PE = lambda fn, r=(), w=(): P.op("tensor", fn, r, w)
        GP = lambda fn, r=(), w=(): P.op("gpsimd", fn, r, w)

        def pvc(l, off, n=1):
            return pvt[:, l * PVL + off: l * PVL + off + n]

        OG1, OG2, OCW, OCB, OLG, OLB, OSD, OGB, OARE, OAIM, OLDT, OSINK = 0, 8, 16, 78, 80, 82, 84, 86, 88, 96, 104, 112
        rr = [0]

        def bank():
            rr[0] = (rr[0] + 1) % 3
            return 4 + rr[0]

        fr = [0]

        def fbank():
            fr[0] = (fr[0] + 1) % 4
            return fr[0]

        cB = B("const")
        Q(lambda e: e.dma_start(out=pvt[:], in_=pv_d[:, :]), w=[cB])
        Q(lambda e: e.dma_start(out=ident[:], in_=ident_d[:, :]), w=[cB])
        Q(lambda e: e.dma_start(out=jj[:], in_=jj_d[:, :]), w=[cB])
        Q(lambda e: e.dma_start(out=abias[:], in_=abias_d[:, :]), w=[cB])
        Q(lambda e: e.dma_start(out=sbias[:].rearrange("p h k -> p (h k)"), in_=sbias_d[:, :]), w=[cB])
        Q(lambda e: e.dma_start(out=sinkt[:], in_=sinkc_d[:, :]), w=[cB])
        V(lambda e: e.memset(ones_f[:], 1.0), w=[cB])
        V(lambda e: e.memset(ones_b[:], 1.0), w=[cB])
        V(lambda e: e.tensor_copy(out=identb[:], in_=ident[:]), r=[cB], w=[B("identb")])
        V(lambda e: e.memset(vp[:].rearrange("p a g c -> p (a g c)"), 0.0), w=[B("vp", i) for i in range(5)])
        V(lambda e: e.memset(vhalo[:].rearrange("p l g c -> p (l g c)"), 0.0), w=[B("vhalo", l) for l in range(L)])
        V(lambda e: e.memset(Vbp[:].rearrange("p g c -> p (g c)"), 0.0), w=[B("Vbp")])
        V(lambda e: e.memset(hcar[:].rearrange("p l s c -> p (l s c)"), 0.0), w=[B("hcar", l) for l in range(L)])
        V(lambda e: e.memset(chalo[:].rearrange("p l c k -> p (l c k)"), 0.0), w=[B("chalo", l) for l in range(L)])

        for l in range(L):
            pB = B("pl")
            are, aim, ldt = pvc(l, OARE, 8), pvc(l, OAIM, 8), pvc(l, OLDT, 8)
            c = lambda i: pl[:, i, :]
            S(lambda e: e.activation(out=c(0), in_=ldt, func=AF.Exp), r=[cB], w=[pB])
            V(lambda e: e.tensor_tensor(out=c(1), in0=are, in1=c(0), op=ALU.mult), r=[pB], w=[pB])
            V(lambda e: e.tensor_tensor(out=c(2), in0=aim, in1=c(0), op=ALU.mult), r=[pB], w=[pB])
            S(lambda e, l=l: e.activation(out=lam[:, l, 0, :], in_=c(1), func=AF.Exp), r=[pB], w=[B("lam", l)])
            V(lambda e: e.tensor_scalar(out=c(3), in0=c(2), scalar1=1.0 / TWO_PI, scalar2=None, op0=ALU.mult), r=[pB], w=[pB])
            V(lambda e: e.tensor_copy(out=bigi[:, 0:8], in_=c(3)), r=[pB], w=[pB])
            V(lambda e: e.tensor_copy(out=c(4), in_=bigi[:, 0:8]), r=[pB], w=[pB])
            V(lambda e: e.tensor_tensor(out=c(4), in0=c(3), in1=c(4), op=ALU.subtract), r=[pB], w=[pB])
            S(lambda e, l=l: e.activation(out=lam[:, l, 2, :], in_=c(4), func=AF.Sin, scale=6.283185), r=[pB], w=[B("lam", l)])
            V(lambda e: e.tensor_scalar(out=c(3), in0=c(3), scalar1=0.25, scalar2=None, op0=ALU.add), r=[pB], w=[pB])
            V(lambda e: e.tensor_copy(out=bigi[:, 0:8], in_=c(3)), r=[pB], w=[pB])
            V(lambda e: e.tensor_copy(out=c(4), in_=bigi[:, 0:8]), r=[pB], w=[pB])
            V(lambda e: e.tensor_tensor(out=c(4), in0=c(3), in1=c(4), op=ALU.subtract), r=[pB], w=[pB])
            S(lambda e, l=l: e.activation(out=lam[:, l, 1, :], in_=c(4), func=AF.Sin, scale=6.283185), r=[pB], w=[B("lam", l)])
            V(lambda e, l=l: e.tensor_tensor(out=c(5), in0=lam[:, l, 0, :], in1=lam[:, l, 1, :], op=ALU.mult), r=[pB, B("lam", l)], w=[pB])
            V(lambda e, l=l: e.tensor_tensor(out=c(6), in0=lam[:, l, 0, :], in1=lam[:, l, 2, :], op=ALU.mult), r=[pB, B("lam", l)], w=[pB])
            V(lambda e: e.tensor_scalar(out=c(7), in0=c(5), scalar1=-1.0, scalar2=None, op0=ALU.add), r=[pB], w=[pB])
            V(lambda e: e.tensor_tensor(out=c(8), in0=are, in1=are, op=ALU.mult), r=[pB], w=[pB])
            V(lambda e: e.tensor_tensor(out=c(9), in0=aim, in1=aim, op=ALU.mult), r=[pB], w=[pB])
            V(lambda e: e.tensor_tensor(out=c(8), in0=c(8), in1=c(9), op=ALU.add), r=[pB], w=[pB])
            V(lambda e: e.reciprocal(out=c(8), in_=c(8)), r=[pB], w=[pB])
            V(lambda e: e.tensor_tensor(out=c(9), in0=c(7), in1=are, op=ALU.mult), r=[pB], w=[pB])
            V(lambda e: e.tensor_tensor(out=c(10), in0=c(6), in1=aim, op=ALU.mult), r=[pB], w=[pB])
            V(lambda e: e.tensor_tensor(out=c(9), in0=c(9), in1=c(10), op=ALU.add), r=[pB], w=[pB])
            V(lambda e: e.tensor_tensor(out=c(11), in0=c(9), in1=c(8), op=ALU.mult), r=[pB], w=[pB])
            V(lambda e: e.tensor_tensor(out=c(9), in0=c(6), in1=are, op=ALU.mult), r=[pB], w=[pB])
            V(lambda e: e.tensor_tensor(out=c(10), in0=c(7), in1=aim, op=ALU.mult), r=[pB], w=[pB])
            V(lambda e: e.tensor_tensor(out=c(9), in0=c(9), in1=c(10), op=ALU.subtract), r=[pB], w=[pB])
            V(lambda e: e.tensor_tensor(out=c(12), in0=c(9), in1=c(8), op=ALU.mult), r=[pB], w=[pB])
            V(lambda e: e.tensor_scalar(out=c(13), in0=c(12), scalar1=-1.0, scalar2=None, op0=ALU.mult), r=[pB], w=[pB])
            bg = big[:, 0:2048].rearrange("p (s c k) -> p s c k", s=8, c=2)
            Q(lambda e, l=l: e.dma_start(out=big[:, 0:2048], in_=Bn_d[l, :, :]), r=[pB], w=[B("big")])
            for s8 in range(8):
                fre, fim, nfim = pl[:, 11, s8:s8 + 1], pl[:, 12, s8:s8 + 1], pl[:, 13, s8:s8 + 1]
                V(lambda e, s8=s8, fre=fre: e.tensor_scalar(out=t1[:, 0:128], in0=bg[:, s8, 0, :], scalar1=fre, scalar2=None, op0=ALU.mult), r=[pB, B("big")], w=[B("t1")])
                V(lambda e, s8=s8, nfim=nfim: e.scalar_tensor_tensor(out=t1[:, 0:128], in0=bg[:, s8, 1, :], scalar=nfim, in1=t1[:, 0:128], op0=ALU.mult, op1=ALU.add), r=[pB, B("big"), B("t1")], w=[B("t1")])
                V(lambda e, s8=s8, fre=fre: e.tensor_scalar(out=t2[:, 0:128], in0=bg[:, s8, 1, :], scalar1=fre, scalar2=None, op0=ALU.mult), r=[pB, B("big")], w=[B("t2")])
                V(lambda e, s8=s8, fim=fim: e.scalar_tensor_tensor(out=t2[:, 0:128], in0=bg[:, s8, 0, :], scalar=fim, in1=t2[:, 0:128], op0=ALU.mult, op1=ALU.add), r=[pB, B("big"), B("t2")], w=[B("t2")])
                PE(lambda e: e.transpose(out=ps[5][:, 0:128], in_=t1[:, 0:128], identity=ident[:]), r=[B("t1"), cB], w=[B("ps", 5)])
                PE(lambda e: e.transpose(out=ps[5][:, 128:256], in_=t2[:, 0:128], identity=ident[:]), r=[B("t2"), cB], w=[B("ps", 5)])
                S(lambda e, l=l, s8=s8: e.activation(out=WB[:, s8 * 2:s8 * 2 + 2, :], in_=ps[5][:, 0:256].rearrange("p (c k) -> p c k", c=2), func=AF.Copy), r=[B("ps", 5)], w=[B("WB")])
            Q(lambda e, l=l: e.dma_start(out=big[:, 0:2048], in_=Cn_d[l, :, :]), w=[B("big")])
            V(lambda e, l=l: e.tensor_copy(out=WC[:, 0:16, :].rearrange("p (s c) k -> p s c k", c=2)[:, :, 0, :], in_=bg[:, :, 0, :]), r=[B("big")], w=[B("WC")])
            V(lambda e, l=l: e.tensor_scalar(out=WC[:, 0:16, :].rearrange("p (s c) k -> p s c k", c=2)[:, :, 1, :], in0=bg[:, :, 1, :], scalar1=-1.0, scalar2=None, op0=ALU.mult), r=[B("big")], w=[B("WC")])
            a8 = big2.rearrange("p (s j) -> p s j", s=8)
            for s8 in range(8):
                V(lambda e, s8=s8: e.tensor_scalar(out=a8[:, s8, :], in0=jj[:], scalar1=pl[:, 2, s8:s8 + 1], scalar2=1.0 / TWO_PI, op0=ALU.mult, op1=ALU.mult), r=[pB, cB], w=[B("big2"), B("rot")])
            rt = big.rearrange("p (s c j) -> p s c j", s=8, c=2)
            for ci, sh in ((1, 0.0), (0, 0.25)):
                if sh:
                    V(lambda e, sh=sh: e.tensor_scalar(out=big2, in0=big2, scalar1=sh, scalar2=None, op0=ALU.add), r=[B("big2")], w=[B("big2"), B("rot")])
                for pc in range(8 * J // 512):
                    V(lambda e, pc=pc: e.tensor_copy(out=bigi, in_=big2[:, pc * 512:(pc + 1) * 512]), r=[B("big2"), B("rot")], w=[B("bigi")])
                    V(lambda e, pc=pc: e.tensor_copy(out=rotflat[:, pc * 512:(pc + 1) * 512], in_=bigi), r=[B("bigi")], w=[B("rot")])
                V(lambda e: e.tensor_tensor(out=rotflat[:, 0:8 * J], in0=big2, in1=rotflat[:, 0:8 * J], op=ALU.subtract), r=[B("big2"), B("rot")], w=[B("rot")])
                S(lambda e, ci=ci: e.activation(out=rt[:, :, ci, :], in_=rotflat[:, 0:8 * J].rearrange("p (s j) -> p s j", s=8), func=AF.Sin, scale=6.283185), r=[B("rot")], w=[B("big")])
            Q(lambda e, l=l: e.dma_start(out=rot_d[l, :, :], in_=big), r=[B("big")], w=[B("rot_d", l)])
            Q(lambda e, l=l: e.dma_start(out=wb_d[l, :, :], in_=WB[:].rearrange("p a k -> p (a k)")), r=[B("WB")], w=[B("wb_d", l)])
            Q(lambda e, l=l: e.dma_start(out=wc_d[l, :, :], in_=WC[:].rearrange("p a k -> p (a k)")), r=[B("WC")], w=[B("wc_d", l)])

        print('MARK prologue_end', P.nops, flush=True)
        slot_i = {"M": 0, "F": 0}

        def wload(which, src_ap, shape_str, wid=None, ch=0, **kw):
            ring = ringM if which == "M" else ringF
            s = slot_i[which] % 3
            if not P.dry:
                slot_i[which] += 1
            n = 1
            for d_ in src_ap.shape[1:]:
                n *= d_
            if ch == 0:
                dst = ring[:, s, 0:n]
                if shape_str:
                    dst = dst.rearrange(shape_str, **kw)
                G(lambda e: e.dma_start(out=dst, in_=src_ap), w=[B("slot" + which, s)])
                if nch > 1:
                    Q(lambda e: e.dma_start(out=wscr[wid, :, 0:n], in_=ring[:, s, 0:n]), r=[B("slot" + which, s)], w=[B("wscr", wid)])
            else:
                G(lambda e: e.dma_start(out=ring[:, s, 0:n], in_=wscr[wid, :, 0:n]), r=[B("wscr", wid)], w=[B("slot" + which, s)])
            return s

        def rmsnorm(x, hb, par, goff, ccs, sq, rstd, pbk, tag):
            for (c0, cn) in ccs:
                pb_ = ps[pbk]
                for k in range(8):
                    S(lambda e, k=k: e.activation(out=sq[:, 0:cn], in_=x[:, k, c0:c0 + cn], func=AF.Square), r=[B("x", par, c0)], w=[B("sq", tag)])
                    PE(lambda e, k=k: e.matmul(pb_[:, 0:cn], lhsT=ones_b[:], rhs=sq[:, 0:cn], start=(k == 0), stop=(k == 7)), r=[B("sq", tag), cB], w=[B("ps", pbk)])
                S(lambda e: e.activation(out=rstd[:, 0:cn], in_=pb_[:, 0:cn], func=AF.Sqrt, scale=1.0 / D, bias=EPS), r=[B("ps", pbk)], w=[B("rstd", tag)])
                V(lambda e: e.reciprocal(out=rstd[:, 0:cn], in_=rstd[:, 0:cn]), r=[B("rstd", tag)], w=[B("rstd", tag)])
                for k in range(8):
                    gk = pvt[:, goff + k: goff + k + 1]
                    V(lambda e, k=k, gk=gk: e.scalar_tensor_tensor(out=hb[:, k, c0:c0 + cn], in0=x[:, k, c0:c0 + cn], scalar=gk, in1=rstd[:, 0:cn], op0=ALU.mult, op1=ALU.mult), r=[B("x", par, c0), B("rstd", tag), cB], w=[B("hb", par, c0)])

        def gen_mix(ch, l):
            par = ch % 2; x = X[par]; hb = HB[par]
            ring, SL = ringM, "slotM"
            first, last = (ch == 0), (ch == nch - 1)
            ccs = [(0, 512)] + ([(T, NS)] if first else [])
            if l == 0:
                for k in range(8):
                    Q(lambda e, k=k: e.dma_start(out=x[:, k, 0:T], in_=xT[k * 128:(k + 1) * 128, ch * T:(ch + 1) * T]), w=[B("x", par, 0), B("big"), B("rot"), B("big2")])
                if first:
                    Q(lambda e: e.dma_start(out=x[:, :, T:NTM], in_=xsT.rearrange("(k p) n -> p k n", p=128)), w=[B("x", par, T), B("big")])
                yield
            if True:
                Q(lambda e, l=l: e.dma_start(out=WB[:].rearrange("p a k -> p (a k)"), in_=wb_d[l, :, :]), r=[B("wb_d", l)], w=[B("WB")])
                Q(lambda e, l=l: e.dma_start(out=WC[:].rearrange("p a k -> p (a k)"), in_=wc_d[l, :, :]), r=[B("wc_d", l)], w=[B("WC")])
                for ct in range(2):
                    V(lambda e, ct=ct: e.tensor_scalar(out=Dd[:, ct, :], in0=ident[:], scalar1=pvc(l, OSD + ct), scalar2=None, op0=ALU.mult), r=[cB], w=[B("Dd")])
                G(lambda e: e.dma_start(out=gluw[:], in_=glu_w[l].rearrange("(c p) n -> p c n", p=128)), w=[B("gluw")])
                s_in = [wload("M", w_in[l, :, i * 512:(i + 1) * 512].rearrange("(k p) n -> p k n", p=128), "p (k n) -> p k n", wid=l * 23 + i, ch=ch, k=8) for i in range(3)]

                def win(o_lo, o_n):
                    s = s_in[o_lo // 512]
                    off = o_lo % 512
                    return lambda k: ring[:, s, k * 512 + off: k * 512 + off + o_n], B(SL, s)

                V(lambda e, l=l: e.tensor_copy(out=kT[:, :, 0:128], in_=khalo[:, l, :, :]), r=[B("khalo", l)], w=[B("kT", "h")])
                V(lambda e, l=l: e.tensor_copy(out=vp[:, 0, :, :], in_=vhalo[:, l, :, :]), r=[B("vhalo", l)], w=[B("vp", 0)])
                V(lambda e, l=l: e.tensor_copy(out=ucv[:, :, 0:30], in_=chalo[:, l, :, :]), r=[B("chalo", l)], w=[B("ucv", "h")])

                rmsnorm(x, hb, par, l * PVL + OG1, ccs, sq, rstd, 6, "m")
                for (c0, cn) in ccs:
                    def mm8(lw, n_out, pbk):
                        f, sb_ = lw
                        for k in range(8):
                            PE(lambda e, k=k: e.matmul(ps[pbk][0:n_out, 0:cn], lhsT=f(k), rhs=hb[:, k, c0:c0 + cn], start=(k == 0), stop=(k == 7)), r=[sb_, B("hb", par, c0)], w=[B("ps", pbk)])
                    for h in range(8):
                        pbk = bank(); mm8(win(64 * h, 64), 64, pbk)
                        S(lambda e, h=h, pbk=pbk: e.activation(out=qT[:, h, c0:c0 + cn], in_=ps[pbk][0:64, 0:cn], func=AF.Copy, scale=0.125), r=[B("ps", pbk)], w=[B("qT", c0)])
                        yield
                    for g in range(2):
                        pbk = bank(); mm8(win(512 + 64 * g, 64), 64, pbk)
                        S(lambda e, g=g, pbk=pbk: e.activation(out=kT[:, g, 128 + c0:128 + c0 + cn], in_=ps[pbk][0:64, 0:cn], func=AF.Copy), r=[B("ps", pbk)], w=[B("kT", c0)])
                        yield
                    for ct in range(2):
                        pa = bank(); mm8(win(768 + 128 * ct, 128), 128, pa)
                        pg = bank(); mm8(win(1024 + 128 * ct, 128), 128, pg)
                        S(lambda e, pg=pg: e.activation(out=g2[:, 0:cn], in_=ps[pg][:, 0:cn], func=AF.Sigmoid), r=[B("ps", pg)], w=[B("g2")])
                        V(lambda e, ct=ct, pa=pa: e.tensor_tensor(out=ucv[:, ct, 30 + c0:30 + c0 + cn], in0=ps[pa][:, 0:cn], in1=g2[:, 0:cn], op=ALU.mult), r=[B("ps", pa), B("g2")], w=[B("ucv", c0)])
                        yield
                    for ct in range(2):
                        pbk = bank(); mm8(win(1280 + 128 * ct, 128), 128, pbk)
                        S(lambda e, ct=ct, pbk=pbk: e.activation(out=us[:, ct, c0:c0 + cn], in_=ps[pbk][:, 0:cn], func=AF.Copy), r=[B("ps", pbk)], w=[B("us", c0)])
                        yield
                fv, sv = win(640, 128)
                fk, sk = win(512, 128)
                for bi in range(4):
                    c0 = bi * 128
                    cc0 = (c0 // 512) * 512
                    pbk = bank()
                    for k in range(8):
                        PE(lambda e, k=k: e.matmul(ps[pbk][:, 0:128], lhsT=hb[:, k, c0:c0 + 128], rhs=fv(k), start=(k == 0), stop=(k == 7)), r=[sv, B("hb", par, cc0)], w=[B("ps", pbk)])
                    S(lambda e, bi=bi, pbk=pbk: e.activation(out=vp[:, bi + 1, :, 64:128], in_=ps[pbk][:, 0:128].rearrange("p (g d) -> p g d", g=2), func=AF.Copy), r=[B("ps", pbk)], w=[B("vp", bi + 1)])
                    yield
                    if last and bi == 3:
                        V(lambda e, pbk=pbk: e.tensor_copy(out=tk[:], in_=ps[pbk][:, 0:128]), r=[B("ps", pbk), B("vp", bi + 1)], w=[B("tk")])
                        Q(lambda e, l=l: e.dma_start(out=ov_p[l, :, :], in_=tk[:]), r=[B("tk")], w=[B("ov_p", l)])
                        pbk2 = bank()
                        for k in range(8):
                            PE(lambda e, k=k: e.matmul(ps[pbk2][:, 0:128], lhsT=hb[:, k, c0:c0 + 128], rhs=fk(k), start=(k == 0), stop=(k == 7)), r=[sk, B("hb", par, cc0)], w=[B("ps", pbk2)])
                        V(lambda e, pbk2=pbk2: e.tensor_copy(out=tk[:], in_=ps[pbk2][:, 0:128]), r=[B("ps", pbk2)], w=[B("tk")])
                        Q(lambda e, l=l: e.dma_start(out=ok_p[l, :, :], in_=tk[:]), r=[B("tk")], w=[B("ok_p", l)])
                if first:
                    pbk = bank()
                    for (f_, s_, off) in ((fk, sk, 0), (fv, sv, 128)):
                        for k in range(8):
                            PE(lambda e, k=k, f_=f_, off=off: e.matmul(ps[pbk][0:NS, off:off + 128], lhsT=hb[:, k, T:NTM], rhs=f_(k), start=(k == 0), stop=(k == 7)), r=[s_, B("hb", par, T)], w=[B("ps", pbk)])
                    V(lambda e, pbk=pbk: e.tensor_copy(out=tkb[:].rearrange("p a k -> p (a k)"), in_=ps[pbk][0:NS, 0:256]), r=[B("ps", pbk)], w=[B("tkb")])
                    for b in range(NS):
                        Q(lambda e, l=l, b=b: e.dma_start(out=ok_s[l, b:b + 1, 0:127, :].rearrange("b r c -> b (r c)"), in_=ck_d[l, b:b + 1, 1:128, :].rearrange("b r c -> b (r c)")), w=[B("ok_s", l)])
                        Q(lambda e, l=l, b=b: e.dma_start(out=ov_s[l, b:b + 1, 0:127, :].rearrange("b r c -> b (r c)"), in_=cv_d[l, b:b + 1, 1:128, :].rearrange("b r c -> b (r c)")), w=[B("ov_s", l)])
                    Q(lambda e, l=l: e.dma_start(out=ok_s[l, :, 127, :], in_=tkb[:, 0, :]), r=[B("tkb")], w=[B("ok_s", l)])
                    Q(lambda e, l=l: e.dma_start(out=ov_s[l, :, 127, :], in_=tkb[:, 1, :]), r=[B("tkb")], w=[B("ov_s", l)])
                if not last:
                    V(lambda e, l=l: e.tensor_copy(out=khalo[:, l, :, :], in_=kT[:, :, T:T + 128]), r=[B("kT", 0)], w=[B("khalo", l)])
                    V(lambda e, l=l: e.tensor_copy(out=vhalo[:, l, :, :], in_=vp[:, 4, :, :]), r=[B("vp", 4)], w=[B("vhalo", l)])
                    V(lambda e, l=l: e.tensor_copy(out=chalo[:, l, :, :], in_=ucv[:, :, T:T + 30]), r=[B("ucv", 0)], w=[B("chalo", l)])
                else:
                    Q(lambda e, l=l: e.dma_start(out=oconv_p[l, :, :].rearrange("p (c k) -> p c k", c=2), in_=ucv[:, :, T:T + 30]), r=[B("ucv", 0)], w=[B("oconv_p", l)])

                for bi in range(4):
                    q0 = bi * 128
                    qcc = (q0 // 512) * 512
                    nokprev = first and bi == 0
                    kread = [B("kT", qcc)] + ([B("kT", "h")] if bi == 0 else [B("kT", ((q0 - 128) // 512) * 512)])
                    for tile in range(4):
                        for r2 in range(2):
                            h = tile * 2 + r2
                            g = h // 4
                            a = h % 2
                            k_lo, k_n = (128, 128) if nokprev else (0, 256)
                            PE(lambda e, h=h, g=g, k_lo=k_lo, k_n=k_n: e.matmul(ps[4][:, k_lo:k_lo + k_n], lhsT=qT[:, h, q0:q0 + 128], rhs=kT[:, g, q0 + k_lo:q0 + k_lo + k_n], start=True, stop=True), r=[B("qT", qcc)] + kread, w=[B("ps", 4)])
                            V(lambda e, h=h, a=a, k_lo=k_lo, k_n=k_n: e.scalar_tensor_tensor(out=sc[:, a, k_lo:k_lo + k_n], in0=abias[:, k_lo:k_lo + k_n], scalar=float(2.0 ** (-(h + 1))), in1=ps[4][:, k_lo:k_lo + k_n], op0=ALU.mult, op1=ALU.add), r=[B("ps", 4), cB], w=[B("sc", a)])
                            V(lambda e, a=a, k_lo=k_lo, k_n=k_n: e.reduce_max(out=sm[:, a, 0:1], in_=sc[:, a, k_lo:k_lo + k_n], axis=AX.X), r=[B("sc", a)], w=[B("sm", a)])
                            sinkc = pvc(l, OSINK + h)
                            V(lambda e, a=a, sinkc=sinkc: e.tensor_scalar(out=sm[:, a, 1:2], in0=sm[:, a, 0:1], scalar1=sinkc, scalar2=-1.0, op0=ALU.max, op1=ALU.mult), r=[B("sm", a), cB], w=[B("sm", a)])
                            S(lambda e, a=a, k_lo=k_lo, k_n=k_n: e.activation(out=pb[:, a, k_lo:k_lo + k_n], in_=sc[:, a, k_lo:k_lo + k_n], func=AF.Exp, bias=sm[:, a, 1:2], accum_out=sm[:, a, 2:3]), r=[B("sc", a), B("sm", a)], w=[B("pb", a), B("sm", a)])
                            S(lambda e, a=a, sinkc=sinkc: e.activation(out=sm[:, a, 3:4], in_=sinkc, func=AF.Exp, bias=sm[:, a, 1:2]), r=[B("sm", a), cB], w=[B("sm", a)])
                            V(lambda e, a=a: e.tensor_tensor(out=sm[:, a, 4:5], in0=sm[:, a, 2:3], in1=sm[:, a, 3:4], op=ALU.add), r=[B("sm", a)], w=[B("sm", a)])
                            V(lambda e, a=a: e.reciprocal(out=sm[:, a, 5:6], in_=sm[:, a, 4:5]), r=[B("sm", a)], w=[B("sm", a)])
                            V(lambda e, a=a, k_lo=k_lo, k_n=k_n: e.tensor_scalar(out=pb[:, a, k_lo:k_lo + k_n], in0=pb[:, a, k_lo:k_lo + k_n], scalar1=sm[:, a, 5:6], scalar2=None, op0=ALU.mult), r=[B("sm", a), B("pb", a)], w=[B("pb", a)])
                            for kb in range(2):
                                if nokprev and kb == 0:
                                    continue
                                PE(lambda e, a=a, kb=kb: e.transpose(out=psb[:, a * 256 + kb * 128:a * 256 + kb * 128 + 128], in_=pb[:, a, kb * 128:kb * 128 + 128], identity=identb[:]), r=[B("pb", a), B("identb")], w=[B("psb", 0)])
                            S(lambda e, a=a, k_lo=k_lo, k_n=k_n: e.activation(out=ptb[:, a, k_lo:k_lo + k_n], in_=psb[:, a * 256 + k_lo:a * 256 + k_lo + k_n], func=AF.Copy), r=[B("psb", 0)], w=[B("ptb", a)])
                            kbs = [1] if nokprev else [0, 1]
                            for kb in kbs:
                                lo = 64 if r2 == 0 else 0
                                PE(lambda e, a=a, kb=kb, g=g, lo=lo, r2=r2, kbs=kbs: e.matmul(ps[5][:, 0:128], lhsT=vp[:, bi + kb, g, lo:lo + 128], rhs=ptb[:, a, kb * 128:kb * 128 + 128], start=(r2 == 0 and kb == kbs[0]), stop=(r2 == 1 and kb == 1)), r=[B("vp", bi + kb), B("ptb", a)], w=[B("ps", 5)])
                        S(lambda e, tile=tile: e.activation(out=hb[:, tile, q0:q0 + 128], in_=ps[5][:, 0:128], func=AF.Copy), r=[B("ps", 5)], w=[B("hb", par, qcc)])
                        yield

                if first:
                    for g in range(2):
                        sk4 = sinkt[:, l * 2 + g:l * 2 + g + 1]
                        for b4 in range(NS // 4):
                            for bb in range(4):
                                b = b4 * 4 + bb
                                Q(lambda e, b=b, l=l: e.dma_start(out=Kb[:, 0, :], in_=ok_s[l, b, :, :]), r=[B("ok_s", l)], w=[B("Kb", 0)])
                                PE(lambda e, b=b, g=g: e.transpose(out=ps[4][0:64, 0:128], in_=Kb[:, 0, g * 64:(g + 1) * 64], identity=ident[:]), r=[B("Kb", 0), cB], w=[B("ps", 4)])
                                S(lambda e, b=b: e.activation(out=KbT[:, b % 2, :], in_=ps[4][0:64, 0:128], func=AF.Copy), r=[B("ps", 4)], w=[B("KbT", b % 2)])
                                PE(lambda e, b=b, g=g, bb=bb: e.matmul(ps[6][0:4, bb * 128:bb * 128 + 128], lhsT=qT[:, 4 * g:4 * g + 4, T + b], rhs=KbT[:, b % 2, :], start=True, stop=True), r=[B("qT", T), B("KbT", b % 2)], w=[B("ps", 6)])
                            V(lambda e, g=g: e.tensor_tensor(out=ssc[:], in0=ps[6][0:4, :].rearrange("p (b k) -> p b k", b=4), in1=sbias[:, g:g + 1, :].to_broadcast([4, 4, 128]), op=ALU.add), r=[B("ps", 6), cB], w=[B("ssc")])
                            V(lambda e: e.reduce_max(out=ssm_[:, 0, :], in_=ssc[:], axis=AX.X), r=[B("ssc")], w=[B("ssm_")])
                            V(lambda e, sk4=sk4: e.tensor_scalar(out=ssm_[:, 0, :], in0=ssm_[:, 0, :], scalar1=sk4, scalar2=None, op0=ALU.max), r=[B("ssm_"), cB], w=[B("ssm_")])
                            V(lambda e: e.tensor_tensor(out=ssc[:], in0=ssc[:], in1=ssm_[:, 0, :].unsqueeze(2).to_broadcast([4, 4, 128]), op=ALU.subtract), r=[B("ssc"), B("ssm_")], w=[B("ssc")])
                            S(lambda e: e.activation(out=ssc[:], in_=ssc[:], func=AF.Exp), r=[B("ssc")], w=[B("ssc")])
                            V(lambda e: e.reduce_sum(out=ssm_[:, 1, :], in_=ssc[:], axis=AX.X), r=[B("ssc")], w=[B("ssm_")])
                            S(lambda e, sk4=sk4: e.activation(out=ssm_[:, 2, :], in_=ssm_[:, 0, :], func=AF.Exp, scale=-1.0, bias=sk4), r=[B("ssm_"), cB], w=[B("ssm_")])
                            V(lambda e: e.tensor_tensor(out=ssm_[:, 1, :], in0=ssm_[:, 1, :], in1=ssm_[:, 2, :], op=ALU.add), r=[B("ssm_")], w=[B("ssm_")])
                            V(lambda e: e.reciprocal(out=ssm_[:, 1, :], in_=ssm_[:, 1, :]), r=[B("ssm_")], w=[B("ssm_")])
                            V(lambda e: e.tensor_tensor(out=spb[:], in0=ssc[:], in1=ssm_[:, 1, :].unsqueeze(2).to_broadcast([4, 4, 128]), op=ALU.mult), r=[B("ssc"), B("ssm_")], w=[B("spb")])
                            for bb in range(4):
                                PE(lambda e, bb=bb: e.transpose(out=psb[:, 512 + bb * 4:512 + bb * 4 + 4], in_=spb[:, bb, :], identity=identb[0:4, 0:4]), r=[B("spb"), B("identb")], w=[B("psb", 0)])
                            S(lambda e, g=g, b4=b4: e.activation(out=sptb[:, g, b4 * 4:b4 * 4 + 4, :], in_=psb[:, 512:512 + 16].rearrange("p (b r) -> p b r", r=4), func=AF.Copy), r=[B("psb", 0)], w=[B("sptb", g)])
                            yield
                    for b in range(NS):
                        Q(lambda e, b=b, l=l: e.dma_start(out=Vb[:, 0, :], in_=ov_s[l, b, :, :]), r=[B("ov_s", l)], w=[B("Vb", 0)])
                        V(lambda e, b=b: e.tensor_copy(out=Vbp[:, :, 64:128], in_=Vb[:, 0, :].rearrange("p (g d) -> p g d", g=2)), r=[B("Vb", 0)], w=[B("Vbp")])
                        for tile in range(4):
                            g = tile // 2
                            for r2 in range(2):
                                rr4 = (tile % 2) * 2 + r2
                                lo = 64 if r2 == 0 else 0
                                PE(lambda e, b=b, g=g, tile=tile, r2=r2, rr4=rr4, lo=lo: e.matmul(ps[5][:, 256 + b * 4 + tile:256 + b * 4 + tile + 1], lhsT=Vbp[:, g, lo:lo + 128], rhs=sptb[:, g, b, rr4:rr4 + 1], start=(r2 == 0), stop=(r2 == 1)), r=[B("Vbp"), B("sptb", g)], w=[B("ps", 5)])
                    S(lambda e: e.activation(out=hb[:, 0:4, T:NTM].rearrange("p t b -> p b t"), in_=ps[5][:, 256:256 + NS * 4].rearrange("p (b t) -> p b t", t=4), func=AF.Copy), r=[B("ps", 5)], w=[B("hb", par, T)])

                if first:
                    Q(lambda e, l=l: e.dma_start(out=cs[:, :, :, 0:30], in_=cconv_d[l, :, :].rearrange("p (c b k) -> p c b k", c=2, b=NS)), w=[B("cs")])
                    V(lambda e: e.tensor_copy(out=cs[:, :, :, 30], in_=ucv[:, :, 30 + T:30 + NTM]), r=[B("ucv", T)], w=[B("cs")])
                    Q(lambda e, l=l: e.dma_start(out=oconv_s[l, :, :].rearrange("p (c b k) -> p c b k", c=2, b=NS), in_=cs[:, :, :, 1:31]), r=[B("cs")], w=[B("oconv_s", l)])
                    for ct in range(2):
                        cw = pvt[:, l * PVL + OCW + ct * 31: l * PVL + OCW + ct * 31 + 31]
                        V(lambda e, ct=ct, cw=cw: e.tensor_tensor(out=cst[:, ct, :, :], in0=cs[:, ct, :, :], in1=cw.unsqueeze(1).to_broadcast([128, NS, 31]), op=ALU.mult), r=[B("cs"), cB], w=[B("ysm", 0), B("ysm", 1)])
                        V(lambda e, ct=ct: e.reduce_sum(out=acc[:, ct, T:NTM], in_=cst[:, ct, :, :], axis=AX.X), r=[B("ysm", 0), B("ysm", 1)], w=[B("acc", T, ct)])
                        V(lambda e, ct=ct: e.tensor_scalar(out=acc[:, ct, T:NTM], in0=acc[:, ct, T:NTM], scalar1=pvc(l, OCB + ct), scalar2=None, op0=ALU.add), r=[B("acc", T, ct), cB], w=[B("acc", T, ct)])
                for (c0, cn) in ccs:
                    if c0 < T:
                        hr = [B("ucv", c0)] + ([B("ucv", "h")] if c0 == 0 else [B("ucv", c0 - 512)])
                        for kk in range(31):
                            for ct in range(2):
                                wk = pvt[:, l * PVL + OCW + ct * 31 + kk: l * PVL + OCW + ct * 31 + kk + 1]
                                if kk == 0:
                                    V(lambda e, ct=ct, wk=wk: e.tensor_scalar(out=acc[:, ct, c0:c0 + cn], in0=ucv[:, ct, c0:c0 + cn], scalar1=wk, scalar2=pvc(l, OCB + ct), op0=ALU.mult, op1=ALU.add), r=hr + [cB], w=[B("acc", c0, ct)])
                                else:
                                    V(lambda e, ct=ct, wk=wk, kk=kk: e.scalar_tensor_tensor(out=acc[:, ct, c0:c0 + cn], in0=ucv[:, ct, c0 + kk:c0 + kk + cn], scalar=wk, in1=acc[:, ct, c0:c0 + cn], op0=ALU.mult, op1=ALU.add), r=hr + [cB, B("acc", c0, ct)], w=[B("acc", c0, ct)])
                    for ct in range(2):
                        S(lambda e, ct=ct: e.activation(out=ysq[:, ct, 0:cn], in_=acc[:, ct, c0:c0 + cn], func=AF.Square), r=[B("acc", c0, 0), B("acc", c0, 1)], w=[B("ysm", 0), B("ysm", 1)])
                    for ct in range(2):
                        PE(lambda e, ct=ct: e.matmul(ps[6][:, 0:cn], lhsT=ones_f[:], rhs=acc[:, ct, c0:c0 + cn], start=(ct == 0), stop=(ct == 1)), r=[B("acc", c0, 0), B("acc", c0, 1), cB], w=[B("ps", 6)])
                    V(lambda e: e.tensor_scalar(out=mean[:, 0:cn], in0=ps[6][:, 0:cn], scalar1=1.0 / 256, scalar2=None, op0=ALU.mult), r=[B("ps", 6)], w=[B("mean")])
                    for ct in range(2):
                        PE(lambda e, ct=ct: e.matmul(ps[6][:, 0:cn], lhsT=ones_f[:], rhs=ysq[:, ct, 0:cn], start=(ct == 0), stop=(ct == 1)), r=[B("ysm", 0), B("ysm", 1), cB], w=[B("ps", 6)])
                    V(lambda e: e.tensor_tensor(out=var[:, 0:cn], in0=mean[:, 0:cn], in1=mean[:, 0:cn], op=ALU.mult), r=[B("mean")], w=[B("g2")])
                    V(lambda e: e.scalar_tensor_tensor(out=var[:, 0:cn], in0=ps[6][:, 0:cn], scalar=1.0 / 256, in1=var[:, 0:cn], op0=ALU.mult, op1=ALU.subtract), r=[B("ps", 6), B("g2")], w=[B("g2")])
                    S(lambda e: e.activation(out=var[:, 0:cn], in_=var[:, 0:cn], func=AF.Sqrt, bias=EPS), r=[B("g2")], w=[B("g2")])
                    V(lambda e: e.reciprocal(out=var[:, 0:cn], in_=var[:, 0:cn]), r=[B("g2")], w=[B("g2")])
                    for ct in range(2):
                        V(lambda e, ct=ct: e.tensor_tensor(out=g1[:, 0:cn], in0=acc[:, ct, c0:c0 + cn], in1=mean[:, 0:cn], op=ALU.subtract), r=[B("acc", c0, 0), B("acc", c0, 1), B("mean")], w=[B("g1")])
                        V(lambda e, ct=ct: e.tensor_tensor(out=g1[:, 0:cn], in0=g1[:, 0:cn], in1=var[:, 0:cn], op=ALU.mult), r=[B("g1"), B("g2")], w=[B("g1")])
                        V(lambda e, ct=ct: e.tensor_scalar(out=g1[:, 0:cn], in0=g1[:, 0:cn], scalar1=pvc(l, OLG + ct), scalar2=pvc(l, OLB + ct), op0=ALU.mult, op1=ALU.add), r=[B("g1"), cB], w=[B("g1")])
                        S(lambda e, ct=ct: e.activation(out=hb[:, 4 + ct, c0:c0 + cn], in_=g1[:, 0:cn], func=AF.Silu), r=[B("g1")], w=[B("hb", par, c0)])
                        yield

                def ssm_out(c0, cn, hsrc_re, hsrc_im, hoff, hbufs):
                    for ct in range(2):
                        for j4 in range(4):
                            s8 = ct * 4 + j4
                            PE(lambda e, ct=ct, s8=s8, j4=j4: e.matmul(ps[6][:, 0:cn], lhsT=WC[:, s8 * 2, :], rhs=hsrc_re(s8), start=(j4 == 0), stop=False), r=hbufs + [B("WC")], w=[B("ps", 6)])
                            PE(lambda e, ct=ct, s8=s8: e.matmul(ps[6][:, 0:cn], lhsT=WC[:, s8 * 2 + 1, :], rhs=hsrc_im(s8), start=False, stop=False), r=hbufs + [B("WC")], w=[B("ps", 6)])
                        PE(lambda e, ct=ct: e.matmul(ps[6][:, 0:cn], lhsT=Dd[:, ct, :], rhs=us[:, ct, c0:c0 + cn], start=False, stop=True), r=[B("us", (c0 // 512) * 512 if c0 < T else T), B("Dd")], w=[B("ps", 6)])
                        V(lambda e, ct=ct: e.tensor_copy(out=ysm[:, ct, 0:cn], in_=ps[6][:, 0:cn]), r=[B("ps", 6)], w=[B("ysm", ct)])
                        V(lambda e, ct=ct: e.tensor_tensor(out=g1[:, 0:cn], in0=ysm[:, ct, 0:cn], in1=ysm[:, ct, 0:cn], op=ALU.mult), r=[B("ysm", ct)], w=[B("g1")])
                        V(lambda e, ct=ct: e.tensor_scalar(out=g1[:, 0:cn], in0=g1[:, 0:cn], scalar1=0.044715, scalar2=1.0, op0=ALU.mult, op1=ALU.add), r=[B("g1")], w=[B("g1")])
                        V(lambda e, ct=ct: e.tensor_tensor(out=g1[:, 0:cn], in0=g1[:, 0:cn], in1=ysm[:, ct, 0:cn], op=ALU.mult), r=[B("g1"), B("ysm", ct)], w=[B("g1")])
                        S(lambda e, ct=ct: e.activation(out=g2[:, 0:cn], in_=g1[:, 0:cn], func=AF.Sigmoid, scale=2.0 * math.sqrt(2.0 / math.pi)), r=[B("g1")], w=[B("g2")])
                        V(lambda e, ct=ct: e.tensor_tensor(out=ysm[:, ct, 0:cn], in0=ysm[:, ct, 0:cn], in1=g2[:, 0:cn], op=ALU.mult), r=[B("g2"), B("ysm", ct)], w=[B("ysm", ct)])
                        V(lambda e, ct=ct: e.tensor_copy(out=yb[:, ct, 0:cn], in_=ysm[:, ct, 0:cn]), r=[B("ysm", ct)], w=[B("yb", ct)])
                    for co in range(2):
                        for ct in range(2):
                            PE(lambda e, ct=ct, co=co: e.matmul(ps[6][:, 0:cn], lhsT=gluw[:, ct, co * 128:(co + 1) * 128], rhs=yb[:, ct, 0:cn], start=(ct == 0), stop=(ct == 1)), r=[B("yb", 0), B("yb", 1), B("gluw")], w=[B("ps", 6)])
                        S(lambda e, co=co: e.activation(out=g2[:, 0:cn], in_=ps[6][:, 0:cn], func=AF.Sigmoid, bias=pvc(l, OGB + co)), r=[B("ps", 6), cB], w=[B("g2")])
                        V(lambda e, co=co: e.tensor_tensor(out=hb[:, 6 + co, c0:c0 + cn], in0=ysm[:, co, 0:cn], in1=g2[:, 0:cn], op=ALU.mult), r=[B("g2"), B("ysm", co)], w=[B("hb", par, (c0 // 512) * 512 if c0 < T else T)])

                for sc_i in range(T // J):
                    c0 = sc_i * J
                    ucc = (c0 // 512) * 512
                    for s8 in range(8):
                        ct, a = s8 // 4, s8 % 2
                        for ci, dst in ((0, 0), (1, J)):
                            PE(lambda e, ci=ci, dst=dst, s8=s8, ct=ct: e.matmul(ps[4][:, dst:dst + J], lhsT=WB[:, s8 * 2 + ci, :], rhs=us[:, ct, c0:c0 + J], start=True, stop=True), r=[B("us", ucc), B("WB")], w=[B("ps", 4)])
                        ra = s8 % 2
                        Q(lambda e, s8=s8, ra=ra: e.dma_start(out=rot2[:, ra, :, :].rearrange("p c j -> p (c j)"), in_=rot_d[l, :, s8 * 2 * J:(s8 + 1) * 2 * J]), r=[B("rot_d", l)], w=[B("rot2", ra)])
                        cosT, sinT = rot2[:, ra, 0, :], rot2[:, ra, 1, :]
                        pre, pim = ps[4][:, 0:J], ps[4][:, J:2 * J]
                        V(lambda e, cosT=cosT, pre=pre: e.tensor_tensor(out=bre[:], in0=pre, in1=cosT, op=ALU.mult), r=[B("ps", 4), B("rot2", ra)], w=[B("bre")])
                        V(lambda e, sinT=sinT, pim=pim: e.tensor_tensor(out=t1[:], in0=pim, in1=sinT, op=ALU.mult), r=[B("ps", 4), B("rot2", ra)], w=[B("t1")])
                        V(lambda e, cosT=cosT, pim=pim: e.tensor_tensor(out=bim[:], in0=pim, in1=cosT, op=ALU.mult), r=[B("ps", 4), B("rot2", ra)], w=[B("bim")])
                        V(lambda e, sinT=sinT, pre=pre: e.tensor_tensor(out=t2[:], in0=pre, in1=sinT, op=ALU.mult), r=[B("ps", 4), B("rot2", ra)], w=[B("t2")])
                        V(lambda e: e.tensor_tensor(out=bre[:], in0=bre[:], in1=t1[:], op=ALU.add), r=[B("t1"), B("bre")], w=[B("bre")])
                        V(lambda e: e.tensor_tensor(out=bim[:], in0=bim[:], in1=t2[:], op=ALU.subtract), r=[B("t2"), B("bim")], w=[B("bim")])
                        rho_b = lam[:, l, 0, s8:s8 + 1].to_broadcast([128, J])
                        V(lambda e, rho_b=rho_b, s8=s8: e.tensor_tensor_scan(out=wre[:], data0=rho_b, data1=bre[:], initial=hcar[:, l, s8, 0:1], op0=ALU.mult, op1=ALU.add), r=[B("bre"), B("lam", l), B("hcar", l)], w=[B("wre")])
                        V(lambda e, rho_b=rho_b, s8=s8: e.tensor_tensor_scan(out=wim[:], data0=rho_b, data1=bim[:], initial=hcar[:, l, s8, 1:2], op0=ALU.mult, op1=ALU.add), r=[B("bim"), B("lam", l), B("hcar", l)], w=[B("wim")])
                        V(lambda e, cosT=cosT: e.tensor_tensor(out=t1[:], in0=wre[:], in1=cosT, op=ALU.mult), r=[B("wre"), B("rot2", ra)], w=[B("t1")])
                        V(lambda e, sinT=sinT: e.tensor_tensor(out=t2[:], in0=wim[:], in1=sinT, op=ALU.mult), r=[B("wim"), B("rot2", ra)], w=[B("t2")])
                        V(lambda e, sinT=sinT: e.tensor_tensor(out=bre[:], in0=wre[:], in1=sinT, op=ALU.mult), r=[B("wre"), B("rot2", ra)], w=[B("bre")])
                        V(lambda e, cosT=cosT: e.tensor_tensor(out=bim[:], in0=wim[:], in1=cosT, op=ALU.mult), r=[B("wim"), B("rot2", ra)], w=[B("bim")])
                        V(lambda e, s8=s8: e.tensor_tensor(out=hbf[:, s8 * J:(s8 + 1) * J], in0=t1[:], in1=t2[:], op=ALU.subtract), r=[B("t1"), B("t2")], w=[B("hbf")])
                        V(lambda e, s8=s8: e.tensor_tensor(out=hbf[:, 8 * J + s8 * J:8 * J + (s8 + 1) * J], in0=bre[:], in1=bim[:], op=ALU.add), r=[B("bre"), B("bim")], w=[B("hbf")])
                        cl, sl = rot2[:, ra, 0, J - 1:J], rot2[:, ra, 1, J - 1:J]
                        V(lambda e, sl=sl: e.tensor_tensor(out=sm[:, 0, 6:7], in0=wim[:, J - 1:J], in1=sl, op=ALU.mult), r=[B("wim"), B("rot2", ra)], w=[B("sm", 0)])
                        V(lambda e, s8=s8, cl=cl: e.scalar_tensor_tensor(out=hcar[:, l, s8, 0:1], in0=wre[:, J - 1:J], scalar=cl, in1=sm[:, 0, 6:7], op0=ALU.mult, op1=ALU.subtract), r=[B("wre"), B("rot2", ra), B("sm", 0)], w=[B("hcar", l)])
                        V(lambda e, sl=sl: e.tensor_tensor(out=sm[:, 0, 7:8], in0=wre[:, J - 1:J], in1=sl, op=ALU.mult), r=[B("wre"), B("rot2", ra)], w=[B("sm", 0)])
                        V(lambda e, s8=s8, cl=cl: e.scalar_tensor_tensor(out=hcar[:, l, s8, 1:2], in0=wim[:, J - 1:J], scalar=cl, in1=sm[:, 0, 7:8], op0=ALU.mult, op1=ALU.add), r=[B("wim"), B("rot2", ra), B("sm", 0)], w=[B("hcar", l)])
                        yield
                    hbre = hbf
                    ssm_out(c0, J, lambda s8: hbre[:, s8 * J:(s8 + 1) * J], lambda s8: hbre[:, 8 * J + s8 * J:8 * J + (s8 + 1) * J], 0, [B("hbf")])
                    yield
                if last:
                    Q(lambda e, l=l: e.dma_start(out=ossm_p[l, :, :].rearrange("p (s c) -> p s c", c=2), in_=hcar[:, l, :, :]), r=[B("hcar", l)], w=[B("ossm_p", l)])
                if first:
                    Q(lambda e, l=l: e.dma_start(out=h0[:, 0, :, :], in_=sre_d[l, :, :].rearrange("p (s b) -> p s b", s=8)), w=[B("h0")])
                    Q(lambda e, l=l: e.dma_start(out=h0[:, 1, :, :], in_=sim_d[l, :, :].rearrange("p (s b) -> p s b", s=8)), w=[B("h0")])
                    for s8 in range(8):
                        ct = s8 // 4
                        for ci in range(2):
                            PE(lambda e, ci=ci, s8=s8, ct=ct: e.matmul(ps[4][:, (s8 * 2 + ci) * NS:(s8 * 2 + ci + 1) * NS], lhsT=WB[:, s8 * 2 + ci, :], rhs=us[:, ct, T:NTM], start=True, stop=True), r=[B("us", T), B("WB")], w=[B("ps", 4)])
                    V(lambda e: e.tensor_copy(out=h1[:].rearrange("p c s b -> p s c b"), in_=ps[4][:, 0:16 * NS].rearrange("p (s c b) -> p s c b", s=8, c=2)), r=[B("ps", 4)], w=[B("h1")])
                    for s8 in range(8):
                        lre, lim = pl[:, 5, s8:s8 + 1], pl[:, 6, s8:s8 + 1]
                        V(lambda e, s8=s8: e.tensor_tensor(out=sm[:, 0, 6:7], in0=lam[:, l, 0, s8:s8 + 1], in1=lam[:, l, 1, s8:s8 + 1], op=ALU.mult), r=[B("lam", l)], w=[B("sm", 0)])
                        V(lambda e, s8=s8: e.tensor_tensor(out=sm[:, 0, 7:8], in0=lam[:, l, 0, s8:s8 + 1], in1=lam[:, l, 2, s8:s8 + 1], op=ALU.mult), r=[B("lam", l)], w=[B("sm", 0)])
                        V(lambda e, s8=s8: e.tensor_scalar(out=sm[:, 1, 7:8], in0=sm[:, 0, 7:8], scalar1=-1.0, scalar2=None, op0=ALU.mult), r=[B("sm", 0)], w=[B("sm", 1)])
                        V(lambda e, s8=s8: e.scalar_tensor_tensor(out=h1[:, 0, s8, :], in0=h0[:, 0, s8, :], scalar=sm[:, 0, 6:7], in1=h1[:, 0, s8, :], op0=ALU.mult, op1=ALU.add), r=[B("h0"), B("sm", 0), B("h1")], w=[B("h1")])
                        V(lambda e, s8=s8: e.scalar_tensor_tensor(out=h1[:, 0, s8, :], in0=h0[:, 1, s8, :], scalar=sm[:, 1, 7:8], in1=h1[:, 0, s8, :], op0=ALU.mult, op1=ALU.add), r=[B("h0"), B("sm", 1), B("h1")], w=[B("h1")])
                        V(lambda e, s8=s8: e.scalar_tensor_tensor(out=h1[:, 1, s8, :], in0=h0[:, 1, s8, :], scalar=sm[:, 0, 6:7], in1=h1[:, 1, s8, :], op0=ALU.mult, op1=ALU.add), r=[B("h0"), B("sm", 0), B("h1")], w=[B("h1")])
                        V(lambda e, s8=s8: e.scalar_tensor_tensor(out=h1[:, 1, s8, :], in0=h0[:, 0, s8, :], scalar=sm[:, 0, 7:8], in1=h1[:, 1, s8, :], op0=ALU.mult, op1=ALU.add), r=[B("h0"), B("sm", 0), B("h1")], w=[B("h1")])
                    Q(lambda e, l=l: e.dma_start(out=ossm_s[l, :, :].rearrange("p (c s b) -> p c s b", c=2, s=8), in_=h1[:]), r=[B("h1")], w=[B("ossm_s", l)])
                    V(lambda e: e.tensor_copy(out=h1b[:], in_=h1[:]), r=[B("h1")], w=[B("h1b")])
                    ssm_out(T, NS, lambda s8: h1b[:, 0, s8, :], lambda s8: h1b[:, 1, s8, :], 0, [B("h1b")])
                    yield

                s_out = [wload("M", w_out[l, :, i * 512:(i + 1) * 512].rearrange("(k p) n -> p k n", p=128), "p (k n) -> p k n", wid=l * 23 + 3 + i, ch=ch, k=8) for i in range(2)]
                for (c0, cn) in ccs:
                    for dt_ in range(8):
                        s = s_out[dt_ // 4]
                        off = (dt_ % 4) * 128
                        pbk = bank()
                        for k in range(8):
                            PE(lambda e, k=k, s=s, off=off, pbk=pbk: e.matmul(ps[pbk][:, 0:cn], lhsT=ring[:, s, k * 512 + off:k * 512 + off + 128], rhs=hb[:, k, c0:c0 + cn], start=(k == 0), stop=(k == 7)), r=[B(SL, s), B("hb", par, c0)], w=[B("ps", pbk)])
                        V(lambda e, dt_=dt_, pbk=pbk: e.tensor_tensor(out=x[:, dt_, c0:c0 + cn], in0=ps[pbk][:, 0:cn], in1=x[:, dt_, c0:c0 + cn], op=ALU.add), r=[B("ps", pbk), B("x", par, c0)], w=[B("x", par, c0)])
                        yield

        def gen_ffn(ch, l):
            par = ch % 2; x = X[par]; hb = HB[par]
            first, last = (ch == 0), (ch == nch - 1)
            ccs = [(0, 512)] + ([(T, NS)] if first else [])
            sq, rstd = sq2, rstd2
            ring, SL = ringF, "slotF"
            bank = fbank
            if True:
                rmsnorm(x, hb, par, l * PVL + OG2, ccs, sq2, rstd2, fbank(), "f")
                for fg in range(6):
                    nf = 4 if fg < 5 else 2
                    sg_ = wload("F", w_g[l, :, fg * 512:fg * 512 + nf * 128].rearrange("(k p) n -> p k n", p=128), "p (k n) -> p k n", wid=l * 23 + 5 + fg * 3, ch=ch, k=8)
                    su_ = wload("F", w_u[l, :, fg * 512:fg * 512 + nf * 128].rearrange("(k p) n -> p k n", p=128), "p (k n) -> p k n", wid=l * 23 + 6 + fg * 3, ch=ch, k=8)
                    sd_ = wload("F", w_d[l, fg * 512:fg * 512 + nf * 128, :].rearrange("(f p) n -> p f n", p=128), "p (f n) -> p f n", wid=l * 23 + 7 + fg * 3, ch=ch, f=nf)
                    W_ = nf * 128
                    for (c0, cn) in ccs:
                        for f in range(nf):
                            pg, pu = bank(), bank()
                            for (s_, pb_) in ((sg_, pg), (su_, pu)):
                                for k in range(8):
                                    PE(lambda e, k=k, s_=s_, pb_=pb_, f=f: e.matmul(ps[pb_][:, 0:cn], lhsT=ring[:, s_, k * W_ + f * 128:k * W_ + f * 128 + 128], rhs=hb[:, k, c0:c0 + cn], start=(k == 0), stop=(k == 7)), r=[B(SL, s_), B("hb", par, c0)], w=[B("ps", pb_)])
                            S(lambda e, pg=pg, f=f: e.activation(out=sgb[:, f % 2, 0:cn], in_=ps[pg][:, 0:cn], func=AF.Silu), r=[B("ps", pg)], w=[B("sgb", f % 2)])
                            V(lambda e, pu=pu, f=f: e.tensor_tensor(out=hid[:, f, c0:c0 + cn], in0=ps[pu][:, 0:cn], in1=sgb[:, f % 2, 0:cn], op=ALU.mult), r=[B("ps", pu), B("sgb", f % 2)], w=[B("hid", c0)])
                            yield
                        for dt_ in range(8):
                            pbk = bank()
                            for f in range(nf):
                                PE(lambda e, f=f, dt_=dt_, pbk=pbk: e.matmul(ps[pbk][:, 0:cn], lhsT=ring[:, sd_, f * 1024 + dt_ * 128:f * 1024 + dt_ * 128 + 128], rhs=hid[:, f, c0:c0 + cn], start=(f == 0), stop=(f == nf - 1)), r=[B(SL, sd_), B("hid", c0)], w=[B("ps", pbk)])
                            V(lambda e, dt_=dt_, pbk=pbk: e.tensor_tensor(out=x[:, dt_, c0:c0 + cn], in0=ps[pbk][:, 0:cn], in1=x[:, dt_, c0:c0 + cn], op=ALU.add), r=[B("ps", pbk), B("x", par, c0)], w=[B("x", par, c0)])
                            yield
            if l == L - 1:
                for (c0, cn) in ccs:
                    for k in range(8):
                        S(lambda e, k=k: e.activation(out=sq[:, 0:cn], in_=x[:, k, c0:c0 + cn], func=AF.Square), r=[B("x", par, c0)], w=[B("sq", "f")])
                        PE(lambda e, k=k: e.matmul(ps[0][:, 0:cn], lhsT=ones_b[:], rhs=sq[:, 0:cn], start=(k == 0), stop=(k == 7)), r=[B("sq", "f"), cB], w=[B("ps", 0)])
                    S(lambda e: e.activation(out=rstd[:, 0:cn], in_=ps[0][:, 0:cn], func=AF.Sqrt, scale=1.0 / D, bias=EPS), r=[B("ps", 0)], w=[B("rstd", "f")])
                    V(lambda e: e.reciprocal(out=rstd[:, 0:cn], in_=rstd[:, 0:cn]), r=[B("rstd", "f")], w=[B("rstd", "f")])
                    for k in range(8):
                        gk = pvt[:, 4 * PVL + k: 4 * PVL + k + 1]
                        V(lambda e, k=k, gk=gk: e.scalar_tensor_tensor(out=x[:, k, c0:c0 + cn], in0=x[:, k, c0:c0 + cn], scalar=gk, in1=rstd[:, 0:cn], op0=ALU.mult, op1=ALU.mult), r=[B("x", par, c0), B("rstd", "f"), cB], w=[B("x", par, c0)])
                        if c0 < T:
                            Q(lambda e, k=k: e.dma_start(out=yT[k * 128:(k + 1) * 128, ch * T + c0:ch * T + c0 + cn], in_=x[:, k, c0:c0 + cn]), r=[B("x", par, c0)], w=[B("yT")])
                        else:
                            Q(lambda e, k=k: e.dma_start(out=ysT[k * 128:(k + 1) * 128, :], in_=x[:, k, c0:c0 + cn]), r=[B("x", par, c0)], w=[B("ysT")])
            yield

        def count_ops(gen):
            P.dry = True
            n0 = P.dryn
            for _ in gen:
                pass
            P.dry = False
            return P.dryn - n0

        def run2(ga, na, gb, nb):
            ia = ib = 0
            alive_a, alive_b = ga is not None, gb is not None
            while alive_a or alive_b:
                pick_a = alive_a and (not alive_b or ia * nb <= ib * na)
                n0 = P.nops
                if pick_a:
                    try:
                        next(ga)
                    except StopIteration:
                        alive_a = False
                    ia += P.nops - n0
                else:
                    try:
                        next(gb)
                    except StopIteration:
                        alive_b = False
                    ib += P.nops - n0

        streams = []
        for ch in range(nch):
            ph = []
            for l in range(L):
                ph.append(("m", ch, l))
                ph.append(("f", ch, l))
            streams.append(ph)
        steps = []
        t = 0
        start = {}
        for ch in range(nch):
            start[ch] = (ch // 2) * 2 * L + (ch % 2)
        nsteps = max(start[c] + 2 * L for c in range(nch))
        for t in range(nsteps):
            cur = []
            for ch in range(nch):
                p = t - start[ch]
                if 0 <= p < 2 * L:
                    cur.append(streams[ch][p])
            steps.append(cur)
        for cur in steps:
            gens = []
            for (kind, ch, l) in cur:
                mk = (lambda: gen_mix(ch, l)) if kind == "m" else (lambda: gen_ffn(ch, l))
                n = count_ops(mk())
                gens.append((mk(), max(n, 1)))
            if len(gens) == 1:
                run2(gens[0][0], gens[0][1], None, 1)
            else:
                run2(gens[0][0], gens[0][1], gens[1][0], gens[1][1])
        print('NOPS', P.nops, {k: len(v) for k, v in P.ops.items()}, flush=True)
        P.emit()
    return nc


def _alibi_tables():
    slopes = 2.0 ** (-8.0 * np.arange(1, 9, dtype=np.float32) / 8)
    qi = np.arange(128)[:, None]
    kj = np.arange(256)[None, :]
    dist = qi - kj + 128
    valid = (dist >= 0) & (dist < 128)
    ab = np.where(valid, -dist.astype(np.float32), -1.0e7).astype(np.float32)
    dj = (127 - np.arange(128)).astype(np.float32)
    sbias = (-slopes.reshape(2, 4, 1) * dj[None, None, :]).astype(np.float32)
    sbias = np.ascontiguousarray(sbias.transpose(1, 0, 2)).reshape(4, 2 * 128)
    return ab, sbias


def _fm(v):
    return np.ascontiguousarray(np.asarray(v, np.float32).reshape(-1, 128).T)


_NC_CACHE = {}


def kernel(nch=16, depth=4, **inp):
    f = lambda k: np.asarray(inp[k], np.float32)
    L = depth
    TT = nch * T
    key = (nch, depth)
    if key not in _NC_CACHE:
        _NC_CACHE[key] = build(nch, depth)
    nc = _NC_CACHE[key]
    ab, sbias = _alibi_tables()
    pv = np.zeros((128, NPV), np.float32)
    for l in range(L):
        o = l * PVL
        pv[:, o + 0:o + 8] = _fm(f("norm_mix_g")[l])
        pv[:, o + 8:o + 16] = _fm(f("norm_ffn_g")[l])
        cw = f("conv_dw_w")[l]
        for ct in range(2):
            pv[:, o + 16 + ct * 31:o + 16 + (ct + 1) * 31] = cw[:, ct * 128:(ct + 1) * 128].T
        pv[:, o + 78:o + 80] = _fm(f("conv_dw_b")[l])
        pv[:, o + 80:o + 82] = _fm(f("conv_ln_g")[l])
        pv[:, o + 82:o + 84] = _fm(f("conv_ln_b")[l])
        pv[:, o + 84:o + 86] = _fm(f("ssm_d")[l])
        pv[:, o + 86:o + 88] = _fm(f("ssm_glu_b")[l])
        pv[:, o + 88:o + 96] = _fm(f("ssm_a_re")[l].reshape(-1))
        pv[:, o + 96:o + 104] = _fm(f("ssm_a_im")[l].reshape(-1))
        pv[:, o + 104:o + 112] = _fm(np.repeat(f("ssm_log_dt")[l], 64))
        pv[:, o + 112:o + 120] = f("attn_sinks")[l][None, :]
    pv[:, 4 * PVL:4 * PVL + 8] = _fm(f("norm_final_g"))
    Bn = np.zeros((L, 128, 8, 2, 128), np.float32)
    Cn = np.zeros((L, 128, 8, 2, 128), np.float32)
    for l in range(L):
        for ci, (bk, ck_) in enumerate((("ssm_b_re", "ssm_c_re"), ("ssm_b_im", "ssm_c_im"))):
            bb = f(bk)[l]
            cc = f(ck_)[l]
            for g in range(16):
                st_, p0 = g // 2, (g % 2) * 64
                col = (g % 8) * 16
                Bn[l, p0:p0 + 64, st_, ci, col:col + 16] = bb[g]
                Cn[l, p0:p0 + 64, st_, ci, col:col + 16] = cc[g].T
    Bn = Bn.reshape(L, 128, -1)
    Cn = Cn.reshape(L, 128, -1)
    ident = np.eye(128, dtype=np.float32)
    jjv = np.broadcast_to(np.arange(1, J + 1, dtype=np.float32)[None, :], (128, J)).copy()
    xp = f("x_prompt")
    xs = f("x_sample")[:, 0, :]
    shared = {
        "w_in": f("w_in")[:L], "w_out": f("w_out")[:L], "w_g": f("w_ff_gate")[:L], "w_u": f("w_ff_up")[:L],
        "w_d": f("w_ff_down")[:L], "glu_w": f("ssm_glu_w")[:L], "pv": pv, "Bn": Bn, "Cn": Cn, "ident": ident,
        "jj": jjv, "abias": ab, "sbias": sbias,
        "sinkc": np.ascontiguousarray(f("attn_sinks")[:L].reshape(L, 2, 4).transpose(2, 0, 1).reshape(4, L * 2)),
    }
    in_maps = []
    for c in range(8):
        sq_, b0 = c // 4, c * NS
        m = dict(shared)
        m["xT"] = np.ascontiguousarray(xp[sq_, :TT, :].T)
        m["xsT"] = np.ascontiguousarray(xs[b0:b0 + NS].T)
        m["ck"] = np.ascontiguousarray(f("cache_swa_k")[:L, b0:b0 + NS].reshape(L, NS, 128, 128))
        m["cv"] = np.ascontiguousarray(f("cache_swa_v")[:L, b0:b0 + NS].reshape(L, NS, 128, 128))
        cc_ = f("cache_conv")[:L, b0:b0 + NS]
        m["cconv"] = np.ascontiguousarray(cc_.reshape(L, NS, 30, 2, 128).transpose(0, 4, 3, 1, 2)).reshape(L, 128, -1)
        for nm, kk in (("sre", "state_ssm_re"), ("sim", "state_ssm_im")):
            s_ = f(kk)[:L, b0:b0 + NS].reshape(L, NS, 8, 128)
            m[nm] = np.ascontiguousarray(s_.transpose(0, 3, 2, 1)).reshape(L, 128, -1)
        in_maps.append(m)
    res = run_bass_kernel_spmd(nc, in_maps, core_ids=list(range(8))).results
    y_p = np.stack([res[0]["yT"].T, res[4]["yT"].T]).astype(np.float32)
    y_s = np.concatenate([res[c]["ysT"].T for c in range(8)])[:, None, :].astype(np.float32)
    pc = (res[0], res[4])
    k_p = np.stack([np.stack([r["ok_p"][l].reshape(128, 2, 64) for r in pc]) for l in range(L)])
    v_p = np.stack([np.stack([r["ov_p"][l].reshape(128, 2, 64) for r in pc]) for l in range(L)])
    conv_p = np.stack([np.stack([r["oconv_p"][l].reshape(128, 2, 30).transpose(2, 1, 0).reshape(30, 256) for r in pc]) for l in range(L)])
    ssm_p = [np.stack([np.stack([r["ossm_p"][l].reshape(128, 8, 2)[:, :, ci].T.reshape(16, 64) for r in pc]) for l in range(L)]) for ci in range(2)]
    k_s = np.concatenate([res[c]["ok_s"].reshape(L, NS, 128, 2, 64) for c in range(8)], 1)
    v_s = np.concatenate([res[c]["ov_s"].reshape(L, NS, 128, 2, 64) for c in range(8)], 1)
    conv_s = np.concatenate([res[c]["oconv_s"].reshape(L, 128, 2, NS, 30).transpose(0, 3, 4, 2, 1).reshape(L, NS, 30, 256) for c in range(8)], 1)
    ssm_s = [np.concatenate([res[c]["ossm_s"].reshape(L, 128, 2, 8, NS)[:, :, ci].transpose(0, 3, 2, 1).reshape(L, NS, 16, 64) for c in range(8)], 1) for ci in range(2)]
    outs = (y_p, y_s, k_p, v_p, conv_p, ssm_p[0], ssm_p[1], k_s, v_s, conv_s, ssm_s[0], ssm_s[1])
    return tuple(np.ascontiguousarray(o, dtype=np.float32) for o in outs)
```

```python
import math
import os
import numpy as np
from contextlib import ExitStack
import concourse.bass as bass
import concourse.mybir as mybir
from concourse.bass_utils import run_bass_kernel_spmd

F32 = mybir.dt.float32
BF16 = mybir.dt.bfloat16
I32 = mybir.dt.int32
AF = mybir.ActivationFunctionType
ALU = mybir.AluOpType
AX = mybir.AxisListType

D = 1024
SEQ = 8192
T = 512
NS = 16
NTM = T + NS
DFF = 2816
NF = 22
J = 256
NSLOT = 4
PVL = 120
NPV = 4 * PVL + 8
EPS = 1e-6
TWO_PI = 2.0 * math.pi


import types


def _freeze(fn):
    if fn.__closure__ is None:
        return fn
    cells = []
    for c in fn.__closure__:
        try:
            cells.append(types.CellType(c.cell_contents))
        except ValueError:
            cells.append(c)
    return types.FunctionType(fn.__code__, fn.__globals__, fn.__name__, fn.__defaults__, tuple(cells))


class _Stub:
    def __init__(self):
        self.closed = True

    def matmul(self, *a, **kw):
        self.closed = bool(kw.get("stop", True))
        return self

    def transpose(self, *a, **kw):
        self.closed = True
        return self

    def then_inc(self, *a, **kw):
        return self


class Buf:
    __slots__ = ("name", "lw", "rd")

    def __init__(self, name):
        self.name = name
        self.lw = None
        self.rd = {}


class Prog:
    ENG = ["tensor", "vector", "scalar", "gpsimd", "sync"]

    def __init__(self, nc, same_eng_sync=("vector", "scalar", "gpsimd")):
        self.nc = nc
        self.ops = {e: [] for e in self.ENG}
        self.cnt = {}
        self.known = {e: {} for e in self.ENG}
        self.same = set(same_eng_sync)
        self.bufs = {}
        self.dry = False
        self.dq = {}
        self.dryn = 0
        self.nops = 0

    def B(self, *key):
        b = self.bufs.get(key)
        if b is None:
            b = self.bufs[key] = Buf(key)
        return b

    def op(self, eng, fn, r=(), w=(), dma=None, inc=None):
        if self.dry:
            self.dryn += 1
            return None
        fn = _freeze(fn)
        if getattr(self, "stopped", False):
            return None
        self.nops += 1
        if eng == "tensor" and "KLIMIT" in os.environ:
            st_ = _Stub()
            try:
                fn(st_)
            except Exception:
                pass
            self.open_grp = not st_.closed
        if self.nops >= int(os.environ.get("KLIMIT", "100000000")) and not getattr(self, "open_grp", False):
            self.stopped = True
        ex = [b for b in r if b.name[0] in ("ps", "psb")]
        if ex:
            r = [b for b in r if b.name[0] not in ("ps", "psb")]
            w = list(w) + ex
        deps = {}
        for b in list(r) + list(w):
            if b.lw is not None and deps.get(b.lw[0], 0) < b.lw[1]:
                deps[b.lw[0]] = b.lw[1]
        for b in w:
            for s, v in b.rd.items():
                if deps.get(s, 0) < v:
                    deps[s] = v
        pre = None
        if dma is None:
            sem, step = eng, 1
        else:
            npool = 32 if eng == "sync" else 8
            i = self.dq.get(eng, 0)
            self.dq[eng] = i + 1
            sem, step = "d%s%d" % (eng[0], i % npool), 16
            if i >= npool:
                pre = (sem, 16 * (i // npool))
        waits = []
        if pre is not None and deps.get(pre[0], 0) < pre[1]:
            deps[pre[0]] = pre[1]
        for s, v in deps.items():
            if s == eng and eng not in self.same:
                continue
            if self.known[eng].get(s, 0) >= v:
                continue
            self.known[eng][s] = v
            waits.append((s, v))
        self.cnt[sem] = self.cnt.get(sem, 0) + step
        tok = (sem, self.cnt[sem])
        self.ops[eng].append((fn, waits, sem, step))
        for b in r:
            if b.rd.get(sem, 0) < tok[1]:
                b.rd[sem] = tok[1]
        for b in w:
            b.lw = tok
            b.rd = {}
        return tok

    def emit(self):
        nc = self.nc
        with ExitStack() as st:
            sems = {s: st.enter_context(nc.semaphore(s)) for s in self.cnt}
            block = st.enter_context(nc.Block())
            final = dict(self.cnt)

            def mk(engname):
                def body(e):
                    for fn, waits, sem, step in self.ops[engname]:
                        for s, v in waits:
                            e.wait_ge(sems[s], v)
                        fn(e).then_inc(sems[sem], step)
                    if engname == "sync":
                        for s, v in final.items():
                            e.wait_ge(sems[s], v)
                return body

            for engname in self.ENG:
                if self.ops[engname] or engname == "sync":
                    getattr(block, engname)(mk(engname))


def build(nch=16, depth=4):
    nc = bass.Bass("TRN2", target_bir_lowering=False)
    L = depth
    TT = nch * T

    def din(name, shape, dt=F32):
        return nc.dram_tensor(name, list(shape), dt, kind="ExternalInput").ap()

    def dout(name, shape, dt=F32):
        return nc.dram_tensor(name, list(shape), dt, kind="ExternalOutput").ap()

    xT = din("xT", [D, TT]); xsT = din("xsT", [D, NS])
    w_in = din("w_in", [L, D, 1536]); w_out = din("w_out", [L, D, D])
    w_g = din("w_g", [L, D, DFF]); w_u = din("w_u", [L, D, DFF]); w_d = din("w_d", [L, DFF, D])
    glu_w = din("glu_w", [L, 256, 256])
    pv_d = din("pv", [128, NPV])
    Bn_d = din("Bn", [L, 128, 8 * 2 * 128]); Cn_d = din("Cn", [L, 128, 8 * 2 * 128])
    ident_d = din("ident", [128, 128]); jj_d = din("jj", [128, J])
    abias_d = din("abias", [128, 256]); sbias_d = din("sbias", [4, 2 * 128]); sinkc_d = din("sinkc", [4, L * 2])
    ck_d = din("ck", [L, NS, 128, 128]); cv_d = din("cv", [L, NS, 128, 128])
    cconv_d = din("cconv", [L, 128, 2 * NS * 30])
    sre_d = din("sre", [L, 128, 8 * NS]); sim_d = din("sim", [L, 128, 8 * NS])

    yT = dout("yT", [D, TT]); ysT = dout("ysT", [D, NS])
    ok_p = dout("ok_p", [L, 128, 128]); ov_p = dout("ov_p", [L, 128, 128])
    oconv_p = dout("oconv_p", [L, 128, 2 * 30]); ossm_p = dout("ossm_p", [L, 128, 16])
    ok_s = dout("ok_s", [L, NS, 128, 128]); ov_s = dout("ov_s", [L, NS, 128, 128])
    oconv_s = dout("oconv_s", [L, 128, 2 * NS * 30]); ossm_s = dout("ossm_s", [L, 128, 2 * 8 * NS])
    wscr = nc.dram_tensor("w_scr", [L * 23, 128, 4096], BF16).ap()
    rot_d = nc.dram_tensor("rot_scr", [L, 128, 8 * 2 * J], F32).ap()
    wb_d = nc.dram_tensor("wb_scr", [L, 128, 16 * 128], BF16).ap()
    wc_d = nc.dram_tensor("wc_scr", [L, 128, 16 * 128], BF16).ap()

    P = Prog(nc)
    B = P.B
    with ExitStack() as st:
        def sb(name, shape, dt=F32):
            return st.enter_context(nc.sbuf_tensor("sb_" + name, list(shape), dt))

        def pst(name, shape, dt=F32):
            return st.enter_context(nc.psum_tensor(name, list(shape), dt))

        x = sb("x", [128, 8, NTM]); hb = sb("hb", [128, 8, NTM], BF16)
        x1 = sb("x1", [128, 8, T]); hb1 = sb("hb1", [128, 8, T], BF16)
        X = [x, x1]; HB = [hb, hb1]
        qT = sb("qT", [64, 8, NTM], BF16); kT = sb("kT", [64, 2, 128 + NTM], BF16)
        vp = sb("vp", [128, 5, 2, 192], BF16)
        ucv = sb("ucv", [128, 2, 30 + NTM]); acc = sb("acc", [128, 2, NTM]); us = sb("us", [128, 2, NTM], BF16)
        hid = sb("hid", [128, 4, NTM], BF16)
        ringM = sb("ringM", [128, 3, 4096], BF16); ringF = sb("ringF", [128, 3, 4096], BF16)
        pvt = sb("pvt", [128, NPV])
        ident = sb("ident", [128, 128]); identb = sb("identb", [128, 128], BF16)
        ones_f = sb("ones_f", [128, 128]); ones_b = sb("ones_b", [128, 128], BF16)
        abias = sb("abias", [128, 256]); sbias = sb("sbias", [4, 2, 128])
        WB = sb("WB", [128, 16, 128], BF16); WC = sb("WC", [128, 16, 128], BF16)
        Dd = sb("Dd", [128, 2, 128], BF16); gluw = sb("gluw", [128, 2, 256], BF16)
        lam = sb("lam", [128, L, 4, 8])
        rot2 = sb("rot2", [128, 2, 2, J])
        hcar = sb("hcar", [128, L, 8, 2])
        khalo = sb("khalo", [64, L, 2, 128], BF16); vhalo = sb("vhalo", [128, L, 2, 192], BF16)
        chalo = sb("chalo", [128, L, 2, 30])
        sq = sb("sq", [128, 512], BF16); rstd = sb("rstd", [128, 512])
        sq2 = sb("sq2", [128, 512], BF16); rstd2 = sb("rstd2", [128, 512])
        sgb = sb("sgb", [128, 2, 512], BF16)
        sc = sb("sc", [128, 3, 256]); pb = sb("pb", [128, 3, 256], BF16); ptb = sb("ptb", [128, 2, 256], BF16)
        sm = sb("sm", [128, 3, 8])
        t1 = sb("t1", [128, J]); t2 = sb("t2", [128, J]); bre = sb("bre", [128, J]); bim = sb("bim", [128, J])
        wre = sb("wre", [128, J]); wim = sb("wim", [128, J])
        ysm = sb("ysm", [128, 2, 512]); yb = sb("yb", [128, 2, 512], BF16); g1 = sb("g1", [128, 512]); g2 = sb("g2", [128, 512])
        ysq = ysm; cst = ysm[:].rearrange("p c n -> p (c n)")[:, 0:2 * NS * 31].rearrange("p (c b k) -> p c b k", c=2, b=NS); mean = sb("mean", [128, 512]); var = g2
        jj = mean[:, 0:J]
        xflat = x[:].rearrange("p k n -> p (k n)")
        rotflat = x1[:].rearrange("p k n -> p (k n)")[:, 0:16 * J]
        big = xflat[:, 0:4096]; big2 = rotflat[:, 8 * J:16 * J]; bigi = sb("bigi", [128, 512], I32)[:]
        hbf = sb("hbf", [128, 16 * J], BF16)
        pl = sb("pl", [128, 16, 8])
        tk = sb("tk", [128, 128]); tkb = sb("tkb", [NS, 2, 128])
        Kb = sb("Kb", [128, 1, 128]); Vb = sb("Vb", [128, 1, 128]); KbT = sb("KbT", [64, 2, 128], BF16)
        Vbp = sb("Vbp", [128, 2, 192], BF16)
        ssc = sb("ssc", [4, 4, 128]); spb = sb("spb", [4, 4, 128], BF16); ssm_ = sb("ssm_", [4, 4, 4]); sinkt = sb("sinkt", [4, L * 2])
        sptb = sb("sptb", [128, 2, NS, 4], BF16)
        cs = sb("cs", [128, 2, NS, 31])
        h0 = sb("h0", [128, 2, 8, NS]); h1 = sb("h1", [128, 2, 8, NS]); h1b = sb("h1b", [128, 2, 8, NS], BF16)

        ps = [pst("ps%d" % i, [128, 512]) for i in range(7)]
        psb = pst("psb", [128, 1024], BF16)

        Q = lambda fn, r=(), w=(), ch="io": P.op("sync", fn, r, w, dma=ch)
        G = lambda fn, r=(), w=(), ch="w": P.op("gpsimd", fn, r, w, dma=ch)
        V = lambda fn, r=(), w=(): P.op("vector", fn, r, w)
        S = lambda fn, r=(), w=(): P.op("scalar", fn, r, w)
        PE = lambda fn, r=(), w=(): P.op("tensor", fn, r, w)
        GP = lambda fn, r=(), w=(): P.op("gpsimd", fn, r, w)

        def XB(par, c0):
            return [B("x", par, c0)] + [B("x", par, c0, d) for d in range(8)]

        def pvc(l, off, n=1):
            return pvt[:, l * PVL + off: l * PVL + off + n]

        OG1, OG2, OCW, OCB, OLG, OLB, OSD, OGB, OARE, OAIM, OLDT, OSINK = 0, 8, 16, 78, 80, 82, 84, 86, 88, 96, 104, 112
        rr = [0]

        def bank():
            rr[0] = (rr[0] + 1) % 3
            return 4 + rr[0]

        fr = [0]

        def fbank():
            fr[0] = (fr[0] + 1) % 4
            return fr[0]

        cB = B("const")
        Q(lambda e: e.dma_start(out=pvt[:], in_=pv_d[:, :]), w=[cB])
        Q(lambda e: e.dma_start(out=ident[:], in_=ident_d[:, :]), w=[cB])
        Q(lambda e: e.dma_start(out=jj[:], in_=jj_d[:, :]), w=[cB])
        Q(lambda e: e.dma_start(out=abias[:], in_=abias_d[:, :]), w=[cB])
        Q(lambda e: e.dma_start(out=sbias[:].rearrange("p h k -> p (h k)"), in_=sbias_d[:, :]), w=[cB])
        Q(lambda e: e.dma_start(out=sinkt[:], in_=sinkc_d[:, :]), w=[cB])
        V(lambda e: e.memset(ones_f[:], 1.0), w=[cB])
        V(lambda e: e.memset(ones_b[:], 1.0), w=[cB])
        V(lambda e: e.tensor_copy(out=identb[:], in_=ident[:]), r=[cB], w=[B("identb")])
        V(lambda e: e.memset(vp[:].rearrange("p a g c -> p (a g c)"), 0.0), w=[B("vp", i) for i in range(5)])
        V(lambda e: e.memset(vhalo[:].rearrange("p l g c -> p (l g c)"), 0.0), w=[B("vhalo", l) for l in range(L)])
        V(lambda e: e.memset(Vbp[:].rearrange("p g c -> p (g c)"), 0.0), w=[B("Vbp")])
        V(lambda e: e.memset(hcar[:].rearrange("p l s c -> p (l s c)"), 0.0), w=[B("hcar", l) for l in range(L)])
        V(lambda e: e.memset(chalo[:].rearrange("p l c k -> p (l c k)"), 0.0), w=[B("chalo", l) for l in range(L)])

        for l in range(L):
            pB = B("pl")
            are, aim, ldt = pvc(l, OARE, 8), pvc(l, OAIM, 8), pvc(l, OLDT, 8)
            c = lambda i: pl[:, i, :]
            S(lambda e: e.activation(out=c(0), in_=ldt, func=AF.Exp), r=[cB], w=[pB])
            V(lambda e: e.tensor_tensor(out=c(1), in0=are, in1=c(0), op=ALU.mult), r=[pB], w=[pB])
            V(lambda e: e.tensor_tensor(out=c(2), in0=aim, in1=c(0), op=ALU.mult), r=[pB], w=[pB])
            S(lambda e, l=l: e.activation(out=lam[:, l, 0, :], in_=c(1), func=AF.Exp), r=[pB], w=[B("lam", l)])
            V(lambda e: e.tensor_scalar(out=c(3), in0=c(2), scalar1=1.0 / TWO_PI, scalar2=None, op0=ALU.mult), r=[pB], w=[pB])
            V(lambda e: e.tensor_copy(out=bigi[:, 0:8], in_=c(3)), r=[pB], w=[pB])
            V(lambda e: e.tensor_copy(out=c(4), in_=bigi[:, 0:8]), r=[pB], w=[pB])
            V(lambda e: e.tensor_tensor(out=c(4), in0=c(3), in1=c(4), op=ALU.subtract), r=[pB], w=[pB])
            S(lambda e, l=l: e.activation(out=lam[:, l, 2, :], in_=c(4), func=AF.Sin, scale=6.283185), r=[pB], w=[B("lam", l)])
            V(lambda e: e.tensor_scalar(out=c(3), in0=c(3), scalar1=0.25, scalar2=None, op0=ALU.add), r=[pB], w=[pB])
            V(lambda e: e.tensor_copy(out=bigi[:, 0:8], in_=c(3)), r=[pB], w=[pB])
            V(lambda e: e.tensor_copy(out=c(4), in_=bigi[:, 0:8]), r=[pB], w=[pB])
            V(lambda e: e.tensor_tensor(out=c(4), in0=c(3), in1=c(4), op=ALU.subtract), r=[pB], w=[pB])
            S(lambda e, l=l: e.activation(out=lam[:, l, 1, :], in_=c(4), func=AF.Sin, scale=6.283185), r=[pB], w=[B("lam", l)])
            V(lambda e, l=l: e.tensor_tensor(out=c(5), in0=lam[:, l, 0, :], in1=lam[:, l, 1, :], op=ALU.mult), r=[pB, B("lam", l)], w=[pB])
            V(lambda e, l=l: e.tensor_tensor(out=c(6), in0=lam[:, l, 0, :], in1=lam[:, l, 2, :], op=ALU.mult), r=[pB, B("lam", l)], w=[pB])
            V(lambda e: e.tensor_scalar(out=c(7), in0=c(5), scalar1=-1.0, scalar2=None, op0=ALU.add), r=[pB], w=[pB])
            V(lambda e: e.tensor_tensor(out=c(8), in0=are, in1=are, op=ALU.mult), r=[pB], w=[pB])
            V(lambda e: e.tensor_tensor(out=c(9), in0=aim, in1=aim, op=ALU.mult), r=[pB], w=[pB])
            V(lambda e: e.tensor_tensor(out=c(8), in0=c(8), in1=c(9), op=ALU.add), r=[pB], w=[pB])
            V(lambda e: e.reciprocal(out=c(8), in_=c(8)), r=[pB], w=[pB])
            V(lambda e: e.tensor_tensor(out=c(9), in0=c(7), in1=are, op=ALU.mult), r=[pB], w=[pB])
            V(lambda e: e.tensor_tensor(out=c(10), in0=c(6), in1=aim, op=ALU.mult), r=[pB], w=[pB])
            V(lambda e: e.tensor_tensor(out=c(9), in0=c(9), in1=c(10), op=ALU.add), r=[pB], w=[pB])
            V(lambda e: e.tensor_tensor(out=c(11), in0=c(9), in1=c(8), op=ALU.mult), r=[pB], w=[pB])
            V(lambda e: e.tensor_tensor(out=c(9), in0=c(6), in1=are, op=ALU.mult), r=[pB], w=[pB])
            V(lambda e: e.tensor_tensor(out=c(10), in0=c(7), in1=aim, op=ALU.mult), r=[pB], w=[pB])
            V(lambda e: e.tensor_tensor(out=c(9), in0=c(9), in1=c(10), op=ALU.subtract), r=[pB], w=[pB])
            V(lambda e: e.tensor_tensor(out=c(12), in0=c(9), in1=c(8), op=ALU.mult), r=[pB], w=[pB])
            V(lambda e: e.tensor_scalar(out=c(13), in0=c(12), scalar1=-1.0, scalar2=None, op0=ALU.mult), r=[pB], w=[pB])
            bg = big[:, 0:2048].rearrange("p (s c k) -> p s c k", s=8, c=2)
            Q(lambda e, l=l: e.dma_start(out=big[:, 0:2048], in_=Bn_d[l, :, :]), r=[pB], w=[B("big")])
            for s8 in range(8):
                fre, fim, nfim = pl[:, 11, s8:s8 + 1], pl[:, 12, s8:s8 + 1], pl[:, 13, s8:s8 + 1]
                V(lambda e, s8=s8, fre=fre: e.tensor_scalar(out=t1[:, 0:128], in0=bg[:, s8, 0, :], scalar1=fre, scalar2=None, op0=ALU.mult), r=[pB, B("big")], w=[B("t1")])
                V(lambda e, s8=s8, nfim=nfim: e.scalar_tensor_tensor(out=t1[:, 0:128], in0=bg[:, s8, 1, :], scalar=nfim, in1=t1[:, 0:128], op0=ALU.mult, op1=ALU.add), r=[pB, B("big"), B("t1")], w=[B("t1")])
                V(lambda e, s8=s8, fre=fre: e.tensor_scalar(out=t2[:, 0:128], in0=bg[:, s8, 1, :], scalar1=fre, scalar2=None, op0=ALU.mult), r=[pB, B("big")], w=[B("t2")])
                V(lambda e, s8=s8, fim=fim: e.scalar_tensor_tensor(out=t2[:, 0:128], in0=bg[:, s8, 0, :], scalar=fim, in1=t2[:, 0:128], op0=ALU.mult, op1=ALU.add), r=[pB, B("big"), B("t2")], w=[B("t2")])
                PE(lambda e: e.transpose(out=ps[5][:, 0:128], in_=t1[:, 0:128], identity=ident[:]), r=[B("t1"), cB], w=[B("ps", 5)])
                PE(lambda e: e.transpose(out=ps[5][:, 128:256], in_=t2[:, 0:128], identity=ident[:]), r=[B("t2"), cB], w=[B("ps", 5)])
                S(lambda e, l=l, s8=s8: e.activation(out=WB[:, s8 * 2:s8 * 2 + 2, :], in_=ps[5][:, 0:256].rearrange("p (c k) -> p c k", c=2), func=AF.Copy), r=[B("ps", 5)], w=[B("WB")])
            Q(lambda e, l=l: e.dma_start(out=big[:, 0:2048], in_=Cn_d[l, :, :]), w=[B("big")])
            V(lambda e, l=l: e.tensor_copy(out=WC[:, 0:16, :].rearrange("p (s c) k -> p s c k", c=2)[:, :, 0, :], in_=bg[:, :, 0, :]), r=[B("big")], w=[B("WC")])
            V(lambda e, l=l: e.tensor_scalar(out=WC[:, 0:16, :].rearrange("p (s c) k -> p s c k", c=2)[:, :, 1, :], in0=bg[:, :, 1, :], scalar1=-1.0, scalar2=None, op0=ALU.mult), r=[B("big")], w=[B("WC")])
            a8 = big2.rearrange("p (s j) -> p s j", s=8)
            for s8 in range(8):
                V(lambda e, s8=s8: e.tensor_scalar(out=a8[:, s8, :], in0=jj[:], scalar1=pl[:, 2, s8:s8 + 1], scalar2=1.0 / TWO_PI, op0=ALU.mult, op1=ALU.mult), r=[pB, cB], w=[B("big2"), B("rot")])
            rt = big.rearrange("p (s c j) -> p s c j", s=8, c=2)
            for ci, sh in ((1, 0.0), (0, 0.25)):
                if sh:
                    V(lambda e, sh=sh: e.tensor_scalar(out=big2, in0=big2, scalar1=sh, scalar2=None, op0=ALU.add), r=[B("big2")], w=[B("big2"), B("rot")])
                for pc in range(8 * J // 512):
                    V(lambda e, pc=pc: e.tensor_copy(out=bigi, in_=big2[:, pc * 512:(pc + 1) * 512]), r=[B("big2"), B("rot")], w=[B("bigi")])
                    V(lambda e, pc=pc: e.tensor_copy(out=rotflat[:, pc * 512:(pc + 1) * 512], in_=bigi), r=[B("bigi")], w=[B("rot")])
                V(lambda e: e.tensor_tensor(out=rotflat[:, 0:8 * J], in0=big2, in1=rotflat[:, 0:8 * J], op=ALU.subtract), r=[B("big2"), B("rot")], w=[B("rot")])
                S(lambda e, ci=ci: e.activation(out=rt[:, :, ci, :], in_=rotflat[:, 0:8 * J].rearrange("p (s j) -> p s j", s=8), func=AF.Sin, scale=6.283185), r=[B("rot")], w=[B("big")])
            Q(lambda e, l=l: e.dma_start(out=rot_d[l, :, :], in_=big), r=[B("big")], w=[B("rot_d", l)])
            Q(lambda e, l=l: e.dma_start(out=wb_d[l, :, :], in_=WB[:].rearrange("p a k -> p (a k)")), r=[B("WB")], w=[B("wb_d", l)])
            Q(lambda e, l=l: e.dma_start(out=wc_d[l, :, :], in_=WC[:].rearrange("p a k -> p (a k)")), r=[B("WC")], w=[B("wc_d", l)])

        print('MARK prologue_end', P.nops, flush=True)
        slot_i = {"M": 0, "F": 0}

        def wload(which, src_ap, shape_str, wid=None, ch=0, **kw):
            ring = ringM if which == "M" else ringF
            s = slot_i[which] % 3
            if not P.dry:
                slot_i[which] += 1
            n = 1
            for d_ in src_ap.shape[1:]:
                n *= d_
            if ch == 0:
                dst = ring[:, s, 0:n]
                if shape_str:
                    dst = dst.rearrange(shape_str, **kw)
                G(lambda e: e.dma_start(out=dst, in_=src_ap), w=[B("slot" + which, s)])
                if nch > 1:
                    Q(lambda e: e.dma_start(out=wscr[wid, :, 0:n], in_=ring[:, s, 0:n]), r=[B("slot" + which, s)], w=[B("wscr", wid)])
            else:
                G(lambda e: e.dma_start(out=ring[:, s, 0:n], in_=wscr[wid, :, 0:n]), r=[B("wscr", wid)], w=[B("slot" + which, s)])
            return s

        def rmsnorm(x, hb, par, goff, ccs, sq, rstd, pbk, tag):
            for (c0, cn) in ccs:
                pb_ = ps[pbk]
                for k in range(8):
                    S(lambda e, k=k: e.activation(out=sq[:, 0:cn], in_=x[:, k, c0:c0 + cn], func=AF.Square), r=[*XB(par, c0)], w=[B("sq", tag)])
                    PE(lambda e, k=k: e.matmul(pb_[:, 0:cn], lhsT=ones_b[:], rhs=sq[:, 0:cn], start=(k == 0), stop=(k == 7)), r=[B("sq", tag), cB], w=[B("ps", pbk)])
                S(lambda e: e.activation(out=rstd[:, 0:cn], in_=pb_[:, 0:cn], func=AF.Sqrt, scale=1.0 / D, bias=EPS), r=[B("ps", pbk)], w=[B("rstd", tag)])
                V(lambda e: e.reciprocal(out=rstd[:, 0:cn], in_=rstd[:, 0:cn]), r=[B("rstd", tag)], w=[B("rstd", tag)])
                for k in range(8):
                    gk = pvt[:, goff + k: goff + k + 1]
                    V(lambda e, k=k, gk=gk: e.scalar_tensor_tensor(out=hb[:, k, c0:c0 + cn], in0=x[:, k, c0:c0 + cn], scalar=gk, in1=rstd[:, 0:cn], op0=ALU.mult, op1=ALU.mult), r=[*XB(par, c0), B("rstd", tag), cB], w=[B("hb", par, c0)])

        def gen_mix(ch, l):
            par = ch % 2; x = X[par]; hb = HB[par]
            ring, SL = ringM, "slotM"
            first, last = (ch == 0), (ch == nch - 1)
            ccs = [(0, 512)] + ([(T, NS)] if first else [])
            if l == 0:
                for k in range(8):
                    Q(lambda e, k=k: e.dma_start(out=x[:, k, 0:T], in_=xT[k * 128:(k + 1) * 128, ch * T:(ch + 1) * T]), w=[*XB(par, 0), B("big"), B("rot"), B("big2")])
                if first:
                    Q(lambda e: e.dma_start(out=x[:, :, T:NTM], in_=xsT.rearrange("(k p) n -> p k n", p=128)), w=[*XB(par, T), B("big")])
                yield
            if True:
                Q(lambda e, l=l: e.dma_start(out=WB[:].rearrange("p a k -> p (a k)"), in_=wb_d[l, :, :]), r=[B("wb_d", l)], w=[B("WB")])
                Q(lambda e, l=l: e.dma_start(out=WC[:].rearrange("p a k -> p (a k)"), in_=wc_d[l, :, :]), r=[B("wc_d", l)], w=[B("WC")])
                for ct in range(2):
                    V(lambda e, ct=ct: e.tensor_scalar(out=Dd[:, ct, :], in0=ident[:], scalar1=pvc(l, OSD + ct), scalar2=None, op0=ALU.mult), r=[cB], w=[B("Dd")])
                G(lambda e: e.dma_start(out=gluw[:], in_=glu_w[l].rearrange("(c p) n -> p c n", p=128)), w=[B("gluw")])
                s_in = [wload("M", w_in[l, :, i * 512:(i + 1) * 512].rearrange("(k p) n -> p k n", p=128), "p (k n) -> p k n", wid=l * 23 + i, ch=ch, k=8) for i in range(3)]

                def win(o_lo, o_n):
                    s = s_in[o_lo // 512]
                    off = o_lo % 512
                    return lambda k: ring[:, s, k * 512 + off: k * 512 + off + o_n], B(SL, s)

                V(lambda e, l=l: e.tensor_copy(out=kT[:, :, 0:128], in_=khalo[:, l, :, :]), r=[B("khalo", l)], w=[B("kT", "h")])
                V(lambda e, l=l: e.tensor_copy(out=vp[:, 0, :, :], in_=vhalo[:, l, :, :]), r=[B("vhalo", l)], w=[B("vp", 0)])
                V(lambda e, l=l: e.tensor_copy(out=ucv[:, :, 0:30], in_=chalo[:, l, :, :]), r=[B("chalo", l)], w=[B("ucv", "h")])

                rmsnorm(x, hb, par, l * PVL + OG1, ccs, sq, rstd, 6, "m")
                for (c0, cn) in ccs:
                    def mm8(lw, n_out, pbk):
                        f, sb_ = lw
                        for k in range(8):
                            PE(lambda e, k=k: e.matmul(ps[pbk][0:n_out, 0:cn], lhsT=f(k), rhs=hb[:, k, c0:c0 + cn], start=(k == 0), stop=(k == 7)), r=[sb_, B("hb", par, c0)], w=[B("ps", pbk)])
                    for h in range(8):
                        pbk = bank(); mm8(win(64 * h, 64), 64, pbk)
                        S(lambda e, h=h, pbk=pbk: e.activation(out=qT[:, h, c0:c0 + cn], in_=ps[pbk][0:64, 0:cn], func=AF.Copy, scale=0.125), r=[B("ps", pbk)], w=[B("qT", c0)])
                        yield
                    for g in range(2):
                        pbk = bank(); mm8(win(512 + 64 * g, 64), 64, pbk)
                        S(lambda e, g=g, pbk=pbk: e.activation(out=kT[:, g, 128 + c0:128 + c0 + cn], in_=ps[pbk][0:64, 0:cn], func=AF.Copy), r=[B("ps", pbk)], w=[B("kT", c0)])
                        yield
                    for ct in range(2):
                        pa = bank(); mm8(win(768 + 128 * ct, 128), 128, pa)
                        pg = bank(); mm8(win(1024 + 128 * ct, 128), 128, pg)
                        S(lambda e, pg=pg: e.activation(out=g2[:, 0:cn], in_=ps[pg][:, 0:cn], func=AF.Sigmoid), r=[B("ps", pg)], w=[B("g2")])
                        V(lambda e, ct=ct, pa=pa: e.tensor_tensor(out=ucv[:, ct, 30 + c0:30 + c0 + cn], in0=ps[pa][:, 0:cn], in1=g2[:, 0:cn], op=ALU.mult), r=[B("ps", pa), B("g2")], w=[B("ucv", c0)])
                        yield
                    for ct in range(2):
                        pbk = bank(); mm8(win(1280 + 128 * ct, 128), 128, pbk)
                        S(lambda e, ct=ct, pbk=pbk: e.activation(out=us[:, ct, c0:c0 + cn], in_=ps[pbk][:, 0:cn], func=AF.Copy), r=[B("ps", pbk)], w=[B("us", c0)])
                        yield
                fv, sv = win(640, 128)
                fk, sk = win(512, 128)
                for bi in range(4):
                    c0 = bi * 128
                    cc0 = (c0 // 512) * 512
                    pbk = bank()
                    for k in range(8):
                        PE(lambda e, k=k: e.matmul(ps[pbk][:, 0:128], lhsT=hb[:, k, c0:c0 + 128], rhs=fv(k), start=(k == 0), stop=(k == 7)), r=[sv, B("hb", par, cc0)], w=[B("ps", pbk)])
                    S(lambda e, bi=bi, pbk=pbk: e.activation(out=vp[:, bi + 1, :, 64:128], in_=ps[pbk][:, 0:128].rearrange("p (g d) -> p g d", g=2), func=AF.Copy), r=[B("ps", pbk)], w=[B("vp", bi + 1)])
                    yield
                    if last and bi == 3:
                        V(lambda e, pbk=pbk: e.tensor_copy(out=tk[:], in_=ps[pbk][:, 0:128]), r=[B("ps", pbk), B("vp", bi + 1)], w=[B("tk")])
                        Q(lambda e, l=l: e.dma_start(out=ov_p[l, :, :], in_=tk[:]), r=[B("tk")], w=[B("ov_p", l)])
                        pbk2 = bank()
                        for k in range(8):
                            PE(lambda e, k=k: e.matmul(ps[pbk2][:, 0:128], lhsT=hb[:, k, c0:c0 + 128], rhs=fk(k), start=(k == 0), stop=(k == 7)), r=[sk, B("hb", par, cc0)], w=[B("ps", pbk2)])
                        V(lambda e, pbk2=pbk2: e.tensor_copy(out=tk[:], in_=ps[pbk2][:, 0:128]), r=[B("ps", pbk2)], w=[B("tk")])
                        Q(lambda e, l=l: e.dma_start(out=ok_p[l, :, :], in_=tk[:]), r=[B("tk")], w=[B("ok_p", l)])
                if first:
                    pbk = bank()
                    for (f_, s_, off) in ((fk, sk, 0), (fv, sv, 128)):
                        for k in range(8):
                            PE(lambda e, k=k, f_=f_, off=off: e.matmul(ps[pbk][0:NS, off:off + 128], lhsT=hb[:, k, T:NTM], rhs=f_(k), start=(k == 0), stop=(k == 7)), r=[s_, B("hb", par, T)], w=[B("ps", pbk)])
                    V(lambda e, pbk=pbk: e.tensor_copy(out=tkb[:].rearrange("p a k -> p (a k)"), in_=ps[pbk][0:NS, 0:256]), r=[B("ps", pbk)], w=[B("tkb")])
                    for b in range(NS):
                        Q(lambda e, l=l, b=b: e.dma_start(out=ok_s[l, b:b + 1, 0:127, :].rearrange("b r c -> b (r c)"), in_=ck_d[l, b:b + 1, 1:128, :].rearrange("b r c -> b (r c)")), w=[B("ok_s", l)])
                        Q(lambda e, l=l, b=b: e.dma_start(out=ov_s[l, b:b + 1, 0:127, :].rearrange("b r c -> b (r c)"), in_=cv_d[l, b:b + 1, 1:128, :].rearrange("b r c -> b (r c)")), w=[B("ov_s", l)])
                    Q(lambda e, l=l: e.dma_start(out=ok_s[l, :, 127, :], in_=tkb[:, 0, :]), r=[B("tkb")], w=[B("ok_s", l)])
                    Q(lambda e, l=l: e.dma_start(out=ov_s[l, :, 127, :], in_=tkb[:, 1, :]), r=[B("tkb")], w=[B("ov_s", l)])
                if not last:
                    V(lambda e, l=l: e.tensor_copy(out=khalo[:, l, :, :], in_=kT[:, :, T:T + 128]), r=[B("kT", 0)], w=[B("khalo", l)])
                    V(lambda e, l=l: e.tensor_copy(out=vhalo[:, l, :, :], in_=vp[:, 4, :, :]), r=[B("vp", 4)], w=[B("vhalo", l)])
                    V(lambda e, l=l: e.tensor_copy(out=chalo[:, l, :, :], in_=ucv[:, :, T:T + 30]), r=[B("ucv", 0)], w=[B("chalo", l)])
                else:
                    Q(lambda e, l=l: e.dma_start(out=oconv_p[l, :, :].rearrange("p (c k) -> p c k", c=2), in_=ucv[:, :, T:T + 30]), r=[B("ucv", 0)], w=[B("oconv_p", l)])

                units = [(bi, tile, r2) for bi in range(4) for tile in range(4) for r2 in range(2)]

                def att_info(u):
                    bi, tile, r2 = units[u]
                    q0 = bi * 128
                    nokprev = first and bi == 0
                    k_lo, k_n = (128, 128) if nokprev else (0, 256)
                    return bi, tile, r2, tile * 2 + r2, (tile * 2 + r2) // 4, q0, nokprev, k_lo, k_n, u % 3

                def att_A1(u):
                    bi, tile, r2, h, g, q0, nokprev, k_lo, k_n, a = att_info(u)
                    SB = 4 if u % 2 == 0 else 6
                    kread = [B("kT", 0)] + ([B("kT", "h")] if bi == 0 else [])
                    PE(lambda e: e.matmul(ps[SB][:, k_lo:k_lo + k_n], lhsT=qT[:, h, q0:q0 + 128], rhs=kT[:, g, q0 + k_lo:q0 + k_lo + k_n], start=True, stop=True), r=[B("qT", 0)] + kread, w=[B("ps", SB)])
                    V(lambda e: e.scalar_tensor_tensor(out=sc[:, a, k_lo:k_lo + k_n], in0=abias[:, k_lo:k_lo + k_n], scalar=float(2.0 ** (-(h + 1))), in1=ps[SB][:, k_lo:k_lo + k_n], op0=ALU.mult, op1=ALU.add), r=[B("ps", SB), cB], w=[B("sc", a)])
                    V(lambda e: e.reduce_max(out=sm[:, a, 0:1], in_=sc[:, a, k_lo:k_lo + k_n], axis=AX.X), r=[B("sc", a)], w=[B("sm", a)])
                    sinkc = pvc(l, OSINK + h)
                    V(lambda e: e.tensor_scalar(out=sm[:, a, 1:2], in0=sm[:, a, 0:1], scalar1=sinkc, scalar2=-1.0, op0=ALU.max, op1=ALU.mult), r=[B("sm", a), cB], w=[B("sm", a)])
                    S(lambda e: e.activation(out=pb[:, a, k_lo:k_lo + k_n], in_=sc[:, a, k_lo:k_lo + k_n], func=AF.Exp, bias=sm[:, a, 1:2], accum_out=sm[:, a, 2:3]), r=[B("sc", a), B("sm", a)], w=[B("pb", a), B("sm", a)])
                    S(lambda e: e.activation(out=sm[:, a, 3:4], in_=sinkc, func=AF.Exp, bias=sm[:, a, 1:2]), r=[B("sm", a), cB], w=[B("sm", a)])

                def att_A2(u):
                    bi, tile, r2, h, g, q0, nokprev, k_lo, k_n, a = att_info(u)
                    V(lambda e: e.tensor_tensor(out=sm[:, a, 4:5], in0=sm[:, a, 2:3], in1=sm[:, a, 3:4], op=ALU.add), r=[B("sm", a)], w=[B("sm", a)])
                    V(lambda e: e.reciprocal(out=sm[:, a, 5:6], in_=sm[:, a, 4:5]), r=[B("sm", a)], w=[B("sm", a)])
                    V(lambda e: e.tensor_scalar(out=pb[:, a, k_lo:k_lo + k_n], in0=pb[:, a, k_lo:k_lo + k_n], scalar1=sm[:, a, 5:6], scalar2=None, op0=ALU.mult), r=[B("sm", a), B("pb", a)], w=[B("pb", a)])

                def att_B(u):
                    bi, tile, r2, h, g, q0, nokprev, k_lo, k_n, a = att_info(u)
                    pa_ = u % 2
                    for kb in range(2):
                        if nokprev and kb == 0:
                            continue
                        PE(lambda e, kb=kb: e.transpose(out=psb[:, pa_ * 256 + kb * 128:pa_ * 256 + kb * 128 + 128], in_=pb[:, a, kb * 128:kb * 128 + 128], identity=identb[:]), r=[B("pb", a), B("identb")], w=[B("psb", 0)])
                    S(lambda e: e.activation(out=ptb[:, pa_, k_lo:k_lo + k_n], in_=psb[:, pa_ * 256 + k_lo:pa_ * 256 + k_lo + k_n], func=AF.Copy), r=[B("psb", 0)], w=[B("ptb", pa_)])
                    kbs = [1] if nokprev else [0, 1]
                    for kb in kbs:
                        lo = 64 if r2 == 0 else 0
                        PE(lambda e, kb=kb, lo=lo: e.matmul(ps[5][:, 0:128], lhsT=vp[:, bi + kb, g, lo:lo + 128], rhs=ptb[:, pa_, kb * 128:kb * 128 + 128], start=(r2 == 0 and kb == kbs[0]), stop=(r2 == 1 and kb == 1)), r=[B("vp", bi + kb), B("ptb", pa_)], w=[B("ps", 5)])
                    if r2 == 1:
                        S(lambda e: e.activation(out=hb[:, tile, q0:q0 + 128], in_=ps[5][:, 0:128], func=AF.Copy), r=[B("ps", 5)], w=[B("hb", par, 0)])

                for u in range(len(units) + 2):
                    if u < len(units):
                        att_A1(u)
                    if 1 <= u <= len(units):
                        att_A2(u - 1)
                    if u >= 2:
                        att_B(u - 2)
                    yield
                if first:
                    for g in range(2):
                        sk4 = sinkt[:, l * 2 + g:l * 2 + g + 1]
                        for b4 in range(NS // 4):
                            for bb in range(4):
                                b = b4 * 4 + bb
                                Q(lambda e, b=b, l=l: e.dma_start(out=Kb[:, 0, :], in_=ok_s[l, b, :, :]), r=[B("ok_s", l)], w=[B("Kb", 0)])
                                PE(lambda e, b=b, g=g: e.transpose(out=ps[4][0:64, 0:128], in_=Kb[:, 0, g * 64:(g + 1) * 64], identity=ident[:]), r=[B("Kb", 0), cB], w=[B("ps", 4)])
                                S(lambda e, b=b: e.activation(out=KbT[:, b % 2, :], in_=ps[4][0:64, 0:128], func=AF.Copy), r=[B("ps", 4)], w=[B("KbT", b % 2)])
                                PE(lambda e, b=b, g=g, bb=bb: e.matmul(ps[6][0:4, bb * 128:bb * 128 + 128], lhsT=qT[:, 4 * g:4 * g + 4, T + b], rhs=KbT[:, b % 2, :], start=True, stop=True), r=[B("qT", T), B("KbT", b % 2)], w=[B("ps", 6)])
                            V(lambda e, g=g: e.tensor_tensor(out=ssc[:], in0=ps[6][0:4, :].rearrange("p (b k) -> p b k", b=4), in1=sbias[:, g:g + 1, :].to_broadcast([4, 4, 128]), op=ALU.add), r=[B("ps", 6), cB], w=[B("ssc")])
                            V(lambda e: e.reduce_max(out=ssm_[:, 0, :], in_=ssc[:], axis=AX.X), r=[B("ssc")], w=[B("ssm_")])
                            V(lambda e, sk4=sk4: e.tensor_scalar(out=ssm_[:, 0, :], in0=ssm_[:, 0, :], scalar1=sk4, scalar2=None, op0=ALU.max), r=[B("ssm_"), cB], w=[B("ssm_")])
                            V(lambda e: e.tensor_tensor(out=ssc[:], in0=ssc[:], in1=ssm_[:, 0, :].unsqueeze(2).to_broadcast([4, 4, 128]), op=ALU.subtract), r=[B("ssc"), B("ssm_")], w=[B("ssc")])
                            S(lambda e: e.activation(out=ssc[:], in_=ssc[:], func=AF.Exp), r=[B("ssc")], w=[B("ssc")])
                            V(lambda e: e.reduce_sum(out=ssm_[:, 1, :], in_=ssc[:], axis=AX.X), r=[B("ssc")], w=[B("ssm_")])
                            S(lambda e, sk4=sk4: e.activation(out=ssm_[:, 2, :], in_=ssm_[:, 0, :], func=AF.Exp, scale=-1.0, bias=sk4), r=[B("ssm_"), cB], w=[B("ssm_")])
                            V(lambda e: e.tensor_tensor(out=ssm_[:, 1, :], in0=ssm_[:, 1, :], in1=ssm_[:, 2, :], op=ALU.add), r=[B("ssm_")], w=[B("ssm_")])
                            V(lambda e: e.reciprocal(out=ssm_[:, 1, :], in_=ssm_[:, 1, :]), r=[B("ssm_")], w=[B("ssm_")])
                            V(lambda e: e.tensor_tensor(out=spb[:], in0=ssc[:], in1=ssm_[:, 1, :].unsqueeze(2).to_broadcast([4, 4, 128]), op=ALU.mult), r=[B("ssc"), B("ssm_")], w=[B("spb")])
                            for bb in range(4):
                                PE(lambda e, bb=bb: e.transpose(out=psb[:, 512 + bb * 4:512 + bb * 4 + 4], in_=spb[:, bb, :], identity=identb[0:4, 0:4]), r=[B("spb"), B("identb")], w=[B("psb", 0)])
                            S(lambda e, g=g, b4=b4: e.activation(out=sptb[:, g, b4 * 4:b4 * 4 + 4, :], in_=psb[:, 512:512 + 16].rearrange("p (b r) -> p b r", r=4), func=AF.Copy), r=[B("psb", 0)], w=[B("sptb", g)])
                            yield
                    for b in range(NS):
                        Q(lambda e, b=b, l=l: e.dma_start(out=Vb[:, 0, :], in_=ov_s[l, b, :, :]), r=[B("ov_s", l)], w=[B("Vb", 0)])
                        V(lambda e, b=b: e.tensor_copy(out=Vbp[:, :, 64:128], in_=Vb[:, 0, :].rearrange("p (g d) -> p g d", g=2)), r=[B("Vb", 0)], w=[B("Vbp")])
                        for tile in range(4):
                            g = tile // 2
                            for r2 in range(2):
                                rr4 = (tile % 2) * 2 + r2
                                lo = 64 if r2 == 0 else 0
                                PE(lambda e, b=b, g=g, tile=tile, r2=r2, rr4=rr4, lo=lo: e.matmul(ps[5][:, 256 + b * 4 + tile:256 + b * 4 + tile + 1], lhsT=Vbp[:, g, lo:lo + 128], rhs=sptb[:, g, b, rr4:rr4 + 1], start=(r2 == 0), stop=(r2 == 1)), r=[B("Vbp"), B("sptb", g)], w=[B("ps", 5)])
                    S(lambda e: e.activation(out=hb[:, 0:4, T:NTM].rearrange("p t b -> p b t"), in_=ps[5][:, 256:256 + NS * 4].rearrange("p (b t) -> p b t", t=4), func=AF.Copy), r=[B("ps", 5)], w=[B("hb", par, T)])

                if first:
                    Q(lambda e, l=l: e.dma_start(out=cs[:, :, :, 0:30], in_=cconv_d[l, :, :].rearrange("p (c b k) -> p c b k", c=2, b=NS)), w=[B("cs")])
                    V(lambda e: e.tensor_copy(out=cs[:, :, :, 30], in_=ucv[:, :, 30 + T:30 + NTM]), r=[B("ucv", T)], w=[B("cs")])
                    Q(lambda e, l=l: e.dma_start(out=oconv_s[l, :, :].rearrange("p (c b k) -> p c b k", c=2, b=NS), in_=cs[:, :, :, 1:31]), r=[B("cs")], w=[B("oconv_s", l)])
                    for ct in range(2):
                        cw = pvt[:, l * PVL + OCW + ct * 31: l * PVL + OCW + ct * 31 + 31]
                        V(lambda e, ct=ct, cw=cw: e.tensor_tensor(out=cst[:, ct, :, :], in0=cs[:, ct, :, :], in1=cw.unsqueeze(1).to_broadcast([128, NS, 31]), op=ALU.mult), r=[B("cs"), cB], w=[B("ysm", 0), B("ysm", 1)])
                        V(lambda e, ct=ct: e.reduce_sum(out=acc[:, ct, T:NTM], in_=cst[:, ct, :, :], axis=AX.X), r=[B("ysm", 0), B("ysm", 1)], w=[B("acc", T, ct)])
                        V(lambda e, ct=ct: e.tensor_scalar(out=acc[:, ct, T:NTM], in0=acc[:, ct, T:NTM], scalar1=pvc(l, OCB + ct), scalar2=None, op0=ALU.add), r=[B("acc", T, ct), cB], w=[B("acc", T, ct)])
                for (c0, cn) in ccs:
                    if c0 < T:
                        hr = [B("ucv", c0)] + ([B("ucv", "h")] if c0 == 0 else [B("ucv", c0 - 512)])
                        for kk in range(31):
                            for ct in range(2):
                                wk = pvt[:, l * PVL + OCW + ct * 31 + kk: l * PVL + OCW + ct * 31 + kk + 1]
                                if kk == 0:
                                    V(lambda e, ct=ct, wk=wk: e.tensor_scalar(out=acc[:, ct, c0:c0 + cn], in0=ucv[:, ct, c0:c0 + cn], scalar1=wk, scalar2=pvc(l, OCB + ct), op0=ALU.mult, op1=ALU.add), r=hr + [cB], w=[B("acc", c0, ct)])
                                else:
                                    V(lambda e, ct=ct, wk=wk, kk=kk: e.scalar_tensor_tensor(out=acc[:, ct, c0:c0 + cn], in0=ucv[:, ct, c0 + kk:c0 + kk + cn], scalar=wk, in1=acc[:, ct, c0:c0 + cn], op0=ALU.mult, op1=ALU.add), r=hr + [cB, B("acc", c0, ct)], w=[B("acc", c0, ct)])
                    for ct in range(2):
                        S(lambda e, ct=ct: e.activation(out=ysq[:, ct, 0:cn], in_=acc[:, ct, c0:c0 + cn], func=AF.Square), r=[B("acc", c0, 0), B("acc", c0, 1)], w=[B("ysm", 0), B("ysm", 1)])
                    for ct in range(2):
                        PE(lambda e, ct=ct: e.matmul(ps[6][:, 0:cn], lhsT=ones_f[:], rhs=acc[:, ct, c0:c0 + cn], start=(ct == 0), stop=(ct == 1)), r=[B("acc", c0, 0), B("acc", c0, 1), cB], w=[B("ps", 6)])
                    V(lambda e: e.tensor_scalar(out=mean[:, 0:cn], in0=ps[6][:, 0:cn], scalar1=1.0 / 256, scalar2=None, op0=ALU.mult), r=[B("ps", 6)], w=[B("mean")])
                    for ct in range(2):
                        PE(lambda e, ct=ct: e.matmul(ps[6][:, 0:cn], lhsT=ones_f[:], rhs=ysq[:, ct, 0:cn], start=(ct == 0), stop=(ct == 1)), r=[B("ysm", 0), B("ysm", 1), cB], w=[B("ps", 6)])
                    V(lambda e: e.tensor_tensor(out=var[:, 0:cn], in0=mean[:, 0:cn], in1=mean[:, 0:cn], op=ALU.mult), r=[B("mean")], w=[B("g2")])
                    V(lambda e: e.scalar_tensor_tensor(out=var[:, 0:cn], in0=ps[6][:, 0:cn], scalar=1.0 / 256, in1=var[:, 0:cn], op0=ALU.mult, op1=ALU.subtract), r=[B("ps", 6), B("g2")], w=[B("g2")])
                    S(lambda e: e.activation(out=var[:, 0:cn], in_=var[:, 0:cn], func=AF.Sqrt, bias=EPS), r=[B("g2")], w=[B("g2")])
                    V(lambda e: e.reciprocal(out=var[:, 0:cn], in_=var[:, 0:cn]), r=[B("g2")], w=[B("g2")])
                    for ct in range(2):
                        V(lambda e, ct=ct: e.tensor_tensor(out=g1[:, 0:cn], in0=acc[:, ct, c0:c0 + cn], in1=mean[:, 0:cn], op=ALU.subtract), r=[B("acc", c0, 0), B("acc", c0, 1), B("mean")], w=[B("g1")])
                        V(lambda e, ct=ct: e.tensor_tensor(out=g1[:, 0:cn], in0=g1[:, 0:cn], in1=var[:, 0:cn], op=ALU.mult), r=[B("g1"), B("g2")], w=[B("g1")])
                        V(lambda e, ct=ct: e.tensor_scalar(out=g1[:, 0:cn], in0=g1[:, 0:cn], scalar1=pvc(l, OLG + ct), scalar2=pvc(l, OLB + ct), op0=ALU.mult, op1=ALU.add), r=[B("g1"), cB], w=[B("g1")])
                        S(lambda e, ct=ct: e.activation(out=hb[:, 4 + ct, c0:c0 + cn], in_=g1[:, 0:cn], func=AF.Silu), r=[B("g1")], w=[B("hb", par, c0)])
                        yield

                def ssm_out(c0, cn, hsrc_re, hsrc_im, hoff, hbufs):
                    for ct in range(2):
                        for j4 in range(4):
                            s8 = ct * 4 + j4
                            PE(lambda e, ct=ct, s8=s8, j4=j4: e.matmul(ps[6][:, 0:cn], lhsT=WC[:, s8 * 2, :], rhs=hsrc_re(s8), start=(j4 == 0), stop=False), r=hbufs + [B("WC")], w=[B("ps", 6)])
                            PE(lambda e, ct=ct, s8=s8: e.matmul(ps[6][:, 0:cn], lhsT=WC[:, s8 * 2 + 1, :], rhs=hsrc_im(s8), start=False, stop=False), r=hbufs + [B("WC")], w=[B("ps", 6)])
                        PE(lambda e, ct=ct: e.matmul(ps[6][:, 0:cn], lhsT=Dd[:, ct, :], rhs=us[:, ct, c0:c0 + cn], start=False, stop=True), r=[B("us", (c0 // 512) * 512 if c0 < T else T), B("Dd")], w=[B("ps", 6)])
                        V(lambda e, ct=ct: e.tensor_copy(out=ysm[:, ct, 0:cn], in_=ps[6][:, 0:cn]), r=[B("ps", 6)], w=[B("ysm", ct)])
                        V(lambda e, ct=ct: e.tensor_tensor(out=g1[:, 0:cn], in0=ysm[:, ct, 0:cn], in1=ysm[:, ct, 0:cn], op=ALU.mult), r=[B("ysm", ct)], w=[B("g1")])
                        V(lambda e, ct=ct: e.tensor_scalar(out=g1[:, 0:cn], in0=g1[:, 0:cn], scalar1=0.044715, scalar2=1.0, op0=ALU.mult, op1=ALU.add), r=[B("g1")], w=[B("g1")])
                        V(lambda e, ct=ct: e.tensor_tensor(out=g1[:, 0:cn], in0=g1[:, 0:cn], in1=ysm[:, ct, 0:cn], op=ALU.mult), r=[B("g1"), B("ysm", ct)], w=[B("g1")])
                        S(lambda e, ct=ct: e.activation(out=g2[:, 0:cn], in_=g1[:, 0:cn], func=AF.Sigmoid, scale=2.0 * math.sqrt(2.0 / math.pi)), r=[B("g1")], w=[B("g2")])
                        V(lambda e, ct=ct: e.tensor_tensor(out=ysm[:, ct, 0:cn], in0=ysm[:, ct, 0:cn], in1=g2[:, 0:cn], op=ALU.mult), r=[B("g2"), B("ysm", ct)], w=[B("ysm", ct)])
                        V(lambda e, ct=ct: e.tensor_copy(out=yb[:, ct, 0:cn], in_=ysm[:, ct, 0:cn]), r=[B("ysm", ct)], w=[B("yb", ct)])
                    for co in range(2):
                        for ct in range(2):
                            PE(lambda e, ct=ct, co=co: e.matmul(ps[6][:, 0:cn], lhsT=gluw[:, ct, co * 128:(co + 1) * 128], rhs=yb[:, ct, 0:cn], start=(ct == 0), stop=(ct == 1)), r=[B("yb", 0), B("yb", 1), B("gluw")], w=[B("ps", 6)])
                        S(lambda e, co=co: e.activation(out=g2[:, 0:cn], in_=ps[6][:, 0:cn], func=AF.Sigmoid, bias=pvc(l, OGB + co)), r=[B("ps", 6), cB], w=[B("g2")])
                        V(lambda e, co=co: e.tensor_tensor(out=hb[:, 6 + co, c0:c0 + cn], in0=ysm[:, co, 0:cn], in1=g2[:, 0:cn], op=ALU.mult), r=[B("g2"), B("ysm", co)], w=[B("hb", par, (c0 // 512) * 512 if c0 < T else T)])

                for sc_i in range(T // J):
                    c0 = sc_i * J
                    ucc = (c0 // 512) * 512
                    for s8 in range(8):
                        ct, a = s8 // 4, s8 % 2
                        for ci, dst in ((0, 0), (1, J)):
                            PE(lambda e, ci=ci, dst=dst, s8=s8, ct=ct: e.matmul(ps[4][:, dst:dst + J], lhsT=WB[:, s8 * 2 + ci, :], rhs=us[:, ct, c0:c0 + J], start=True, stop=True), r=[B("us", ucc), B("WB")], w=[B("ps", 4)])
                        ra = s8 % 2
                        Q(lambda e, s8=s8, ra=ra: e.dma_start(out=rot2[:, ra, :, :].rearrange("p c j -> p (c j)"), in_=rot_d[l, :, s8 * 2 * J:(s8 + 1) * 2 * J]), r=[B("rot_d", l)], w=[B("rot2", ra)])
                        cosT, sinT = rot2[:, ra, 0, :], rot2[:, ra, 1, :]
                        pre, pim = ps[4][:, 0:J], ps[4][:, J:2 * J]
                        V(lambda e, cosT=cosT, pre=pre: e.tensor_tensor(out=bre[:], in0=pre, in1=cosT, op=ALU.mult), r=[B("ps", 4), B("rot2", ra)], w=[B("bre")])
                        V(lambda e, sinT=sinT, pim=pim: e.tensor_tensor(out=t1[:], in0=pim, in1=sinT, op=ALU.mult), r=[B("ps", 4), B("rot2", ra)], w=[B("t1")])
                        V(lambda e, cosT=cosT, pim=pim: e.tensor_tensor(out=bim[:], in0=pim, in1=cosT, op=ALU.mult), r=[B("ps", 4), B("rot2", ra)], w=[B("bim")])
                        V(lambda e, sinT=sinT, pre=pre: e.tensor_tensor(out=t2[:], in0=pre, in1=sinT, op=ALU.mult), r=[B("ps", 4), B("rot2", ra)], w=[B("t2")])
                        V(lambda e: e.tensor_tensor(out=bre[:], in0=bre[:], in1=t1[:], op=ALU.add), r=[B("t1"), B("bre")], w=[B("bre")])
                        V(lambda e: e.tensor_tensor(out=bim[:], in0=bim[:], in1=t2[:], op=ALU.subtract), r=[B("t2"), B("bim")], w=[B("bim")])
                        rho_b = lam[:, l, 0, s8:s8 + 1].to_broadcast([128, J])
                        V(lambda e, rho_b=rho_b, s8=s8: e.tensor_tensor_scan(out=wre[:], data0=rho_b, data1=bre[:], initial=hcar[:, l, s8, 0:1], op0=ALU.mult, op1=ALU.add), r=[B("bre"), B("lam", l), B("hcar", l)], w=[B("wre")])
                        V(lambda e, rho_b=rho_b, s8=s8: e.tensor_tensor_scan(out=wim[:], data0=rho_b, data1=bim[:], initial=hcar[:, l, s8, 1:2], op0=ALU.mult, op1=ALU.add), r=[B("bim"), B("lam", l), B("hcar", l)], w=[B("wim")])
                        V(lambda e, cosT=cosT: e.tensor_tensor(out=t1[:], in0=wre[:], in1=cosT, op=ALU.mult), r=[B("wre"), B("rot2", ra)], w=[B("t1")])
                        V(lambda e, sinT=sinT: e.tensor_tensor(out=t2[:], in0=wim[:], in1=sinT, op=ALU.mult), r=[B("wim"), B("rot2", ra)], w=[B("t2")])
                        V(lambda e, sinT=sinT: e.tensor_tensor(out=bre[:], in0=wre[:], in1=sinT, op=ALU.mult), r=[B("wre"), B("rot2", ra)], w=[B("bre")])
                        V(lambda e, cosT=cosT: e.tensor_tensor(out=bim[:], in0=wim[:], in1=cosT, op=ALU.mult), r=[B("wim"), B("rot2", ra)], w=[B("bim")])
                        V(lambda e, s8=s8: e.tensor_tensor(out=hbf[:, s8 * J:(s8 + 1) * J], in0=t1[:], in1=t2[:], op=ALU.subtract), r=[B("t1"), B("t2")], w=[B("hbf")])
                        V(lambda e, s8=s8: e.tensor_tensor(out=hbf[:, 8 * J + s8 * J:8 * J + (s8 + 1) * J], in0=bre[:], in1=bim[:], op=ALU.add), r=[B("bre"), B("bim")], w=[B("hbf")])
                        cl, sl = rot2[:, ra, 0, J - 1:J], rot2[:, ra, 1, J - 1:J]
                        V(lambda e, sl=sl: e.tensor_tensor(out=sm[:, 0, 6:7], in0=wim[:, J - 1:J], in1=sl, op=ALU.mult), r=[B("wim"), B("rot2", ra)], w=[B("sm", 0)])
                        V(lambda e, s8=s8, cl=cl: e.scalar_tensor_tensor(out=hcar[:, l, s8, 0:1], in0=wre[:, J - 1:J], scalar=cl, in1=sm[:, 0, 6:7], op0=ALU.mult, op1=ALU.subtract), r=[B("wre"), B("rot2", ra), B("sm", 0)], w=[B("hcar", l)])
                        V(lambda e, sl=sl: e.tensor_tensor(out=sm[:, 0, 7:8], in0=wre[:, J - 1:J], in1=sl, op=ALU.mult), r=[B("wre"), B("rot2", ra)], w=[B("sm", 0)])
                        V(lambda e, s8=s8, cl=cl: e.scalar_tensor_tensor(out=hcar[:, l, s8, 1:2], in0=wim[:, J - 1:J], scalar=cl, in1=sm[:, 0, 7:8], op0=ALU.mult, op1=ALU.add), r=[B("wim"), B("rot2", ra), B("sm", 0)], w=[B("hcar", l)])
                        yield
                    hbre = hbf
                    ssm_out(c0, J, lambda s8: hbre[:, s8 * J:(s8 + 1) * J], lambda s8: hbre[:, 8 * J + s8 * J:8 * J + (s8 + 1) * J], 0, [B("hbf")])
                    yield
                if last:
                    Q(lambda e, l=l: e.dma_start(out=ossm_p[l, :, :].rearrange("p (s c) -> p s c", c=2), in_=hcar[:, l, :, :]), r=[B("hcar", l)], w=[B("ossm_p", l)])
                if first:
                    Q(lambda e, l=l: e.dma_start(out=h0[:, 0, :, :], in_=sre_d[l, :, :].rearrange("p (s b) -> p s b", s=8)), w=[B("h0")])
                    Q(lambda e, l=l: e.dma_start(out=h0[:, 1, :, :], in_=sim_d[l, :, :].rearrange("p (s b) -> p s b", s=8)), w=[B("h0")])
                    for s8 in range(8):
                        ct = s8 // 4
                        for ci in range(2):
                            PE(lambda e, ci=ci, s8=s8, ct=ct: e.matmul(ps[4][:, (s8 * 2 + ci) * NS:(s8 * 2 + ci + 1) * NS], lhsT=WB[:, s8 * 2 + ci, :], rhs=us[:, ct, T:NTM], start=True, stop=True), r=[B("us", T), B("WB")], w=[B("ps", 4)])
                    V(lambda e: e.tensor_copy(out=h1[:].rearrange("p c s b -> p s c b"), in_=ps[4][:, 0:16 * NS].rearrange("p (s c b) -> p s c b", s=8, c=2)), r=[B("ps", 4)], w=[B("h1")])
                    for s8 in range(8):
                        lre, lim = pl[:, 5, s8:s8 + 1], pl[:, 6, s8:s8 + 1]
                        V(lambda e, s8=s8: e.tensor_tensor(out=sm[:, 0, 6:7], in0=lam[:, l, 0, s8:s8 + 1], in1=lam[:, l, 1, s8:s8 + 1], op=ALU.mult), r=[B("lam", l)], w=[B("sm", 0)])
                        V(lambda e, s8=s8: e.tensor_tensor(out=sm[:, 0, 7:8], in0=lam[:, l, 0, s8:s8 + 1], in1=lam[:, l, 2, s8:s8 + 1], op=ALU.mult), r=[B("lam", l)], w=[B("sm", 0)])
                        V(lambda e, s8=s8: e.tensor_scalar(out=sm[:, 1, 7:8], in0=sm[:, 0, 7:8], scalar1=-1.0, scalar2=None, op0=ALU.mult), r=[B("sm", 0)], w=[B("sm", 1)])
                        V(lambda e, s8=s8: e.scalar_tensor_tensor(out=h1[:, 0, s8, :], in0=h0[:, 0, s8, :], scalar=sm[:, 0, 6:7], in1=h1[:, 0, s8, :], op0=ALU.mult, op1=ALU.add), r=[B("h0"), B("sm", 0), B("h1")], w=[B("h1")])
                        V(lambda e, s8=s8: e.scalar_tensor_tensor(out=h1[:, 0, s8, :], in0=h0[:, 1, s8, :], scalar=sm[:, 1, 7:8], in1=h1[:, 0, s8, :], op0=ALU.mult, op1=ALU.add), r=[B("h0"), B("sm", 1), B("h1")], w=[B("h1")])
                        V(lambda e, s8=s8: e.scalar_tensor_tensor(out=h1[:, 1, s8, :], in0=h0[:, 1, s8, :], scalar=sm[:, 0, 6:7], in1=h1[:, 1, s8, :], op0=ALU.mult, op1=ALU.add), r=[B("h0"), B("sm", 0), B("h1")], w=[B("h1")])
                        V(lambda e, s8=s8: e.scalar_tensor_tensor(out=h1[:, 1, s8, :], in0=h0[:, 0, s8, :], scalar=sm[:, 0, 7:8], in1=h1[:, 1, s8, :], op0=ALU.mult, op1=ALU.add), r=[B("h0"), B("sm", 0), B("h1")], w=[B("h1")])
                    Q(lambda e, l=l: e.dma_start(out=ossm_s[l, :, :].rearrange("p (c s b) -> p c s b", c=2, s=8), in_=h1[:]), r=[B("h1")], w=[B("ossm_s", l)])
                    V(lambda e: e.tensor_copy(out=h1b[:], in_=h1[:]), r=[B("h1")], w=[B("h1b")])
                    ssm_out(T, NS, lambda s8: h1b[:, 0, s8, :], lambda s8: h1b[:, 1, s8, :], 0, [B("h1b")])
                    yield

                s_out = [wload("M", w_out[l, :, i * 512:(i + 1) * 512].rearrange("(k p) n -> p k n", p=128), "p (k n) -> p k n", wid=l * 23 + 3 + i, ch=ch, k=8) for i in range(2)]
                for (c0, cn) in ccs:
                    for dt_ in range(8):
                        s = s_out[dt_ // 4]
                        off = (dt_ % 4) * 128
                        pbk = bank()
                        PE(lambda e, dt_=dt_, pbk=pbk: e.matmul(ps[pbk][:, 0:cn], lhsT=ident[:], rhs=x[:, dt_, c0:c0 + cn], start=True, stop=False), r=[cB, B("x", par, c0, dt_)], w=[B("ps", pbk)])
                        for k in range(8):
                            PE(lambda e, k=k, s=s, off=off, pbk=pbk: e.matmul(ps[pbk][:, 0:cn], lhsT=ring[:, s, k * 512 + off:k * 512 + off + 128], rhs=hb[:, k, c0:c0 + cn], start=False, stop=(k == 7)), r=[B(SL, s), B("hb", par, c0)], w=[B("ps", pbk)])
                        S(lambda e, dt_=dt_, pbk=pbk: e.activation(out=x[:, dt_, c0:c0 + cn], in_=ps[pbk][:, 0:cn], func=AF.Copy), r=[B("ps", pbk)], w=[B("x", par, c0, dt_)])
                        yield

        def gen_ffn(ch, l):
            par = ch % 2; x = X[par]; hb = HB[par]
            first, last = (ch == 0), (ch == nch - 1)
            ccs = [(0, 512)] + ([(T, NS)] if first else [])
            sq, rstd = sq2, rstd2
            ring, SL = ringF, "slotF"
            bank = fbank
            if True:
                rmsnorm(x, hb, par, l * PVL + OG2, ccs, sq2, rstd2, fbank(), "f")
                for fg in range(6):
                    nf = 4 if fg < 5 else 2
                    sg_ = wload("F", w_g[l, :, fg * 512:fg * 512 + nf * 128].rearrange("(k p) n -> p k n", p=128), "p (k n) -> p k n", wid=l * 23 + 5 + fg * 3, ch=ch, k=8)
                    su_ = wload("F", w_u[l, :, fg * 512:fg * 512 + nf * 128].rearrange("(k p) n -> p k n", p=128), "p (k n) -> p k n", wid=l * 23 + 6 + fg * 3, ch=ch, k=8)
                    sd_ = wload("F", w_d[l, fg * 512:fg * 512 + nf * 128, :].rearrange("(f p) n -> p f n", p=128), "p (f n) -> p f n", wid=l * 23 + 7 + fg * 3, ch=ch, f=nf)
                    W_ = nf * 128
                    for (c0, cn) in ccs:
                        for f in range(nf):
                            pg, pu = bank(), bank()
                            for (s_, pb_) in ((sg_, pg), (su_, pu)):
                                for k in range(8):
                                    PE(lambda e, k=k, s_=s_, pb_=pb_, f=f: e.matmul(ps[pb_][:, 0:cn], lhsT=ring[:, s_, k * W_ + f * 128:k * W_ + f * 128 + 128], rhs=hb[:, k, c0:c0 + cn], start=(k == 0), stop=(k == 7)), r=[B(SL, s_), B("hb", par, c0)], w=[B("ps", pb_)])
                            S(lambda e, pg=pg, f=f: e.activation(out=sgb[:, f % 2, 0:cn], in_=ps[pg][:, 0:cn], func=AF.Silu), r=[B("ps", pg)], w=[B("sgb", f % 2)])
                            V(lambda e, pu=pu, f=f: e.tensor_tensor(out=hid[:, f, c0:c0 + cn], in0=ps[pu][:, 0:cn], in1=sgb[:, f % 2, 0:cn], op=ALU.mult), r=[B("ps", pu), B("sgb", f % 2)], w=[B("hid", c0)])
                            yield
                        for dt_ in range(8):
                            pbk = bank()
                            PE(lambda e, dt_=dt_, pbk=pbk: e.matmul(ps[pbk][:, 0:cn], lhsT=ident[:], rhs=x[:, dt_, c0:c0 + cn], start=True, stop=False), r=[cB, B("x", par, c0, dt_)], w=[B("ps", pbk)])
                            for f in range(nf):
                                PE(lambda e, f=f, dt_=dt_, pbk=pbk: e.matmul(ps[pbk][:, 0:cn], lhsT=ring[:, sd_, f * 1024 + dt_ * 128:f * 1024 + dt_ * 128 + 128], rhs=hid[:, f, c0:c0 + cn], start=False, stop=(f == nf - 1)), r=[B(SL, sd_), B("hid", c0)], w=[B("ps", pbk)])
                            S(lambda e, dt_=dt_, pbk=pbk: e.activation(out=x[:, dt_, c0:c0 + cn], in_=ps[pbk][:, 0:cn], func=AF.Copy), r=[B("ps", pbk)], w=[B("x", par, c0, dt_)])
                            yield
            if l == L - 1:
                for (c0, cn) in ccs:
                    for k in range(8):
                        S(lambda e, k=k: e.activation(out=sq[:, 0:cn], in_=x[:, k, c0:c0 + cn], func=AF.Square), r=[*XB(par, c0)], w=[B("sq", "f")])
                        PE(lambda e, k=k: e.matmul(ps[0][:, 0:cn], lhsT=ones_b[:], rhs=sq[:, 0:cn], start=(k == 0), stop=(k == 7)), r=[B("sq", "f"), cB], w=[B("ps", 0)])
                    S(lambda e: e.activation(out=rstd[:, 0:cn], in_=ps[0][:, 0:cn], func=AF.Sqrt, scale=1.0 / D, bias=EPS), r=[B("ps", 0)], w=[B("rstd", "f")])
                    V(lambda e: e.reciprocal(out=rstd[:, 0:cn], in_=rstd[:, 0:cn]), r=[B("rstd", "f")], w=[B("rstd", "f")])
                    for k in range(8):
                        gk = pvt[:, 4 * PVL + k: 4 * PVL + k + 1]
                        V(lambda e, k=k, gk=gk: e.scalar_tensor_tensor(out=x[:, k, c0:c0 + cn], in0=x[:, k, c0:c0 + cn], scalar=gk, in1=rstd[:, 0:cn], op0=ALU.mult, op1=ALU.mult), r=[*XB(par, c0), B("rstd", "f"), cB], w=[*XB(par, c0)])
                        if c0 < T:
                            Q(lambda e, k=k: e.dma_start(out=yT[k * 128:(k + 1) * 128, ch * T + c0:ch * T + c0 + cn], in_=x[:, k, c0:c0 + cn]), r=[*XB(par, c0)], w=[B("yT")])
                        else:
                            Q(lambda e, k=k: e.dma_start(out=ysT[k * 128:(k + 1) * 128, :], in_=x[:, k, c0:c0 + cn]), r=[*XB(par, c0)], w=[B("ysT")])
            yield

        def count_ops(gen):
            P.dry = True
            n0 = P.dryn
            for _ in gen:
                pass
            P.dry = False
            return P.dryn - n0

        def run2(ga, na, gb, nb):
            ia = ib = 0
            alive_a, alive_b = ga is not None, gb is not None
            while alive_a or alive_b:
                pick_a = alive_a and (not alive_b or ia * nb <= ib * na)
                n0 = P.nops
                if pick_a:
                    try:
                        next(ga)
                    except StopIteration:
                        alive_a = False
                    ia += P.nops - n0
                else:
                    try:
                        next(gb)
                    except StopIteration:
                        alive_b = False
                    ib += P.nops - n0

        streams = []
        for ch in range(nch):
            ph = []
            for l in range(L):
                ph.append(("m", ch, l))
                ph.append(("f", ch, l))
            streams.append(ph)
        steps = []
        t = 0
        start = {}
        for ch in range(nch):
            start[ch] = (ch // 2) * 2 * L + (ch % 2)
        nsteps = max(start[c] + 2 * L for c in range(nch))
        for t in range(nsteps):
            cur = []
            for ch in range(nch):
                p = t - start[ch]
                if 0 <= p < 2 * L:
                    cur.append(streams[ch][p])
            steps.append(cur)
        for cur in steps:
            gens = []
            for (kind, ch, l) in cur:
                mk = (lambda: gen_mix(ch, l)) if kind == "m" else (lambda: gen_ffn(ch, l))
                n = count_ops(mk())
                gens.append((mk(), max(n, 1)))
            if len(gens) == 1:
                run2(gens[0][0], gens[0][1], None, 1)
            else:
                run2(gens[0][0], gens[0][1], gens[1][0], gens[1][1])
        print('NOPS', P.nops, {k: len(v) for k, v in P.ops.items()}, flush=True)
        P.emit()
    return nc


def _alibi_tables():
    slopes = 2.0 ** (-8.0 * np.arange(1, 9, dtype=np.float32) / 8)
    qi = np.arange(128)[:, None]
    kj = np.arange(256)[None, :]
    dist = qi - kj + 128
    valid = (dist >= 0) & (dist < 128)
    ab = np.where(valid, -dist.astype(np.float32), -1.0e7).astype(np.float32)
    dj = (127 - np.arange(128)).astype(np.float32)
    sbias = (-slopes.reshape(2, 4, 1) * dj[None, None, :]).astype(np.float32)
    sbias = np.ascontiguousarray(sbias.transpose(1, 0, 2)).reshape(4, 2 * 128)
    return ab, sbias


def _fm(v):
    return np.ascontiguousarray(np.asarray(v, np.float32).reshape(-1, 128).T)


_NC_CACHE = {}


def kernel(nch=16, depth=4, **inp):
    f = lambda k: np.asarray(inp[k], np.float32)
    L = depth
    TT = nch * T
    key = (nch, depth)
    if key not in _NC_CACHE:
        _NC_CACHE[key] = build(nch, depth)
    nc = _NC_CACHE[key]
    ab, sbias = _alibi_tables()
    pv = np.zeros((128, NPV), np.float32)
    for l in range(L):
        o = l * PVL
        pv[:, o + 0:o + 8] = _fm(f("norm_mix_g")[l])
        pv[:, o + 8:o + 16] = _fm(f("norm_ffn_g")[l])
        cw = f("conv_dw_w")[l]
        for ct in range(2):
            pv[:, o + 16 + ct * 31:o + 16 + (ct + 1) * 31] = cw[:, ct * 128:(ct + 1) * 128].T
        pv[:, o + 78:o + 80] = _fm(f("conv_dw_b")[l])
        pv[:, o + 80:o + 82] = _fm(f("conv_ln_g")[l])
        pv[:, o + 82:o + 84] = _fm(f("conv_ln_b")[l])
        pv[:, o + 84:o + 86] = _fm(f("ssm_d")[l])
        pv[:, o + 86:o + 88] = _fm(f("ssm_glu_b")[l])
        pv[:, o + 88:o + 96] = _fm(f("ssm_a_re")[l].reshape(-1))
        pv[:, o + 96:o + 104] = _fm(f("ssm_a_im")[l].reshape(-1))
        pv[:, o + 104:o + 112] = _fm(np.repeat(f("ssm_log_dt")[l], 64))
        pv[:, o + 112:o + 120] = f("attn_sinks")[l][None, :]
    pv[:, 4 * PVL:4 * PVL + 8] = _fm(f("norm_final_g"))
    Bn = np.zeros((L, 128, 8, 2, 128), np.float32)
    Cn = np.zeros((L, 128, 8, 2, 128), np.float32)
    for l in range(L):
        for ci, (bk, ck_) in enumerate((("ssm_b_re", "ssm_c_re"), ("ssm_b_im", "ssm_c_im"))):
            bb = f(bk)[l]
            cc = f(ck_)[l]
            for g in range(16):
                st_, p0 = g // 2, (g % 2) * 64
                col = (g % 8) * 16
                Bn[l, p0:p0 + 64, st_, ci, col:col + 16] = bb[g]
                Cn[l, p0:p0 + 64, st_, ci, col:col + 16] = cc[g].T
    Bn = Bn.reshape(L, 128, -1)
    Cn = Cn.reshape(L, 128, -1)
    ident = np.eye(128, dtype=np.float32)
    jjv = np.broadcast_to(np.arange(1, J + 1, dtype=np.float32)[None, :], (128, J)).copy()
    xp = f("x_prompt")
    xs = f("x_sample")[:, 0, :]
    shared = {
        "w_in": f("w_in")[:L], "w_out": f("w_out")[:L], "w_g": f("w_ff_gate")[:L], "w_u": f("w_ff_up")[:L],
        "w_d": f("w_ff_down")[:L], "glu_w": f("ssm_glu_w")[:L], "pv": pv, "Bn": Bn, "Cn": Cn, "ident": ident,
        "jj": jjv, "abias": ab, "sbias": sbias,
        "sinkc": np.ascontiguousarray(f("attn_sinks")[:L].reshape(L, 2, 4).transpose(2, 0, 1).reshape(4, L * 2)),
    }
    in_maps = []
    for c in range(8):
        sq_, b0 = c // 4, c * NS
        m = dict(shared)
        m["xT"] = np.ascontiguousarray(xp[sq_, :TT, :].T)
        m["xsT"] = np.ascontiguousarray(xs[b0:b0 + NS].T)
        m["ck"] = np.ascontiguousarray(f("cache_swa_k")[:L, b0:b0 + NS].reshape(L, NS, 128, 128))
        m["cv"] = np.ascontiguousarray(f("cache_swa_v")[:L, b0:b0 + NS].reshape(L, NS, 128, 128))
        cc_ = f("cache_conv")[:L, b0:b0 + NS]
        m["cconv"] = np.ascontiguousarray(cc_.reshape(L, NS, 30, 2, 128).transpose(0, 4, 3, 1, 2)).reshape(L, 128, -1)
        for nm, kk in (("sre", "state_ssm_re"), ("sim", "state_ssm_im")):
            s_ = f(kk)[:L, b0:b0 + NS].reshape(L, NS, 8, 128)
            m[nm] = np.ascontiguousarray(s_.transpose(0, 3, 2, 1)).reshape(L, 128, -1)
        in_maps.append(m)
    res = run_bass_kernel_spmd(nc, in_maps, core_ids=list(range(8))).results
    y_p = np.stack([res[0]["yT"].T, res[4]["yT"].T]).astype(np.float32)
    y_s = np.concatenate([res[c]["ysT"].T for c in range(8)])[:, None, :].astype(np.float32)
    pc = (res[0], res[4])
    k_p = np.stack([np.stack([r["ok_p"][l].reshape(128, 2, 64) for r in pc]) for l in range(L)])
    v_p = np.stack([np.stack([r["ov_p"][l].reshape(128, 2, 64) for r in pc]) for l in range(L)])
    conv_p = np.stack([np.stack([r["oconv_p"][l].reshape(128, 2, 30).transpose(2, 1, 0).reshape(30, 256) for r in pc]) for l in range(L)])
    ssm_p = [np.stack([np.stack([r["ossm_p"][l].reshape(128, 8, 2)[:, :, ci].T.reshape(16, 64) for r in pc]) for l in range(L)]) for ci in range(2)]
    k_s = np.concatenate([res[c]["ok_s"].reshape(L, NS, 128, 2, 64) for c in range(8)], 1)
    v_s = np.concatenate([res[c]["ov_s"].reshape(L, NS, 128, 2, 64) for c in range(8)], 1)
    conv_s = np.concatenate([res[c]["oconv_s"].reshape(L, 128, 2, NS, 30).transpose(0, 3, 4, 2, 1).reshape(L, NS, 30, 256) for c in range(8)], 1)
    ssm_s = [np.concatenate([res[c]["ossm_s"].reshape(L, 128, 2, 8, NS)[:, :, ci].transpose(0, 3, 2, 1).reshape(L, NS, 16, 64) for c in range(8)], 1) for ci in range(2)]
    outs = (y_p, y_s, k_p, v_p, conv_p, ssm_p[0], ssm_p[1], k_s, v_s, conv_s, ssm_s[0], ssm_s[1])
    return tuple(np.ascontiguousarray(o, dtype=np.float32) for o in outs)
```

```python
import math
import os
import numpy as np
from contextlib import ExitStack
import concourse.bass as bass
import concourse.mybir as mybir
from concourse.bass_utils import run_bass_kernel_spmd

F32 = mybir.dt.float32
BF16 = mybir.dt.bfloat16
I32 = mybir.dt.int32
AF = mybir.ActivationFunctionType
ALU = mybir.AluOpType
AX = mybir.AxisListType

D = 1024
SEQ = 8192
T = 512
NS = 16
NTM = T + NS
DFF = 2816
NF = 22
J = 256
NSLOT = 4
PVL = 120
NPV = 4 * PVL + 8
EPS = 1e-6
TWO_PI = 2.0 * math.pi


import types


def _freeze(fn):
    if fn.__closure__ is None:
        return fn
    cells = []
    for c in fn.__closure__:
        try:
            cells.append(types.CellType(c.cell_contents))
        except ValueError:
            cells.append(c)
    return types.FunctionType(fn.__code__, fn.__globals__, fn.__name__, fn.__defaults__, tuple(cells))


class _Stub:
    def __init__(self):
        self.closed = True

    def matmul(self, *a, **kw):
        self.closed = bool(kw.get("stop", True))
        return self

    def transpose(self, *a, **kw):
        self.closed = True
        return self

    def then_inc(self, *a, **kw):
        return self


class Buf:
    __slots__ = ("name", "lw", "rd")

    def __init__(self, name):
        self.name = name
        self.lw = None
        self.rd = {}


class Prog:
    ENG = ["tensor", "vector", "scalar", "gpsimd", "sync"]

    def __init__(self, nc, same_eng_sync=("vector", "scalar", "gpsimd")):
        self.nc = nc
        self.ops = {e: [] for e in self.ENG}
        self.cnt = {}
        self.known = {e: {} for e in self.ENG}
        self.same = set(same_eng_sync)
        self.bufs = {}
        self.dry = False
        self.dq = {}
        self.dryn = 0
        self.nops = 0

    def B(self, *key):
        b = self.bufs.get(key)
        if b is None:
            b = self.bufs[key] = Buf(key)
        return b

    def op(self, eng, fn, r=(), w=(), dma=None, inc=None):
        if self.dry:
            self.dryn += 1
            return None
        fn = _freeze(fn)
        if getattr(self, "stopped", False):
            return None
        self.nops += 1
        if eng == "tensor" and "KLIMIT" in os.environ:
            st_ = _Stub()
            try:
                fn(st_)
            except Exception:
                pass
            self.open_grp = not st_.closed
        if self.nops >= int(os.environ.get("KLIMIT", "100000000")) and not getattr(self, "open_grp", False):
            self.stopped = True
        ex = [b for b in r if b.name[0] in ("ps", "psb")]
        if ex:
            r = [b for b in r if b.name[0] not in ("ps", "psb")]
            w = list(w) + ex
        deps = {}
        for b in list(r) + list(w):
            if b.lw is not None and deps.get(b.lw[0], 0) < b.lw[1]:
                deps[b.lw[0]] = b.lw[1]
        for b in w:
            for s, v in b.rd.items():
                if deps.get(s, 0) < v:
                    deps[s] = v
        pre = None
        if dma is None:
            sem, step = eng, 1
        else:
            npool = 32 if eng == "sync" else 8
            i = self.dq.get(eng, 0)
            self.dq[eng] = i + 1
            sem, step = "d%s%d" % (eng[0], i % npool), 16
            if i >= npool:
                pre = (sem, 16 * (i // npool))
        waits = []
        if pre is not None and deps.get(pre[0], 0) < pre[1]:
            deps[pre[0]] = pre[1]
        for s, v in deps.items():
            if s == eng and eng not in self.same:
                continue
            if self.known[eng].get(s, 0) >= v:
                continue
            self.known[eng][s] = v
            waits.append((s, v))
        self.cnt[sem] = self.cnt.get(sem, 0) + step
        tok = (sem, self.cnt[sem])
        self.ops[eng].append((fn, waits, sem, step))
        for b in r:
            if b.rd.get(sem, 0) < tok[1]:
                b.rd[sem] = tok[1]
        for b in w:
            b.lw = tok
            b.rd = {}
        return tok

    def emit(self):
        nc = self.nc
        with ExitStack() as st:
            sems = {s: st.enter_context(nc.semaphore(s)) for s in self.cnt}
            block = st.enter_context(nc.Block())
            final = dict(self.cnt)

            def mk(engname):
                def body(e):
                    for fn, waits, sem, step in self.ops[engname]:
                        for s, v in waits:
                            e.wait_ge(sems[s], v)
                        fn(e).then_inc(sems[sem], step)
                    if engname == "sync":
                        for s, v in final.items():
                            e.wait_ge(sems[s], v)
                return body

            for engname in self.ENG:
                if self.ops[engname] or engname == "sync":
                    getattr(block, engname)(mk(engname))


def build(nch=16, depth=4):
    nc = bass.Bass("TRN2", target_bir_lowering=False)
    L = depth
    TT = nch * T

    def din(name, shape, dt=F32):
        return nc.dram_tensor(name, list(shape), dt, kind="ExternalInput").ap()

    def dout(name, shape, dt=F32):
        return nc.dram_tensor(name, list(shape), dt, kind="ExternalOutput").ap()

    xT = din("xT", [D, TT]); xsT = din("xsT", [D, NS])
    w_in = din("w_in", [L, D, 1536]); w_out = din("w_out", [L, D, D])
    w_g = din("w_g", [L, D, DFF]); w_u = din("w_u", [L, D, DFF]); w_d = din("w_d", [L, DFF, D])
    glu_w = din("glu_w", [L, 256, 256])
    pv_d = din("pv", [128, NPV])
    Bn_d = din("Bn", [L, 128, 8 * 2 * 128]); Cn_d = din("Cn", [L, 128, 8 * 2 * 128])
    ident_d = din("ident", [128, 128]); jj_d = din("jj", [128, J])
    abias_d = din("abias", [128, 256]); sbias_d = din("sbias", [4, 2 * 128]); sinkc_d = din("sinkc", [4, L * 2])
    ck_d = din("ck", [L, NS, 128, 128]); cv_d = din("cv", [L, NS, 128, 128])
    cconv_d = din("cconv", [L, 128, 2 * NS * 30])
    sre_d = din("sre", [L, 128, 8 * NS]); sim_d = din("sim", [L, 128, 8 * NS])

    yT = dout("yT", [D, TT]); ysT = dout("ysT", [D, NS])
    ok_p = dout("ok_p", [L, 128, 128]); ov_p = dout("ov_p", [L, 128, 128])
    oconv_p = dout("oconv_p", [L, 128, 2 * 30]); ossm_p = dout("ossm_p", [L, 128, 16])
    ok_s = dout("ok_s", [L, NS, 128, 128]); ov_s = dout("ov_s", [L, NS, 128, 128])
    oconv_s = dout("oconv_s", [L, 128, 2 * NS * 30]); ossm_s = dout("ossm_s", [L, 128, 2 * 8 * NS])
    cdg_d = nc.dram_tensor("cdg_scr", [L * 2, 128, 4096], BF16).ap()
    wscr = nc.dram_tensor("w_scr", [L * 23, 128, 4096], BF16).ap()
    rot_d = nc.dram_tensor("rot_scr", [L, 128, 8 * 2 * J], F32).ap()
    wb_d = nc.dram_tensor("wb_scr", [L, 128, 16 * 128], BF16).ap()
    wc_d = nc.dram_tensor("wc_scr", [L, 128, 16 * 128], BF16).ap()

    P = Prog(nc)
    B = P.B
    with ExitStack() as st:
        def sb(name, shape, dt=F32):
            return st.enter_context(nc.sbuf_tensor("sb_" + name, list(shape), dt))

        def pst(name, shape, dt=F32):
            return st.enter_context(nc.psum_tensor(name, list(shape), dt))

        x = sb("x", [128, 8, NTM]); hb = sb("hb", [128, 8, NTM], BF16)
        x1 = sb("x1", [128, 8, T]); hb1 = sb("hb1", [128, 8, T], BF16)
        X = [x, x1]; HB = [hb, hb1]
        qT = sb("qT", [64, 8, NTM], BF16); kT = sb("kT", [64, 2, 128 + NTM], BF16)
        vp = sb("vp", [128, 5, 2, 192], BF16)
        ucv = sb("ucv", [128, 2, 30 + NTM], BF16); cstg = sb("cstg", [128, 2, 30]); acc = sb("acc", [128, 2, NTM]); us = sb("us", [128, 2, NTM], BF16)
        hid = sb("hid", [128, 4, NTM], BF16)
        ringM = sb("ringM", [128, 3, 4096], BF16); ringF = sb("ringF", [128, 3, 4096], BF16)
        pvt = sb("pvt", [128, NPV])
        ident = sb("ident", [128, 128]); identb = sb("identb", [128, 128], BF16)
        ones_f = sb("ones_f", [128, 128]); ones_b = sb("ones_b", [128, 128], BF16)
        abias = sb("abias", [128, 256]); sbias = sb("sbias", [4, 2, 128])
        WB = sb("WB", [128, 16, 128], BF16); WC = sb("WC", [128, 16, 128], BF16)
        Dd = sb("Dd", [128, 2, 128], BF16); gluw = sb("gluw", [128, 2, 256], BF16)
        lam = sb("lam", [128, L, 4, 8])
        rot2 = sb("rot2", [128, 2, 2, J])
        hcar = sb("hcar", [128, L, 8, 2])
        khalo = sb("khalo", [64, L, 2, 128], BF16); vhalo = sb("vhalo", [128, L, 2, 192], BF16)
        chalo = sb("chalo", [128, L, 2, 30], BF16)
        sq = sb("sq", [128, 512], BF16); rstd = sb("rstd", [128, 512])
        sq2 = sb("sq2", [128, 512], BF16); rstd2 = sb("rstd2", [128, 512])
        sgb = sb("sgb", [128, 2, 512], BF16)
        sc = sb("sc", [128, 3, 256]); pb = sb("pb", [128, 3, 256], BF16); ptb = sb("ptb", [128, 2, 256], BF16)
        sm = sb("sm", [128, 3, 8])
        t1 = sb("t1", [128, J]); t2 = sb("t2", [128, J]); bre = sb("bre", [128, J]); bim = sb("bim", [128, J])
        wre = sb("wre", [128, J]); wim = sb("wim", [128, J])
        ysm = sb("ysm", [128, 2, 512]); yb = sb("yb", [128, 2, 512], BF16); g1 = sb("g1", [128, 512]); g2 = sb("g2", [128, 512])
        ysq = ysm; cst = ysm[:].rearrange("p c n -> p (c n)")[:, 0:2 * NS * 31].rearrange("p (c b k) -> p c b k", c=2, b=NS); mean = sb("mean", [128, 512]); var = g2
        jj = mean[:, 0:J]
        xflat = x[:].rearrange("p k n -> p (k n)")
        rotflat = x1[:].rearrange("p k n -> p (k n)")[:, 0:16 * J]
        big = xflat[:, 0:4096]; big2 = rotflat[:, 8 * J:16 * J]; bigi = sb("bigi", [128, 512], I32)[:]
        hbf = sb("hbf", [128, 16 * J], BF16)
        pl = sb("pl", [128, 16, 8])
        tk = sb("tk", [128, 128]); tkb = sb("tkb", [NS, 2, 128])
        Kb = sb("Kb", [128, 1, 128]); Vb = sb("Vb", [128, 1, 128]); KbT = sb("KbT", [64, 2, 128], BF16)
        Vbp = sb("Vbp", [128, 2, 192], BF16)
        ssc = sb("ssc", [4, 4, 128]); spb = sb("spb", [4, 4, 128], BF16); ssm_ = sb("ssm_", [4, 4, 4]); sinkt = sb("sinkt", [4, L * 2])
        sptb = sb("sptb", [128, 2, NS, 4], BF16)
        cs = sb("cs", [128, 2, NS, 31])
        h0 = sb("h0", [128, 2, 8, NS]); h1 = sb("h1", [128, 2, 8, NS]); h1b = sb("h1b", [128, 2, 8, NS], BF16)

        ps = [pst("ps%d" % i, [128, 512]) for i in range(7)]
        psb = pst("psb", [128, 1024], BF16)

        Q = lambda fn, r=(), w=(), ch="io": P.op("sync", fn, r, w, dma=ch)
        G = lambda fn, r=(), w=(), ch="w": P.op("gpsimd", fn, r, w, dma=ch)
        V = lambda fn, r=(), w=(): P.op("vector", fn, r, w)
        S = lambda fn, r=(), w=(): P.op("scalar", fn, r, w)
        PE = lambda fn, r=(), w=(): P.op("tensor", fn, r, w)
        GP = lambda fn, r=(), w=(): P.op("gpsimd", fn, r, w)

        def XB(par, c0):
            return [B("x", par, c0)] + [B("x", par, c0, d) for d in range(8)]

        def pvc(l, off, n=1):
            return pvt[:, l * PVL + off: l * PVL + off + n]

        OG1, OG2, OCW, OCB, OLG, OLB, OSD, OGB, OARE, OAIM, OLDT, OSINK = 0, 8, 16, 78, 80, 82, 84, 86, 88, 96, 104, 112
        rr = [0]

        def bank():
            rr[0] = (rr[0] + 1) % 3
            return 4 + rr[0]

        fr = [0]

        def fbank():
            fr[0] = (fr[0] + 1) % 4
            return fr[0]

        cB = B("const")
        Q(lambda e: e.dma_start(out=pvt[:], in_=pv_d[:, :]), w=[cB])
        Q(lambda e: e.dma_start(out=ident[:], in_=ident_d[:, :]), w=[cB])
        Q(lambda e: e.dma_start(out=jj[:], in_=jj_d[:, :]), w=[cB])
        Q(lambda e: e.dma_start(out=abias[:], in_=abias_d[:, :]), w=[cB])
        Q(lambda e: e.dma_start(out=sbias[:].rearrange("p h k -> p (h k)"), in_=sbias_d[:, :]), w=[cB])
        Q(lambda e: e.dma_start(out=sinkt[:], in_=sinkc_d[:, :]), w=[cB])
        V(lambda e: e.memset(ones_f[:], 1.0), w=[cB])
        V(lambda e: e.memset(ones_b[:], 1.0), w=[cB])
        V(lambda e: e.tensor_copy(out=identb[:], in_=ident[:]), r=[cB], w=[B("identb")])
        V(lambda e: e.memset(vp[:].rearrange("p a g c -> p (a g c)"), 0.0), w=[B("vp", i) for i in range(5)])
        V(lambda e: e.memset(vhalo[:].rearrange("p l g c -> p (l g c)"), 0.0), w=[B("vhalo", l) for l in range(L)])
        V(lambda e: e.memset(Vbp[:].rearrange("p g c -> p (g c)"), 0.0), w=[B("Vbp")])
        V(lambda e: e.memset(hcar[:].rearrange("p l s c -> p (l s c)"), 0.0), w=[B("hcar", l) for l in range(L)])
        V(lambda e: e.memset(chalo[:].rearrange("p l c k -> p (l c k)"), 0.0), w=[B("chalo", l) for l in range(L)])

        for l in range(L):
            pB = B("pl")
            are, aim, ldt = pvc(l, OARE, 8), pvc(l, OAIM, 8), pvc(l, OLDT, 8)
            c = lambda i: pl[:, i, :]
            S(lambda e: e.activation(out=c(0), in_=ldt, func=AF.Exp), r=[cB], w=[pB])
            V(lambda e: e.tensor_tensor(out=c(1), in0=are, in1=c(0), op=ALU.mult), r=[pB], w=[pB])
            V(lambda e: e.tensor_tensor(out=c(2), in0=aim, in1=c(0), op=ALU.mult), r=[pB], w=[pB])
            S(lambda e, l=l: e.activation(out=lam[:, l, 0, :], in_=c(1), func=AF.Exp), r=[pB], w=[B("lam", l)])
            V(lambda e: e.tensor_scalar(out=c(3), in0=c(2), scalar1=1.0 / TWO_PI, scalar2=None, op0=ALU.mult), r=[pB], w=[pB])
            V(lambda e: e.tensor_copy(out=bigi[:, 0:8], in_=c(3)), r=[pB], w=[pB])
            V(lambda e: e.tensor_copy(out=c(4), in_=bigi[:, 0:8]), r=[pB], w=[pB])
            V(lambda e: e.tensor_tensor(out=c(4), in0=c(3), in1=c(4), op=ALU.subtract), r=[pB], w=[pB])
            S(lambda e, l=l: e.activation(out=lam[:, l, 2, :], in_=c(4), func=AF.Sin, scale=6.283185), r=[pB], w=[B("lam", l)])
            V(lambda e: e.tensor_scalar(out=c(3), in0=c(3), scalar1=0.25, scalar2=None, op0=ALU.add), r=[pB], w=[pB])
            V(lambda e: e.tensor_copy(out=bigi[:, 0:8], in_=c(3)), r=[pB], w=[pB])
            V(lambda e: e.tensor_copy(out=c(4), in_=bigi[:, 0:8]), r=[pB], w=[pB])
            V(lambda e: e.tensor_tensor(out=c(4), in0=c(3), in1=c(4), op=ALU.subtract), r=[pB], w=[pB])
            S(lambda e, l=l: e.activation(out=lam[:, l, 1, :], in_=c(4), func=AF.Sin, scale=6.283185), r=[pB], w=[B("lam", l)])
            V(lambda e, l=l: e.tensor_tensor(out=c(5), in0=lam[:, l, 0, :], in1=lam[:, l, 1, :], op=ALU.mult), r=[pB, B("lam", l)], w=[pB])
            V(lambda e, l=l: e.tensor_tensor(out=c(6), in0=lam[:, l, 0, :], in1=lam[:, l, 2, :], op=ALU.mult), r=[pB, B("lam", l)], w=[pB])
            V(lambda e: e.tensor_scalar(out=c(7), in0=c(5), scalar1=-1.0, scalar2=None, op0=ALU.add), r=[pB], w=[pB])
            V(lambda e: e.tensor_tensor(out=c(8), in0=are, in1=are, op=ALU.mult), r=[pB], w=[pB])
            V(lambda e: e.tensor_tensor(out=c(9), in0=aim, in1=aim, op=ALU.mult), r=[pB], w=[pB])
            V(lambda e: e.tensor_tensor(out=c(8), in0=c(8), in1=c(9), op=ALU.add), r=[pB], w=[pB])
            V(lambda e: e.reciprocal(out=c(8), in_=c(8)), r=[pB], w=[pB])
            V(lambda e: e.tensor_tensor(out=c(9), in0=c(7), in1=are, op=ALU.mult), r=[pB], w=[pB])
            V(lambda e: e.tensor_tensor(out=c(10), in0=c(6), in1=aim, op=ALU.mult), r=[pB], w=[pB])
            V(lambda e: e.tensor_tensor(out=c(9), in0=c(9), in1=c(10), op=ALU.add), r=[pB], w=[pB])
            V(lambda e: e.tensor_tensor(out=c(11), in0=c(9), in1=c(8), op=ALU.mult), r=[pB], w=[pB])
            V(lambda e: e.tensor_tensor(out=c(9), in0=c(6), in1=are, op=ALU.mult), r=[pB], w=[pB])
            V(lambda e: e.tensor_tensor(out=c(10), in0=c(7), in1=aim, op=ALU.mult), r=[pB], w=[pB])
            V(lambda e: e.tensor_tensor(out=c(9), in0=c(9), in1=c(10), op=ALU.subtract), r=[pB], w=[pB])
            V(lambda e: e.tensor_tensor(out=c(12), in0=c(9), in1=c(8), op=ALU.mult), r=[pB], w=[pB])
            V(lambda e: e.tensor_scalar(out=c(13), in0=c(12), scalar1=-1.0, scalar2=None, op0=ALU.mult), r=[pB], w=[pB])
            bg = big[:, 0:2048].rearrange("p (s c k) -> p s c k", s=8, c=2)
            Q(lambda e, l=l: e.dma_start(out=big[:, 0:2048], in_=Bn_d[l, :, :]), r=[pB], w=[B("big")])
            for s8 in range(8):
                fre, fim, nfim = pl[:, 11, s8:s8 + 1], pl[:, 12, s8:s8 + 1], pl[:, 13, s8:s8 + 1]
                V(lambda e, s8=s8, fre=fre: e.tensor_scalar(out=t1[:, 0:128], in0=bg[:, s8, 0, :], scalar1=fre, scalar2=None, op0=ALU.mult), r=[pB, B("big")], w=[B("t1")])
                V(lambda e, s8=s8, nfim=nfim: e.scalar_tensor_tensor(out=t1[:, 0:128], in0=bg[:, s8, 1, :], scalar=nfim, in1=t1[:, 0:128], op0=ALU.mult, op1=ALU.add), r=[pB, B("big"), B("t1")], w=[B("t1")])
                V(lambda e, s8=s8, fre=fre: e.tensor_scalar(out=t2[:, 0:128], in0=bg[:, s8, 1, :], scalar1=fre, scalar2=None, op0=ALU.mult), r=[pB, B("big")], w=[B("t2")])
                V(lambda e, s8=s8, fim=fim: e.scalar_tensor_tensor(out=t2[:, 0:128], in0=bg[:, s8, 0, :], scalar=fim, in1=t2[:, 0:128], op0=ALU.mult, op1=ALU.add), r=[pB, B("big"), B("t2")], w=[B("t2")])
                PE(lambda e: e.transpose(out=ps[5][:, 0:128], in_=t1[:, 0:128], identity=ident[:]), r=[B("t1"), cB], w=[B("ps", 5)])
                PE(lambda e: e.transpose(out=ps[5][:, 128:256], in_=t2[:, 0:128], identity=ident[:]), r=[B("t2"), cB], w=[B("ps", 5)])
                S(lambda e, l=l, s8=s8: e.activation(out=WB[:, s8 * 2:s8 * 2 + 2, :], in_=ps[5][:, 0:256].rearrange("p (c k) -> p c k", c=2), func=AF.Copy), r=[B("ps", 5)], w=[B("WB")])
            Q(lambda e, l=l: e.dma_start(out=big[:, 0:2048], in_=Cn_d[l, :, :]), w=[B("big")])
            V(lambda e, l=l: e.tensor_copy(out=WC[:, 0:16, :].rearrange("p (s c) k -> p s c k", c=2)[:, :, 0, :], in_=bg[:, :, 0, :]), r=[B("big")], w=[B("WC")])
            V(lambda e, l=l: e.tensor_scalar(out=WC[:, 0:16, :].rearrange("p (s c) k -> p s c k", c=2)[:, :, 1, :], in0=bg[:, :, 1, :], scalar1=-1.0, scalar2=None, op0=ALU.mult), r=[B("big")], w=[B("WC")])
            for ct in range(2):
                for kk in range(31):
                    wk = pvt[:, l * PVL + OCW + ct * 31 + kk: l * PVL + OCW + ct * 31 + kk + 1]
                    V(lambda e, kk=kk, wk=wk: e.tensor_scalar(out=hbf[:, kk * 128:(kk + 1) * 128], in0=ident[:], scalar1=wk, scalar2=None, op0=ALU.mult), r=[cB], w=[B("hbf")])
                Q(lambda e, l=l, ct=ct: e.dma_start(out=cdg_d[l * 2 + ct, :, 0:3968], in_=hbf[:, 0:3968]), r=[B("hbf")], w=[B("cdg_d", l, ct)])
            a8 = big2.rearrange("p (s j) -> p s j", s=8)
            for s8 in range(8):
                V(lambda e, s8=s8: e.tensor_scalar(out=a8[:, s8, :], in0=jj[:], scalar1=pl[:, 2, s8:s8 + 1], scalar2=1.0 / TWO_PI, op0=ALU.mult, op1=ALU.mult), r=[pB, cB], w=[B("big2"), B("rot")])
            rt = big.rearrange("p (s c j) -> p s c j", s=8, c=2)
            for ci, sh in ((1, 0.0), (0, 0.25)):
                if sh:
                    V(lambda e, sh=sh: e.tensor_scalar(out=big2, in0=big2, scalar1=sh, scalar2=None, op0=ALU.add), r=[B("big2")], w=[B("big2"), B("rot")])
                for pc in range(8 * J // 512):
                    V(lambda e, pc=pc: e.tensor_copy(out=bigi, in_=big2[:, pc * 512:(pc + 1) * 512]), r=[B("big2"), B("rot")], w=[B("bigi")])
                    V(lambda e, pc=pc: e.tensor_copy(out=rotflat[:, pc * 512:(pc + 1) * 512], in_=bigi), r=[B("bigi")], w=[B("rot")])
                V(lambda e: e.tensor_tensor(out=rotflat[:, 0:8 * J], in0=big2, in1=rotflat[:, 0:8 * J], op=ALU.subtract), r=[B("big2"), B("rot")], w=[B("rot")])
                S(lambda e, ci=ci: e.activation(out=rt[:, :, ci, :], in_=rotflat[:, 0:8 * J].rearrange("p (s j) -> p s j", s=8), func=AF.Sin, scale=6.283185), r=[B("rot")], w=[B("big")])
            Q(lambda e, l=l: e.dma_start(out=rot_d[l, :, :], in_=big), r=[B("big")], w=[B("rot_d", l)])
            Q(lambda e, l=l: e.dma_start(out=wb_d[l, :, :], in_=WB[:].rearrange("p a k -> p (a k)")), r=[B("WB")], w=[B("wb_d", l)])
            Q(lambda e, l=l: e.dma_start(out=wc_d[l, :, :], in_=WC[:].rearrange("p a k -> p (a k)")), r=[B("WC")], w=[B("wc_d", l)])

        print('MARK prologue_end', P.nops, flush=True)
        slot_i = {"M": 0, "F": 0}

        def wload(which, src_ap, shape_str, wid=None, ch=0, **kw):
            ring = ringM if which == "M" else ringF
            s = slot_i[which] % 3
            if not P.dry:
                slot_i[which] += 1
            n = 1
            for d_ in src_ap.shape[1:]:
                n *= d_
            if ch == 0:
                dst = ring[:, s, 0:n]
                if shape_str:
                    dst = dst.rearrange(shape_str, **kw)
                G(lambda e: e.dma_start(out=dst, in_=src_ap), w=[B("slot" + which, s)])
                if nch > 1:
                    Q(lambda e: e.dma_start(out=wscr[wid, :, 0:n], in_=ring[:, s, 0:n]), r=[B("slot" + which, s)], w=[B("wscr", wid)])
            else:
                G(lambda e: e.dma_start(out=ring[:, s, 0:n], in_=wscr[wid, :, 0:n]), r=[B("wscr", wid)], w=[B("slot" + which, s)])
            return s

        def rmsnorm(x, hb, par, goff, ccs, sq, rstd, pbk, tag):
            for (c0, cn) in ccs:
                pb_ = ps[pbk]
                for k in range(8):
                    S(lambda e, k=k: e.activation(out=sq[:, 0:cn], in_=x[:, k, c0:c0 + cn], func=AF.Square), r=[*XB(par, c0)], w=[B("sq", tag)])
                    PE(lambda e, k=k: e.matmul(pb_[:, 0:cn], lhsT=ones_b[:], rhs=sq[:, 0:cn], start=(k == 0), stop=(k == 7)), r=[B("sq", tag), cB], w=[B("ps", pbk)])
                S(lambda e: e.activation(out=rstd[:, 0:cn], in_=pb_[:, 0:cn], func=AF.Sqrt, scale=1.0 / D, bias=EPS), r=[B("ps", pbk)], w=[B("rstd", tag)])
                V(lambda e: e.reciprocal(out=rstd[:, 0:cn], in_=rstd[:, 0:cn]), r=[B("rstd", tag)], w=[B("rstd", tag)])
                for k in range(8):
                    gk = pvt[:, goff + k: goff + k + 1]
                    V(lambda e, k=k, gk=gk: e.scalar_tensor_tensor(out=hb[:, k, c0:c0 + cn], in0=x[:, k, c0:c0 + cn], scalar=gk, in1=rstd[:, 0:cn], op0=ALU.mult, op1=ALU.mult), r=[*XB(par, c0), B("rstd", tag), cB], w=[B("hb", par, c0)])

        def gen_mix(ch, l):
            par = ch % 2; x = X[par]; hb = HB[par]
            ring, SL = ringM, "slotM"
            first, last = (ch == 0), (ch == nch - 1)
            ccs = [(0, 512)] + ([(T, NS)] if first else [])
            if l == 0:
                for k in range(8):
                    Q(lambda e, k=k: e.dma_start(out=x[:, k, 0:T], in_=xT[k * 128:(k + 1) * 128, ch * T:(ch + 1) * T]), w=[*XB(par, 0), B("big"), B("rot"), B("big2")])
                if first:
                    Q(lambda e: e.dma_start(out=x[:, :, T:NTM], in_=xsT.rearrange("(k p) n -> p k n", p=128)), w=[*XB(par, T), B("big")])
                yield
            if True:
                Q(lambda e, l=l: e.dma_start(out=WB[:].rearrange("p a k -> p (a k)"), in_=wb_d[l, :, :]), r=[B("wb_d", l)], w=[B("WB")])
                Q(lambda e, l=l: e.dma_start(out=WC[:].rearrange("p a k -> p (a k)"), in_=wc_d[l, :, :]), r=[B("wc_d", l)], w=[B("WC")])
                for ct in range(2):
                    V(lambda e, ct=ct: e.tensor_scalar(out=Dd[:, ct, :], in0=ident[:], scalar1=pvc(l, OSD + ct), scalar2=None, op0=ALU.mult), r=[cB], w=[B("Dd")])
                G(lambda e: e.dma_start(out=gluw[:], in_=glu_w[l].rearrange("(c p) n -> p c n", p=128)), w=[B("gluw")])
                s_in = [wload("M", w_in[l, :, i * 512:(i + 1) * 512].rearrange("(k p) n -> p k n", p=128), "p (k n) -> p k n", wid=l * 23 + i, ch=ch, k=8) for i in range(3)]

                def win(o_lo, o_n):
                    s = s_in[o_lo // 512]
                    off = o_lo % 512
                    return lambda k: ring[:, s, k * 512 + off: k * 512 + off + o_n], B(SL, s)

                V(lambda e, l=l: e.tensor_copy(out=kT[:, :, 0:128], in_=khalo[:, l, :, :]), r=[B("khalo", l)], w=[B("kT", "h")])
                V(lambda e, l=l: e.tensor_copy(out=vp[:, 0, :, :], in_=vhalo[:, l, :, :]), r=[B("vhalo", l)], w=[B("vp", 0)])
                V(lambda e, l=l: e.tensor_copy(out=ucv[:, :, 0:30], in_=chalo[:, l, :, :]), r=[B("chalo", l)], w=[B("ucv", "h")])

                rmsnorm(x, hb, par, l * PVL + OG1, ccs, sq, rstd, 6, "m")
                for (c0, cn) in ccs:
                    def mm8(lw, n_out, pbk):
                        f, sb_ = lw
                        for k in range(8):
                            PE(lambda e, k=k: e.matmul(ps[pbk][0:n_out, 0:cn], lhsT=f(k), rhs=hb[:, k, c0:c0 + cn], start=(k == 0), stop=(k == 7)), r=[sb_, B("hb", par, c0)], w=[B("ps", pbk)])
                    for h in range(8):
                        pbk = bank(); mm8(win(64 * h, 64), 64, pbk)
                        S(lambda e, h=h, pbk=pbk: e.activation(out=qT[:, h, c0:c0 + cn], in_=ps[pbk][0:64, 0:cn], func=AF.Copy, scale=0.125), r=[B("ps", pbk)], w=[B("qT", c0)])
                        yield
                    for g in range(2):
                        pbk = bank(); mm8(win(512 + 64 * g, 64), 64, pbk)
                        S(lambda e, g=g, pbk=pbk: e.activation(out=kT[:, g, 128 + c0:128 + c0 + cn], in_=ps[pbk][0:64, 0:cn], func=AF.Copy), r=[B("ps", pbk)], w=[B("kT", c0)])
                        yield
                    for ct in range(2):
                        pa = bank(); mm8(win(768 + 128 * ct, 128), 128, pa)
                        pg = bank(); mm8(win(1024 + 128 * ct, 128), 128, pg)
                        S(lambda e, pg=pg: e.activation(out=g2[:, 0:cn], in_=ps[pg][:, 0:cn], func=AF.Sigmoid), r=[B("ps", pg)], w=[B("g2")])
                        V(lambda e, ct=ct, pa=pa: e.tensor_tensor(out=ucv[:, ct, 30 + c0:30 + c0 + cn], in0=ps[pa][:, 0:cn], in1=g2[:, 0:cn], op=ALU.mult), r=[B("ps", pa), B("g2")], w=[B("ucv", c0)])
                        yield
                    for ct in range(2):
                        pbk = bank(); mm8(win(1280 + 128 * ct, 128), 128, pbk)
                        S(lambda e, ct=ct, pbk=pbk: e.activation(out=us[:, ct, c0:c0 + cn], in_=ps[pbk][:, 0:cn], func=AF.Copy), r=[B("ps", pbk)], w=[B("us", c0)])
                        yield
                fv, sv = win(640, 128)
                fk, sk = win(512, 128)
                for bi in range(4):
                    c0 = bi * 128
                    cc0 = (c0 // 512) * 512
                    pbk = bank()
                    for k in range(8):
                        PE(lambda e, k=k: e.matmul(ps[pbk][:, 0:128], lhsT=hb[:, k, c0:c0 + 128], rhs=fv(k), start=(k == 0), stop=(k == 7)), r=[sv, B("hb", par, cc0)], w=[B("ps", pbk)])
                    S(lambda e, bi=bi, pbk=pbk: e.activation(out=vp[:, bi + 1, :, 64:128], in_=ps[pbk][:, 0:128].rearrange("p (g d) -> p g d", g=2), func=AF.Copy), r=[B("ps", pbk)], w=[B("vp", bi + 1)])
                    yield
                    if last and bi == 3:
                        V(lambda e, pbk=pbk: e.tensor_copy(out=tk[:], in_=ps[pbk][:, 0:128]), r=[B("ps", pbk), B("vp", bi + 1)], w=[B("tk")])
                        Q(lambda e, l=l: e.dma_start(out=ov_p[l, :, :], in_=tk[:]), r=[B("tk")], w=[B("ov_p", l)])
                        pbk2 = bank()
                        for k in range(8):
                            PE(lambda e, k=k: e.matmul(ps[pbk2][:, 0:128], lhsT=hb[:, k, c0:c0 + 128], rhs=fk(k), start=(k == 0), stop=(k == 7)), r=[sk, B("hb", par, cc0)], w=[B("ps", pbk2)])
                        V(lambda e, pbk2=pbk2: e.tensor_copy(out=tk[:], in_=ps[pbk2][:, 0:128]), r=[B("ps", pbk2)], w=[B("tk")])
                        Q(lambda e, l=l: e.dma_start(out=ok_p[l, :, :], in_=tk[:]), r=[B("tk")], w=[B("ok_p", l)])
                if first:
                    pbk = bank()
                    for (f_, s_, off) in ((fk, sk, 0), (fv, sv, 128)):
                        for k in range(8):
                            PE(lambda e, k=k, f_=f_, off=off: e.matmul(ps[pbk][0:NS, off:off + 128], lhsT=hb[:, k, T:NTM], rhs=f_(k), start=(k == 0), stop=(k == 7)), r=[s_, B("hb", par, T)], w=[B("ps", pbk)])
                    V(lambda e, pbk=pbk: e.tensor_copy(out=tkb[:].rearrange("p a k -> p (a k)"), in_=ps[pbk][0:NS, 0:256]), r=[B("ps", pbk)], w=[B("tkb")])
                    for b in range(NS):
                        Q(lambda e, l=l, b=b: e.dma_start(out=ok_s[l, b:b + 1, 0:127, :].rearrange("b r c -> b (r c)"), in_=ck_d[l, b:b + 1, 1:128, :].rearrange("b r c -> b (r c)")), w=[B("ok_s", l)])
                        Q(lambda e, l=l, b=b: e.dma_start(out=ov_s[l, b:b + 1, 0:127, :].rearrange("b r c -> b (r c)"), in_=cv_d[l, b:b + 1, 1:128, :].rearrange("b r c -> b (r c)")), w=[B("ov_s", l)])
                    Q(lambda e, l=l: e.dma_start(out=ok_s[l, :, 127, :], in_=tkb[:, 0, :]), r=[B("tkb")], w=[B("ok_s", l)])
                    Q(lambda e, l=l: e.dma_start(out=ov_s[l, :, 127, :], in_=tkb[:, 1, :]), r=[B("tkb")], w=[B("ov_s", l)])
                if not last:
                    V(lambda e, l=l: e.tensor_copy(out=khalo[:, l, :, :], in_=kT[:, :, T:T + 128]), r=[B("kT", 0)], w=[B("khalo", l)])
                    V(lambda e, l=l: e.tensor_copy(out=vhalo[:, l, :, :], in_=vp[:, 4, :, :]), r=[B("vp", 4)], w=[B("vhalo", l)])
                    V(lambda e, l=l: e.tensor_copy(out=chalo[:, l, :, :], in_=ucv[:, :, T:T + 30]), r=[B("ucv", 0)], w=[B("chalo", l)])
                else:
                    V(lambda e: e.tensor_copy(out=cstg[:], in_=ucv[:, :, T:T + 30]), r=[B("ucv", 0)], w=[B("cstg")])
                    Q(lambda e, l=l: e.dma_start(out=oconv_p[l, :, :].rearrange("p (c k) -> p c k", c=2), in_=cstg[:]), r=[B("cstg")], w=[B("oconv_p", l)])

                s_cd = []
                for ct in range(2):
                    s_ = slot_i["M"] % 3
                    if not P.dry:
                        slot_i["M"] += 1
                    G(lambda e, s_=s_, ct=ct: e.dma_start(out=ringM[:, s_, 0:3968], in_=cdg_d[l * 2 + ct, :, 0:3968]), r=[B("cdg_d", l, ct)], w=[B("slotM", s_)])
                    s_cd.append(s_)
                units = [(bi, tile, r2) for bi in range(4) for tile in range(4) for r2 in range(2)]

                def att_info(u):
                    bi, tile, r2 = units[u]
                    q0 = bi * 128
                    nokprev = first and bi == 0
                    k_lo, k_n = (128, 128) if nokprev else (0, 256)
                    return bi, tile, r2, tile * 2 + r2, (tile * 2 + r2) // 4, q0, nokprev, k_lo, k_n, u % 3

                def att_A1(u):
                    bi, tile, r2, h, g, q0, nokprev, k_lo, k_n, a = att_info(u)
                    SB = 4 if u % 2 == 0 else 6
                    kread = [B("kT", 0)] + ([B("kT", "h")] if bi == 0 else [])
                    PE(lambda e: e.matmul(ps[SB][:, k_lo:k_lo + k_n], lhsT=qT[:, h, q0:q0 + 128], rhs=kT[:, g, q0 + k_lo:q0 + k_lo + k_n], start=True, stop=True), r=[B("qT", 0)] + kread, w=[B("ps", SB)])
                    V(lambda e: e.scalar_tensor_tensor(out=sc[:, a, k_lo:k_lo + k_n], in0=abias[:, k_lo:k_lo + k_n], scalar=float(2.0 ** (-(h + 1))), in1=ps[SB][:, k_lo:k_lo + k_n], op0=ALU.mult, op1=ALU.add), r=[B("ps", SB), cB], w=[B("sc", a)])
                    V(lambda e: e.reduce_max(out=sm[:, a, 0:1], in_=sc[:, a, k_lo:k_lo + k_n], axis=AX.X), r=[B("sc", a)], w=[B("sm", a)])
                    sinkc = pvc(l, OSINK + h)
                    V(lambda e: e.tensor_scalar(out=sm[:, a, 1:2], in0=sm[:, a, 0:1], scalar1=sinkc, scalar2=-1.0, op0=ALU.max, op1=ALU.mult), r=[B("sm", a), cB], w=[B("sm", a)])
                    S(lambda e: e.activation(out=pb[:, a, k_lo:k_lo + k_n], in_=sc[:, a, k_lo:k_lo + k_n], func=AF.Exp, bias=sm[:, a, 1:2], accum_out=sm[:, a, 2:3]), r=[B("sc", a), B("sm", a)], w=[B("pb", a), B("sm", a)])
                    S(lambda e: e.activation(out=sm[:, a, 3:4], in_=sinkc, func=AF.Exp, bias=sm[:, a, 1:2]), r=[B("sm", a), cB], w=[B("sm", a)])

                def att_A2(u):
                    bi, tile, r2, h, g, q0, nokprev, k_lo, k_n, a = att_info(u)
                    V(lambda e: e.tensor_tensor(out=sm[:, a, 4:5], in0=sm[:, a, 2:3], in1=sm[:, a, 3:4], op=ALU.add), r=[B("sm", a)], w=[B("sm", a)])
                    V(lambda e: e.reciprocal(out=sm[:, a, 5:6], in_=sm[:, a, 4:5]), r=[B("sm", a)], w=[B("sm", a)])
                    V(lambda e: e.tensor_scalar(out=pb[:, a, k_lo:k_lo + k_n], in0=pb[:, a, k_lo:k_lo + k_n], scalar1=sm[:, a, 5:6], scalar2=None, op0=ALU.mult), r=[B("sm", a), B("pb", a)], w=[B("pb", a)])

                def att_B(u):
                    bi, tile, r2, h, g, q0, nokprev, k_lo, k_n, a = att_info(u)
                    pa_ = u % 2
                    for kb in range(2):
                        if nokprev and kb == 0:
                            continue
                        PE(lambda e, kb=kb: e.transpose(out=psb[:, pa_ * 256 + kb * 128:pa_ * 256 + kb * 128 + 128], in_=pb[:, a, kb * 128:kb * 128 + 128], identity=identb[:]), r=[B("pb", a), B("identb")], w=[B("psb", 0)])
                    S(lambda e: e.activation(out=ptb[:, pa_, k_lo:k_lo + k_n], in_=psb[:, pa_ * 256 + k_lo:pa_ * 256 + k_lo + k_n], func=AF.Copy), r=[B("psb", 0)], w=[B("ptb", pa_)])
                    kbs = [1] if nokprev else [0, 1]
                    for kb in kbs:
                        lo = 64 if r2 == 0 else 0
                        PE(lambda e, kb=kb, lo=lo: e.matmul(ps[5][:, 0:128], lhsT=vp[:, bi + kb, g, lo:lo + 128], rhs=ptb[:, pa_, kb * 128:kb * 128 + 128], start=(r2 == 0 and kb == kbs[0]), stop=(r2 == 1 and kb == 1)), r=[B("vp", bi + kb), B("ptb", pa_)], w=[B("ps", 5)])
                    if r2 == 1:
                        S(lambda e: e.activation(out=hb[:, tile, q0:q0 + 128], in_=ps[5][:, 0:128], func=AF.Copy), r=[B("ps", 5)], w=[B("hb", par, 0)])

                for u in range(len(units) + 2):
                    if u < len(units):
                        att_A1(u)
                    if 1 <= u <= len(units):
                        att_A2(u - 1)
                    if u >= 2:
                        att_B(u - 2)
                    yield
                if first:
                    for g in range(2):
                        sk4 = sinkt[:, l * 2 + g:l * 2 + g + 1]
                        for b4 in range(NS // 4):
                            for bb in range(4):
                                b = b4 * 4 + bb
                                Q(lambda e, b=b, l=l: e.dma_start(out=Kb[:, 0, :], in_=ok_s[l, b, :, :]), r=[B("ok_s", l)], w=[B("Kb", 0)])
                                PE(lambda e, b=b, g=g: e.transpose(out=ps[4][0:64, 0:128], in_=Kb[:, 0, g * 64:(g + 1) * 64], identity=ident[:]), r=[B("Kb", 0), cB], w=[B("ps", 4)])
                                S(lambda e, b=b: e.activation(out=KbT[:, b % 2, :], in_=ps[4][0:64, 0:128], func=AF.Copy), r=[B("ps", 4)], w=[B("KbT", b % 2)])
                                PE(lambda e, b=b, g=g, bb=bb: e.matmul(ps[6][0:4, bb * 128:bb * 128 + 128], lhsT=qT[:, 4 * g:4 * g + 4, T + b], rhs=KbT[:, b % 2, :], start=True, stop=True), r=[B("qT", T), B("KbT", b % 2)], w=[B("ps", 6)])
                            V(lambda e, g=g: e.tensor_tensor(out=ssc[:], in0=ps[6][0:4, :].rearrange("p (b k) -> p b k", b=4), in1=sbias[:, g:g + 1, :].to_broadcast([4, 4, 128]), op=ALU.add), r=[B("ps", 6), cB], w=[B("ssc")])
                            V(lambda e: e.reduce_max(out=ssm_[:, 0, :], in_=ssc[:], axis=AX.X), r=[B("ssc")], w=[B("ssm_")])
                            V(lambda e, sk4=sk4: e.tensor_scalar(out=ssm_[:, 0, :], in0=ssm_[:, 0, :], scalar1=sk4, scalar2=None, op0=ALU.max), r=[B("ssm_"), cB], w=[B("ssm_")])
                            V(lambda e: e.tensor_tensor(out=ssc[:], in0=ssc[:], in1=ssm_[:, 0, :].unsqueeze(2).to_broadcast([4, 4, 128]), op=ALU.subtract), r=[B("ssc"), B("ssm_")], w=[B("ssc")])
                            S(lambda e: e.activation(out=ssc[:], in_=ssc[:], func=AF.Exp), r=[B("ssc")], w=[B("ssc")])
                            V(lambda e: e.reduce_sum(out=ssm_[:, 1, :], in_=ssc[:], axis=AX.X), r=[B("ssc")], w=[B("ssm_")])
                            S(lambda e, sk4=sk4: e.activation(out=ssm_[:, 2, :], in_=ssm_[:, 0, :], func=AF.Exp, scale=-1.0, bias=sk4), r=[B("ssm_"), cB], w=[B("ssm_")])
                            V(lambda e: e.tensor_tensor(out=ssm_[:, 1, :], in0=ssm_[:, 1, :], in1=ssm_[:, 2, :], op=ALU.add), r=[B("ssm_")], w=[B("ssm_")])
                            V(lambda e: e.reciprocal(out=ssm_[:, 1, :], in_=ssm_[:, 1, :]), r=[B("ssm_")], w=[B("ssm_")])
                            V(lambda e: e.tensor_tensor(out=spb[:], in0=ssc[:], in1=ssm_[:, 1, :].unsqueeze(2).to_broadcast([4, 4, 128]), op=ALU.mult), r=[B("ssc"), B("ssm_")], w=[B("spb")])
                            for bb in range(4):
                                PE(lambda e, bb=bb: e.transpose(out=psb[:, 512 + bb * 4:512 + bb * 4 + 4], in_=spb[:, bb, :], identity=identb[0:4, 0:4]), r=[B("spb"), B("identb")], w=[B("psb", 0)])
                            S(lambda e, g=g, b4=b4: e.activation(out=sptb[:, g, b4 * 4:b4 * 4 + 4, :], in_=psb[:, 512:512 + 16].rearrange("p (b r) -> p b r", r=4), func=AF.Copy), r=[B("psb", 0)], w=[B("sptb", g)])
                            yield
                    for b in range(NS):
                        Q(lambda e, b=b, l=l: e.dma_start(out=Vb[:, 0, :], in_=ov_s[l, b, :, :]), r=[B("ov_s", l)], w=[B("Vb", 0)])
                        V(lambda e, b=b: e.tensor_copy(out=Vbp[:, :, 64:128], in_=Vb[:, 0, :].rearrange("p (g d) -> p g d", g=2)), r=[B("Vb", 0)], w=[B("Vbp")])
                        for tile in range(4):
                            g = tile // 2
                            for r2 in range(2):
                                rr4 = (tile % 2) * 2 + r2
                                lo = 64 if r2 == 0 else 0
                                PE(lambda e, b=b, g=g, tile=tile, r2=r2, rr4=rr4, lo=lo: e.matmul(ps[5][:, 256 + b * 4 + tile:256 + b * 4 + tile + 1], lhsT=Vbp[:, g, lo:lo + 128], rhs=sptb[:, g, b, rr4:rr4 + 1], start=(r2 == 0), stop=(r2 == 1)), r=[B("Vbp"), B("sptb", g)], w=[B("ps", 5)])
                    S(lambda e: e.activation(out=hb[:, 0:4, T:NTM].rearrange("p t b -> p b t"), in_=ps[5][:, 256:256 + NS * 4].rearrange("p (b t) -> p b t", t=4), func=AF.Copy), r=[B("ps", 5)], w=[B("hb", par, T)])

                if first:
                    Q(lambda e, l=l: e.dma_start(out=cs[:, :, :, 0:30], in_=cconv_d[l, :, :].rearrange("p (c b k) -> p c b k", c=2, b=NS)), w=[B("cs")])
                    V(lambda e: e.tensor_copy(out=cs[:, :, :, 30], in_=ucv[:, :, 30 + T:30 + NTM]), r=[B("ucv", T)], w=[B("cs")])
                    Q(lambda e, l=l: e.dma_start(out=oconv_s[l, :, :].rearrange("p (c b k) -> p c b k", c=2, b=NS), in_=cs[:, :, :, 1:31]), r=[B("cs")], w=[B("oconv_s", l)])
                    for ct in range(2):
                        cw = pvt[:, l * PVL + OCW + ct * 31: l * PVL + OCW + ct * 31 + 31]
                        V(lambda e, ct=ct, cw=cw: e.tensor_tensor(out=cst[:, ct, :, :], in0=cs[:, ct, :, :], in1=cw.unsqueeze(1).to_broadcast([128, NS, 31]), op=ALU.mult), r=[B("cs"), cB], w=[B("ysm", 0), B("ysm", 1)])
                        V(lambda e, ct=ct: e.reduce_sum(out=acc[:, ct, T:NTM], in_=cst[:, ct, :, :], axis=AX.X), r=[B("ysm", 0), B("ysm", 1)], w=[B("acc", T, ct)])
                        V(lambda e, ct=ct: e.tensor_scalar(out=acc[:, ct, T:NTM], in0=acc[:, ct, T:NTM], scalar1=pvc(l, OCB + ct), scalar2=None, op0=ALU.add), r=[B("acc", T, ct), cB], w=[B("acc", T, ct)])
                for (c0, cn) in ccs:
                    if c0 < T:
                        hr = [B("ucv", c0)] + ([B("ucv", "h")] if c0 == 0 else [B("ucv", c0 - 512)])
                        for ct in range(2):
                            pc = bank()
                            for kk in range(31):
                                PE(lambda e, ct=ct, kk=kk, pc=pc: e.matmul(ps[pc][:, 0:cn], lhsT=ringM[:, s_cd[ct], kk * 128:(kk + 1) * 128], rhs=ucv[:, ct, c0 + kk:c0 + kk + cn], start=(kk == 0), stop=(kk == 30)), r=hr + [B("slotM", s_cd[ct])], w=[B("ps", pc)])
                            S(lambda e, ct=ct, pc=pc: e.activation(out=acc[:, ct, c0:c0 + cn], in_=ps[pc][:, 0:cn], func=AF.Identity, bias=pvc(l, OCB + ct)), r=[B("ps", pc), cB], w=[B("acc", c0, ct)])
                    for ct in range(2):
                        S(lambda e, ct=ct: e.activation(out=ysq[:, ct, 0:cn], in_=acc[:, ct, c0:c0 + cn], func=AF.Square), r=[B("acc", c0, 0), B("acc", c0, 1)], w=[B("ysm", 0), B("ysm", 1)])
                    for ct in range(2):
                        PE(lambda e, ct=ct: e.matmul(ps[6][:, 0:cn], lhsT=ones_f[:], rhs=acc[:, ct, c0:c0 + cn], start=(ct == 0), stop=(ct == 1)), r=[B("acc", c0, 0), B("acc", c0, 1), cB], w=[B("ps", 6)])
                    V(lambda e: e.tensor_scalar(out=mean[:, 0:cn], in0=ps[6][:, 0:cn], scalar1=1.0 / 256, scalar2=None, op0=ALU.mult), r=[B("ps", 6)], w=[B("mean")])
                    for ct in range(2):
                        PE(lambda e, ct=ct: e.matmul(ps[6][:, 0:cn], lhsT=ones_f[:], rhs=ysq[:, ct, 0:cn], start=(ct == 0), stop=(ct == 1)), r=[B("ysm", 0), B("ysm", 1), cB], w=[B("ps", 6)])
                    V(lambda e: e.tensor_tensor(out=var[:, 0:cn], in0=mean[:, 0:cn], in1=mean[:, 0:cn], op=ALU.mult), r=[B("mean")], w=[B("g2")])
                    V(lambda e: e.scalar_tensor_tensor(out=var[:, 0:cn], in0=ps[6][:, 0:cn], scalar=1.0 / 256, in1=var[:, 0:cn], op0=ALU.mult, op1=ALU.subtract), r=[B("ps", 6), B("g2")], w=[B("g2")])
                    S(lambda e: e.activation(out=var[:, 0:cn], in_=var[:, 0:cn], func=AF.Sqrt, bias=EPS), r=[B("g2")], w=[B("g2")])
                    V(lambda e: e.reciprocal(out=var[:, 0:cn], in_=var[:, 0:cn]), r=[B("g2")], w=[B("g2")])
                    for ct in range(2):
                        V(lambda e, ct=ct: e.tensor_tensor(out=g1[:, 0:cn], in0=acc[:, ct, c0:c0 + cn], in1=mean[:, 0:cn], op=ALU.subtract), r=[B("acc", c0, 0), B("acc", c0, 1), B("mean")], w=[B("g1")])
                        V(lambda e, ct=ct: e.tensor_tensor(out=g1[:, 0:cn], in0=g1[:, 0:cn], in1=var[:, 0:cn], op=ALU.mult), r=[B("g1"), B("g2")], w=[B("g1")])
                        V(lambda e, ct=ct: e.tensor_scalar(out=g1[:, 0:cn], in0=g1[:, 0:cn], scalar1=pvc(l, OLG + ct), scalar2=pvc(l, OLB + ct), op0=ALU.mult, op1=ALU.add), r=[B("g1"), cB], w=[B("g1")])
                        S(lambda e, ct=ct: e.activation(out=hb[:, 4 + ct, c0:c0 + cn], in_=g1[:, 0:cn], func=AF.Silu), r=[B("g1")], w=[B("hb", par, c0)])
                        yield

                def ssm_out(c0, cn, hsrc_re, hsrc_im, hoff, hbufs):
                    for ct in range(2):
                        for j4 in range(4):
                            s8 = ct * 4 + j4
                            PE(lambda e, ct=ct, s8=s8, j4=j4: e.matmul(ps[6][:, 0:cn], lhsT=WC[:, s8 * 2, :], rhs=hsrc_re(s8), start=(j4 == 0), stop=False), r=hbufs + [B("WC")], w=[B("ps", 6)])
                            PE(lambda e, ct=ct, s8=s8: e.matmul(ps[6][:, 0:cn], lhsT=WC[:, s8 * 2 + 1, :], rhs=hsrc_im(s8), start=False, stop=False), r=hbufs + [B("WC")], w=[B("ps", 6)])
                        PE(lambda e, ct=ct: e.matmul(ps[6][:, 0:cn], lhsT=Dd[:, ct, :], rhs=us[:, ct, c0:c0 + cn], start=False, stop=True), r=[B("us", (c0 // 512) * 512 if c0 < T else T), B("Dd")], w=[B("ps", 6)])
                        V(lambda e, ct=ct: e.tensor_copy(out=ysm[:, ct, 0:cn], in_=ps[6][:, 0:cn]), r=[B("ps", 6)], w=[B("ysm", ct)])
                        V(lambda e, ct=ct: e.tensor_tensor(out=g1[:, 0:cn], in0=ysm[:, ct, 0:cn], in1=ysm[:, ct, 0:cn], op=ALU.mult), r=[B("ysm", ct)], w=[B("g1")])
                        V(lambda e, ct=ct: e.tensor_scalar(out=g1[:, 0:cn], in0=g1[:, 0:cn], scalar1=0.044715, scalar2=1.0, op0=ALU.mult, op1=ALU.add), r=[B("g1")], w=[B("g1")])
                        V(lambda e, ct=ct: e.tensor_tensor(out=g1[:, 0:cn], in0=g1[:, 0:cn], in1=ysm[:, ct, 0:cn], op=ALU.mult), r=[B("g1"), B("ysm", ct)], w=[B("g1")])
                        S(lambda e, ct=ct: e.activation(out=g2[:, 0:cn], in_=g1[:, 0:cn], func=AF.Sigmoid, scale=2.0 * math.sqrt(2.0 / math.pi)), r=[B("g1")], w=[B("g2")])
                        V(lambda e, ct=ct: e.tensor_tensor(out=ysm[:, ct, 0:cn], in0=ysm[:, ct, 0:cn], in1=g2[:, 0:cn], op=ALU.mult), r=[B("g2"), B("ysm", ct)], w=[B("ysm", ct)])
                        V(lambda e, ct=ct: e.tensor_copy(out=yb[:, ct, 0:cn], in_=ysm[:, ct, 0:cn]), r=[B("ysm", ct)], w=[B("yb", ct)])
                    for co in range(2):
                        for ct in range(2):
                            PE(lambda e, ct=ct, co=co: e.matmul(ps[6][:, 0:cn], lhsT=gluw[:, ct, co * 128:(co + 1) * 128], rhs=yb[:, ct, 0:cn], start=(ct == 0), stop=(ct == 1)), r=[B("yb", 0), B("yb", 1), B("gluw")], w=[B("ps", 6)])
                        S(lambda e, co=co: e.activation(out=g2[:, 0:cn], in_=ps[6][:, 0:cn], func=AF.Sigmoid, bias=pvc(l, OGB + co)), r=[B("ps", 6), cB], w=[B("g2")])
                        V(lambda e, co=co: e.tensor_tensor(out=hb[:, 6 + co, c0:c0 + cn], in0=ysm[:, co, 0:cn], in1=g2[:, 0:cn], op=ALU.mult), r=[B("g2"), B("ysm", co)], w=[B("hb", par, (c0 // 512) * 512 if c0 < T else T)])

                for sc_i in range(T // J):
                    c0 = sc_i * J
                    ucc = (c0 // 512) * 512
                    for s8 in range(8):
                        ct, a = s8 // 4, s8 % 2
                        for ci, dst in ((0, 0), (1, J)):
                            PE(lambda e, ci=ci, dst=dst, s8=s8, ct=ct: e.matmul(ps[4][:, dst:dst + J], lhsT=WB[:, s8 * 2 + ci, :], rhs=us[:, ct, c0:c0 + J], start=True, stop=True), r=[B("us", ucc), B("WB")], w=[B("ps", 4)])
                        ra = s8 % 2
                        Q(lambda e, s8=s8, ra=ra: e.dma_start(out=rot2[:, ra, :, :].rearrange("p c j -> p (c j)"), in_=rot_d[l, :, s8 * 2 * J:(s8 + 1) * 2 * J]), r=[B("rot_d", l)], w=[B("rot2", ra)])
                        cosT, sinT = rot2[:, ra, 0, :], rot2[:, ra, 1, :]
                        pre, pim = ps[4][:, 0:J], ps[4][:, J:2 * J]
                        V(lambda e, cosT=cosT, pre=pre: e.tensor_tensor(out=bre[:], in0=pre, in1=cosT, op=ALU.mult), r=[B("ps", 4), B("rot2", ra)], w=[B("bre")])
                        V(lambda e, sinT=sinT, pim=pim: e.tensor_tensor(out=t1[:], in0=pim, in1=sinT, op=ALU.mult), r=[B("ps", 4), B("rot2", ra)], w=[B("t1")])
                        V(lambda e, cosT=cosT, pim=pim: e.tensor_tensor(out=bim[:], in0=pim, in1=cosT, op=ALU.mult), r=[B("ps", 4), B("rot2", ra)], w=[B("bim")])
                        V(lambda e, sinT=sinT, pre=pre: e.tensor_tensor(out=t2[:], in0=pre, in1=sinT, op=ALU.mult), r=[B("ps", 4), B("rot2", ra)], w=[B("t2")])
                        V(lambda e: e.tensor_tensor(out=bre[:], in0=bre[:], in1=t1[:], op=ALU.add), r=[B("t1"), B("bre")], w=[B("bre")])
                        V(lambda e: e.tensor_tensor(out=bim[:], in0=bim[:], in1=t2[:], op=ALU.subtract), r=[B("t2"), B("bim")], w=[B("bim")])
                        rho_b = lam[:, l, 0, s8:s8 + 1].to_broadcast([128, J])
                        V(lambda e, rho_b=rho_b, s8=s8: e.tensor_tensor_scan(out=wre[:], data0=rho_b, data1=bre[:], initial=hcar[:, l, s8, 0:1], op0=ALU.mult, op1=ALU.add), r=[B("bre"), B("lam", l), B("hcar", l)], w=[B("wre")])
                        V(lambda e, rho_b=rho_b, s8=s8: e.tensor_tensor_scan(out=wim[:], data0=rho_b, data1=bim[:], initial=hcar[:, l, s8, 1:2], op0=ALU.mult, op1=ALU.add), r=[B("bim"), B("lam", l), B("hcar", l)], w=[B("wim")])
                        V(lambda e, cosT=cosT: e.tensor_tensor(out=t1[:], in0=wre[:], in1=cosT, op=ALU.mult), r=[B("wre"), B("rot2", ra)], w=[B("t1")])
                        V(lambda e, sinT=sinT: e.tensor_tensor(out=t2[:], in0=wim[:], in1=sinT, op=ALU.mult), r=[B("wim"), B("rot2", ra)], w=[B("t2")])
                        V(lambda e, sinT=sinT: e.tensor_tensor(out=bre[:], in0=wre[:], in1=sinT, op=ALU.mult), r=[B("wre"), B("rot2", ra)], w=[B("bre")])
                        V(lambda e, cosT=cosT: e.tensor_tensor(out=bim[:], in0=wim[:], in1=cosT, op=ALU.mult), r=[B("wim"), B("rot2", ra)], w=[B("bim")])
                        V(lambda e, s8=s8: e.tensor_tensor(out=hbf[:, s8 * J:(s8 + 1) * J], in0=t1[:], in1=t2[:], op=ALU.subtract), r=[B("t1"), B("t2")], w=[B("hbf")])
                        V(lambda e, s8=s8: e.tensor_tensor(out=hbf[:, 8 * J + s8 * J:8 * J + (s8 + 1) * J], in0=bre[:], in1=bim[:], op=ALU.add), r=[B("bre"), B("bim")], w=[B("hbf")])
                        cl, sl = rot2[:, ra, 0, J - 1:J], rot2[:, ra, 1, J - 1:J]
                        V(lambda e, sl=sl: e.tensor_tensor(out=sm[:, 0, 6:7], in0=wim[:, J - 1:J], in1=sl, op=ALU.mult), r=[B("wim"), B("rot2", ra)], w=[B("sm", 0)])
                        V(lambda e, s8=s8, cl=cl: e.scalar_tensor_tensor(out=hcar[:, l, s8, 0:1], in0=wre[:, J - 1:J], scalar=cl, in1=sm[:, 0, 6:7], op0=ALU.mult, op1=ALU.subtract), r=[B("wre"), B("rot2", ra), B("sm", 0)], w=[B("hcar", l)])
                        V(lambda e, sl=sl: e.tensor_tensor(out=sm[:, 0, 7:8], in0=wre[:, J - 1:J], in1=sl, op=ALU.mult), r=[B("wre"), B("rot2", ra)], w=[B("sm", 0)])
                        V(lambda e, s8=s8, cl=cl: e.scalar_tensor_tensor(out=hcar[:, l, s8, 1:2], in0=wim[:, J - 1:J], scalar=cl, in1=sm[:, 0, 7:8], op0=ALU.mult, op1=ALU.add), r=[B("wim"), B("rot2", ra), B("sm", 0)], w=[B("hcar", l)])
                        yield
                    hbre = hbf
                    ssm_out(c0, J, lambda s8: hbre[:, s8 * J:(s8 + 1) * J], lambda s8: hbre[:, 8 * J + s8 * J:8 * J + (s8 + 1) * J], 0, [B("hbf")])
                    yield
                if last:
                    Q(lambda e, l=l: e.dma_start(out=ossm_p[l, :, :].rearrange("p (s c) -> p s c", c=2), in_=hcar[:, l, :, :]), r=[B("hcar", l)], w=[B("ossm_p", l)])
                if first:
                    Q(lambda e, l=l: e.dma_start(out=h0[:, 0, :, :], in_=sre_d[l, :, :].rearrange("p (s b) -> p s b", s=8)), w=[B("h0")])
                    Q(lambda e, l=l: e.dma_start(out=h0[:, 1, :, :], in_=sim_d[l, :, :].rearrange("p (s b) -> p s b", s=8)), w=[B("h0")])
                    for s8 in range(8):
                        ct = s8 // 4
                        for ci in range(2):
                            PE(lambda e, ci=ci, s8=s8, ct=ct: e.matmul(ps[4][:, (s8 * 2 + ci) * NS:(s8 * 2 + ci + 1) * NS], lhsT=WB[:, s8 * 2 + ci, :], rhs=us[:, ct, T:NTM], start=True, stop=True), r=[B("us", T), B("WB")], w=[B("ps", 4)])
                    V(lambda e: e.tensor_copy(out=h1[:].rearrange("p c s b -> p s c b"), in_=ps[4][:, 0:16 * NS].rearrange("p (s c b) -> p s c b", s=8, c=2)), r=[B("ps", 4)], w=[B("h1")])
                    for s8 in range(8):
                        lre, lim = pl[:, 5, s8:s8 + 1], pl[:, 6, s8:s8 + 1]
                        V(lambda e, s8=s8: e.tensor_tensor(out=sm[:, 0, 6:7], in0=lam[:, l, 0, s8:s8 + 1], in1=lam[:, l, 1, s8:s8 + 1], op=ALU.mult), r=[B("lam", l)], w=[B("sm", 0)])
                        V(lambda e, s8=s8: e.tensor_tensor(out=sm[:, 0, 7:8], in0=lam[:, l, 0, s8:s8 + 1], in1=lam[:, l, 2, s8:s8 + 1], op=ALU.mult), r=[B("lam", l)], w=[B("sm", 0)])
                        V(lambda e, s8=s8: e.tensor_scalar(out=sm[:, 1, 7:8], in0=sm[:, 0, 7:8], scalar1=-1.0, scalar2=None, op0=ALU.mult), r=[B("sm", 0)], w=[B("sm", 1)])
                        V(lambda e, s8=s8: e.scalar_tensor_tensor(out=h1[:, 0, s8, :], in0=h0[:, 0, s8, :], scalar=sm[:, 0, 6:7], in1=h1[:, 0, s8, :], op0=ALU.mult, op1=ALU.add), r=[B("h0"), B("sm", 0), B("h1")], w=[B("h1")])
                        V(lambda e, s8=s8: e.scalar_tensor_tensor(out=h1[:, 0, s8, :], in0=h0[:, 1, s8, :], scalar=sm[:, 1, 7:8], in1=h1[:, 0, s8, :], op0=ALU.mult, op1=ALU.add), r=[B("h0"), B("sm", 1), B("h1")], w=[B("h1")])
                        V(lambda e, s8=s8: e.scalar_tensor_tensor(out=h1[:, 1, s8, :], in0=h0[:, 1, s8, :], scalar=sm[:, 0, 6:7], in1=h1[:, 1, s8, :], op0=ALU.mult, op1=ALU.add), r=[B("h0"), B("sm", 0), B("h1")], w=[B("h1")])
                        V(lambda e, s8=s8: e.scalar_tensor_tensor(out=h1[:, 1, s8, :], in0=h0[:, 0, s8, :], scalar=sm[:, 0, 7:8], in1=h1[:, 1, s8, :], op0=ALU.mult, op1=ALU.add), r=[B("h0"), B("sm", 0), B("h1")], w=[B("h1")])
                    Q(lambda e, l=l: e.dma_start(out=ossm_s[l, :, :].rearrange("p (c s b) -> p c s b", c=2, s=8), in_=h1[:]), r=[B("h1")], w=[B("ossm_s", l)])
                    V(lambda e: e.tensor_copy(out=h1b[:], in_=h1[:]), r=[B("h1")], w=[B("h1b")])
                    ssm_out(T, NS, lambda s8: h1b[:, 0, s8, :], lambda s8: h1b[:, 1, s8, :], 0, [B("h1b")])
                    yield

                s_out = [wload("M", w_out[l, :, i * 512:(i + 1) * 512].rearrange("(k p) n -> p k n", p=128), "p (k n) -> p k n", wid=l * 23 + 3 + i, ch=ch, k=8) for i in range(2)]
                for (c0, cn) in ccs:
                    for dt_ in range(8):
                        s = s_out[dt_ // 4]
                        off = (dt_ % 4) * 128
                        pbk = bank()
                        PE(lambda e, dt_=dt_, pbk=pbk: e.matmul(ps[pbk][:, 0:cn], lhsT=ident[:], rhs=x[:, dt_, c0:c0 + cn], start=True, stop=False), r=[cB, B("x", par, c0, dt_)], w=[B("ps", pbk)])
                        for k in range(8):
                            PE(lambda e, k=k, s=s, off=off, pbk=pbk: e.matmul(ps[pbk][:, 0:cn], lhsT=ring[:, s, k * 512 + off:k * 512 + off + 128], rhs=hb[:, k, c0:c0 + cn], start=False, stop=(k == 7)), r=[B(SL, s), B("hb", par, c0)], w=[B("ps", pbk)])
                        S(lambda e, dt_=dt_, pbk=pbk: e.activation(out=x[:, dt_, c0:c0 + cn], in_=ps[pbk][:, 0:cn], func=AF.Copy), r=[B("ps", pbk)], w=[B("x", par, c0, dt_)])
                        yield

        def gen_ffn(ch, l):
            par = ch % 2; x = X[par]; hb = HB[par]
            first, last = (ch == 0), (ch == nch - 1)
            ccs = [(0, 512)] + ([(T, NS)] if first else [])
            sq, rstd = sq2, rstd2
            ring, SL = ringF, "slotF"
            bank = fbank
            if True:
                rmsnorm(x, hb, par, l * PVL + OG2, ccs, sq2, rstd2, fbank(), "f")
                for fg in range(6):
                    nf = 4 if fg < 5 else 2
                    sg_ = wload("F", w_g[l, :, fg * 512:fg * 512 + nf * 128].rearrange("(k p) n -> p k n", p=128), "p (k n) -> p k n", wid=l * 23 + 5 + fg * 3, ch=ch, k=8)
                    su_ = wload("F", w_u[l, :, fg * 512:fg * 512 + nf * 128].rearrange("(k p) n -> p k n", p=128), "p (k n) -> p k n", wid=l * 23 + 6 + fg * 3, ch=ch, k=8)
                    sd_ = wload("F", w_d[l, fg * 512:fg * 512 + nf * 128, :].rearrange("(f p) n -> p f n", p=128), "p (f n) -> p f n", wid=l * 23 + 7 + fg * 3, ch=ch, f=nf)
                    W_ = nf * 128
                    for (c0, cn) in ccs:
                        for f in range(nf):
                            pg, pu = bank(), bank()
                            for (s_, pb_) in ((sg_, pg), (su_, pu)):
                                for k in range(8):
                                    PE(lambda e, k=k, s_=s_, pb_=pb_, f=f: e.matmul(ps[pb_][:, 0:cn], lhsT=ring[:, s_, k * W_ + f * 128:k * W_ + f * 128 + 128], rhs=hb[:, k, c0:c0 + cn], start=(k == 0), stop=(k == 7)), r=[B(SL, s_), B("hb", par, c0)], w=[B("ps", pb_)])
                            S(lambda e, pg=pg, f=f: e.activation(out=sgb[:, f % 2, 0:cn], in_=ps[pg][:, 0:cn], func=AF.Silu), r=[B("ps", pg)], w=[B("sgb", f % 2)])
                            V(lambda e, pu=pu, f=f: e.tensor_tensor(out=hid[:, f, c0:c0 + cn], in0=ps[pu][:, 0:cn], in1=sgb[:, f % 2, 0:cn], op=ALU.mult), r=[B("ps", pu), B("sgb", f % 2)], w=[B("hid", c0)])
                            yield
                        for dt_ in range(8):
                            pbk = bank()
                            PE(lambda e, dt_=dt_, pbk=pbk: e.matmul(ps[pbk][:, 0:cn], lhsT=ident[:], rhs=x[:, dt_, c0:c0 + cn], start=True, stop=False), r=[cB, B("x", par, c0, dt_)], w=[B("ps", pbk)])
                            for f in range(nf):
                                PE(lambda e, f=f, dt_=dt_, pbk=pbk: e.matmul(ps[pbk][:, 0:cn], lhsT=ring[:, sd_, f * 1024 + dt_ * 128:f * 1024 + dt_ * 128 + 128], rhs=hid[:, f, c0:c0 + cn], start=False, stop=(f == nf - 1)), r=[B(SL, sd_), B("hid", c0)], w=[B("ps", pbk)])
                            S(lambda e, dt_=dt_, pbk=pbk: e.activation(out=x[:, dt_, c0:c0 + cn], in_=ps[pbk][:, 0:cn], func=AF.Copy), r=[B("ps", pbk)], w=[B("x", par, c0, dt_)])
                            yield
            if l == L - 1:
                for (c0, cn) in ccs:
                    for k in range(8):
                        S(lambda e, k=k: e.activation(out=sq[:, 0:cn], in_=x[:, k, c0:c0 + cn], func=AF.Square), r=[*XB(par, c0)], w=[B("sq", "f")])
                        PE(lambda e, k=k: e.matmul(ps[0][:, 0:cn], lhsT=ones_b[:], rhs=sq[:, 0:cn], start=(k == 0), stop=(k == 7)), r=[B("sq", "f"), cB], w=[B("ps", 0)])
                    S(lambda e: e.activation(out=rstd[:, 0:cn], in_=ps[0][:, 0:cn], func=AF.Sqrt, scale=1.0 / D, bias=EPS), r=[B("ps", 0)], w=[B("rstd", "f")])
                    V(lambda e: e.reciprocal(out=rstd[:, 0:cn], in_=rstd[:, 0:cn]), r=[B("rstd", "f")], w=[B("rstd", "f")])
                    for k in range(8):
                        gk = pvt[:, 4 * PVL + k: 4 * PVL + k + 1]
                        V(lambda e, k=k, gk=gk: e.scalar_tensor_tensor(out=x[:, k, c0:c0 + cn], in0=x[:, k, c0:c0 + cn], scalar=gk, in1=rstd[:, 0:cn], op0=ALU.mult, op1=ALU.mult), r=[*XB(par, c0), B("rstd", "f"), cB], w=[*XB(par, c0)])
                        if c0 < T:
                            Q(lambda e, k=k: e.dma_start(out=yT[k * 128:(k + 1) * 128, ch * T + c0:ch * T + c0 + cn], in_=x[:, k, c0:c0 + cn]), r=[*XB(par, c0)], w=[B("yT")])
                        else:
                            Q(lambda e, k=k: e.dma_start(out=ysT[k * 128:(k + 1) * 128, :], in_=x[:, k, c0:c0 + cn]), r=[*XB(par, c0)], w=[B("ysT")])
            yield

        def count_ops(gen):
            P.dry = True
            n0 = P.dryn
            for _ in gen:
                pass
            P.dry = False
            return P.dryn - n0

        def run2(ga, na, gb, nb):
            ia = ib = 0
            alive_a, alive_b = ga is not None, gb is not None
            while alive_a or alive_b:
                pick_a = alive_a and (not alive_b or ia * nb <= ib * na)
                n0 = P.nops
                if pick_a:
                    try:
                        next(ga)
                    except StopIteration:
                        alive_a = False
                    ia += P.nops - n0
                else:
                    try:
                        next(gb)
                    except StopIteration:
                        alive_b = False
                    ib += P.nops - n0

        streams = []
        for ch in range(nch):
            ph = []
            for l in range(L):
                ph.append(("m", ch, l))
                ph.append(("f", ch, l))
            streams.append(ph)
        steps = []
        t = 0
        start = {}
        for ch in range(nch):
            start[ch] = (ch // 2) * 2 * L + (ch % 2)
        nsteps = max(start[c] + 2 * L for c in range(nch))
        for t in range(nsteps):
            cur = []
            for ch in range(nch):
                p = t - start[ch]
                if 0 <= p < 2 * L:
                    cur.append(streams[ch][p])
            steps.append(cur)
        for cur in steps:
            gens = []
            for (kind, ch, l) in cur:
                mk = (lambda: gen_mix(ch, l)) if kind == "m" else (lambda: gen_ffn(ch, l))
                n = count_ops(mk())
                gens.append((mk(), max(n, 1)))
            if len(gens) == 1:
                run2(gens[0][0], gens[0][1], None, 1)
            else:
                run2(gens[0][0], gens[0][1], gens[1][0], gens[1][1])
        print('NOPS', P.nops, {k: len(v) for k, v in P.ops.items()}, flush=True)
        P.emit()
    return nc


def _alibi_tables():
    slopes = 2.0 ** (-8.0 * np.arange(1, 9, dtype=np.float32) / 8)
    qi = np.arange(128)[:, None]
    kj = np.arange(256)[None, :]
    dist = qi - kj + 128
    valid = (dist >= 0) & (dist < 128)
    ab = np.where(valid, -dist.astype(np.float32), -1.0e7).astype(np.float32)
    dj = (127 - np.arange(128)).astype(np.float32)
    sbias = (-slopes.reshape(2, 4, 1) * dj[None, None, :]).astype(np.float32)
    sbias = np.ascontiguousarray(sbias.transpose(1, 0, 2)).reshape(4, 2 * 128)
    return ab, sbias


def _fm(v):
    return np.ascontiguousarray(np.asarray(v, np.float32).reshape(-1, 128).T)


_NC_CACHE = {}


def kernel(nch=16, depth=4, **inp):
    f = lambda k: np.asarray(inp[k], np.float32)
    L = depth
    TT = nch * T
    key = (nch, depth)
    if key not in _NC_CACHE:
        _NC_CACHE[key] = build(nch, depth)
    nc = _NC_CACHE[key]
    ab, sbias = _alibi_tables()
    pv = np.zeros((128, NPV), np.float32)
    for l in range(L):
        o = l * PVL
        pv[:, o + 0:o + 8] = _fm(f("norm_mix_g")[l])
        pv[:, o + 8:o + 16] = _fm(f("norm_ffn_g")[l])
        cw = f("conv_dw_w")[l]
        for ct in range(2):
            pv[:, o + 16 + ct * 31:o + 16 + (ct + 1) * 31] = cw[:, ct * 128:(ct + 1) * 128].T
        pv[:, o + 78:o + 80] = _fm(f("conv_dw_b")[l])
        pv[:, o + 80:o + 82] = _fm(f("conv_ln_g")[l])
        pv[:, o + 82:o + 84] = _fm(f("conv_ln_b")[l])
        pv[:, o + 84:o + 86] = _fm(f("ssm_d")[l])
        pv[:, o + 86:o + 88] = _fm(f("ssm_glu_b")[l])
        pv[:, o + 88:o + 96] = _fm(f("ssm_a_re")[l].reshape(-1))
        pv[:, o + 96:o + 104] = _fm(f("ssm_a_im")[l].reshape(-1))
        pv[:, o + 104:o + 112] = _fm(np.repeat(f("ssm_log_dt")[l], 64))
        pv[:, o + 112:o + 120] = f("attn_sinks")[l][None, :]
    pv[:, 4 * PVL:4 * PVL + 8] = _fm(f("norm_final_g"))
    Bn = np.zeros((L, 128, 8, 2, 128), np.float32)
    Cn = np.zeros((L, 128, 8, 2, 128), np.float32)
    for l in range(L):
        for ci, (bk, ck_) in enumerate((("ssm_b_re", "ssm_c_re"), ("ssm_b_im", "ssm_c_im"))):
            bb = f(bk)[l]
            cc = f(ck_)[l]
            for g in range(16):
                st_, p0 = g // 2, (g % 2) * 64
                col = (g % 8) * 16
                Bn[l, p0:p0 + 64, st_, ci, col:col + 16] = bb[g]
                Cn[l, p0:p0 + 64, st_, ci, col:col + 16] = cc[g].T
    Bn = Bn.reshape(L, 128, -1)
    Cn = Cn.reshape(L, 128, -1)
    ident = np.eye(128, dtype=np.float32)
    jjv = np.broadcast_to(np.arange(1, J + 1, dtype=np.float32)[None, :], (128, J)).copy()
    xp = f("x_prompt")
    xs = f("x_sample")[:, 0, :]
    shared = {
        "w_in": f("w_in")[:L], "w_out": f("w_out")[:L], "w_g": f("w_ff_gate")[:L], "w_u": f("w_ff_up")[:L],
        "w_d": f("w_ff_down")[:L], "glu_w": f("ssm_glu_w")[:L], "pv": pv, "Bn": Bn, "Cn": Cn, "ident": ident,
        "jj": jjv, "abias": ab, "sbias": sbias,
        "sinkc": np.ascontiguousarray(f("attn_sinks")[:L].reshape(L, 2, 4).transpose(2, 0, 1).reshape(4, L * 2)),
    }
    in_maps = []
    for c in range(8):
        sq_, b0 = c // 4, c * NS
        m = dict(shared)
        m["xT"] = np.ascontiguousarray(xp[sq_, :TT, :].T)
        m["xsT"] = np.ascontiguousarray(xs[b0:b0 + NS].T)
        m["ck"] = np.ascontiguousarray(f("cache_swa_k")[:L, b0:b0 + NS].reshape(L, NS, 128, 128))
        m["cv"] = np.ascontiguousarray(f("cache_swa_v")[:L, b0:b0 + NS].reshape(L, NS, 128, 128))
        cc_ = f("cache_conv")[:L, b0:b0 + NS]
        m["cconv"] = np.ascontiguousarray(cc_.reshape(L, NS, 30, 2, 128).transpose(0, 4, 3, 1, 2)).reshape(L, 128, -1)
        for nm, kk in (("sre", "state_ssm_re"), ("sim", "state_ssm_im")):
            s_ = f(kk)[:L, b0:b0 + NS].reshape(L, NS, 8, 128)
            m[nm] = np.ascontiguousarray(s_.transpose(0, 3, 2, 1)).reshape(L, 128, -1)
        in_maps.append(m)
    res = run_bass_kernel_spmd(nc, in_maps, core_ids=list(range(8))).results
    y_p = np.stack([res[0]["yT"].T, res[4]["yT"].T]).astype(np.float32)
    y_s = np.concatenate([res[c]["ysT"].T for c in range(8)])[:, None, :].astype(np.float32)
    pc = (res[0], res[4])
    k_p = np.stack([np.stack([r["ok_p"][l].reshape(128, 2, 64) for r in pc]) for l in range(L)])
    v_p = np.stack([np.stack([r["ov_p"][l].reshape(128, 2, 64) for r in pc]) for l in range(L)])
    conv_p = np.stack([np.stack([r["oconv_p"][l].reshape(128, 2, 30).transpose(2, 1, 0).reshape(30, 256) for r in pc]) for l in range(L)])
    ssm_p = [np.stack([np.stack([r["ossm_p"][l].reshape(128, 8, 2)[:, :, ci].T.reshape(16, 64) for r in pc]) for l in range(L)]) for ci in range(2)]
    k_s = np.concatenate([res[c]["ok_s"].reshape(L, NS, 128, 2, 64) for c in range(8)], 1)
    v_s = np.concatenate([res[c]["ov_s"].reshape(L, NS, 128, 2, 64) for c in range(8)], 1)
    conv_s = np.concatenate([res[c]["oconv_s"].reshape(L, 128, 2, NS, 30).transpose(0, 3, 4, 2, 1).reshape(L, NS, 30, 256) for c in range(8)], 1)
    ssm_s = [np.concatenate([res[c]["ossm_s"].reshape(L, 128, 2, 8, NS)[:, :, ci].transpose(0, 3, 2, 1).reshape(L, NS, 16, 64) for c in range(8)], 1) for ci in range(2)]
    outs = (y_p, y_s, k_p, v_p, conv_p, ssm_p[0], ssm_p[1], k_s, v_s, conv_s, ssm_s[0], ssm_s[1])
    return tuple(np.ascontiguousarray(o, dtype=np.float32) for o in outs)
```

```python
import math
import os
import numpy as np
from contextlib import ExitStack
import concourse.bass as bass
import concourse.mybir as mybir
from concourse.bass_utils import run_bass_kernel_spmd

F32 = mybir.dt.float32
BF16 = mybir.dt.bfloat16
I32 = mybir.dt.int32
AF = mybir.ActivationFunctionType
ALU = mybir.AluOpType
AX = mybir.AxisListType

D = 1024
SEQ = 8192
T = 512
NS = 16
NTM = T + NS
DFF = 2816
NF = 22
J = 256
NSLOT = 4
PVL = 120
NPV = 4 * PVL + 8
EPS = 1e-6
TWO_PI = 2.0 * math.pi


import types


def _freeze(fn):
    if fn.__closure__ is None:
        return fn
    cells = []
    for c in fn.__closure__:
        try:
            cells.append(types.CellType(c.cell_contents))
        except ValueError:
            cells.append(c)
    return types.FunctionType(fn.__code__, fn.__globals__, fn.__name__, fn.__defaults__, tuple(cells))


class _Stub:
    def __init__(self):
        self.closed = True

    def matmul(self, *a, **kw):
        self.closed = bool(kw.get("stop", True))
        return self

    def transpose(self, *a, **kw):
        self.closed = True
        return self

    def then_inc(self, *a, **kw):
        return self


class Buf:
    __slots__ = ("name", "lw", "rd")

    def __init__(self, name):
        self.name = name
        self.lw = None
        self.rd = {}


class Prog:
    ENG = ["tensor", "vector", "scalar", "gpsimd", "sync"]

    def __init__(self, nc, same_eng_sync=("vector", "scalar", "gpsimd")):
        self.nc = nc
        self.ops = {e: [] for e in self.ENG}
        self.cnt = {}
        self.known = {e: {} for e in self.ENG}
        self.same = set(same_eng_sync)
        self.bufs = {}
        self.dry = False
        self.dq = {}
        self.dryn = 0
        self.nops = 0

    def B(self, *key):
        b = self.bufs.get(key)
        if b is None:
            b = self.bufs[key] = Buf(key)
        return b

    def op(self, eng, fn, r=(), w=(), dma=None, inc=None):
        if self.dry:
            self.dryn += 1
            return None
        fn = _freeze(fn)
        if getattr(self, "stopped", False):
            return None
        self.nops += 1
        if eng == "tensor" and "KLIMIT" in os.environ:
            st_ = _Stub()
            try:
                fn(st_)
            except Exception:
                pass
            self.open_grp = not st_.closed
        if self.nops >= int(os.environ.get("KLIMIT", "100000000")) and not getattr(self, "open_grp", False):
            self.stopped = True
        ex = [b for b in r if b.name[0] in ("ps", "psb")]
        if ex:
            r = [b for b in r if b.name[0] not in ("ps", "psb")]
            w = list(w) + ex
        deps = {}
        for b in list(r) + list(w):
            if b.lw is not None and deps.get(b.lw[0], 0) < b.lw[1]:
                deps[b.lw[0]] = b.lw[1]
        for b in w:
            for s, v in b.rd.items():
                if deps.get(s, 0) < v:
                    deps[s] = v
        pre = None
        if dma is None:
            sem, step = eng, 1
        else:
            npool = 32 if eng == "sync" else 8
            i = self.dq.get(eng, 0)
            self.dq[eng] = i + 1
            sem, step = "d%s%d" % (eng[0], i % npool), 16
            if i >= npool:
                pre = (sem, 16 * (i // npool))
        waits = []
        if pre is not None and deps.get(pre[0], 0) < pre[1]:
            deps[pre[0]] = pre[1]
        for s, v in deps.items():
            if s == eng and eng not in self.same:
                continue
            if self.known[eng].get(s, 0) >= v:
                continue
            self.known[eng][s] = v
            waits.append((s, v))
        self.cnt[sem] = self.cnt.get(sem, 0) + step
        tok = (sem, self.cnt[sem])
        self.ops[eng].append((fn, waits, sem, step))
        for b in r:
            if b.rd.get(sem, 0) < tok[1]:
                b.rd[sem] = tok[1]
        for b in w:
            b.lw = tok
            b.rd = {}
        return tok

    def emit(self):
        nc = self.nc
        with ExitStack() as st:
            sems = {s: st.enter_context(nc.semaphore(s)) for s in self.cnt}
            block = st.enter_context(nc.Block())
            final = dict(self.cnt)

            def mk(engname):
                def body(e):
                    for fn, waits, sem, step in self.ops[engname]:
                        for s, v in waits:
                            e.wait_ge(sems[s], v)
                        fn(e).then_inc(sems[sem], step)
                    if engname == "sync":
                        for s, v in final.items():
                            e.wait_ge(sems[s], v)
                return body

            for engname in self.ENG:
                if self.ops[engname] or engname == "sync":
                    getattr(block, engname)(mk(engname))


def build(nch=16, depth=4):
    nc = bass.Bass("TRN2", target_bir_lowering=False)
    L = depth
    TT = nch * T

    def din(name, shape, dt=F32):
        return nc.dram_tensor(name, list(shape), dt, kind="ExternalInput").ap()

    def dout(name, shape, dt=F32):
        return nc.dram_tensor(name, list(shape), dt, kind="ExternalOutput").ap()

    xT = din("xT", [D, TT]); xsT = din("xsT", [D, NS])
    w_in = din("w_in", [L, D, 1536]); w_out = din("w_out", [L, D, D])
    w_g = din("w_g", [L, D, DFF]); w_u = din("w_u", [L, D, DFF]); w_d = din("w_d", [L, DFF, D])
    glu_w = din("glu_w", [L, 256, 256])
    pv_d = din("pv", [128, NPV])
    Bn_d = din("Bn", [L, 128, 8 * 2 * 128]); Cn_d = din("Cn", [L, 128, 8 * 2 * 128])
    ident_d = din("ident", [128, 128]); jj_d = din("jj", [128, J])
    abias_d = din("abias", [128, 256]); sbias_d = din("sbias", [4, 2 * 128]); sinkc_d = din("sinkc", [4, L * 2])
    ck_d = din("ck", [L, NS, 128, 128]); cv_d = din("cv", [L, NS, 128, 128])
    cconv_d = din("cconv", [L, 128, 2 * NS * 30])
    sre_d = din("sre", [L, 128, 8 * NS]); sim_d = din("sim", [L, 128, 8 * NS])

    yT = dout("yT", [D, TT]); ysT = dout("ysT", [D, NS])
    ok_p = dout("ok_p", [L, 128, 128]); ov_p = dout("ov_p", [L, 128, 128])
    oconv_p = dout("oconv_p", [L, 128, 2 * 30]); ossm_p = dout("ossm_p", [L, 128, 16])
    ok_s = dout("ok_s", [L, NS, 128, 128]); ov_s = dout("ov_s", [L, NS, 128, 128])
    oconv_s = dout("oconv_s", [L, 128, 2 * NS * 30]); ossm_s = dout("ossm_s", [L, 128, 2 * 8 * NS])
    cdg_d = nc.dram_tensor("cdg_scr", [L * 2, 128, 4096], BF16).ap()
    wscr = nc.dram_tensor("w_scr", [L * 23, 128, 4096], BF16).ap()
    rot_d = nc.dram_tensor("rot_scr", [L, 128, 8 * 2 * J], F32).ap()
    wb_d = nc.dram_tensor("wb_scr", [L, 128, 16 * 128], BF16).ap()
    wc_d = nc.dram_tensor("wc_scr", [L, 128, 16 * 128], BF16).ap()

    P = Prog(nc)
    B = P.B
    with ExitStack() as st:
        def sb(name, shape, dt=F32):
            return st.enter_context(nc.sbuf_tensor("sb_" + name, list(shape), dt))

        def pst(name, shape, dt=F32):
            return st.enter_context(nc.psum_tensor(name, list(shape), dt))

        x = sb("x", [128, 8, NTM]); hb = sb("hb", [128, 8, NTM], BF16)
        x1 = sb("x1", [128, 8, T]); hb1 = sb("hb1", [128, 8, T], BF16)
        X = [x, x1]; HB = [hb, hb1]
        qT = sb("qT", [64, 8, NTM], BF16); kT = sb("kT", [64, 2, 128 + NTM], BF16)
        vp = sb("vp", [128, 5, 2, 192], BF16)
        ucv = sb("ucv", [128, 2, 30 + NTM], BF16); cstg = sb("cstg", [128, 2, 30]); acc = sb("acc", [128, 2, NTM]); us = sb("us", [128, 2, NTM], BF16)
        hid = sb("hid", [128, 4, NTM], BF16)
        ringM = sb("ringM", [128, 3, 4096], BF16); ringF = sb("ringF", [128, 3, 4096], BF16)
        pvt = sb("pvt", [128, NPV])
        ident = sb("ident", [128, 128]); identb = sb("identb", [128, 128], BF16)
        ones_f = sb("ones_f", [128, 128]); ones_b = sb("ones_b", [128, 128], BF16)
        abias = sb("abias", [128, 256]); sbias = sb("sbias", [4, 2, 128])
        WB = sb("WB", [128, 16, 128], BF16); WC = sb("WC", [128, 16, 128], BF16)
        Dd = sb("Dd", [128, 2, 128], BF16); gluw = sb("gluw", [128, 2, 256], BF16)
        lam = sb("lam", [128, L, 4, 8])
        rot2 = sb("rot2", [128, 2, 2, J])
        hcar = sb("hcar", [128, L, 8, 2])
        khalo = sb("khalo", [64, L, 2, 128], BF16); vhalo = sb("vhalo", [128, L, 2, 192], BF16)
        chalo = sb("chalo", [128, L, 2, 30], BF16)
        sq = sb("sq", [128, 512], BF16); rstd = sb("rstd", [128, 512])
        sq2 = sb("sq2", [128, 512], BF16); rstd2 = sb("rstd2", [128, 512])
        sgb = sb("sgb", [128, 2, 512], BF16)
        sc = sb("sc", [128, 3, 256]); pb = sb("pb", [128, 3, 256], BF16); ptb = sb("ptb", [128, 2, 256], BF16)
        sm = sb("sm", [128, 3, 8]); smc = sb("smc", [128, 2])
        t1 = sb("t1", [128, J]); t2 = sb("t2", [128, J]); bre = sb("bre", [128, J]); bim = sb("bim", [128, J])
        wre = sb("wre", [128, J]); wim = sb("wim", [128, J])
        ysm = sb("ysm", [128, 2, 512]); yb = sb("yb", [128, 2, 512], BF16); g1 = sb("g1", [128, 512]); g2 = sb("g2", [128, 512])
        ysq = ysm; cst = ysm[:].rearrange("p c n -> p (c n)")[:, 0:2 * NS * 31].rearrange("p (c b k) -> p c b k", c=2, b=NS); mean = sb("mean", [128, 512]); var = g2
        jj = mean[:, 0:J]
        xflat = x[:].rearrange("p k n -> p (k n)")
        rotflat = x1[:].rearrange("p k n -> p (k n)")[:, 0:16 * J]
        big = xflat[:, 0:4096]; big2 = rotflat[:, 8 * J:16 * J]; bigi = sb("bigi", [128, 512], I32)[:]
        hbf = sb("hbf", [128, 16 * J], BF16)
        pl = sb("pl", [128, 16, 8])
        tk = sb("tk", [128, 128]); tkb = sb("tkb", [NS, 2, 128])
        Kb = sb("Kb", [128, 1, 128]); Vb = sb("Vb", [128, 1, 128]); KbT = sb("KbT", [64, 2, 128], BF16)
        Vbp = sb("Vbp", [128, 2, 192], BF16)
        ssc = sb("ssc", [4, 4, 128]); spb = sb("spb", [4, 4, 128], BF16); ssm_ = sb("ssm_", [4, 4, 4]); sinkt = sb("sinkt", [4, L * 2])
        sptb = sb("sptb", [128, 2, NS, 4], BF16)
        cs = sb("cs", [128, 2, NS, 31])
        h0 = sb("h0", [128, 2, 8, NS]); h1 = sb("h1", [128, 2, 8, NS]); h1b = sb("h1b", [128, 2, 8, NS], BF16)

        ps = [pst("ps%d" % i, [128, 512]) for i in range(7)]
        psb = pst("psb", [128, 1024], BF16)

        Q = lambda fn, r=(), w=(), ch="io": P.op("sync", fn, r, w, dma=ch)
        G = lambda fn, r=(), w=(), ch="w": P.op("gpsimd", fn, r, w, dma=ch)
        V = lambda fn, r=(), w=(): P.op("vector", fn, r, w)
        S = lambda fn, r=(), w=(): P.op("scalar", fn, r, w)
        PE = lambda fn, r=(), w=(): P.op("tensor", fn, r, w)
        GP = lambda fn, r=(), w=(): P.op("gpsimd", fn, r, w)

        def XB(par, c0):
            return [B("x", par, c0)] + [B("x", par, c0, d) for d in range(8)]

        def pvc(l, off, n=1):
            return pvt[:, l * PVL + off: l * PVL + off + n]

        OG1, OG2, OCW, OCB, OLG, OLB, OSD, OGB, OARE, OAIM, OLDT, OSINK = 0, 8, 16, 78, 80, 82, 84, 86, 88, 96, 104, 112
        rr = [0]

        def bank():
            rr[0] = (rr[0] + 1) % 3
            return 4 + rr[0]

        fr = [0]

        def fbank():
            fr[0] = (fr[0] + 1) % 3
            return fr[0]

        cB = B("const")
        Q(lambda e: e.dma_start(out=pvt[:], in_=pv_d[:, :]), w=[cB])
        Q(lambda e: e.dma_start(out=ident[:], in_=ident_d[:, :]), w=[cB])
        Q(lambda e: e.dma_start(out=jj[:], in_=jj_d[:, :]), w=[cB])
        Q(lambda e: e.dma_start(out=abias[:], in_=abias_d[:, :]), w=[cB])
        Q(lambda e: e.dma_start(out=sbias[:].rearrange("p h k -> p (h k)"), in_=sbias_d[:, :]), w=[cB])
        Q(lambda e: e.dma_start(out=sinkt[:], in_=sinkc_d[:, :]), w=[cB])
        V(lambda e: e.memset(ones_f[:], 1.0), w=[cB])
        V(lambda e: e.memset(ones_b[:], 1.0), w=[cB])
        V(lambda e: e.tensor_copy(out=identb[:], in_=ident[:]), r=[cB], w=[B("identb")])
        V(lambda e: e.memset(vp[:].rearrange("p a g c -> p (a g c)"), 0.0), w=[B("vp", i) for i in range(5)])
        V(lambda e: e.memset(vhalo[:].rearrange("p l g c -> p (l g c)"), 0.0), w=[B("vhalo", l) for l in range(L)])
        V(lambda e: e.memset(Vbp[:].rearrange("p g c -> p (g c)"), 0.0), w=[B("Vbp")])
        V(lambda e: e.memset(hcar[:].rearrange("p l s c -> p (l s c)"), 0.0), w=[B("hcar", l) for l in range(L)])
        V(lambda e: e.memset(chalo[:].rearrange("p l c k -> p (l c k)"), 0.0), w=[B("chalo", l) for l in range(L)])

        for l in range(L):
            pB = B("pl")
            are, aim, ldt = pvc(l, OARE, 8), pvc(l, OAIM, 8), pvc(l, OLDT, 8)
            c = lambda i: pl[:, i, :]
            S(lambda e: e.activation(out=c(0), in_=ldt, func=AF.Exp), r=[cB], w=[pB])
            V(lambda e: e.tensor_tensor(out=c(1), in0=are, in1=c(0), op=ALU.mult), r=[pB], w=[pB])
            V(lambda e: e.tensor_tensor(out=c(2), in0=aim, in1=c(0), op=ALU.mult), r=[pB], w=[pB])
            S(lambda e, l=l: e.activation(out=lam[:, l, 0, :], in_=c(1), func=AF.Exp), r=[pB], w=[B("lam", l)])
            V(lambda e: e.tensor_scalar(out=c(3), in0=c(2), scalar1=1.0 / TWO_PI, scalar2=None, op0=ALU.mult), r=[pB], w=[pB])
            V(lambda e: e.tensor_copy(out=bigi[:, 0:8], in_=c(3)), r=[pB], w=[pB])
            V(lambda e: e.tensor_copy(out=c(4), in_=bigi[:, 0:8]), r=[pB], w=[pB])
            V(lambda e: e.tensor_tensor(out=c(4), in0=c(3), in1=c(4), op=ALU.subtract), r=[pB], w=[pB])
            S(lambda e, l=l: e.activation(out=lam[:, l, 2, :], in_=c(4), func=AF.Sin, scale=6.283185), r=[pB], w=[B("lam", l)])
            V(lambda e: e.tensor_scalar(out=c(3), in0=c(3), scalar1=0.25, scalar2=None, op0=ALU.add), r=[pB], w=[pB])
            V(lambda e: e.tensor_copy(out=bigi[:, 0:8], in_=c(3)), r=[pB], w=[pB])
            V(lambda e: e.tensor_copy(out=c(4), in_=bigi[:, 0:8]), r=[pB], w=[pB])
            V(lambda e: e.tensor_tensor(out=c(4), in0=c(3), in1=c(4), op=ALU.subtract), r=[pB], w=[pB])
            S(lambda e, l=l: e.activation(out=lam[:, l, 1, :], in_=c(4), func=AF.Sin, scale=6.283185), r=[pB], w=[B("lam", l)])
            V(lambda e, l=l: e.tensor_tensor(out=c(5), in0=lam[:, l, 0, :], in1=lam[:, l, 1, :], op=ALU.mult), r=[pB, B("lam", l)], w=[pB])
            V(lambda e, l=l: e.tensor_tensor(out=c(6), in0=lam[:, l, 0, :], in1=lam[:, l, 2, :], op=ALU.mult), r=[pB, B("lam", l)], w=[pB])
            V(lambda e: e.tensor_scalar(out=c(7), in0=c(5), scalar1=-1.0, scalar2=None, op0=ALU.add), r=[pB], w=[pB])
            V(lambda e: e.tensor_tensor(out=c(8), in0=are, in1=are, op=ALU.mult), r=[pB], w=[pB])
            V(lambda e: e.tensor_tensor(out=c(9), in0=aim, in1=aim, op=ALU.mult), r=[pB], w=[pB])
            V(lambda e: e.tensor_tensor(out=c(8), in0=c(8), in1=c(9), op=ALU.add), r=[pB], w=[pB])
            V(lambda e: e.reciprocal(out=c(8), in_=c(8)), r=[pB], w=[pB])
            V(lambda e: e.tensor_tensor(out=c(9), in0=c(7), in1=are, op=ALU.mult), r=[pB], w=[pB])
            V(lambda e: e.tensor_tensor(out=c(10), in0=c(6), in1=aim, op=ALU.mult), r=[pB], w=[pB])
            V(lambda e: e.tensor_tensor(out=c(9), in0=c(9), in1=c(10), op=ALU.add), r=[pB], w=[pB])
            V(lambda e: e.tensor_tensor(out=c(11), in0=c(9), in1=c(8), op=ALU.mult), r=[pB], w=[pB])
            V(lambda e: e.tensor_tensor(out=c(9), in0=c(6), in1=are, op=ALU.mult), r=[pB], w=[pB])
            V(lambda e: e.tensor_tensor(out=c(10), in0=c(7), in1=aim, op=ALU.mult), r=[pB], w=[pB])
            V(lambda e: e.tensor_tensor(out=c(9), in0=c(9), in1=c(10), op=ALU.subtract), r=[pB], w=[pB])
            V(lambda e: e.tensor_tensor(out=c(12), in0=c(9), in1=c(8), op=ALU.mult), r=[pB], w=[pB])
            V(lambda e: e.tensor_scalar(out=c(13), in0=c(12), scalar1=-1.0, scalar2=None, op0=ALU.mult), r=[pB], w=[pB])
            bg = big[:, 0:2048].rearrange("p (s c k) -> p s c k", s=8, c=2)
            Q(lambda e, l=l: e.dma_start(out=big[:, 0:2048], in_=Bn_d[l, :, :]), r=[pB], w=[B("big")])
            for s8 in range(8):
                fre, fim, nfim = pl[:, 11, s8:s8 + 1], pl[:, 12, s8:s8 + 1], pl[:, 13, s8:s8 + 1]
                V(lambda e, s8=s8, fre=fre: e.tensor_scalar(out=t1[:, 0:128], in0=bg[:, s8, 0, :], scalar1=fre, scalar2=None, op0=ALU.mult), r=[pB, B("big")], w=[B("t1")])
                V(lambda e, s8=s8, nfim=nfim: e.scalar_tensor_tensor(out=t1[:, 0:128], in0=bg[:, s8, 1, :], scalar=nfim, in1=t1[:, 0:128], op0=ALU.mult, op1=ALU.add), r=[pB, B("big"), B("t1")], w=[B("t1")])
                V(lambda e, s8=s8, fre=fre: e.tensor_scalar(out=t2[:, 0:128], in0=bg[:, s8, 1, :], scalar1=fre, scalar2=None, op0=ALU.mult), r=[pB, B("big")], w=[B("t2")])
                V(lambda e, s8=s8, fim=fim: e.scalar_tensor_tensor(out=t2[:, 0:128], in0=bg[:, s8, 0, :], scalar=fim, in1=t2[:, 0:128], op0=ALU.mult, op1=ALU.add), r=[pB, B("big"), B("t2")], w=[B("t2")])
                PE(lambda e: e.transpose(out=ps[5][:, 0:128], in_=t1[:, 0:128], identity=ident[:]), r=[B("t1"), cB], w=[B("ps", 5)])
                PE(lambda e: e.transpose(out=ps[5][:, 128:256], in_=t2[:, 0:128], identity=ident[:]), r=[B("t2"), cB], w=[B("ps", 5)])
                S(lambda e, l=l, s8=s8: e.activation(out=WB[:, s8 * 2:s8 * 2 + 2, :], in_=ps[5][:, 0:256].rearrange("p (c k) -> p c k", c=2), func=AF.Copy), r=[B("ps", 5)], w=[B("WB")])
            Q(lambda e, l=l: e.dma_start(out=big[:, 0:2048], in_=Cn_d[l, :, :]), w=[B("big")])
            V(lambda e, l=l: e.tensor_copy(out=WC[:, 0:16, :].rearrange("p (s c) k -> p s c k", c=2)[:, :, 0, :], in_=bg[:, :, 0, :]), r=[B("big")], w=[B("WC")])
            V(lambda e, l=l: e.tensor_scalar(out=WC[:, 0:16, :].rearrange("p (s c) k -> p s c k", c=2)[:, :, 1, :], in0=bg[:, :, 1, :], scalar1=-1.0, scalar2=None, op0=ALU.mult), r=[B("big")], w=[B("WC")])
            for ct in range(2):
                for kk in range(31):
                    wk = pvt[:, l * PVL + OCW + ct * 31 + kk: l * PVL + OCW + ct * 31 + kk + 1]
                    V(lambda e, kk=kk, wk=wk: e.tensor_scalar(out=hbf[:, kk * 128:(kk + 1) * 128], in0=ident[:], scalar1=wk, scalar2=None, op0=ALU.mult), r=[cB], w=[B("hbf")])
                Q(lambda e, l=l, ct=ct: e.dma_start(out=cdg_d[l * 2 + ct, :, 0:3968], in_=hbf[:, 0:3968]), r=[B("hbf")], w=[B("cdg_d", l, ct)])
            a8 = big2.rearrange("p (s j) -> p s j", s=8)
            for s8 in range(8):
                V(lambda e, s8=s8: e.tensor_scalar(out=a8[:, s8, :], in0=jj[:], scalar1=pl[:, 2, s8:s8 + 1], scalar2=1.0 / TWO_PI, op0=ALU.mult, op1=ALU.mult), r=[pB, cB], w=[B("big2"), B("rot")])
            rt = big.rearrange("p (s c j) -> p s c j", s=8, c=2)
            for ci, sh in ((1, 0.0), (0, 0.25)):
                if sh:
                    V(lambda e, sh=sh: e.tensor_scalar(out=big2, in0=big2, scalar1=sh, scalar2=None, op0=ALU.add), r=[B("big2")], w=[B("big2"), B("rot")])
                for pc in range(8 * J // 512):
                    V(lambda e, pc=pc: e.tensor_copy(out=bigi, in_=big2[:, pc * 512:(pc + 1) * 512]), r=[B("big2"), B("rot")], w=[B("bigi")])
                    V(lambda e, pc=pc: e.tensor_copy(out=rotflat[:, pc * 512:(pc + 1) * 512], in_=bigi), r=[B("bigi")], w=[B("rot")])
                V(lambda e: e.tensor_tensor(out=rotflat[:, 0:8 * J], in0=big2, in1=rotflat[:, 0:8 * J], op=ALU.subtract), r=[B("big2"), B("rot")], w=[B("rot")])
                S(lambda e, ci=ci: e.activation(out=rt[:, :, ci, :], in_=rotflat[:, 0:8 * J].rearrange("p (s j) -> p s j", s=8), func=AF.Sin, scale=6.283185), r=[B("rot")], w=[B("big")])
            Q(lambda e, l=l: e.dma_start(out=rot_d[l, :, :], in_=big), r=[B("big")], w=[B("rot_d", l)])
            Q(lambda e, l=l: e.dma_start(out=wb_d[l, :, :], in_=WB[:].rearrange("p a k -> p (a k)")), r=[B("WB")], w=[B("wb_d", l)])
            Q(lambda e, l=l: e.dma_start(out=wc_d[l, :, :], in_=WC[:].rearrange("p a k -> p (a k)")), r=[B("WC")], w=[B("wc_d", l)])

        print('MARK prologue_end', P.nops, flush=True)
        slot_i = {"M": 0, "F": 0}

        def wload(which, src_ap, shape_str, wid=None, ch=0, **kw):
            ring = ringM if which == "M" else ringF
            s = slot_i[which] % 3
            if not P.dry:
                slot_i[which] += 1
            n = 1
            for d_ in src_ap.shape[1:]:
                n *= d_
            if ch == 0:
                dst = ring[:, s, 0:n]
                if shape_str:
                    dst = dst.rearrange(shape_str, **kw)
                G(lambda e: e.dma_start(out=dst, in_=src_ap), w=[B("slot" + which, s)])
                if nch > 1:
                    Q(lambda e: e.dma_start(out=wscr[wid, :, 0:n], in_=ring[:, s, 0:n]), r=[B("slot" + which, s)], w=[B("wscr", wid)])
            else:
                G(lambda e: e.dma_start(out=ring[:, s, 0:n], in_=wscr[wid, :, 0:n]), r=[B("wscr", wid)], w=[B("slot" + which, s)])
            return s

        def rmsnorm(x, hb, par, goff, ccs, sq, rstd, pbk, tag):
            for (c0, cn) in ccs:
                pb_ = ps[pbk]
                for k in range(8):
                    S(lambda e, k=k: e.activation(out=sq[:, 0:cn], in_=x[:, k, c0:c0 + cn], func=AF.Square), r=[*XB(par, c0)], w=[B("sq", tag)])
                    PE(lambda e, k=k: e.matmul(pb_[:, 0:cn], lhsT=ones_b[:], rhs=sq[:, 0:cn], start=(k == 0), stop=(k == 7)), r=[B("sq", tag), cB], w=[B("ps", pbk)])
                S(lambda e: e.activation(out=rstd[:, 0:cn], in_=pb_[:, 0:cn], func=AF.Sqrt, scale=1.0 / D, bias=EPS), r=[B("ps", pbk)], w=[B("rstd", tag)])
                V(lambda e: e.reciprocal(out=rstd[:, 0:cn], in_=rstd[:, 0:cn]), r=[B("rstd", tag)], w=[B("rstd", tag)])
                for k in range(8):
                    gk = pvt[:, goff + k: goff + k + 1]
                    V(lambda e, k=k, gk=gk: e.scalar_tensor_tensor(out=hb[:, k, c0:c0 + cn], in0=x[:, k, c0:c0 + cn], scalar=gk, in1=rstd[:, 0:cn], op0=ALU.mult, op1=ALU.mult), r=[*XB(par, c0), B("rstd", tag), cB], w=[B("hb", par, c0)])

        def gen_mix(ch, l):
            par = ch % 2; x = X[par]; hb = HB[par]
            ring, SL = ringM, "slotM"
            first, last = (ch == 0), (ch == nch - 1)
            ccs = [(0, 512)] + ([(T, NS)] if first else [])
            if l == 0:
                for k in range(8):
                    Q(lambda e, k=k: e.dma_start(out=x[:, k, 0:T], in_=xT[k * 128:(k + 1) * 128, ch * T:(ch + 1) * T]), w=[*XB(par, 0), B("big"), B("rot"), B("big2")])
                if first:
                    Q(lambda e: e.dma_start(out=x[:, :, T:NTM], in_=xsT.rearrange("(k p) n -> p k n", p=128)), w=[*XB(par, T), B("big")])
                yield
            if True:
                Q(lambda e, l=l: e.dma_start(out=WB[:].rearrange("p a k -> p (a k)"), in_=wb_d[l, :, :]), r=[B("wb_d", l)], w=[B("WB")])
                Q(lambda e, l=l: e.dma_start(out=WC[:].rearrange("p a k -> p (a k)"), in_=wc_d[l, :, :]), r=[B("wc_d", l)], w=[B("WC")])
                for ct in range(2):
                    V(lambda e, ct=ct: e.tensor_scalar(out=Dd[:, ct, :], in0=ident[:], scalar1=pvc(l, OSD + ct), scalar2=None, op0=ALU.mult), r=[cB], w=[B("Dd")])
                G(lambda e: e.dma_start(out=gluw[:], in_=glu_w[l].rearrange("(c p) n -> p c n", p=128)), w=[B("gluw")])
                s_in = [wload("M", w_in[l, :, i * 512:(i + 1) * 512].rearrange("(k p) n -> p k n", p=128), "p (k n) -> p k n", wid=l * 23 + i, ch=ch, k=8) for i in range(3)]

                def win(o_lo, o_n):
                    s = s_in[o_lo // 512]
                    off = o_lo % 512
                    return lambda k: ring[:, s, k * 512 + off: k * 512 + off + o_n], B(SL, s)

                V(lambda e, l=l: e.tensor_copy(out=kT[:, :, 0:128], in_=khalo[:, l, :, :]), r=[B("khalo", l)], w=[B("kT", "h")])
                V(lambda e, l=l: e.tensor_copy(out=vp[:, 0, :, :], in_=vhalo[:, l, :, :]), r=[B("vhalo", l)], w=[B("vp", 0)])
                V(lambda e, l=l: e.tensor_copy(out=ucv[:, :, 0:30], in_=chalo[:, l, :, :]), r=[B("chalo", l)], w=[B("ucv", "h")])

                rmsnorm(x, hb, par, l * PVL + OG1, ccs, sq, rstd, 6, "m")
                for (c0, cn) in ccs:
                    def mm8(lw, n_out, pbk):
                        f, sb_ = lw
                        for k in range(8):
                            PE(lambda e, k=k: e.matmul(ps[pbk][0:n_out, 0:cn], lhsT=f(k), rhs=hb[:, k, c0:c0 + cn], start=(k == 0), stop=(k == 7)), r=[sb_, B("hb", par, c0)], w=[B("ps", pbk)])
                    for h in range(8):
                        pbk = bank(); mm8(win(64 * h, 64), 64, pbk)
                        S(lambda e, h=h, pbk=pbk: e.activation(out=qT[:, h, c0:c0 + cn], in_=ps[pbk][0:64, 0:cn], func=AF.Copy, scale=0.125), r=[B("ps", pbk)], w=[B("qT", c0)])
                        yield
                    for g in range(2):
                        pbk = bank(); mm8(win(512 + 64 * g, 64), 64, pbk)
                        S(lambda e, g=g, pbk=pbk: e.activation(out=kT[:, g, 128 + c0:128 + c0 + cn], in_=ps[pbk][0:64, 0:cn], func=AF.Copy), r=[B("ps", pbk)], w=[B("kT", c0)])
                        yield
                    for ct in range(2):
                        pa = bank(); mm8(win(768 + 128 * ct, 128), 128, pa)
                        pg = bank(); mm8(win(1024 + 128 * ct, 128), 128, pg)
                        S(lambda e, pg=pg: e.activation(out=g2[:, 0:cn], in_=ps[pg][:, 0:cn], func=AF.Sigmoid), r=[B("ps", pg)], w=[B("g2")])
                        V(lambda e, ct=ct, pa=pa: e.tensor_tensor(out=ucv[:, ct, 30 + c0:30 + c0 + cn], in0=ps[pa][:, 0:cn], in1=g2[:, 0:cn], op=ALU.mult), r=[B("ps", pa), B("g2")], w=[B("ucv", c0)])
                        yield
                    for ct in range(2):
                        pbk = bank(); mm8(win(1280 + 128 * ct, 128), 128, pbk)
                        S(lambda e, ct=ct, pbk=pbk: e.activation(out=us[:, ct, c0:c0 + cn], in_=ps[pbk][:, 0:cn], func=AF.Copy), r=[B("ps", pbk)], w=[B("us", c0)])
                        yield
                fv, sv = win(640, 128)
                fk, sk = win(512, 128)
                for bi in range(4):
                    c0 = bi * 128
                    cc0 = (c0 // 512) * 512
                    pbk = bank()
                    for k in range(8):
                        PE(lambda e, k=k: e.matmul(ps[pbk][:, 0:128], lhsT=hb[:, k, c0:c0 + 128], rhs=fv(k), start=(k == 0), stop=(k == 7)), r=[sv, B("hb", par, cc0)], w=[B("ps", pbk)])
                    S(lambda e, bi=bi, pbk=pbk: e.activation(out=vp[:, bi + 1, :, 64:128], in_=ps[pbk][:, 0:128].rearrange("p (g d) -> p g d", g=2), func=AF.Copy), r=[B("ps", pbk)], w=[B("vp", bi + 1)])
                    yield
                    if last and bi == 3:
                        V(lambda e, pbk=pbk: e.tensor_copy(out=tk[:], in_=ps[pbk][:, 0:128]), r=[B("ps", pbk), B("vp", bi + 1)], w=[B("tk")])
                        Q(lambda e, l=l: e.dma_start(out=ov_p[l, :, :], in_=tk[:]), r=[B("tk")], w=[B("ov_p", l)])
                        pbk2 = bank()
                        for k in range(8):
                            PE(lambda e, k=k: e.matmul(ps[pbk2][:, 0:128], lhsT=hb[:, k, c0:c0 + 128], rhs=fk(k), start=(k == 0), stop=(k == 7)), r=[sk, B("hb", par, cc0)], w=[B("ps", pbk2)])
                        V(lambda e, pbk2=pbk2: e.tensor_copy(out=tk[:], in_=ps[pbk2][:, 0:128]), r=[B("ps", pbk2)], w=[B("tk")])
                        Q(lambda e, l=l: e.dma_start(out=ok_p[l, :, :], in_=tk[:]), r=[B("tk")], w=[B("ok_p", l)])
                if first:
                    pbk = bank()
                    for (f_, s_, off) in ((fk, sk, 0), (fv, sv, 128)):
                        for k in range(8):
                            PE(lambda e, k=k, f_=f_, off=off: e.matmul(ps[pbk][0:NS, off:off + 128], lhsT=hb[:, k, T:NTM], rhs=f_(k), start=(k == 0), stop=(k == 7)), r=[s_, B("hb", par, T)], w=[B("ps", pbk)])
                    V(lambda e, pbk=pbk: e.tensor_copy(out=tkb[:].rearrange("p a k -> p (a k)"), in_=ps[pbk][0:NS, 0:256]), r=[B("ps", pbk)], w=[B("tkb")])
                    for b in range(NS):
                        Q(lambda e, l=l, b=b: e.dma_start(out=ok_s[l, b:b + 1, 0:127, :].rearrange("b r c -> b (r c)"), in_=ck_d[l, b:b + 1, 1:128, :].rearrange("b r c -> b (r c)")), w=[B("ok_s", l)])
                        Q(lambda e, l=l, b=b: e.dma_start(out=ov_s[l, b:b + 1, 0:127, :].rearrange("b r c -> b (r c)"), in_=cv_d[l, b:b + 1, 1:128, :].rearrange("b r c -> b (r c)")), w=[B("ov_s", l)])
                    Q(lambda e, l=l: e.dma_start(out=ok_s[l, :, 127, :], in_=tkb[:, 0, :]), r=[B("tkb")], w=[B("ok_s", l)])
                    Q(lambda e, l=l: e.dma_start(out=ov_s[l, :, 127, :], in_=tkb[:, 1, :]), r=[B("tkb")], w=[B("ov_s", l)])
                if not last:
                    V(lambda e, l=l: e.tensor_copy(out=khalo[:, l, :, :], in_=kT[:, :, T:T + 128]), r=[B("kT", 0)], w=[B("khalo", l)])
                    V(lambda e, l=l: e.tensor_copy(out=vhalo[:, l, :, :], in_=vp[:, 4, :, :]), r=[B("vp", 4)], w=[B("vhalo", l)])
                    V(lambda e, l=l: e.tensor_copy(out=chalo[:, l, :, :], in_=ucv[:, :, T:T + 30]), r=[B("ucv", 0)], w=[B("chalo", l)])
                else:
                    V(lambda e: e.tensor_copy(out=cstg[:], in_=ucv[:, :, T:T + 30]), r=[B("ucv", 0)], w=[B("cstg")])
                    Q(lambda e, l=l: e.dma_start(out=oconv_p[l, :, :].rearrange("p (c k) -> p c k", c=2), in_=cstg[:]), r=[B("cstg")], w=[B("oconv_p", l)])

                s_cd = []
                for ct in range(2):
                    s_ = slot_i["M"] % 3
                    if not P.dry:
                        slot_i["M"] += 1
                    G(lambda e, s_=s_, ct=ct: e.dma_start(out=ringM[:, s_, 0:3968], in_=cdg_d[l * 2 + ct, :, 0:3968]), r=[B("cdg_d", l, ct)], w=[B("slotM", s_)])
                    s_cd.append(s_)
                units = [(bi, tile, r2) for bi in range(4) for tile in range(4) for r2 in range(2)]

                def att_info(u):
                    bi, tile, r2 = units[u]
                    q0 = bi * 128
                    nokprev = first and bi == 0
                    k_lo, k_n = (128, 128) if nokprev else (0, 256)
                    return bi, tile, r2, tile * 2 + r2, (tile * 2 + r2) // 4, q0, nokprev, k_lo, k_n, u % 3

                def att_A1(u):
                    bi, tile, r2, h, g, q0, nokprev, k_lo, k_n, a = att_info(u)
                    SB = 4 if u % 2 == 0 else 6
                    kread = [B("kT", 0)] + ([B("kT", "h")] if bi == 0 else [])
                    PE(lambda e: e.matmul(ps[SB][:, k_lo:k_lo + k_n], lhsT=qT[:, h, q0:q0 + 128], rhs=kT[:, g, q0 + k_lo:q0 + k_lo + k_n], start=True, stop=True), r=[B("qT", 0)] + kread, w=[B("ps", SB)])
                    V(lambda e: e.scalar_tensor_tensor(out=sc[:, a, k_lo:k_lo + k_n], in0=abias[:, k_lo:k_lo + k_n], scalar=float(2.0 ** (-(h + 1))), in1=ps[SB][:, k_lo:k_lo + k_n], op0=ALU.mult, op1=ALU.add), r=[B("ps", SB), cB], w=[B("sc", a)])
                    V(lambda e: e.reduce_max(out=sm[:, a, 0:1], in_=sc[:, a, k_lo:k_lo + k_n], axis=AX.X), r=[B("sc", a)], w=[B("sm", a)])
                    sinkc = pvc(l, OSINK + h)
                    V(lambda e: e.tensor_scalar(out=sm[:, a, 1:2], in0=sm[:, a, 0:1], scalar1=sinkc, scalar2=-1.0, op0=ALU.max, op1=ALU.mult), r=[B("sm", a), cB], w=[B("sm", a)])
                    S(lambda e: e.activation(out=pb[:, a, k_lo:k_lo + k_n], in_=sc[:, a, k_lo:k_lo + k_n], func=AF.Exp, bias=sm[:, a, 1:2], accum_out=sm[:, a, 2:3]), r=[B("sc", a), B("sm", a)], w=[B("pb", a), B("sm", a)])
                    S(lambda e: e.activation(out=sm[:, a, 3:4], in_=sinkc, func=AF.Exp, bias=sm[:, a, 1:2]), r=[B("sm", a), cB], w=[B("sm", a)])

                def att_A2(u):
                    bi, tile, r2, h, g, q0, nokprev, k_lo, k_n, a = att_info(u)
                    V(lambda e: e.tensor_tensor(out=sm[:, a, 4:5], in0=sm[:, a, 2:3], in1=sm[:, a, 3:4], op=ALU.add), r=[B("sm", a)], w=[B("sm", a)])
                    V(lambda e: e.reciprocal(out=sm[:, a, 5:6], in_=sm[:, a, 4:5]), r=[B("sm", a)], w=[B("sm", a)])
                    V(lambda e: e.tensor_scalar(out=pb[:, a, k_lo:k_lo + k_n], in0=pb[:, a, k_lo:k_lo + k_n], scalar1=sm[:, a, 5:6], scalar2=None, op0=ALU.mult), r=[B("sm", a), B("pb", a)], w=[B("pb", a)])

                def att_B(u):
                    bi, tile, r2, h, g, q0, nokprev, k_lo, k_n, a = att_info(u)
                    pa_ = u % 2
                    for kb in range(2):
                        if nokprev and kb == 0:
                            continue
                        PE(lambda e, kb=kb: e.transpose(out=psb[:, pa_ * 256 + kb * 128:pa_ * 256 + kb * 128 + 128], in_=pb[:, a, kb * 128:kb * 128 + 128], identity=identb[:]), r=[B("pb", a), B("identb")], w=[B("psb", 0)])
                    S(lambda e: e.activation(out=ptb[:, pa_, k_lo:k_lo + k_n], in_=psb[:, pa_ * 256 + k_lo:pa_ * 256 + k_lo + k_n], func=AF.Copy), r=[B("psb", 0)], w=[B("ptb", pa_)])
                    kbs = [1] if nokprev else [0, 1]
                    for kb in kbs:
                        lo = 64 if r2 == 0 else 0
                        PE(lambda e, kb=kb, lo=lo: e.matmul(ps[5][:, 0:128], lhsT=vp[:, bi + kb, g, lo:lo + 128], rhs=ptb[:, pa_, kb * 128:kb * 128 + 128], start=(r2 == 0 and kb == kbs[0]), stop=(r2 == 1 and kb == 1)), r=[B("vp", bi + kb), B("ptb", pa_)], w=[B("ps", 5)])
                    if r2 == 1:
                        S(lambda e: e.activation(out=hb[:, tile, q0:q0 + 128], in_=ps[5][:, 0:128], func=AF.Copy), r=[B("ps", 5)], w=[B("hb", par, 0)])

                def ssm_out(c0, cn, hsrc_re, hsrc_im, hoff, hbufs, ob=6):
                    for ct in range(2):
                        for j4 in range(4):
                            s8 = ct * 4 + j4
                            PE(lambda e, ct=ct, s8=s8, j4=j4: e.matmul(ps[ob][:, 0:cn], lhsT=WC[:, s8 * 2, :], rhs=hsrc_re(s8), start=(j4 == 0), stop=False), r=hbufs + [B("WC")], w=[B("ps", ob)])
                            PE(lambda e, ct=ct, s8=s8: e.matmul(ps[ob][:, 0:cn], lhsT=WC[:, s8 * 2 + 1, :], rhs=hsrc_im(s8), start=False, stop=False), r=hbufs + [B("WC")], w=[B("ps", ob)])
                        PE(lambda e, ct=ct: e.matmul(ps[ob][:, 0:cn], lhsT=Dd[:, ct, :], rhs=us[:, ct, c0:c0 + cn], start=False, stop=True), r=[B("us", (c0 // 512) * 512 if c0 < T else T), B("Dd")], w=[B("ps", ob)])
                        V(lambda e, ct=ct: e.tensor_copy(out=ysm[:, ct, 0:cn], in_=ps[ob][:, 0:cn]), r=[B("ps", ob)], w=[B("ysm", ct)])
                        V(lambda e, ct=ct: e.tensor_tensor(out=g1[:, 0:cn], in0=ysm[:, ct, 0:cn], in1=ysm[:, ct, 0:cn], op=ALU.mult), r=[B("ysm", ct)], w=[B("g1")])
                        V(lambda e, ct=ct: e.tensor_scalar(out=g1[:, 0:cn], in0=g1[:, 0:cn], scalar1=0.044715, scalar2=1.0, op0=ALU.mult, op1=ALU.add), r=[B("g1")], w=[B("g1")])
                        V(lambda e, ct=ct: e.tensor_tensor(out=g1[:, 0:cn], in0=g1[:, 0:cn], in1=ysm[:, ct, 0:cn], op=ALU.mult), r=[B("g1"), B("ysm", ct)], w=[B("g1")])
                        S(lambda e, ct=ct: e.activation(out=g2[:, 0:cn], in_=g1[:, 0:cn], func=AF.Sigmoid, scale=2.0 * math.sqrt(2.0 / math.pi)), r=[B("g1")], w=[B("g2")])
                        V(lambda e, ct=ct: e.tensor_tensor(out=ysm[:, ct, 0:cn], in0=ysm[:, ct, 0:cn], in1=g2[:, 0:cn], op=ALU.mult), r=[B("g2"), B("ysm", ct)], w=[B("ysm", ct)])
                        V(lambda e, ct=ct: e.tensor_copy(out=yb[:, ct, 0:cn], in_=ysm[:, ct, 0:cn]), r=[B("ysm", ct)], w=[B("yb", ct)])
                    for co in range(2):
                        for ct in range(2):
                            PE(lambda e, ct=ct, co=co: e.matmul(ps[ob][:, 0:cn], lhsT=gluw[:, ct, co * 128:(co + 1) * 128], rhs=yb[:, ct, 0:cn], start=(ct == 0), stop=(ct == 1)), r=[B("yb", 0), B("yb", 1), B("gluw")], w=[B("ps", ob)])
                        S(lambda e, co=co: e.activation(out=g2[:, 0:cn], in_=ps[ob][:, 0:cn], func=AF.Sigmoid, bias=pvc(l, OGB + co)), r=[B("ps", ob), cB], w=[B("g2")])
                        V(lambda e, co=co: e.tensor_tensor(out=hb[:, 6 + co, c0:c0 + cn], in0=ysm[:, co, 0:cn], in1=g2[:, 0:cn], op=ALU.mult), r=[B("g2"), B("ysm", co)], w=[B("hb", par, (c0 // 512) * 512 if c0 < T else T)])

                def att_gen():
                    for u in range(len(units) + 2):
                        if u < len(units):
                            att_A1(u)
                        if 1 <= u <= len(units):
                            att_A2(u - 1)
                        if u >= 2:
                            att_B(u - 2)
                        yield

                def ssm_gen():
                    for sc_i in range(T // J):
                        c0 = sc_i * J
                        ucc = (c0 // 512) * 512
                        for s8 in range(8):
                            ct, a = s8 // 4, s8 % 2
                            for ci, dst in ((0, 0), (1, J)):
                                PE(lambda e, ci=ci, dst=dst, s8=s8, ct=ct: e.matmul(ps[3][:, dst:dst + J], lhsT=WB[:, s8 * 2 + ci, :], rhs=us[:, ct, c0:c0 + J], start=True, stop=True), r=[B("us", ucc), B("WB")], w=[B("ps", 3)])
                            ra = s8 % 2
                            Q(lambda e, s8=s8, ra=ra: e.dma_start(out=rot2[:, ra, :, :].rearrange("p c j -> p (c j)"), in_=rot_d[l, :, s8 * 2 * J:(s8 + 1) * 2 * J]), r=[B("rot_d", l)], w=[B("rot2", ra)])
                            cosT, sinT = rot2[:, ra, 0, :], rot2[:, ra, 1, :]
                            pre, pim = ps[3][:, 0:J], ps[3][:, J:2 * J]
                            V(lambda e, cosT=cosT, pre=pre: e.tensor_tensor(out=bre[:], in0=pre, in1=cosT, op=ALU.mult), r=[B("ps", 3), B("rot2", ra)], w=[B("bre")])
                            V(lambda e, sinT=sinT, pim=pim: e.tensor_tensor(out=t1[:], in0=pim, in1=sinT, op=ALU.mult), r=[B("ps", 3), B("rot2", ra)], w=[B("t1")])
                            V(lambda e, cosT=cosT, pim=pim: e.tensor_tensor(out=bim[:], in0=pim, in1=cosT, op=ALU.mult), r=[B("ps", 3), B("rot2", ra)], w=[B("bim")])
                            V(lambda e, sinT=sinT, pre=pre: e.tensor_tensor(out=t2[:], in0=pre, in1=sinT, op=ALU.mult), r=[B("ps", 3), B("rot2", ra)], w=[B("t2")])
                            V(lambda e: e.tensor_tensor(out=bre[:], in0=bre[:], in1=t1[:], op=ALU.add), r=[B("t1"), B("bre")], w=[B("bre")])
                            V(lambda e: e.tensor_tensor(out=bim[:], in0=bim[:], in1=t2[:], op=ALU.subtract), r=[B("t2"), B("bim")], w=[B("bim")])
                            rho_b = lam[:, l, 0, s8:s8 + 1].to_broadcast([128, J])
                            V(lambda e, rho_b=rho_b, s8=s8: e.tensor_tensor_scan(out=wre[:], data0=rho_b, data1=bre[:], initial=hcar[:, l, s8, 0:1], op0=ALU.mult, op1=ALU.add), r=[B("bre"), B("lam", l), B("hcar", l)], w=[B("wre")])
                            V(lambda e, rho_b=rho_b, s8=s8: e.tensor_tensor_scan(out=wim[:], data0=rho_b, data1=bim[:], initial=hcar[:, l, s8, 1:2], op0=ALU.mult, op1=ALU.add), r=[B("bim"), B("lam", l), B("hcar", l)], w=[B("wim")])
                            V(lambda e, cosT=cosT: e.tensor_tensor(out=t1[:], in0=wre[:], in1=cosT, op=ALU.mult), r=[B("wre"), B("rot2", ra)], w=[B("t1")])
                            V(lambda e, sinT=sinT: e.tensor_tensor(out=t2[:], in0=wim[:], in1=sinT, op=ALU.mult), r=[B("wim"), B("rot2", ra)], w=[B("t2")])
                            V(lambda e, sinT=sinT: e.tensor_tensor(out=bre[:], in0=wre[:], in1=sinT, op=ALU.mult), r=[B("wre"), B("rot2", ra)], w=[B("bre")])
                            V(lambda e, cosT=cosT: e.tensor_tensor(out=bim[:], in0=wim[:], in1=cosT, op=ALU.mult), r=[B("wim"), B("rot2", ra)], w=[B("bim")])
                            V(lambda e, s8=s8: e.tensor_tensor(out=hbf[:, s8 * J:(s8 + 1) * J], in0=t1[:], in1=t2[:], op=ALU.subtract), r=[B("t1"), B("t2")], w=[B("hbf")])
                            V(lambda e, s8=s8: e.tensor_tensor(out=hbf[:, 8 * J + s8 * J:8 * J + (s8 + 1) * J], in0=bre[:], in1=bim[:], op=ALU.add), r=[B("bre"), B("bim")], w=[B("hbf")])
                            cl, sl = rot2[:, ra, 0, J - 1:J], rot2[:, ra, 1, J - 1:J]
                            V(lambda e, sl=sl: e.tensor_tensor(out=smc[:, 0:1], in0=wim[:, J - 1:J], in1=sl, op=ALU.mult), r=[B("wim"), B("rot2", ra)], w=[B("smc")])
                            V(lambda e, s8=s8, cl=cl: e.scalar_tensor_tensor(out=hcar[:, l, s8, 0:1], in0=wre[:, J - 1:J], scalar=cl, in1=smc[:, 0:1], op0=ALU.mult, op1=ALU.subtract), r=[B("wre"), B("rot2", ra), B("smc")], w=[B("hcar", l)])
                            V(lambda e, sl=sl: e.tensor_tensor(out=smc[:, 1:2], in0=wre[:, J - 1:J], in1=sl, op=ALU.mult), r=[B("wre"), B("rot2", ra)], w=[B("smc")])
                            V(lambda e, s8=s8, cl=cl: e.scalar_tensor_tensor(out=hcar[:, l, s8, 1:2], in0=wim[:, J - 1:J], scalar=cl, in1=smc[:, 1:2], op0=ALU.mult, op1=ALU.add), r=[B("wim"), B("rot2", ra), B("smc")], w=[B("hcar", l)])
                            yield
                        hbre = hbf
                        ssm_out(c0, J, lambda s8: hbre[:, s8 * J:(s8 + 1) * J], lambda s8: hbre[:, 8 * J + s8 * J:8 * J + (s8 + 1) * J], 0, [B("hbf")], ob=3)
                        yield

                ga_, gs_ = att_gen(), ssm_gen()
                alive_ = [True, True]
                while alive_[0] or alive_[1]:
                    for idx_, (g_, reps_) in enumerate(((ga_, 2), (gs_, 1))):
                        for _r in range(reps_):
                            if alive_[idx_]:
                                try:
                                    next(g_)
                                except StopIteration:
                                    alive_[idx_] = False
                    yield
                if first:
                    for g in range(2):
                        sk4 = sinkt[:, l * 2 + g:l * 2 + g + 1]
                        for b4 in range(NS // 4):
                            for bb in range(4):
                                b = b4 * 4 + bb
                                Q(lambda e, b=b, l=l: e.dma_start(out=Kb[:, 0, :], in_=ok_s[l, b, :, :]), r=[B("ok_s", l)], w=[B("Kb", 0)])
                                PE(lambda e, b=b, g=g: e.transpose(out=ps[4][0:64, 0:128], in_=Kb[:, 0, g * 64:(g + 1) * 64], identity=ident[:]), r=[B("Kb", 0), cB], w=[B("ps", 4)])
                                S(lambda e, b=b: e.activation(out=KbT[:, b % 2, :], in_=ps[4][0:64, 0:128], func=AF.Copy), r=[B("ps", 4)], w=[B("KbT", b % 2)])
                                PE(lambda e, b=b, g=g, bb=bb: e.matmul(ps[6][0:4, bb * 128:bb * 128 + 128], lhsT=qT[:, 4 * g:4 * g + 4, T + b], rhs=KbT[:, b % 2, :], start=True, stop=True), r=[B("qT", T), B("KbT", b % 2)], w=[B("ps", 6)])
                            V(lambda e, g=g: e.tensor_tensor(out=ssc[:], in0=ps[6][0:4, :].rearrange("p (b k) -> p b k", b=4), in1=sbias[:, g:g + 1, :].to_broadcast([4, 4, 128]), op=ALU.add), r=[B("ps", 6), cB], w=[B("ssc")])
                            V(lambda e: e.reduce_max(out=ssm_[:, 0, :], in_=ssc[:], axis=AX.X), r=[B("ssc")], w=[B("ssm_")])
                            V(lambda e, sk4=sk4: e.tensor_scalar(out=ssm_[:, 0, :], in0=ssm_[:, 0, :], scalar1=sk4, scalar2=None, op0=ALU.max), r=[B("ssm_"), cB], w=[B("ssm_")])
                            V(lambda e: e.tensor_tensor(out=ssc[:], in0=ssc[:], in1=ssm_[:, 0, :].unsqueeze(2).to_broadcast([4, 4, 128]), op=ALU.subtract), r=[B("ssc"), B("ssm_")], w=[B("ssc")])
                            S(lambda e: e.activation(out=ssc[:], in_=ssc[:], func=AF.Exp), r=[B("ssc")], w=[B("ssc")])
                            V(lambda e: e.reduce_sum(out=ssm_[:, 1, :], in_=ssc[:], axis=AX.X), r=[B("ssc")], w=[B("ssm_")])
                            S(lambda e, sk4=sk4: e.activation(out=ssm_[:, 2, :], in_=ssm_[:, 0, :], func=AF.Exp, scale=-1.0, bias=sk4), r=[B("ssm_"), cB], w=[B("ssm_")])
                            V(lambda e: e.tensor_tensor(out=ssm_[:, 1, :], in0=ssm_[:, 1, :], in1=ssm_[:, 2, :], op=ALU.add), r=[B("ssm_")], w=[B("ssm_")])
                            V(lambda e: e.reciprocal(out=ssm_[:, 1, :], in_=ssm_[:, 1, :]), r=[B("ssm_")], w=[B("ssm_")])
                            V(lambda e: e.tensor_tensor(out=spb[:], in0=ssc[:], in1=ssm_[:, 1, :].unsqueeze(2).to_broadcast([4, 4, 128]), op=ALU.mult), r=[B("ssc"), B("ssm_")], w=[B("spb")])
                            for bb in range(4):
                                PE(lambda e, bb=bb: e.transpose(out=psb[:, 512 + bb * 4:512 + bb * 4 + 4], in_=spb[:, bb, :], identity=identb[0:4, 0:4]), r=[B("spb"), B("identb")], w=[B("psb", 0)])
                            S(lambda e, g=g, b4=b4: e.activation(out=sptb[:, g, b4 * 4:b4 * 4 + 4, :], in_=psb[:, 512:512 + 16].rearrange("p (b r) -> p b r", r=4), func=AF.Copy), r=[B("psb", 0)], w=[B("sptb", g)])
                            yield
                    for b in range(NS):
                        Q(lambda e, b=b, l=l: e.dma_start(out=Vb[:, 0, :], in_=ov_s[l, b, :, :]), r=[B("ov_s", l)], w=[B("Vb", 0)])
                        V(lambda e, b=b: e.tensor_copy(out=Vbp[:, :, 64:128], in_=Vb[:, 0, :].rearrange("p (g d) -> p g d", g=2)), r=[B("Vb", 0)], w=[B("Vbp")])
                        for tile in range(4):
                            g = tile // 2
                            for r2 in range(2):
                                rr4 = (tile % 2) * 2 + r2
                                lo = 64 if r2 == 0 else 0
                                PE(lambda e, b=b, g=g, tile=tile, r2=r2, rr4=rr4, lo=lo: e.matmul(ps[5][:, 256 + b * 4 + tile:256 + b * 4 + tile + 1], lhsT=Vbp[:, g, lo:lo + 128], rhs=sptb[:, g, b, rr4:rr4 + 1], start=(r2 == 0), stop=(r2 == 1)), r=[B("Vbp"), B("sptb", g)], w=[B("ps", 5)])
                    S(lambda e: e.activation(out=hb[:, 0:4, T:NTM].rearrange("p t b -> p b t"), in_=ps[5][:, 256:256 + NS * 4].rearrange("p (b t) -> p b t", t=4), func=AF.Copy), r=[B("ps", 5)], w=[B("hb", par, T)])

                if first:
                    Q(lambda e, l=l: e.dma_start(out=cs[:, :, :, 0:30], in_=cconv_d[l, :, :].rearrange("p (c b k) -> p c b k", c=2, b=NS)), w=[B("cs")])
                    V(lambda e: e.tensor_copy(out=cs[:, :, :, 30], in_=ucv[:, :, 30 + T:30 + NTM]), r=[B("ucv", T)], w=[B("cs")])
                    Q(lambda e, l=l: e.dma_start(out=oconv_s[l, :, :].rearrange("p (c b k) -> p c b k", c=2, b=NS), in_=cs[:, :, :, 1:31]), r=[B("cs")], w=[B("oconv_s", l)])
                    for ct in range(2):
                        cw = pvt[:, l * PVL + OCW + ct * 31: l * PVL + OCW + ct * 31 + 31]
                        V(lambda e, ct=ct, cw=cw: e.tensor_tensor(out=cst[:, ct, :, :], in0=cs[:, ct, :, :], in1=cw.unsqueeze(1).to_broadcast([128, NS, 31]), op=ALU.mult), r=[B("cs"), cB], w=[B("ysm", 0), B("ysm", 1)])
                        V(lambda e, ct=ct: e.reduce_sum(out=acc[:, ct, T:NTM], in_=cst[:, ct, :, :], axis=AX.X), r=[B("ysm", 0), B("ysm", 1)], w=[B("acc", T, ct)])
                        V(lambda e, ct=ct: e.tensor_scalar(out=acc[:, ct, T:NTM], in0=acc[:, ct, T:NTM], scalar1=pvc(l, OCB + ct), scalar2=None, op0=ALU.add), r=[B("acc", T, ct), cB], w=[B("acc", T, ct)])
                for (c0, cn) in ccs:
                    if c0 < T:
                        hr = [B("ucv", c0)] + ([B("ucv", "h")] if c0 == 0 else [B("ucv", c0 - 512)])
                        for ct in range(2):
                            pc = bank()
                            for kk in range(31):
                                PE(lambda e, ct=ct, kk=kk, pc=pc: e.matmul(ps[pc][:, 0:cn], lhsT=ringM[:, s_cd[ct], kk * 128:(kk + 1) * 128], rhs=ucv[:, ct, c0 + kk:c0 + kk + cn], start=(kk == 0), stop=(kk == 30)), r=hr + [B("slotM", s_cd[ct])], w=[B("ps", pc)])
                            S(lambda e, ct=ct, pc=pc: e.activation(out=acc[:, ct, c0:c0 + cn], in_=ps[pc][:, 0:cn], func=AF.Identity, bias=pvc(l, OCB + ct)), r=[B("ps", pc), cB], w=[B("acc", c0, ct)])
                    for ct in range(2):
                        S(lambda e, ct=ct: e.activation(out=ysq[:, ct, 0:cn], in_=acc[:, ct, c0:c0 + cn], func=AF.Square), r=[B("acc", c0, 0), B("acc", c0, 1)], w=[B("ysm", 0), B("ysm", 1)])
                    for ct in range(2):
                        PE(lambda e, ct=ct: e.matmul(ps[6][:, 0:cn], lhsT=ones_f[:], rhs=acc[:, ct, c0:c0 + cn], start=(ct == 0), stop=(ct == 1)), r=[B("acc", c0, 0), B("acc", c0, 1), cB], w=[B("ps", 6)])
                    V(lambda e: e.tensor_scalar(out=mean[:, 0:cn], in0=ps[6][:, 0:cn], scalar1=1.0 / 256, scalar2=None, op0=ALU.mult), r=[B("ps", 6)], w=[B("mean")])
                    for ct in range(2):
                        PE(lambda e, ct=ct: e.matmul(ps[6][:, 0:cn], lhsT=ones_f[:], rhs=ysq[:, ct, 0:cn], start=(ct == 0), stop=(ct == 1)), r=[B("ysm", 0), B("ysm", 1), cB], w=[B("ps", 6)])
                    V(lambda e: e.tensor_tensor(out=var[:, 0:cn], in0=mean[:, 0:cn], in1=mean[:, 0:cn], op=ALU.mult), r=[B("mean")], w=[B("g2")])
                    V(lambda e: e.scalar_tensor_tensor(out=var[:, 0:cn], in0=ps[6][:, 0:cn], scalar=1.0 / 256, in1=var[:, 0:cn], op0=ALU.mult, op1=ALU.subtract), r=[B("ps", 6), B("g2")], w=[B("g2")])
                    S(lambda e: e.activation(out=var[:, 0:cn], in_=var[:, 0:cn], func=AF.Sqrt, bias=EPS), r=[B("g2")], w=[B("g2")])
                    V(lambda e: e.reciprocal(out=var[:, 0:cn], in_=var[:, 0:cn]), r=[B("g2")], w=[B("g2")])
                    for ct in range(2):
                        V(lambda e, ct=ct: e.tensor_tensor(out=g1[:, 0:cn], in0=acc[:, ct, c0:c0 + cn], in1=mean[:, 0:cn], op=ALU.subtract), r=[B("acc", c0, 0), B("acc", c0, 1), B("mean")], w=[B("g1")])
                        V(lambda e, ct=ct: e.tensor_tensor(out=g1[:, 0:cn], in0=g1[:, 0:cn], in1=var[:, 0:cn], op=ALU.mult), r=[B("g1"), B("g2")], w=[B("g1")])
                        V(lambda e, ct=ct: e.tensor_scalar(out=g1[:, 0:cn], in0=g1[:, 0:cn], scalar1=pvc(l, OLG + ct), scalar2=pvc(l, OLB + ct), op0=ALU.mult, op1=ALU.add), r=[B("g1"), cB], w=[B("g1")])
                        S(lambda e, ct=ct: e.activation(out=hb[:, 4 + ct, c0:c0 + cn], in_=g1[:, 0:cn], func=AF.Silu), r=[B("g1")], w=[B("hb", par, c0)])
                        yield

                if last:
                    Q(lambda e, l=l: e.dma_start(out=ossm_p[l, :, :].rearrange("p (s c) -> p s c", c=2), in_=hcar[:, l, :, :]), r=[B("hcar", l)], w=[B("ossm_p", l)])
                if first:
                    Q(lambda e, l=l: e.dma_start(out=h0[:, 0, :, :], in_=sre_d[l, :, :].rearrange("p (s b) -> p s b", s=8)), w=[B("h0")])
                    Q(lambda e, l=l: e.dma_start(out=h0[:, 1, :, :], in_=sim_d[l, :, :].rearrange("p (s b) -> p s b", s=8)), w=[B("h0")])
                    for s8 in range(8):
                        ct = s8 // 4
                        for ci in range(2):
                            PE(lambda e, ci=ci, s8=s8, ct=ct: e.matmul(ps[4][:, (s8 * 2 + ci) * NS:(s8 * 2 + ci + 1) * NS], lhsT=WB[:, s8 * 2 + ci, :], rhs=us[:, ct, T:NTM], start=True, stop=True), r=[B("us", T), B("WB")], w=[B("ps", 4)])
                    V(lambda e: e.tensor_copy(out=h1[:].rearrange("p c s b -> p s c b"), in_=ps[4][:, 0:16 * NS].rearrange("p (s c b) -> p s c b", s=8, c=2)), r=[B("ps", 4)], w=[B("h1")])
                    for s8 in range(8):
                        lre, lim = pl[:, 5, s8:s8 + 1], pl[:, 6, s8:s8 + 1]
                        V(lambda e, s8=s8: e.tensor_tensor(out=sm[:, 0, 6:7], in0=lam[:, l, 0, s8:s8 + 1], in1=lam[:, l, 1, s8:s8 + 1], op=ALU.mult), r=[B("lam", l)], w=[B("sm", 0)])
                        V(lambda e, s8=s8: e.tensor_tensor(out=sm[:, 0, 7:8], in0=lam[:, l, 0, s8:s8 + 1], in1=lam[:, l, 2, s8:s8 + 1], op=ALU.mult), r=[B("lam", l)], w=[B("sm", 0)])
                        V(lambda e, s8=s8: e.tensor_scalar(out=sm[:, 1, 7:8], in0=sm[:, 0, 7:8], scalar1=-1.0, scalar2=None, op0=ALU.mult), r=[B("sm", 0)], w=[B("sm", 1)])
                        V(lambda e, s8=s8: e.scalar_tensor_tensor(out=h1[:, 0, s8, :], in0=h0[:, 0, s8, :], scalar=sm[:, 0, 6:7], in1=h1[:, 0, s8, :], op0=ALU.mult, op1=ALU.add), r=[B("h0"), B("sm", 0), B("h1")], w=[B("h1")])
                        V(lambda e, s8=s8: e.scalar_tensor_tensor(out=h1[:, 0, s8, :], in0=h0[:, 1, s8, :], scalar=sm[:, 1, 7:8], in1=h1[:, 0, s8, :], op0=ALU.mult, op1=ALU.add), r=[B("h0"), B("sm", 1), B("h1")], w=[B("h1")])
                        V(lambda e, s8=s8: e.scalar_tensor_tensor(out=h1[:, 1, s8, :], in0=h0[:, 1, s8, :], scalar=sm[:, 0, 6:7], in1=h1[:, 1, s8, :], op0=ALU.mult, op1=ALU.add), r=[B("h0"), B("sm", 0), B("h1")], w=[B("h1")])
                        V(lambda e, s8=s8: e.scalar_tensor_tensor(out=h1[:, 1, s8, :], in0=h0[:, 0, s8, :], scalar=sm[:, 0, 7:8], in1=h1[:, 1, s8, :], op0=ALU.mult, op1=ALU.add), r=[B("h0"), B("sm", 0), B("h1")], w=[B("h1")])
                    Q(lambda e, l=l: e.dma_start(out=ossm_s[l, :, :].rearrange("p (c s b) -> p c s b", c=2, s=8), in_=h1[:]), r=[B("h1")], w=[B("ossm_s", l)])
                    V(lambda e: e.tensor_copy(out=h1b[:], in_=h1[:]), r=[B("h1")], w=[B("h1b")])
                    ssm_out(T, NS, lambda s8: h1b[:, 0, s8, :], lambda s8: h1b[:, 1, s8, :], 0, [B("h1b")])
                    yield

                s_out = [wload("M", w_out[l, :, i * 512:(i + 1) * 512].rearrange("(k p) n -> p k n", p=128), "p (k n) -> p k n", wid=l * 23 + 3 + i, ch=ch, k=8) for i in range(2)]
                for (c0, cn) in ccs:
                    for dt_ in range(8):
                        s = s_out[dt_ // 4]
                        off = (dt_ % 4) * 128
                        pbk = bank()
                        PE(lambda e, dt_=dt_, pbk=pbk: e.matmul(ps[pbk][:, 0:cn], lhsT=ident[:], rhs=x[:, dt_, c0:c0 + cn], start=True, stop=False), r=[cB, B("x", par, c0, dt_)], w=[B("ps", pbk)])
                        for k in range(8):
                            PE(lambda e, k=k, s=s, off=off, pbk=pbk: e.matmul(ps[pbk][:, 0:cn], lhsT=ring[:, s, k * 512 + off:k * 512 + off + 128], rhs=hb[:, k, c0:c0 + cn], start=False, stop=(k == 7)), r=[B(SL, s), B("hb", par, c0)], w=[B("ps", pbk)])
                        S(lambda e, dt_=dt_, pbk=pbk: e.activation(out=x[:, dt_, c0:c0 + cn], in_=ps[pbk][:, 0:cn], func=AF.Copy), r=[B("ps", pbk)], w=[B("x", par, c0, dt_)])
                        yield

        def gen_ffn(ch, l):
            par = ch % 2; x = X[par]; hb = HB[par]
            first, last = (ch == 0), (ch == nch - 1)
            ccs = [(0, 512)] + ([(T, NS)] if first else [])
            sq, rstd = sq2, rstd2
            ring, SL = ringF, "slotF"
            bank = fbank
            if True:
                rmsnorm(x, hb, par, l * PVL + OG2, ccs, sq2, rstd2, fbank(), "f")
                for fg in range(6):
                    nf = 4 if fg < 5 else 2
                    sg_ = wload("F", w_g[l, :, fg * 512:fg * 512 + nf * 128].rearrange("(k p) n -> p k n", p=128), "p (k n) -> p k n", wid=l * 23 + 5 + fg * 3, ch=ch, k=8)
                    su_ = wload("F", w_u[l, :, fg * 512:fg * 512 + nf * 128].rearrange("(k p) n -> p k n", p=128), "p (k n) -> p k n", wid=l * 23 + 6 + fg * 3, ch=ch, k=8)
                    sd_ = wload("F", w_d[l, fg * 512:fg * 512 + nf * 128, :].rearrange("(f p) n -> p f n", p=128), "p (f n) -> p f n", wid=l * 23 + 7 + fg * 3, ch=ch, f=nf)
                    W_ = nf * 128
                    for (c0, cn) in ccs:
                        for f in range(nf):
                            pg, pu = bank(), bank()
                            for (s_, pb_) in ((sg_, pg), (su_, pu)):
                                for k in range(8):
                                    PE(lambda e, k=k, s_=s_, pb_=pb_, f=f: e.matmul(ps[pb_][:, 0:cn], lhsT=ring[:, s_, k * W_ + f * 128:k * W_ + f * 128 + 128], rhs=hb[:, k, c0:c0 + cn], start=(k == 0), stop=(k == 7)), r=[B(SL, s_), B("hb", par, c0)], w=[B("ps", pb_)])
                            S(lambda e, pg=pg, f=f: e.activation(out=sgb[:, f % 2, 0:cn], in_=ps[pg][:, 0:cn], func=AF.Silu), r=[B("ps", pg)], w=[B("sgb", f % 2)])
                            V(lambda e, pu=pu, f=f: e.tensor_tensor(out=hid[:, f, c0:c0 + cn], in0=ps[pu][:, 0:cn], in1=sgb[:, f % 2, 0:cn], op=ALU.mult), r=[B("ps", pu), B("sgb", f % 2)], w=[B("hid", c0)])
                            yield
                        for dt_ in range(8):
                            pbk = bank()
                            PE(lambda e, dt_=dt_, pbk=pbk: e.matmul(ps[pbk][:, 0:cn], lhsT=ident[:], rhs=x[:, dt_, c0:c0 + cn], start=True, stop=False), r=[cB, B("x", par, c0, dt_)], w=[B("ps", pbk)])
                            for f in range(nf):
                                PE(lambda e, f=f, dt_=dt_, pbk=pbk: e.matmul(ps[pbk][:, 0:cn], lhsT=ring[:, sd_, f * 1024 + dt_ * 128:f * 1024 + dt_ * 128 + 128], rhs=hid[:, f, c0:c0 + cn], start=False, stop=(f == nf - 1)), r=[B(SL, sd_), B("hid", c0)], w=[B("ps", pbk)])
                            S(lambda e, dt_=dt_, pbk=pbk: e.activation(out=x[:, dt_, c0:c0 + cn], in_=ps[pbk][:, 0:cn], func=AF.Copy), r=[B("ps", pbk)], w=[B("x", par, c0, dt_)])
                            yield
            if l == L - 1:
                for (c0, cn) in ccs:
                    for k in range(8):
                        S(lambda e, k=k: e.activation(out=sq[:, 0:cn], in_=x[:, k, c0:c0 + cn], func=AF.Square), r=[*XB(par, c0)], w=[B("sq", "f")])
                        PE(lambda e, k=k: e.matmul(ps[0][:, 0:cn], lhsT=ones_b[:], rhs=sq[:, 0:cn], start=(k == 0), stop=(k == 7)), r=[B("sq", "f"), cB], w=[B("ps", 0)])
                    S(lambda e: e.activation(out=rstd[:, 0:cn], in_=ps[0][:, 0:cn], func=AF.Sqrt, scale=1.0 / D, bias=EPS), r=[B("ps", 0)], w=[B("rstd", "f")])
                    V(lambda e: e.reciprocal(out=rstd[:, 0:cn], in_=rstd[:, 0:cn]), r=[B("rstd", "f")], w=[B("rstd", "f")])
                    for k in range(8):
                        gk = pvt[:, 4 * PVL + k: 4 * PVL + k + 1]
                        V(lambda e, k=k, gk=gk: e.scalar_tensor_tensor(out=x[:, k, c0:c0 + cn], in0=x[:, k, c0:c0 + cn], scalar=gk, in1=rstd[:, 0:cn], op0=ALU.mult, op1=ALU.mult), r=[*XB(par, c0), B("rstd", "f"), cB], w=[*XB(par, c0)])
                        if c0 < T:
                            Q(lambda e, k=k: e.dma_start(out=yT[k * 128:(k + 1) * 128, ch * T + c0:ch * T + c0 + cn], in_=x[:, k, c0:c0 + cn]), r=[*XB(par, c0)], w=[B("yT")])
                        else:
                            Q(lambda e, k=k: e.dma_start(out=ysT[k * 128:(k + 1) * 128, :], in_=x[:, k, c0:c0 + cn]), r=[*XB(par, c0)], w=[B("ysT")])
            yield

        def count_ops(gen):
            P.dry = True
            n0 = P.dryn
            for _ in gen:
                pass
            P.dry = False
            return P.dryn - n0

        def run2(ga, na, gb, nb):
            ia = ib = 0
            alive_a, alive_b = ga is not None, gb is not None
            while alive_a or alive_b:
                pick_a = alive_a and (not alive_b or ia * nb <= ib * na)
                n0 = P.nops
                if pick_a:
                    try:
                        next(ga)
                    except StopIteration:
                        alive_a = False
                    ia += P.nops - n0
                else:
                    try:
                        next(gb)
                    except StopIteration:
                        alive_b = False
                    ib += P.nops - n0

        streams = []
        for ch in range(nch):
            ph = []
            for l in range(L):
                ph.append(("m", ch, l))
                ph.append(("f", ch, l))
            streams.append(ph)
        steps = []
        t = 0
        start = {}
        for ch in range(nch):
            start[ch] = (ch // 2) * 2 * L + (ch % 2)
        nsteps = max(start[c] + 2 * L for c in range(nch))
        for t in range(nsteps):
            cur = []
            for ch in range(nch):
                p = t - start[ch]
                if 0 <= p < 2 * L:
                    cur.append(streams[ch][p])
            steps.append(cur)
        for cur in steps:
            gens = []
            for (kind, ch, l) in cur:
                mk = (lambda: gen_mix(ch, l)) if kind == "m" else (lambda: gen_ffn(ch, l))
                n = count_ops(mk())
                gens.append((mk(), max(n, 1)))
            if len(gens) == 1:
                run2(gens[0][0], gens[0][1], None, 1)
            else:
                run2(gens[0][0], gens[0][1], gens[1][0], gens[1][1])
        print('NOPS', P.nops, {k: len(v) for k, v in P.ops.items()}, flush=True)
        P.emit()
    return nc


def _alibi_tables():
    slopes = 2.0 ** (-8.0 * np.arange(1, 9, dtype=np.float32) / 8)
    qi = np.arange(128)[:, None]
    kj = np.arange(256)[None, :]
    dist = qi - kj + 128
    valid = (dist >= 0) & (dist < 128)
    ab = np.where(valid, -dist.astype(np.float32), -1.0e7).astype(np.float32)
    dj = (127 - np.arange(128)).astype(np.float32)
    sbias = (-slopes.reshape(2, 4, 1) * dj[None, None, :]).astype(np.float32)
    sbias = np.ascontiguousarray(sbias.transpose(1, 0, 2)).reshape(4, 2 * 128)
    return ab, sbias


def _fm(v):
    return np.ascontiguousarray(np.asarray(v, np.float32).reshape(-1, 128).T)


_NC_CACHE = {}


def kernel(nch=16, depth=4, **inp):
    f = lambda k: np.asarray(inp[k], np.float32)
    L = depth
    TT = nch * T
    key = (nch, depth)
    if key not in _NC_CACHE:
        _NC_CACHE[key] = build(nch, depth)
    nc = _NC_CACHE[key]
    ab, sbias = _alibi_tables()
    pv = np.zeros((128, NPV), np.float32)
    for l in range(L):
        o = l * PVL
        pv[:, o + 0:o + 8] = _fm(f("norm_mix_g")[l])
        pv[:, o + 8:o + 16] = _fm(f("norm_ffn_g")[l])
        cw = f("conv_dw_w")[l]
        for ct in range(2):
            pv[:, o + 16 + ct * 31:o + 16 + (ct + 1) * 31] = cw[:, ct * 128:(ct + 1) * 128].T
        pv[:, o + 78:o + 80] = _fm(f("conv_dw_b")[l])
        pv[:, o + 80:o + 82] = _fm(f("conv_ln_g")[l])
        pv[:, o + 82:o + 84] = _fm(f("conv_ln_b")[l])
        pv[:, o + 84:o + 86] = _fm(f("ssm_d")[l])
        pv[:, o + 86:o + 88] = _fm(f("ssm_glu_b")[l])
        pv[:, o + 88:o + 96] = _fm(f("ssm_a_re")[l].reshape(-1))
        pv[:, o + 96:o + 104] = _fm(f("ssm_a_im")[l].reshape(-1))
        pv[:, o + 104:o + 112] = _fm(np.repeat(f("ssm_log_dt")[l], 64))
        pv[:, o + 112:o + 120] = f("attn_sinks")[l][None, :]
    pv[:, 4 * PVL:4 * PVL + 8] = _fm(f("norm_final_g"))
    Bn = np.zeros((L, 128, 8, 2, 128), np.float32)
    Cn = np.zeros((L, 128, 8, 2, 128), np.float32)
    for l in range(L):
        for ci, (bk, ck_) in enumerate((("ssm_b_re", "ssm_c_re"), ("ssm_b_im", "ssm_c_im"))):
            bb = f(bk)[l]
            cc = f(ck_)[l]
            for g in range(16):
                st_, p0 = g // 2, (g % 2) * 64
                col = (g % 8) * 16
                Bn[l, p0:p0 + 64, st_, ci, col:col + 16] = bb[g]
                Cn[l, p0:p0 + 64, st_, ci, col:col + 16] = cc[g].T
    Bn = Bn.reshape(L, 128, -1)
    Cn = Cn.reshape(L, 128, -1)
    ident = np.eye(128, dtype=np.float32)
    jjv = np.broadcast_to(np.arange(1, J + 1, dtype=np.float32)[None, :], (128, J)).copy()
    xp = f("x_prompt")
    xs = f("x_sample")[:, 0, :]
    shared = {
        "w_in": f("w_in")[:L], "w_out": f("w_out")[:L], "w_g": f("w_ff_gate")[:L], "w_u": f("w_ff_up")[:L],
        "w_d": f("w_ff_down")[:L], "glu_w": f("ssm_glu_w")[:L], "pv": pv, "Bn": Bn, "Cn": Cn, "ident": ident,
        "jj": jjv, "abias": ab, "sbias": sbias,
        "sinkc": np.ascontiguousarray(f("attn_sinks")[:L].reshape(L, 2, 4).transpose(2, 0, 1).reshape(4, L * 2)),
    }
    in_maps = []
    for c in range(8):
        sq_, b0 = c // 4, c * NS
        m = dict(shared)
        m["xT"] = np.ascontiguousarray(xp[sq_, :TT, :].T)
        m["xsT"] = np.ascontiguousarray(xs[b0:b0 + NS].T)
        m["ck"] = np.ascontiguousarray(f("cache_swa_k")[:L, b0:b0 + NS].reshape(L, NS, 128, 128))
        m["cv"] = np.ascontiguousarray(f("cache_swa_v")[:L, b0:b0 + NS].reshape(L, NS, 128, 128))
        cc_ = f("cache_conv")[:L, b0:b0 + NS]
        m["cconv"] = np.ascontiguousarray(cc_.reshape(L, NS, 30, 2, 128).transpose(0, 4, 3, 1, 2)).reshape(L, 128, -1)
        for nm, kk in (("sre", "state_ssm_re"), ("sim", "state_ssm_im")):
            s_ = f(kk)[:L, b0:b0 + NS].reshape(L, NS, 8, 128)
            m[nm] = np.ascontiguousarray(s_.transpose(0, 3, 2, 1)).reshape(L, 128, -1)
        in_maps.append(m)
    res = run_bass_kernel_spmd(nc, in_maps, core_ids=list(range(8))).results
    y_p = np.stack([res[0]["yT"].T, res[4]["yT"].T]).astype(np.float32)
    y_s = np.concatenate([res[c]["ysT"].T for c in range(8)])[:, None, :].astype(np.float32)
    pc = (res[0], res[4])
    k_p = np.stack([np.stack([r["ok_p"][l].reshape(128, 2, 64) for r in pc]) for l in range(L)])
    v_p = np.stack([np.stack([r["ov_p"][l].reshape(128, 2, 64) for r in pc]) for l in range(L)])
    conv_p = np.stack([np.stack([r["oconv_p"][l].reshape(128, 2, 30).transpose(2, 1, 0).reshape(30, 256) for r in pc]) for l in range(L)])
    ssm_p = [np.stack([np.stack([r["ossm_p"][l].reshape(128, 8, 2)[:, :, ci].T.reshape(16, 64) for r in pc]) for l in range(L)]) for ci in range(2)]
    k_s = np.concatenate([res[c]["ok_s"].reshape(L, NS, 128, 2, 64) for c in range(8)], 1)
    v_s = np.concatenate([res[c]["ov_s"].reshape(L, NS, 128, 2, 64) for c in range(8)], 1)
    conv_s = np.concatenate([res[c]["oconv_s"].reshape(L, 128, 2, NS, 30).transpose(0, 3, 4, 2, 1).reshape(L, NS, 30, 256) for c in range(8)], 1)
    ssm_s = [np.concatenate([res[c]["ossm_s"].reshape(L, 128, 2, 8, NS)[:, :, ci].transpose(0, 3, 2, 1).reshape(L, NS, 16, 64) for c in range(8)], 1) for ci in range(2)]
    outs = (y_p, y_s, k_p, v_p, conv_p, ssm_p[0], ssm_p[1], k_s, v_s, conv_s, ssm_s[0], ssm_s[1])
    return tuple(np.ascontiguousarray(o, dtype=np.float32) for o in outs)
```

```python
import math
import os
import numpy as np
from contextlib import ExitStack
import concourse.bass as bass
import concourse.mybir as mybir
from concourse.bass_utils import run_bass_kernel_spmd

F32 = mybir.dt.float32
BF16 = mybir.dt.bfloat16
I32 = mybir.dt.int32
AF = mybir.ActivationFunctionType
ALU = mybir.AluOpType
AX = mybir.AxisListType

D = 1024
SEQ = 8192
T = 512
NS = 16
NTM = T + NS
DFF = 2816
NF = 22
J = 256
NSLOT = 4
PVL = 120
NPV = 4 * PVL + 8
EPS = 1e-6
TWO_PI = 2.0 * math.pi


import types


def _freeze(fn):
    if fn.__closure__ is None:
        return fn
    cells = []
    for c in fn.__closure__:
        try:
            cells.append(types.CellType(c.cell_contents))
        except ValueError:
            cells.append(c)
    return types.FunctionType(fn.__code__, fn.__globals__, fn.__name__, fn.__defaults__, tuple(cells))


class _Stub:
    def __init__(self):
        self.closed = True

    def matmul(self, *a, **kw):
        self.closed = bool(kw.get("stop", True))
        return self

    def transpose(self, *a, **kw):
        self.closed = True
        return self

    def then_inc(self, *a, **kw):
        return self


class Buf:
    __slots__ = ("name", "lw", "rd")

    def __init__(self, name):
        self.name = name
        self.lw = None
        self.rd = {}


class Prog:
    ENG = ["tensor", "vector", "scalar", "gpsimd", "sync"]

    def __init__(self, nc, same_eng_sync=("vector", "scalar", "gpsimd")):
        self.nc = nc
        self.ops = {e: [] for e in self.ENG}
        self.cnt = {}
        self.known = {e: {} for e in self.ENG}
        self.same = set(same_eng_sync)
        self.bufs = {}
        self.dry = False
        self.dq = {}
        self.dryn = 0
        self.nops = 0

    def B(self, *key):
        b = self.bufs.get(key)
        if b is None:
            b = self.bufs[key] = Buf(key)
        return b

    def op(self, eng, fn, r=(), w=(), dma=None, inc=None):
        if self.dry:
            self.dryn += 1
            return None
        fn = _freeze(fn)
        if getattr(self, "stopped", False):
            return None
        self.nops += 1
        if eng == "tensor" and "KLIMIT" in os.environ:
            st_ = _Stub()
            try:
                fn(st_)
            except Exception:
                pass
            self.open_grp = not st_.closed
        if self.nops >= int(os.environ.get("KLIMIT", "100000000")) and not getattr(self, "open_grp", False):
            self.stopped = True
        ex = [b for b in r if b.name[0] in ("ps", "psb")]
        if ex:
            r = [b for b in r if b.name[0] not in ("ps", "psb")]
            w = list(w) + ex
        deps = {}
        for b in list(r) + list(w):
            if b.lw is not None and deps.get(b.lw[0], 0) < b.lw[1]:
                deps[b.lw[0]] = b.lw[1]
        for b in w:
            for s, v in b.rd.items():
                if deps.get(s, 0) < v:
                    deps[s] = v
        pre = None
        if dma is None:
            sem, step = eng, 1
        else:
            npool = 32 if eng == "sync" else 8
            i = self.dq.get(eng, 0)
            self.dq[eng] = i + 1
            sem, step = "d%s%d" % (eng[0], i % npool), 16
            if i >= npool:
                pre = (sem, 16 * (i // npool))
        waits = []
        if pre is not None and deps.get(pre[0], 0) < pre[1]:
            deps[pre[0]] = pre[1]
        for s, v in deps.items():
            if s == eng and eng not in self.same:
                continue
            if self.known[eng].get(s, 0) >= v:
                continue
            self.known[eng][s] = v
            waits.append((s, v))
        self.cnt[sem] = self.cnt.get(sem, 0) + step
        tok = (sem, self.cnt[sem])
        self.ops[eng].append((fn, waits, sem, step))
        for b in r:
            if b.rd.get(sem, 0) < tok[1]:
                b.rd[sem] = tok[1]
        for b in w:
            b.lw = tok
            b.rd = {}
        return tok

    def emit(self):
        nc = self.nc
        with ExitStack() as st:
            sems = {s: st.enter_context(nc.semaphore(s)) for s in self.cnt}
            block = st.enter_context(nc.Block())
            final = dict(self.cnt)

            def mk(engname):
                def body(e):
                    for fn, waits, sem, step in self.ops[engname]:
                        for s, v in waits:
                            e.wait_ge(sems[s], v)
                        fn(e).then_inc(sems[sem], step)
                    if engname == "sync":
                        for s, v in final.items():
                            e.wait_ge(sems[s], v)
                return body

            for engname in self.ENG:
                if self.ops[engname] or engname == "sync":
                    getattr(block, engname)(mk(engname))


def build(nch=16, depth=4):
    nc = bass.Bass("TRN2", target_bir_lowering=False)
    L = depth
    TT = nch * T

    def din(name, shape, dt=F32):
        return nc.dram_tensor(name, list(shape), dt, kind="ExternalInput").ap()

    def dout(name, shape, dt=F32):
        return nc.dram_tensor(name, list(shape), dt, kind="ExternalOutput").ap()

    xT = din("xT", [D, TT]); xsT = din("xsT", [D, NS])
    w_in = din("w_in", [L, D, 1536]); w_out = din("w_out", [L, D, D])
    w_g = din("w_g", [L, D, DFF]); w_u = din("w_u", [L, D, DFF]); w_d = din("w_d", [L, DFF, D])
    glu_w = din("glu_w", [L, 256, 256])
    pv_d = din("pv", [128, NPV])
    Bn_d = din("Bn", [L, 128, 8 * 2 * 128]); Cn_d = din("Cn", [L, 128, 8 * 2 * 128])
    ident_d = din("ident", [128, 128]); jj_d = din("jj", [128, J])
    abias_d = din("abias", [128, 256]); sbias_d = din("sbias", [4, 2 * 128]); sinkc_d = din("sinkc", [4, L * 2])
    ck_d = din("ck", [L, NS, 128, 128]); cv_d = din("cv", [L, NS, 128, 128])
    cconv_d = din("cconv", [L, 128, 2 * NS * 30])
    sre_d = din("sre", [L, 128, 8 * NS]); sim_d = din("sim", [L, 128, 8 * NS])

    yT = dout("yT", [D, TT]); ysT = dout("ysT", [D, NS])
    ok_p = dout("ok_p", [L, 128, 128]); ov_p = dout("ov_p", [L, 128, 128])
    oconv_p = dout("oconv_p", [L, 128, 2 * 30]); ossm_p = dout("ossm_p", [L, 128, 16])
    ok_s = dout("ok_s", [L, NS, 128, 128]); ov_s = dout("ov_s", [L, NS, 128, 128])
    oconv_s = dout("oconv_s", [L, 128, 2 * NS * 30]); ossm_s = dout("ossm_s", [L, 128, 2 * 8 * NS])
    cdg_d = nc.dram_tensor("cdg_scr", [L * 2, 128, 4096], BF16).ap()
    wscr = nc.dram_tensor("w_scr", [L * 23, 128, 4096], BF16).ap()
    rot_d = nc.dram_tensor("rot_scr", [L, 128, 8 * 2 * J], F32).ap()
    wb_d = nc.dram_tensor("wb_scr", [L, 128, 16 * 128], BF16).ap()
    wc_d = nc.dram_tensor("wc_scr", [L, 128, 16 * 128], BF16).ap()

    P = Prog(nc)
    B = P.B
    with ExitStack() as st:
        def sb(name, shape, dt=F32):
            return st.enter_context(nc.sbuf_tensor("sb_" + name, list(shape), dt))

        def pst(name, shape, dt=F32):
            return st.enter_context(nc.psum_tensor(name, list(shape), dt))

        x = sb("x", [128, 8, NTM]); hb = sb("hb", [128, 8, NTM], BF16)
        x1 = sb("x1", [128, 8, T]); hb1 = sb("hb1", [128, 8, T], BF16)
        X = [x, x1]; HB = [hb, hb1]
        qT = sb("qT", [64, 8, NTM], BF16); kT = sb("kT", [64, 2, 128 + NTM], BF16)
        vp = sb("vp", [128, 5, 2, 192], BF16)
        ucv = sb("ucv", [128, 2, 30 + NTM], BF16); cstg = sb("cstg", [128, 2, 30]); acc = sb("acc", [128, 2, NTM]); us = sb("us", [128, 2, NTM], BF16)
        hid = sb("hid", [128, 4, NTM], BF16)
        ringM = sb("ringM", [128, 3, 4096], BF16); ringF = sb("ringF", [128, 3, 4096], BF16)
        pvt = sb("pvt", [128, NPV])
        ident = sb("ident", [128, 128]); identb = sb("identb", [128, 128], BF16)
        ones_f = sb("ones_f", [128, 128]); ones_b = sb("ones_b", [128, 128], BF16)
        abias = sb("abias", [128, 256]); sbias = sb("sbias", [4, 2, 128])
        WB = sb("WB", [128, 16, 128], BF16); WC = sb("WC", [128, 16, 128], BF16)
        Dd = sb("Dd", [128, 2, 128], BF16); gluw = sb("gluw", [128, 2, 256], BF16)
        lam = sb("lam", [128, L, 4, 8])
        rot2 = sb("rot2", [128, 2, 2, J])
        hcar = sb("hcar", [128, L, 8, 2])
        khalo = sb("khalo", [64, L, 2, 128], BF16); vhalo = sb("vhalo", [128, L, 2, 192], BF16)
        chalo = sb("chalo", [128, L, 2, 30], BF16)
        sq = sb("sq", [128, 512], BF16); rstd = sb("rstd", [128, 512]); sq_m = sq
        sq2 = sb("sq2", [128, 512], BF16); rstd2 = sb("rstd2", [128, 512])
        sgb = sb("sgb", [128, 2, 512], BF16)
        sc = sb("sc", [128, 3, 256]); pb = sb("pb", [128, 3, 256], BF16); ptb = sb("ptb", [128, 2, 256], BF16)
        sm = sb("sm", [128, 3, 8]); smc = sb("smc", [128, 2])
        t1 = sb("t1", [128, J]); t2 = sb("t2", [128, J]); bre = sb("bre", [128, J]); bim = sb("bim", [128, J])
        wre = sb("wre", [128, J]); wim = sb("wim", [128, J])
        ysm = sb("ysm", [128, 2, 512]); yb = sb("yb", [128, 2, 512], BF16); g1 = sb("g1", [128, 512]); g2 = sb("g2", [128, 512])
        ysq = ysm; cst = ysm[:].rearrange("p c n -> p (c n)")[:, 0:2 * NS * 31].rearrange("p (c b k) -> p c b k", c=2, b=NS); mean = sb("mean", [128, 512]); var = g2
        jj = mean[:, 0:J]
        xflat = x[:].rearrange("p k n -> p (k n)")
        rotflat = x1[:].rearrange("p k n -> p (k n)")[:, 0:16 * J]
        big = xflat[:, 0:4096]; big2 = rotflat[:, 8 * J:16 * J]; bigi = sb("bigi", [128, 512], I32)[:]
        hbf = sb("hbf", [128, 16 * J], BF16)
        pl = sb("pl", [128, 16, 8])
        tk = sb("tk", [128, 128]); tkb = sb("tkb", [NS, 2, 128])
        Kb = sb("Kb", [128, 1, 128]); Vb = sb("Vb", [128, 1, 128]); KbT = sb("KbT", [64, 2, 128], BF16)
        Vbp = sb("Vbp", [128, 2, 192], BF16)
        ssc = sb("ssc", [4, 4, 128]); spb = sb("spb", [4, 4, 128], BF16); ssm_ = sb("ssm_", [4, 4, 4]); sinkt = sb("sinkt", [4, L * 2])
        sptb = sb("sptb", [128, 2, NS, 4], BF16)
        cs = sb("cs", [128, 2, NS, 31])
        h0 = sb("h0", [128, 2, 8, NS]); h1 = sb("h1", [128, 2, 8, NS]); h1b = sb("h1b", [128, 2, 8, NS], BF16)

        ps = [pst("ps%d" % i, [128, 512]) for i in range(7)]
        psb = pst("psb", [128, 1024], BF16)

        Q = lambda fn, r=(), w=(), ch="io": P.op("sync", fn, r, w, dma=ch)
        G = lambda fn, r=(), w=(), ch="w": P.op("gpsimd", fn, r, w, dma=ch)
        V = lambda fn, r=(), w=(): P.op("vector", fn, r, w)
        S = lambda fn, r=(), w=(): P.op("scalar", fn, r, w)
        PE = lambda fn, r=(), w=(): P.op("tensor", fn, r, w)
        GP = lambda fn, r=(), w=(): P.op("gpsimd", fn, r, w)

        def XB(par, c0):
            return [B("x", par, c0)] + [B("x", par, c0, d) for d in range(8)]

        def pvc(l, off, n=1):
            return pvt[:, l * PVL + off: l * PVL + off + n]

        OG1, OG2, OCW, OCB, OLG, OLB, OSD, OGB, OARE, OAIM, OLDT, OSINK = 0, 8, 16, 78, 80, 82, 84, 86, 88, 96, 104, 112
        rr = [0]

        def bank():
            rr[0] = (rr[0] + 1) % 3
            return 4 + rr[0]

        fr = [0]

        def fbank():
            fr[0] = (fr[0] + 1) % 3
            return fr[0]

        cB = B("const")
        Q(lambda e: e.dma_start(out=pvt[:], in_=pv_d[:, :]), w=[cB])
        Q(lambda e: e.dma_start(out=ident[:], in_=ident_d[:, :]), w=[cB])
        Q(lambda e: e.dma_start(out=jj[:], in_=jj_d[:, :]), w=[cB])
        Q(lambda e: e.dma_start(out=abias[:], in_=abias_d[:, :]), w=[cB])
        Q(lambda e: e.dma_start(out=sbias[:].rearrange("p h k -> p (h k)"), in_=sbias_d[:, :]), w=[cB])
        Q(lambda e: e.dma_start(out=sinkt[:], in_=sinkc_d[:, :]), w=[cB])
        V(lambda e: e.memset(ones_f[:], 1.0), w=[cB])
        V(lambda e: e.memset(ones_b[:], 1.0), w=[cB])
        V(lambda e: e.tensor_copy(out=identb[:], in_=ident[:]), r=[cB], w=[B("identb")])
        V(lambda e: e.memset(vp[:].rearrange("p a g c -> p (a g c)"), 0.0), w=[B("vp", i) for i in range(5)])
        V(lambda e: e.memset(vhalo[:].rearrange("p l g c -> p (l g c)"), 0.0), w=[B("vhalo", l) for l in range(L)])
        V(lambda e: e.memset(Vbp[:].rearrange("p g c -> p (g c)"), 0.0), w=[B("Vbp")])
        V(lambda e: e.memset(hcar[:].rearrange("p l s c -> p (l s c)"), 0.0), w=[B("hcar", l) for l in range(L)])
        V(lambda e: e.memset(chalo[:].rearrange("p l c k -> p (l c k)"), 0.0), w=[B("chalo", l) for l in range(L)])

        for l in range(L):
            pB = B("pl")
            are, aim, ldt = pvc(l, OARE, 8), pvc(l, OAIM, 8), pvc(l, OLDT, 8)
            c = lambda i: pl[:, i, :]
            S(lambda e: e.activation(out=c(0), in_=ldt, func=AF.Exp), r=[cB], w=[pB])
            V(lambda e: e.tensor_tensor(out=c(1), in0=are, in1=c(0), op=ALU.mult), r=[pB], w=[pB])
            V(lambda e: e.tensor_tensor(out=c(2), in0=aim, in1=c(0), op=ALU.mult), r=[pB], w=[pB])
            S(lambda e, l=l: e.activation(out=lam[:, l, 0, :], in_=c(1), func=AF.Exp), r=[pB], w=[B("lam", l)])
            V(lambda e: e.tensor_scalar(out=c(3), in0=c(2), scalar1=1.0 / TWO_PI, scalar2=None, op0=ALU.mult), r=[pB], w=[pB])
            V(lambda e: e.tensor_copy(out=bigi[:, 0:8], in_=c(3)), r=[pB], w=[pB])
            V(lambda e: e.tensor_copy(out=c(4), in_=bigi[:, 0:8]), r=[pB], w=[pB])
            V(lambda e: e.tensor_tensor(out=c(4), in0=c(3), in1=c(4), op=ALU.subtract), r=[pB], w=[pB])
            S(lambda e, l=l: e.activation(out=lam[:, l, 2, :], in_=c(4), func=AF.Sin, scale=6.283185), r=[pB], w=[B("lam", l)])
            V(lambda e: e.tensor_scalar(out=c(3), in0=c(3), scalar1=0.25, scalar2=None, op0=ALU.add), r=[pB], w=[pB])
            V(lambda e: e.tensor_copy(out=bigi[:, 0:8], in_=c(3)), r=[pB], w=[pB])
            V(lambda e: e.tensor_copy(out=c(4), in_=bigi[:, 0:8]), r=[pB], w=[pB])
            V(lambda e: e.tensor_tensor(out=c(4), in0=c(3), in1=c(4), op=ALU.subtract), r=[pB], w=[pB])
            S(lambda e, l=l: e.activation(out=lam[:, l, 1, :], in_=c(4), func=AF.Sin, scale=6.283185), r=[pB], w=[B("lam", l)])
            V(lambda e, l=l: e.tensor_tensor(out=c(5), in0=lam[:, l, 0, :], in1=lam[:, l, 1, :], op=ALU.mult), r=[pB, B("lam", l)], w=[pB])
            V(lambda e, l=l: e.tensor_tensor(out=c(6), in0=lam[:, l, 0, :], in1=lam[:, l, 2, :], op=ALU.mult), r=[pB, B("lam", l)], w=[pB])
            V(lambda e: e.tensor_scalar(out=c(7), in0=c(5), scalar1=-1.0, scalar2=None, op0=ALU.add), r=[pB], w=[pB])
            V(lambda e: e.tensor_tensor(out=c(8), in0=are, in1=are, op=ALU.mult), r=[pB], w=[pB])
            V(lambda e: e.tensor_tensor(out=c(9), in0=aim, in1=aim, op=ALU.mult), r=[pB], w=[pB])
            V(lambda e: e.tensor_tensor(out=c(8), in0=c(8), in1=c(9), op=ALU.add), r=[pB], w=[pB])
            V(lambda e: e.reciprocal(out=c(8), in_=c(8)), r=[pB], w=[pB])
            V(lambda e: e.tensor_tensor(out=c(9), in0=c(7), in1=are, op=ALU.mult), r=[pB], w=[pB])
            V(lambda e: e.tensor_tensor(out=c(10), in0=c(6), in1=aim, op=ALU.mult), r=[pB], w=[pB])
            V(lambda e: e.tensor_tensor(out=c(9), in0=c(9), in1=c(10), op=ALU.add), r=[pB], w=[pB])
            V(lambda e: e.tensor_tensor(out=c(11), in0=c(9), in1=c(8), op=ALU.mult), r=[pB], w=[pB])
            V(lambda e: e.tensor_tensor(out=c(9), in0=c(6), in1=are, op=ALU.mult), r=[pB], w=[pB])
            V(lambda e: e.tensor_tensor(out=c(10), in0=c(7), in1=aim, op=ALU.mult), r=[pB], w=[pB])
            V(lambda e: e.tensor_tensor(out=c(9), in0=c(9), in1=c(10), op=ALU.subtract), r=[pB], w=[pB])
            V(lambda e: e.tensor_tensor(out=c(12), in0=c(9), in1=c(8), op=ALU.mult), r=[pB], w=[pB])
            V(lambda e: e.tensor_scalar(out=c(13), in0=c(12), scalar1=-1.0, scalar2=None, op0=ALU.mult), r=[pB], w=[pB])
            bg = big[:, 0:2048].rearrange("p (s c k) -> p s c k", s=8, c=2)
            Q(lambda e, l=l: e.dma_start(out=big[:, 0:2048], in_=Bn_d[l, :, :]), r=[pB], w=[B("big")])
            for s8 in range(8):
                fre, fim, nfim = pl[:, 11, s8:s8 + 1], pl[:, 12, s8:s8 + 1], pl[:, 13, s8:s8 + 1]
                V(lambda e, s8=s8, fre=fre: e.tensor_scalar(out=t1[:, 0:128], in0=bg[:, s8, 0, :], scalar1=fre, scalar2=None, op0=ALU.mult), r=[pB, B("big")], w=[B("t1")])
                V(lambda e, s8=s8, nfim=nfim: e.scalar_tensor_tensor(out=t1[:, 0:128], in0=bg[:, s8, 1, :], scalar=nfim, in1=t1[:, 0:128], op0=ALU.mult, op1=ALU.add), r=[pB, B("big"), B("t1")], w=[B("t1")])
                V(lambda e, s8=s8, fre=fre: e.tensor_scalar(out=t2[:, 0:128], in0=bg[:, s8, 1, :], scalar1=fre, scalar2=None, op0=ALU.mult), r=[pB, B("big")], w=[B("t2")])
                V(lambda e, s8=s8, fim=fim: e.scalar_tensor_tensor(out=t2[:, 0:128], in0=bg[:, s8, 0, :], scalar=fim, in1=t2[:, 0:128], op0=ALU.mult, op1=ALU.add), r=[pB, B("big"), B("t2")], w=[B("t2")])
                PE(lambda e: e.transpose(out=ps[5][:, 0:128], in_=t1[:, 0:128], identity=ident[:]), r=[B("t1"), cB], w=[B("ps", 5)])
                PE(lambda e: e.transpose(out=ps[5][:, 128:256], in_=t2[:, 0:128], identity=ident[:]), r=[B("t2"), cB], w=[B("ps", 5)])
                S(lambda e, l=l, s8=s8: e.activation(out=WB[:, s8 * 2:s8 * 2 + 2, :], in_=ps[5][:, 0:256].rearrange("p (c k) -> p c k", c=2), func=AF.Copy), r=[B("ps", 5)], w=[B("WB")])
            Q(lambda e, l=l: e.dma_start(out=big[:, 0:2048], in_=Cn_d[l, :, :]), w=[B("big")])
            V(lambda e, l=l: e.tensor_copy(out=WC[:, 0:16, :].rearrange("p (s c) k -> p s c k", c=2)[:, :, 0, :], in_=bg[:, :, 0, :]), r=[B("big")], w=[B("WC")])
            V(lambda e, l=l: e.tensor_scalar(out=WC[:, 0:16, :].rearrange("p (s c) k -> p s c k", c=2)[:, :, 1, :], in0=bg[:, :, 1, :], scalar1=-1.0, scalar2=None, op0=ALU.mult), r=[B("big")], w=[B("WC")])
            for ct in range(2):
                for kk in range(31):
                    wk = pvt[:, l * PVL + OCW + ct * 31 + kk: l * PVL + OCW + ct * 31 + kk + 1]
                    V(lambda e, kk=kk, wk=wk: e.tensor_scalar(out=hbf[:, kk * 128:(kk + 1) * 128], in0=ident[:], scalar1=wk, scalar2=None, op0=ALU.mult), r=[cB], w=[B("hbf")])
                Q(lambda e, l=l, ct=ct: e.dma_start(out=cdg_d[l * 2 + ct, :, 0:3968], in_=hbf[:, 0:3968]), r=[B("hbf")], w=[B("cdg_d", l, ct)])
            a8 = big2.rearrange("p (s j) -> p s j", s=8)
            for s8 in range(8):
                V(lambda e, s8=s8: e.tensor_scalar(out=a8[:, s8, :], in0=jj[:], scalar1=pl[:, 2, s8:s8 + 1], scalar2=1.0 / TWO_PI, op0=ALU.mult, op1=ALU.mult), r=[pB, cB], w=[B("big2"), B("rot")])
            rt = big.rearrange("p (s c j) -> p s c j", s=8, c=2)
            for ci, sh in ((1, 0.0), (0, 0.25)):
                if sh:
                    V(lambda e, sh=sh: e.tensor_scalar(out=big2, in0=big2, scalar1=sh, scalar2=None, op0=ALU.add), r=[B("big2")], w=[B("big2"), B("rot")])
                for pc in range(8 * J // 512):
                    V(lambda e, pc=pc: e.tensor_copy(out=bigi, in_=big2[:, pc * 512:(pc + 1) * 512]), r=[B("big2"), B("rot")], w=[B("bigi")])
                    V(lambda e, pc=pc: e.tensor_copy(out=rotflat[:, pc * 512:(pc + 1) * 512], in_=bigi), r=[B("bigi")], w=[B("rot")])
                V(lambda e: e.tensor_tensor(out=rotflat[:, 0:8 * J], in0=big2, in1=rotflat[:, 0:8 * J], op=ALU.subtract), r=[B("big2"), B("rot")], w=[B("rot")])
                S(lambda e, ci=ci: e.activation(out=rt[:, :, ci, :], in_=rotflat[:, 0:8 * J].rearrange("p (s j) -> p s j", s=8), func=AF.Sin, scale=6.283185), r=[B("rot")], w=[B("big")])
            Q(lambda e, l=l: e.dma_start(out=rot_d[l, :, :], in_=big), r=[B("big")], w=[B("rot_d", l)])
            Q(lambda e, l=l: e.dma_start(out=wb_d[l, :, :], in_=WB[:].rearrange("p a k -> p (a k)")), r=[B("WB")], w=[B("wb_d", l)])
            Q(lambda e, l=l: e.dma_start(out=wc_d[l, :, :], in_=WC[:].rearrange("p a k -> p (a k)")), r=[B("WC")], w=[B("wc_d", l)])

        print('MARK prologue_end', P.nops, flush=True)
        slot_i = {"M": 0, "F": 0}

        def wload(which, src_ap, shape_str, wid=None, ch=0, **kw):
            ring = ringM if which == "M" else ringF
            s = slot_i[which] % 3
            if not P.dry:
                slot_i[which] += 1
            n = 1
            for d_ in src_ap.shape[1:]:
                n *= d_
            if ch == 0:
                dst = ring[:, s, 0:n]
                if shape_str:
                    dst = dst.rearrange(shape_str, **kw)
                G(lambda e: e.dma_start(out=dst, in_=src_ap), w=[B("slot" + which, s)])
                if nch > 1:
                    Q(lambda e: e.dma_start(out=wscr[wid, :, 0:n], in_=ring[:, s, 0:n]), r=[B("slot" + which, s)], w=[B("wscr", wid)])
            else:
                G(lambda e: e.dma_start(out=ring[:, s, 0:n], in_=wscr[wid, :, 0:n]), r=[B("wscr", wid)], w=[B("slot" + which, s)])
            return s

        def rmsnorm(x, hb, par, goff, ccs, sq, rstd, pbk, tag):
            for (c0, cn) in ccs:
                pb_ = ps[pbk]
                S(lambda e: e.activation(out=hb[:, :, c0:c0 + cn], in_=x[:, :, c0:c0 + cn], func=AF.Square), r=[*XB(par, c0)], w=[B("hb", par, c0)])
                for k in range(8):
                    PE(lambda e, k=k: e.matmul(pb_[:, 0:cn], lhsT=ones_b[:], rhs=hb[:, k, c0:c0 + cn], start=(k == 0), stop=(k == 7)), r=[B("hb", par, c0), cB], w=[B("ps", pbk)])
                S(lambda e: e.activation(out=rstd[:, 0:cn], in_=pb_[:, 0:cn], func=AF.Sqrt, scale=1.0 / D, bias=EPS), r=[B("ps", pbk)], w=[B("rstd", tag)])
                V(lambda e: e.reciprocal(out=rstd[:, 0:cn], in_=rstd[:, 0:cn]), r=[B("rstd", tag)], w=[B("rstd", tag)])
                for k in range(8):
                    gk = pvt[:, goff + k: goff + k + 1]
                    V(lambda e, k=k, gk=gk: e.scalar_tensor_tensor(out=hb[:, k, c0:c0 + cn], in0=x[:, k, c0:c0 + cn], scalar=gk, in1=rstd[:, 0:cn], op0=ALU.mult, op1=ALU.mult), r=[*XB(par, c0), B("rstd", tag), cB], w=[B("hb", par, c0)])

        def gen_mix(ch, l):
            par = ch % 2; x = X[par]; hb = HB[par]
            ring, SL = ringM, "slotM"
            first, last = (ch == 0), (ch == nch - 1)
            ccs = [(0, 512)] + ([(T, NS)] if first else [])
            if l == 0:
                for k in range(8):
                    Q(lambda e, k=k: e.dma_start(out=x[:, k, 0:T], in_=xT[k * 128:(k + 1) * 128, ch * T:(ch + 1) * T]), w=[*XB(par, 0), B("big"), B("rot"), B("big2")])
                if first:
                    Q(lambda e: e.dma_start(out=x[:, :, T:NTM], in_=xsT.rearrange("(k p) n -> p k n", p=128)), w=[*XB(par, T), B("big")])
                yield
            if True:
                Q(lambda e, l=l: e.dma_start(out=WB[:].rearrange("p a k -> p (a k)"), in_=wb_d[l, :, :]), r=[B("wb_d", l)], w=[B("WB")])
                Q(lambda e, l=l: e.dma_start(out=WC[:].rearrange("p a k -> p (a k)"), in_=wc_d[l, :, :]), r=[B("wc_d", l)], w=[B("WC")])
                for ct in range(2):
                    V(lambda e, ct=ct: e.tensor_scalar(out=Dd[:, ct, :], in0=ident[:], scalar1=pvc(l, OSD + ct), scalar2=None, op0=ALU.mult), r=[cB], w=[B("Dd")])
                G(lambda e: e.dma_start(out=gluw[:], in_=glu_w[l].rearrange("(c p) n -> p c n", p=128)), w=[B("gluw")])
                s_in = [wload("M", w_in[l, :, i * 512:(i + 1) * 512].rearrange("(k p) n -> p k n", p=128), "p (k n) -> p k n", wid=l * 23 + i, ch=ch, k=8) for i in range(3)]

                def win(o_lo, o_n):
                    s = s_in[o_lo // 512]
                    off = o_lo % 512
                    return lambda k: ring[:, s, k * 512 + off: k * 512 + off + o_n], B(SL, s)

                V(lambda e, l=l: e.tensor_copy(out=kT[:, :, 0:128], in_=khalo[:, l, :, :]), r=[B("khalo", l)], w=[B("kT", "h")])
                V(lambda e, l=l: e.tensor_copy(out=vp[:, 0, :, :], in_=vhalo[:, l, :, :]), r=[B("vhalo", l)], w=[B("vp", 0)])
                V(lambda e, l=l: e.tensor_copy(out=ucv[:, :, 0:30], in_=chalo[:, l, :, :]), r=[B("chalo", l)], w=[B("ucv", "h")])

                rmsnorm(x, hb, par, l * PVL + OG1, ccs, sq, rstd, 6, "m")
                for (c0, cn) in ccs:
                    def mm8(lw, n_out, pbk):
                        f, sb_ = lw
                        for k in range(8):
                            PE(lambda e, k=k: e.matmul(ps[pbk][0:n_out, 0:cn], lhsT=f(k), rhs=hb[:, k, c0:c0 + cn], start=(k == 0), stop=(k == 7)), r=[sb_, B("hb", par, c0)], w=[B("ps", pbk)])
                    for h in range(8):
                        pbk = bank(); mm8(win(64 * h, 64), 64, pbk)
                        S(lambda e, h=h, pbk=pbk: e.activation(out=qT[:, h, c0:c0 + cn], in_=ps[pbk][0:64, 0:cn], func=AF.Copy, scale=0.125), r=[B("ps", pbk)], w=[B("qT", c0)])
                        yield
                    for g in range(2):
                        pbk = bank(); mm8(win(512 + 64 * g, 64), 64, pbk)
                        S(lambda e, g=g, pbk=pbk: e.activation(out=kT[:, g, 128 + c0:128 + c0 + cn], in_=ps[pbk][0:64, 0:cn], func=AF.Copy), r=[B("ps", pbk)], w=[B("kT", c0)])
                        yield
                    for ct in range(2):
                        pa = bank(); mm8(win(768 + 128 * ct, 128), 128, pa)
                        pg = bank(); mm8(win(1024 + 128 * ct, 128), 128, pg)
                        S(lambda e, pg=pg: e.activation(out=g2[:, 0:cn], in_=ps[pg][:, 0:cn], func=AF.Sigmoid), r=[B("ps", pg)], w=[B("g2")])
                        V(lambda e, ct=ct, pa=pa: e.tensor_tensor(out=ucv[:, ct, 30 + c0:30 + c0 + cn], in0=ps[pa][:, 0:cn], in1=g2[:, 0:cn], op=ALU.mult), r=[B("ps", pa), B("g2")], w=[B("ucv", c0)])
                        yield
                    for ct in range(2):
                        pbk = bank(); mm8(win(1280 + 128 * ct, 128), 128, pbk)
                        S(lambda e, ct=ct, pbk=pbk: e.activation(out=us[:, ct, c0:c0 + cn], in_=ps[pbk][:, 0:cn], func=AF.Copy), r=[B("ps", pbk)], w=[B("us", c0)])
                        yield
                fv, sv = win(640, 128)
                fk, sk = win(512, 128)
                for bi in range(4):
                    c0 = bi * 128
                    cc0 = (c0 // 512) * 512
                    pbk = bank()
                    for k in range(8):
                        PE(lambda e, k=k: e.matmul(ps[pbk][:, 0:128], lhsT=hb[:, k, c0:c0 + 128], rhs=fv(k), start=(k == 0), stop=(k == 7)), r=[sv, B("hb", par, cc0)], w=[B("ps", pbk)])
                    S(lambda e, bi=bi, pbk=pbk: e.activation(out=vp[:, bi + 1, :, 64:128], in_=ps[pbk][:, 0:128].rearrange("p (g d) -> p g d", g=2), func=AF.Copy), r=[B("ps", pbk)], w=[B("vp", bi + 1)])
                    yield
                    if last and bi == 3:
                        V(lambda e, pbk=pbk: e.tensor_copy(out=tk[:], in_=ps[pbk][:, 0:128]), r=[B("ps", pbk), B("vp", bi + 1)], w=[B("tk")])
                        Q(lambda e, l=l: e.dma_start(out=ov_p[l, :, :], in_=tk[:]), r=[B("tk")], w=[B("ov_p", l)])
                        pbk2 = bank()
                        for k in range(8):
                            PE(lambda e, k=k: e.matmul(ps[pbk2][:, 0:128], lhsT=hb[:, k, c0:c0 + 128], rhs=fk(k), start=(k == 0), stop=(k == 7)), r=[sk, B("hb", par, cc0)], w=[B("ps", pbk2)])
                        V(lambda e, pbk2=pbk2: e.tensor_copy(out=tk[:], in_=ps[pbk2][:, 0:128]), r=[B("ps", pbk2)], w=[B("tk")])
                        Q(lambda e, l=l: e.dma_start(out=ok_p[l, :, :], in_=tk[:]), r=[B("tk")], w=[B("ok_p", l)])
                if first:
                    pbk = bank()
                    for (f_, s_, off) in ((fk, sk, 0), (fv, sv, 128)):
                        for k in range(8):
                            PE(lambda e, k=k, f_=f_, off=off: e.matmul(ps[pbk][0:NS, off:off + 128], lhsT=hb[:, k, T:NTM], rhs=f_(k), start=(k == 0), stop=(k == 7)), r=[s_, B("hb", par, T)], w=[B("ps", pbk)])
                    V(lambda e, pbk=pbk: e.tensor_copy(out=tkb[:].rearrange("p a k -> p (a k)"), in_=ps[pbk][0:NS, 0:256]), r=[B("ps", pbk)], w=[B("tkb")])
                    for b in range(NS):
                        Q(lambda e, l=l, b=b: e.dma_start(out=ok_s[l, b:b + 1, 0:127, :].rearrange("b r c -> b (r c)"), in_=ck_d[l, b:b + 1, 1:128, :].rearrange("b r c -> b (r c)")), w=[B("ok_s", l)])
                        Q(lambda e, l=l, b=b: e.dma_start(out=ov_s[l, b:b + 1, 0:127, :].rearrange("b r c -> b (r c)"), in_=cv_d[l, b:b + 1, 1:128, :].rearrange("b r c -> b (r c)")), w=[B("ov_s", l)])
                    Q(lambda e, l=l: e.dma_start(out=ok_s[l, :, 127, :], in_=tkb[:, 0, :]), r=[B("tkb")], w=[B("ok_s", l)])
                    Q(lambda e, l=l: e.dma_start(out=ov_s[l, :, 127, :], in_=tkb[:, 1, :]), r=[B("tkb")], w=[B("ov_s", l)])
                if not last:
                    V(lambda e, l=l: e.tensor_copy(out=khalo[:, l, :, :], in_=kT[:, :, T:T + 128]), r=[B("kT", 0)], w=[B("khalo", l)])
                    V(lambda e, l=l: e.tensor_copy(out=vhalo[:, l, :, :], in_=vp[:, 4, :, :]), r=[B("vp", 4)], w=[B("vhalo", l)])
                    V(lambda e, l=l: e.tensor_copy(out=chalo[:, l, :, :], in_=ucv[:, :, T:T + 30]), r=[B("ucv", 0)], w=[B("chalo", l)])
                else:
                    V(lambda e: e.tensor_copy(out=cstg[:], in_=ucv[:, :, T:T + 30]), r=[B("ucv", 0)], w=[B("cstg")])
                    Q(lambda e, l=l: e.dma_start(out=oconv_p[l, :, :].rearrange("p (c k) -> p c k", c=2), in_=cstg[:]), r=[B("cstg")], w=[B("oconv_p", l)])

                s_cd = []
                for ct in range(2):
                    s_ = slot_i["M"] % 3
                    if not P.dry:
                        slot_i["M"] += 1
                    G(lambda e, s_=s_, ct=ct: e.dma_start(out=ringM[:, s_, 0:3968], in_=cdg_d[l * 2 + ct, :, 0:3968]), r=[B("cdg_d", l, ct)], w=[B("slotM", s_)])
                    s_cd.append(s_)
                units = [(bi, tile, r2) for bi in range(4) for tile in range(4) for r2 in range(2)]

                def att_info(u):
                    bi, tile, r2 = units[u]
                    q0 = bi * 128
                    nokprev = first and bi == 0
                    k_lo, k_n = (128, 128) if nokprev else (0, 256)
                    return bi, tile, r2, tile * 2 + r2, (tile * 2 + r2) // 4, q0, nokprev, k_lo, k_n, u % 3

                def att_A1(u):
                    bi, tile, r2, h, g, q0, nokprev, k_lo, k_n, a = att_info(u)
                    SB = 4 if u % 2 == 0 else 6
                    kread = [B("kT", 0)] + ([B("kT", "h")] if bi == 0 else [])
                    PE(lambda e: e.matmul(ps[SB][:, k_lo:k_lo + k_n], lhsT=qT[:, h, q0:q0 + 128], rhs=kT[:, g, q0 + k_lo:q0 + k_lo + k_n], start=True, stop=True), r=[B("qT", 0)] + kread, w=[B("ps", SB)])
                    V(lambda e: e.scalar_tensor_tensor(out=sc[:, a, k_lo:k_lo + k_n], in0=abias[:, k_lo:k_lo + k_n], scalar=float(2.0 ** (-(h + 1))), in1=ps[SB][:, k_lo:k_lo + k_n], op0=ALU.mult, op1=ALU.add), r=[B("ps", SB), cB], w=[B("sc", a)])
                    V(lambda e: e.reduce_max(out=sm[:, a, 0:1], in_=sc[:, a, k_lo:k_lo + k_n], axis=AX.X), r=[B("sc", a)], w=[B("sm", a)])
                    sinkc = pvc(l, OSINK + h)
                    V(lambda e: e.tensor_scalar(out=sm[:, a, 1:2], in0=sm[:, a, 0:1], scalar1=sinkc, scalar2=-1.0, op0=ALU.max, op1=ALU.mult), r=[B("sm", a), cB], w=[B("sm", a)])
                    S(lambda e: e.activation(out=pb[:, a, k_lo:k_lo + k_n], in_=sc[:, a, k_lo:k_lo + k_n], func=AF.Exp, bias=sm[:, a, 1:2], accum_out=sm[:, a, 2:3]), r=[B("sc", a), B("sm", a)], w=[B("pb", a), B("sm", a)])
                    S(lambda e: e.activation(out=sm[:, a, 3:4], in_=sinkc, func=AF.Exp, bias=sm[:, a, 1:2]), r=[B("sm", a), cB], w=[B("sm", a)])

                def att_A2(u):
                    bi, tile, r2, h, g, q0, nokprev, k_lo, k_n, a = att_info(u)
                    V(lambda e: e.tensor_tensor(out=sm[:, a, 4:5], in0=sm[:, a, 2:3], in1=sm[:, a, 3:4], op=ALU.add), r=[B("sm", a)], w=[B("sm", a)])
                    V(lambda e: e.reciprocal(out=sm[:, a, 5:6], in_=sm[:, a, 4:5]), r=[B("sm", a)], w=[B("sm", a)])
                    V(lambda e: e.tensor_scalar(out=pb[:, a, k_lo:k_lo + k_n], in0=pb[:, a, k_lo:k_lo + k_n], scalar1=sm[:, a, 5:6], scalar2=None, op0=ALU.mult), r=[B("sm", a), B("pb", a)], w=[B("pb", a)])

                def att_B(u):
                    bi, tile, r2, h, g, q0, nokprev, k_lo, k_n, a = att_info(u)
                    pa_ = u % 2
                    for kb in range(2):
                        if nokprev and kb == 0:
                            continue
                        PE(lambda e, kb=kb: e.transpose(out=psb[:, pa_ * 256 + kb * 128:pa_ * 256 + kb * 128 + 128], in_=pb[:, a, kb * 128:kb * 128 + 128], identity=identb[:]), r=[B("pb", a), B("identb")], w=[B("psb", 0)])
                    S(lambda e: e.activation(out=ptb[:, pa_, k_lo:k_lo + k_n], in_=psb[:, pa_ * 256 + k_lo:pa_ * 256 + k_lo + k_n], func=AF.Copy), r=[B("psb", 0)], w=[B("ptb", pa_)])
                    kbs = [1] if nokprev else [0, 1]
                    for kb in kbs:
                        lo = 64 if r2 == 0 else 0
                        PE(lambda e, kb=kb, lo=lo: e.matmul(ps[5][:, 0:128], lhsT=vp[:, bi + kb, g, lo:lo + 128], rhs=ptb[:, pa_, kb * 128:kb * 128 + 128], start=(r2 == 0 and kb == kbs[0]), stop=(r2 == 1 and kb == 1)), r=[B("vp", bi + kb), B("ptb", pa_)], w=[B("ps", 5)])
                    if r2 == 1:
                        S(lambda e: e.activation(out=hb[:, tile, q0:q0 + 128], in_=ps[5][:, 0:128], func=AF.Copy), r=[B("ps", 5)], w=[B("hb", par, 0)])

                def ssm_out(c0, cn, hsrc_re, hsrc_im, hoff, hbufs, ob=6):
                    for ct in range(2):
                        for j4 in range(4):
                            s8 = ct * 4 + j4
                            PE(lambda e, ct=ct, s8=s8, j4=j4: e.matmul(ps[ob][:, 0:cn], lhsT=WC[:, s8 * 2, :], rhs=hsrc_re(s8), start=(j4 == 0), stop=False), r=hbufs + [B("WC")], w=[B("ps", ob)])
                            PE(lambda e, ct=ct, s8=s8: e.matmul(ps[ob][:, 0:cn], lhsT=WC[:, s8 * 2 + 1, :], rhs=hsrc_im(s8), start=False, stop=False), r=hbufs + [B("WC")], w=[B("ps", ob)])
                        PE(lambda e, ct=ct: e.matmul(ps[ob][:, 0:cn], lhsT=Dd[:, ct, :], rhs=us[:, ct, c0:c0 + cn], start=False, stop=True), r=[B("us", (c0 // 512) * 512 if c0 < T else T), B("Dd")], w=[B("ps", ob)])
                        V(lambda e, ct=ct: e.tensor_copy(out=ysm[:, ct, 0:cn], in_=ps[ob][:, 0:cn]), r=[B("ps", ob)], w=[B("ysm", ct)])
                        V(lambda e, ct=ct: e.tensor_tensor(out=g1[:, 0:cn], in0=ysm[:, ct, 0:cn], in1=ysm[:, ct, 0:cn], op=ALU.mult), r=[B("ysm", ct)], w=[B("g1")])
                        V(lambda e, ct=ct: e.tensor_scalar(out=g1[:, 0:cn], in0=g1[:, 0:cn], scalar1=0.044715, scalar2=1.0, op0=ALU.mult, op1=ALU.add), r=[B("g1")], w=[B("g1")])
                        V(lambda e, ct=ct: e.tensor_tensor(out=g1[:, 0:cn], in0=g1[:, 0:cn], in1=ysm[:, ct, 0:cn], op=ALU.mult), r=[B("g1"), B("ysm", ct)], w=[B("g1")])
                        S(lambda e, ct=ct: e.activation(out=g2[:, 0:cn], in_=g1[:, 0:cn], func=AF.Sigmoid, scale=2.0 * math.sqrt(2.0 / math.pi)), r=[B("g1")], w=[B("g2")])
                        V(lambda e, ct=ct: e.tensor_tensor(out=ysm[:, ct, 0:cn], in0=ysm[:, ct, 0:cn], in1=g2[:, 0:cn], op=ALU.mult), r=[B("g2"), B("ysm", ct)], w=[B("ysm", ct)])
                        V(lambda e, ct=ct: e.tensor_copy(out=yb[:, ct, 0:cn], in_=ysm[:, ct, 0:cn]), r=[B("ysm", ct)], w=[B("yb", ct)])
                    for co in range(2):
                        for ct in range(2):
                            PE(lambda e, ct=ct, co=co: e.matmul(ps[ob][:, 0:cn], lhsT=gluw[:, ct, co * 128:(co + 1) * 128], rhs=yb[:, ct, 0:cn], start=(ct == 0), stop=(ct == 1)), r=[B("yb", 0), B("yb", 1), B("gluw")], w=[B("ps", ob)])
                        S(lambda e, co=co: e.activation(out=g2[:, 0:cn], in_=ps[ob][:, 0:cn], func=AF.Sigmoid, bias=pvc(l, OGB + co)), r=[B("ps", ob), cB], w=[B("g2")])
                        V(lambda e, co=co: e.tensor_tensor(out=hb[:, 6 + co, c0:c0 + cn], in0=ysm[:, co, 0:cn], in1=g2[:, 0:cn], op=ALU.mult), r=[B("g2"), B("ysm", co)], w=[B("hb", par, (c0 // 512) * 512 if c0 < T else T)])

                def att_gen():
                    for u in range(len(units) + 2):
                        if u < len(units):
                            att_A1(u)
                        if 1 <= u <= len(units):
                            att_A2(u - 1)
                        if u >= 2:
                            att_B(u - 2)
                        yield

                def ssm_gen():
                    for sc_i in range(T // J):
                        c0 = sc_i * J
                        ucc = (c0 // 512) * 512
                        for s8 in range(8):
                            ct, a = s8 // 4, s8 % 2
                            for ci, dst in ((0, 0), (1, J)):
                                PE(lambda e, ci=ci, dst=dst, s8=s8, ct=ct: e.matmul(ps[3][:, dst:dst + J], lhsT=WB[:, s8 * 2 + ci, :], rhs=us[:, ct, c0:c0 + J], start=True, stop=True), r=[B("us", ucc), B("WB")], w=[B("ps", 3)])
                            ra = s8 % 2
                            Q(lambda e, s8=s8, ra=ra: e.dma_start(out=rot2[:, ra, :, :].rearrange("p c j -> p (c j)"), in_=rot_d[l, :, s8 * 2 * J:(s8 + 1) * 2 * J]), r=[B("rot_d", l)], w=[B("rot2", ra)])
                            cosT, sinT = rot2[:, ra, 0, :], rot2[:, ra, 1, :]
                            pre, pim = ps[3][:, 0:J], ps[3][:, J:2 * J]
                            V(lambda e, cosT=cosT, pre=pre: e.tensor_tensor(out=bre[:], in0=pre, in1=cosT, op=ALU.mult), r=[B("ps", 3), B("rot2", ra)], w=[B("bre")])
                            V(lambda e, sinT=sinT, pim=pim: e.tensor_tensor(out=t1[:], in0=pim, in1=sinT, op=ALU.mult), r=[B("ps", 3), B("rot2", ra)], w=[B("t1")])
                            V(lambda e, cosT=cosT, pim=pim: e.tensor_tensor(out=bim[:], in0=pim, in1=cosT, op=ALU.mult), r=[B("ps", 3), B("rot2", ra)], w=[B("bim")])
                            V(lambda e, sinT=sinT, pre=pre: e.tensor_tensor(out=t2[:], in0=pre, in1=sinT, op=ALU.mult), r=[B("ps", 3), B("rot2", ra)], w=[B("t2")])
                            V(lambda e: e.tensor_tensor(out=bre[:], in0=bre[:], in1=t1[:], op=ALU.add), r=[B("t1"), B("bre")], w=[B("bre")])
                            V(lambda e: e.tensor_tensor(out=bim[:], in0=bim[:], in1=t2[:], op=ALU.subtract), r=[B("t2"), B("bim")], w=[B("bim")])
                            rho_b = lam[:, l, 0, s8:s8 + 1].to_broadcast([128, J])
                            V(lambda e, rho_b=rho_b, s8=s8: e.tensor_tensor_scan(out=wre[:], data0=rho_b, data1=bre[:], initial=hcar[:, l, s8, 0:1], op0=ALU.mult, op1=ALU.add), r=[B("bre"), B("lam", l), B("hcar", l)], w=[B("wre")])
                            V(lambda e, rho_b=rho_b, s8=s8: e.tensor_tensor_scan(out=wim[:], data0=rho_b, data1=bim[:], initial=hcar[:, l, s8, 1:2], op0=ALU.mult, op1=ALU.add), r=[B("bim"), B("lam", l), B("hcar", l)], w=[B("wim")])
                            V(lambda e, cosT=cosT: e.tensor_tensor(out=t1[:], in0=wre[:], in1=cosT, op=ALU.mult), r=[B("wre"), B("rot2", ra)], w=[B("t1")])
                            V(lambda e, sinT=sinT: e.tensor_tensor(out=t2[:], in0=wim[:], in1=sinT, op=ALU.mult), r=[B("wim"), B("rot2", ra)], w=[B("t2")])
                            V(lambda e, sinT=sinT: e.tensor_tensor(out=bre[:], in0=wre[:], in1=sinT, op=ALU.mult), r=[B("wre"), B("rot2", ra)], w=[B("bre")])
                            V(lambda e, cosT=cosT: e.tensor_tensor(out=bim[:], in0=wim[:], in1=cosT, op=ALU.mult), r=[B("wim"), B("rot2", ra)], w=[B("bim")])
                            V(lambda e, s8=s8: e.tensor_tensor(out=hbf[:, s8 * J:(s8 + 1) * J], in0=t1[:], in1=t2[:], op=ALU.subtract), r=[B("t1"), B("t2")], w=[B("hbf")])
                            V(lambda e, s8=s8: e.tensor_tensor(out=hbf[:, 8 * J + s8 * J:8 * J + (s8 + 1) * J], in0=bre[:], in1=bim[:], op=ALU.add), r=[B("bre"), B("bim")], w=[B("hbf")])
                            cl, sl = rot2[:, ra, 0, J - 1:J], rot2[:, ra, 1, J - 1:J]
                            V(lambda e, sl=sl: e.tensor_tensor(out=smc[:, 0:1], in0=wim[:, J - 1:J], in1=sl, op=ALU.mult), r=[B("wim"), B("rot2", ra)], w=[B("smc")])
                            V(lambda e, s8=s8, cl=cl: e.scalar_tensor_tensor(out=hcar[:, l, s8, 0:1], in0=wre[:, J - 1:J], scalar=cl, in1=smc[:, 0:1], op0=ALU.mult, op1=ALU.subtract), r=[B("wre"), B("rot2", ra), B("smc")], w=[B("hcar", l)])
                            V(lambda e, sl=sl: e.tensor_tensor(out=smc[:, 1:2], in0=wre[:, J - 1:J], in1=sl, op=ALU.mult), r=[B("wre"), B("rot2", ra)], w=[B("smc")])
                            V(lambda e, s8=s8, cl=cl: e.scalar_tensor_tensor(out=hcar[:, l, s8, 1:2], in0=wim[:, J - 1:J], scalar=cl, in1=smc[:, 1:2], op0=ALU.mult, op1=ALU.add), r=[B("wim"), B("rot2", ra), B("smc")], w=[B("hcar", l)])
                            yield
                        hbre = hbf
                        ssm_out(c0, J, lambda s8: hbre[:, s8 * J:(s8 + 1) * J], lambda s8: hbre[:, 8 * J + s8 * J:8 * J + (s8 + 1) * J], 0, [B("hbf")], ob=3)
                        yield

                ga_, gs_ = att_gen(), ssm_gen()
                alive_ = [True, True]
                while alive_[0] or alive_[1]:
                    for idx_, (g_, reps_) in enumerate(((ga_, 2), (gs_, 1))):
                        for _r in range(reps_):
                            if alive_[idx_]:
                                try:
                                    next(g_)
                                except StopIteration:
                                    alive_[idx_] = False
                    yield
                if first:
                    for g in range(2):
                        sk4 = sinkt[:, l * 2 + g:l * 2 + g + 1]
                        for b4 in range(NS // 4):
                            for bb in range(4):
                                b = b4 * 4 + bb
                                Q(lambda e, b=b, l=l: e.dma_start(out=Kb[:, 0, :], in_=ok_s[l, b, :, :]), r=[B("ok_s", l)], w=[B("Kb", 0)])
                                PE(lambda e, b=b, g=g: e.transpose(out=ps[4][0:64, 0:128], in_=Kb[:, 0, g * 64:(g + 1) * 64], identity=ident[:]), r=[B("Kb", 0), cB], w=[B("ps", 4)])
                                S(lambda e, b=b: e.activation(out=KbT[:, b % 2, :], in_=ps[4][0:64, 0:128], func=AF.Copy), r=[B("ps", 4)], w=[B("KbT", b % 2)])
                                PE(lambda e, b=b, g=g, bb=bb: e.matmul(ps[6][0:4, bb * 128:bb * 128 + 128], lhsT=qT[:, 4 * g:4 * g + 4, T + b], rhs=KbT[:, b % 2, :], start=True, stop=True), r=[B("qT", T), B("KbT", b % 2)], w=[B("ps", 6)])
                            V(lambda e, g=g: e.tensor_tensor(out=ssc[:], in0=ps[6][0:4, :].rearrange("p (b k) -> p b k", b=4), in1=sbias[:, g:g + 1, :].to_broadcast([4, 4, 128]), op=ALU.add), r=[B("ps", 6), cB], w=[B("ssc")])
                            V(lambda e: e.reduce_max(out=ssm_[:, 0, :], in_=ssc[:], axis=AX.X), r=[B("ssc")], w=[B("ssm_")])
                            V(lambda e, sk4=sk4: e.tensor_scalar(out=ssm_[:, 0, :], in0=ssm_[:, 0, :], scalar1=sk4, scalar2=None, op0=ALU.max), r=[B("ssm_"), cB], w=[B("ssm_")])
                            V(lambda e: e.tensor_tensor(out=ssc[:], in0=ssc[:], in1=ssm_[:, 0, :].unsqueeze(2).to_broadcast([4, 4, 128]), op=ALU.subtract), r=[B("ssc"), B("ssm_")], w=[B("ssc")])
                            S(lambda e: e.activation(out=ssc[:], in_=ssc[:], func=AF.Exp), r=[B("ssc")], w=[B("ssc")])
                            V(lambda e: e.reduce_sum(out=ssm_[:, 1, :], in_=ssc[:], axis=AX.X), r=[B("ssc")], w=[B("ssm_")])
                            S(lambda e, sk4=sk4: e.activation(out=ssm_[:, 2, :], in_=ssm_[:, 0, :], func=AF.Exp, scale=-1.0, bias=sk4), r=[B("ssm_"), cB], w=[B("ssm_")])
                            V(lambda e: e.tensor_tensor(out=ssm_[:, 1, :], in0=ssm_[:, 1, :], in1=ssm_[:, 2, :], op=ALU.add), r=[B("ssm_")], w=[B("ssm_")])
                            V(lambda e: e.reciprocal(out=ssm_[:, 1, :], in_=ssm_[:, 1, :]), r=[B("ssm_")], w=[B("ssm_")])
                            V(lambda e: e.tensor_tensor(out=spb[:], in0=ssc[:], in1=ssm_[:, 1, :].unsqueeze(2).to_broadcast([4, 4, 128]), op=ALU.mult), r=[B("ssc"), B("ssm_")], w=[B("spb")])
                            for bb in range(4):
                                PE(lambda e, bb=bb: e.transpose(out=psb[:, 512 + bb * 4:512 + bb * 4 + 4], in_=spb[:, bb, :], identity=identb[0:4, 0:4]), r=[B("spb"), B("identb")], w=[B("psb", 0)])
                            S(lambda e, g=g, b4=b4: e.activation(out=sptb[:, g, b4 * 4:b4 * 4 + 4, :], in_=psb[:, 512:512 + 16].rearrange("p (b r) -> p b r", r=4), func=AF.Copy), r=[B("psb", 0)], w=[B("sptb", g)])
                            yield
                    for b in range(NS):
                        Q(lambda e, b=b, l=l: e.dma_start(out=Vb[:, 0, :], in_=ov_s[l, b, :, :]), r=[B("ov_s", l)], w=[B("Vb", 0)])
                        V(lambda e, b=b: e.tensor_copy(out=Vbp[:, :, 64:128], in_=Vb[:, 0, :].rearrange("p (g d) -> p g d", g=2)), r=[B("Vb", 0)], w=[B("Vbp")])
                        for tile in range(4):
                            g = tile // 2
                            for r2 in range(2):
                                rr4 = (tile % 2) * 2 + r2
                                lo = 64 if r2 == 0 else 0
                                PE(lambda e, b=b, g=g, tile=tile, r2=r2, rr4=rr4, lo=lo: e.matmul(ps[5][:, 256 + b * 4 + tile:256 + b * 4 + tile + 1], lhsT=Vbp[:, g, lo:lo + 128], rhs=sptb[:, g, b, rr4:rr4 + 1], start=(r2 == 0), stop=(r2 == 1)), r=[B("Vbp"), B("sptb", g)], w=[B("ps", 5)])
                    S(lambda e: e.activation(out=hb[:, 0:4, T:NTM].rearrange("p t b -> p b t"), in_=ps[5][:, 256:256 + NS * 4].rearrange("p (b t) -> p b t", t=4), func=AF.Copy), r=[B("ps", 5)], w=[B("hb", par, T)])

                if first:
                    Q(lambda e, l=l: e.dma_start(out=cs[:, :, :, 0:30], in_=cconv_d[l, :, :].rearrange("p (c b k) -> p c b k", c=2, b=NS)), w=[B("cs")])
                    V(lambda e: e.tensor_copy(out=cs[:, :, :, 30], in_=ucv[:, :, 30 + T:30 + NTM]), r=[B("ucv", T)], w=[B("cs")])
                    Q(lambda e, l=l: e.dma_start(out=oconv_s[l, :, :].rearrange("p (c b k) -> p c b k", c=2, b=NS), in_=cs[:, :, :, 1:31]), r=[B("cs")], w=[B("oconv_s", l)])
                    for ct in range(2):
                        cw = pvt[:, l * PVL + OCW + ct * 31: l * PVL + OCW + ct * 31 + 31]
                        V(lambda e, ct=ct, cw=cw: e.tensor_tensor(out=cst[:, ct, :, :], in0=cs[:, ct, :, :], in1=cw.unsqueeze(1).to_broadcast([128, NS, 31]), op=ALU.mult), r=[B("cs"), cB], w=[B("ysm", 0), B("ysm", 1)])
                        V(lambda e, ct=ct: e.reduce_sum(out=acc[:, ct, T:NTM], in_=cst[:, ct, :, :], axis=AX.X), r=[B("ysm", 0), B("ysm", 1)], w=[B("acc", T, ct)])
                        V(lambda e, ct=ct: e.tensor_scalar(out=acc[:, ct, T:NTM], in0=acc[:, ct, T:NTM], scalar1=pvc(l, OCB + ct), scalar2=None, op0=ALU.add), r=[B("acc", T, ct), cB], w=[B("acc", T, ct)])
                for (c0, cn) in ccs:
                    if c0 < T:
                        hr = [B("ucv", c0)] + ([B("ucv", "h")] if c0 == 0 else [B("ucv", c0 - 512)])
                        for ct in range(2):
                            pc = bank()
                            for kk in range(31):
                                PE(lambda e, ct=ct, kk=kk, pc=pc: e.matmul(ps[pc][:, 0:cn], lhsT=ringM[:, s_cd[ct], kk * 128:(kk + 1) * 128], rhs=ucv[:, ct, c0 + kk:c0 + kk + cn], start=(kk == 0), stop=(kk == 30)), r=hr + [B("slotM", s_cd[ct])], w=[B("ps", pc)])
                            S(lambda e, ct=ct, pc=pc: e.activation(out=acc[:, ct, c0:c0 + cn], in_=ps[pc][:, 0:cn], func=AF.Identity, bias=pvc(l, OCB + ct)), r=[B("ps", pc), cB], w=[B("acc", c0, ct)])
                    for ct in range(2):
                        S(lambda e, ct=ct: e.activation(out=ysq[:, ct, 0:cn], in_=acc[:, ct, c0:c0 + cn], func=AF.Square), r=[B("acc", c0, 0), B("acc", c0, 1)], w=[B("ysm", 0), B("ysm", 1)])
                    for ct in range(2):
                        PE(lambda e, ct=ct: e.matmul(ps[6][:, 0:cn], lhsT=ones_f[:], rhs=acc[:, ct, c0:c0 + cn], start=(ct == 0), stop=(ct == 1)), r=[B("acc", c0, 0), B("acc", c0, 1), cB], w=[B("ps", 6)])
                    V(lambda e: e.tensor_scalar(out=mean[:, 0:cn], in0=ps[6][:, 0:cn], scalar1=1.0 / 256, scalar2=None, op0=ALU.mult), r=[B("ps", 6)], w=[B("mean")])
                    for ct in range(2):
                        PE(lambda e, ct=ct: e.matmul(ps[6][:, 0:cn], lhsT=ones_f[:], rhs=ysq[:, ct, 0:cn], start=(ct == 0), stop=(ct == 1)), r=[B("ysm", 0), B("ysm", 1), cB], w=[B("ps", 6)])
                    V(lambda e: e.tensor_tensor(out=var[:, 0:cn], in0=mean[:, 0:cn], in1=mean[:, 0:cn], op=ALU.mult), r=[B("mean")], w=[B("g2")])
                    V(lambda e: e.scalar_tensor_tensor(out=var[:, 0:cn], in0=ps[6][:, 0:cn], scalar=1.0 / 256, in1=var[:, 0:cn], op0=ALU.mult, op1=ALU.subtract), r=[B("ps", 6), B("g2")], w=[B("g2")])
                    S(lambda e: e.activation(out=var[:, 0:cn], in_=var[:, 0:cn], func=AF.Sqrt, bias=EPS), r=[B("g2")], w=[B("g2")])
                    V(lambda e: e.reciprocal(out=var[:, 0:cn], in_=var[:, 0:cn]), r=[B("g2")], w=[B("g2")])
                    for ct in range(2):
                        V(lambda e, ct=ct: e.tensor_tensor(out=g1[:, 0:cn], in0=acc[:, ct, c0:c0 + cn], in1=mean[:, 0:cn], op=ALU.subtract), r=[B("acc", c0, 0), B("acc", c0, 1), B("mean")], w=[B("g1")])
                        V(lambda e, ct=ct: e.tensor_tensor(out=g1[:, 0:cn], in0=g1[:, 0:cn], in1=var[:, 0:cn], op=ALU.mult), r=[B("g1"), B("g2")], w=[B("g1")])
                        V(lambda e, ct=ct: e.tensor_scalar(out=g1[:, 0:cn], in0=g1[:, 0:cn], scalar1=pvc(l, OLG + ct), scalar2=pvc(l, OLB + ct), op0=ALU.mult, op1=ALU.add), r=[B("g1"), cB], w=[B("g1")])
                        S(lambda e, ct=ct: e.activation(out=hb[:, 4 + ct, c0:c0 + cn], in_=g1[:, 0:cn], func=AF.Silu), r=[B("g1")], w=[B("hb", par, c0)])
                        yield

                if last:
                    Q(lambda e, l=l: e.dma_start(out=ossm_p[l, :, :].rearrange("p (s c) -> p s c", c=2), in_=hcar[:, l, :, :]), r=[B("hcar", l)], w=[B("ossm_p", l)])
                if first:
                    Q(lambda e, l=l: e.dma_start(out=h0[:, 0, :, :], in_=sre_d[l, :, :].rearrange("p (s b) -> p s b", s=8)), w=[B("h0")])
                    Q(lambda e, l=l: e.dma_start(out=h0[:, 1, :, :], in_=sim_d[l, :, :].rearrange("p (s b) -> p s b", s=8)), w=[B("h0")])
                    for s8 in range(8):
                        ct = s8 // 4
                        for ci in range(2):
                            PE(lambda e, ci=ci, s8=s8, ct=ct: e.matmul(ps[4][:, (s8 * 2 + ci) * NS:(s8 * 2 + ci + 1) * NS], lhsT=WB[:, s8 * 2 + ci, :], rhs=us[:, ct, T:NTM], start=True, stop=True), r=[B("us", T), B("WB")], w=[B("ps", 4)])
                    V(lambda e: e.tensor_copy(out=h1[:].rearrange("p c s b -> p s c b"), in_=ps[4][:, 0:16 * NS].rearrange("p (s c b) -> p s c b", s=8, c=2)), r=[B("ps", 4)], w=[B("h1")])
                    for s8 in range(8):
                        lre, lim = pl[:, 5, s8:s8 + 1], pl[:, 6, s8:s8 + 1]
                        V(lambda e, s8=s8: e.tensor_tensor(out=sm[:, 0, 6:7], in0=lam[:, l, 0, s8:s8 + 1], in1=lam[:, l, 1, s8:s8 + 1], op=ALU.mult), r=[B("lam", l)], w=[B("sm", 0)])
                        V(lambda e, s8=s8: e.tensor_tensor(out=sm[:, 0, 7:8], in0=lam[:, l, 0, s8:s8 + 1], in1=lam[:, l, 2, s8:s8 + 1], op=ALU.mult), r=[B("lam", l)], w=[B("sm", 0)])
                        V(lambda e, s8=s8: e.tensor_scalar(out=sm[:, 1, 7:8], in0=sm[:, 0, 7:8], scalar1=-1.0, scalar2=None, op0=ALU.mult), r=[B("sm", 0)], w=[B("sm", 1)])
                        V(lambda e, s8=s8: e.scalar_tensor_tensor(out=h1[:, 0, s8, :], in0=h0[:, 0, s8, :], scalar=sm[:, 0, 6:7], in1=h1[:, 0, s8, :], op0=ALU.mult, op1=ALU.add), r=[B("h0"), B("sm", 0), B("h1")], w=[B("h1")])
                        V(lambda e, s8=s8: e.scalar_tensor_tensor(out=h1[:, 0, s8, :], in0=h0[:, 1, s8, :], scalar=sm[:, 1, 7:8], in1=h1[:, 0, s8, :], op0=ALU.mult, op1=ALU.add), r=[B("h0"), B("sm", 1), B("h1")], w=[B("h1")])
                        V(lambda e, s8=s8: e.scalar_tensor_tensor(out=h1[:, 1, s8, :], in0=h0[:, 1, s8, :], scalar=sm[:, 0, 6:7], in1=h1[:, 1, s8, :], op0=ALU.mult, op1=ALU.add), r=[B("h0"), B("sm", 0), B("h1")], w=[B("h1")])
                        V(lambda e, s8=s8: e.scalar_tensor_tensor(out=h1[:, 1, s8, :], in0=h0[:, 0, s8, :], scalar=sm[:, 0, 7:8], in1=h1[:, 1, s8, :], op0=ALU.mult, op1=ALU.add), r=[B("h0"), B("sm", 0), B("h1")], w=[B("h1")])
                    Q(lambda e, l=l: e.dma_start(out=ossm_s[l, :, :].rearrange("p (c s b) -> p c s b", c=2, s=8), in_=h1[:]), r=[B("h1")], w=[B("ossm_s", l)])
                    V(lambda e: e.tensor_copy(out=h1b[:], in_=h1[:]), r=[B("h1")], w=[B("h1b")])
                    ssm_out(T, NS, lambda s8: h1b[:, 0, s8, :], lambda s8: h1b[:, 1, s8, :], 0, [B("h1b")])
                    yield

                s_out = [wload("M", w_out[l, :, i * 512:(i + 1) * 512].rearrange("(k p) n -> p k n", p=128), "p (k n) -> p k n", wid=l * 23 + 3 + i, ch=ch, k=8) for i in range(2)]
                for (c0, cn) in ccs:
                    for dt_ in range(8):
                        s = s_out[dt_ // 4]
                        off = (dt_ % 4) * 128
                        pbk = bank()
                        PE(lambda e, dt_=dt_, pbk=pbk: e.matmul(ps[pbk][:, 0:cn], lhsT=ident[:], rhs=x[:, dt_, c0:c0 + cn], start=True, stop=False), r=[cB, B("x", par, c0, dt_)], w=[B("ps", pbk)])
                        for k in range(8):
                            PE(lambda e, k=k, s=s, off=off, pbk=pbk: e.matmul(ps[pbk][:, 0:cn], lhsT=ring[:, s, k * 512 + off:k * 512 + off + 128], rhs=hb[:, k, c0:c0 + cn], start=False, stop=(k == 7)), r=[B(SL, s), B("hb", par, c0)], w=[B("ps", pbk)])
                        S(lambda e, dt_=dt_, pbk=pbk: e.activation(out=x[:, dt_, c0:c0 + cn], in_=ps[pbk][:, 0:cn], func=AF.Copy), r=[B("ps", pbk)], w=[B("x", par, c0, dt_)])
                        yield

        def gen_ffn(ch, l):
            par = ch % 2; x = X[par]; hb = HB[par]
            first, last = (ch == 0), (ch == nch - 1)
            ccs = [(0, 512)] + ([(T, NS)] if first else [])
            sq, rstd = sq2, rstd2
            ring, SL = ringF, "slotF"
            bank = fbank
            if True:
                rmsnorm(x, hb, par, l * PVL + OG2, ccs, sq2, rstd2, fbank(), "f")
                for fg in range(6):
                    nf = 4 if fg < 5 else 2
                    sg_ = wload("F", w_g[l, :, fg * 512:fg * 512 + nf * 128].rearrange("(k p) n -> p k n", p=128), "p (k n) -> p k n", wid=l * 23 + 5 + fg * 3, ch=ch, k=8)
                    su_ = wload("F", w_u[l, :, fg * 512:fg * 512 + nf * 128].rearrange("(k p) n -> p k n", p=128), "p (k n) -> p k n", wid=l * 23 + 6 + fg * 3, ch=ch, k=8)
                    sd_ = wload("F", w_d[l, fg * 512:fg * 512 + nf * 128, :].rearrange("(f p) n -> p f n", p=128), "p (f n) -> p f n", wid=l * 23 + 7 + fg * 3, ch=ch, f=nf)
                    W_ = nf * 128
                    for (c0, cn) in ccs:
                        for f in range(nf):
                            pg, pu = bank(), bank()
                            for (s_, pb_) in ((sg_, pg), (su_, pu)):
                                for k in range(8):
                                    PE(lambda e, k=k, s_=s_, pb_=pb_, f=f: e.matmul(ps[pb_][:, 0:cn], lhsT=ring[:, s_, k * W_ + f * 128:k * W_ + f * 128 + 128], rhs=hb[:, k, c0:c0 + cn], start=(k == 0), stop=(k == 7)), r=[B(SL, s_), B("hb", par, c0)], w=[B("ps", pb_)])
                            ubt = (sq_m, sq2)[f % 2]
                            ubb = (B("ubq"), B("sq", "f"))[f % 2]
                            S(lambda e, pg=pg, f=f: e.activation(out=sgb[:, f % 2, 0:cn], in_=ps[pg][:, 0:cn], func=AF.Silu), r=[B("ps", pg)], w=[B("sgb", f % 2)])
                            S(lambda e, pu=pu, ubt=ubt: e.activation(out=ubt[:, 0:cn], in_=ps[pu][:, 0:cn], func=AF.Copy), r=[B("ps", pu)], w=[ubb])
                            GP(lambda e, f=f, ubt=ubt: e.tensor_tensor(out=hid[:, f, c0:c0 + cn], in0=sgb[:, f % 2, 0:cn], in1=ubt[:, 0:cn], op=ALU.mult), r=[B("sgb", f % 2), ubb], w=[B("hid", c0)])
                            yield
                        for dt_ in range(8):
                            pbk = bank()
                            PE(lambda e, dt_=dt_, pbk=pbk: e.matmul(ps[pbk][:, 0:cn], lhsT=ident[:], rhs=x[:, dt_, c0:c0 + cn], start=True, stop=False), r=[cB, B("x", par, c0, dt_)], w=[B("ps", pbk)])
                            for f in range(nf):
                                PE(lambda e, f=f, dt_=dt_, pbk=pbk: e.matmul(ps[pbk][:, 0:cn], lhsT=ring[:, sd_, f * 1024 + dt_ * 128:f * 1024 + dt_ * 128 + 128], rhs=hid[:, f, c0:c0 + cn], start=False, stop=(f == nf - 1)), r=[B(SL, sd_), B("hid", c0)], w=[B("ps", pbk)])
                            S(lambda e, dt_=dt_, pbk=pbk: e.activation(out=x[:, dt_, c0:c0 + cn], in_=ps[pbk][:, 0:cn], func=AF.Copy), r=[B("ps", pbk)], w=[B("x", par, c0, dt_)])
                            yield
            if l == L - 1:
                for (c0, cn) in ccs:
                    for k in range(8):
                        S(lambda e, k=k: e.activation(out=sq[:, 0:cn], in_=x[:, k, c0:c0 + cn], func=AF.Square), r=[*XB(par, c0)], w=[B("sq", "f")])
                        PE(lambda e, k=k: e.matmul(ps[0][:, 0:cn], lhsT=ones_b[:], rhs=sq[:, 0:cn], start=(k == 0), stop=(k == 7)), r=[B("sq", "f"), cB], w=[B("ps", 0)])
                    S(lambda e: e.activation(out=rstd[:, 0:cn], in_=ps[0][:, 0:cn], func=AF.Sqrt, scale=1.0 / D, bias=EPS), r=[B("ps", 0)], w=[B("rstd", "f")])
                    V(lambda e: e.reciprocal(out=rstd[:, 0:cn], in_=rstd[:, 0:cn]), r=[B("rstd", "f")], w=[B("rstd", "f")])
                    for k in range(8):
                        gk = pvt[:, 4 * PVL + k: 4 * PVL + k + 1]
                        V(lambda e, k=k, gk=gk: e.scalar_tensor_tensor(out=x[:, k, c0:c0 + cn], in0=x[:, k, c0:c0 + cn], scalar=gk, in1=rstd[:, 0:cn], op0=ALU.mult, op1=ALU.mult), r=[*XB(par, c0), B("rstd", "f"), cB], w=[*XB(par, c0)])
                        if c0 < T:
                            Q(lambda e, k=k: e.dma_start(out=yT[k * 128:(k + 1) * 128, ch * T + c0:ch * T + c0 + cn], in_=x[:, k, c0:c0 + cn]), r=[*XB(par, c0)], w=[B("yT")])
                        else:
                            Q(lambda e, k=k: e.dma_start(out=ysT[k * 128:(k + 1) * 128, :], in_=x[:, k, c0:c0 + cn]), r=[*XB(par, c0)], w=[B("ysT")])
            yield

        def count_ops(gen):
            P.dry = True
            n0 = P.dryn
            for _ in gen:
                pass
            P.dry = False
            return P.dryn - n0

        def run2(ga, na, gb, nb):
            ia = ib = 0
            alive_a, alive_b = ga is not None, gb is not None
            while alive_a or alive_b:
                pick_a = alive_a and (not alive_b or ia * nb <= ib * na)
                n0 = P.nops
                if pick_a:
                    try:
                        next(ga)
                    except StopIteration:
                        alive_a = False
                    ia += P.nops - n0
                else:
                    try:
                        next(gb)
                    except StopIteration:
                        alive_b = False
                    ib += P.nops - n0

        streams = []
        for ch in range(nch):
            ph = []
            for l in range(L):
                ph.append(("m", ch, l))
                ph.append(("f", ch, l))
            streams.append(ph)
        steps = []
        t = 0
        start = {}
        for ch in range(nch):
            start[ch] = (ch // 2) * 2 * L + (ch % 2)
        nsteps = max(start[c] + 2 * L for c in range(nch))
        for t in range(nsteps):
            cur = []
            for ch in range(nch):
                p = t - start[ch]
                if 0 <= p < 2 * L:
                    cur.append(streams[ch][p])
            steps.append(cur)
        for cur in steps:
            gens = []
            for (kind, ch, l) in cur:
                mk = (lambda: gen_mix(ch, l)) if kind == "m" else (lambda: gen_ffn(ch, l))
                n = count_ops(mk())
                gens.append((mk(), max(n, 1)))
            if len(gens) == 1:
                run2(gens[0][0], gens[0][1], None, 1)
            else:
                run2(gens[0][0], gens[0][1], gens[1][0], gens[1][1])
        print('NOPS', P.nops, {k: len(v) for k, v in P.ops.items()}, flush=True)
        P.emit()
    return nc


def _alibi_tables():
    slopes = 2.0 ** (-8.0 * np.arange(1, 9, dtype=np.float32) / 8)
    qi = np.arange(128)[:, None]
    kj = np.arange(256)[None, :]
    dist = qi - kj + 128
    valid = (dist >= 0) & (dist < 128)
    ab = np.where(valid, -dist.astype(np.float32), -1.0e7).astype(np.float32)
    dj = (127 - np.arange(128)).astype(np.float32)
    sbias = (-slopes.reshape(2, 4, 1) * dj[None, None, :]).astype(np.float32)
    sbias = np.ascontiguousarray(sbias.transpose(1, 0, 2)).reshape(4, 2 * 128)
    return ab, sbias


def _fm(v):
    return np.ascontiguousarray(np.asarray(v, np.float32).reshape(-1, 128).T)


_NC_CACHE = {}


def kernel(nch=16, depth=4, **inp):
    f = lambda k: np.asarray(inp[k], np.float32)
    L = depth
    TT = nch * T
    key = (nch, depth)
    if key not in _NC_CACHE:
        _NC_CACHE[key] = build(nch, depth)
    nc = _NC_CACHE[key]
    ab, sbias = _alibi_tables()
    pv = np.zeros((128, NPV), np.float32)
    for l in range(L):
        o = l * PVL
        pv[:, o + 0:o + 8] = _fm(f("norm_mix_g")[l])
        pv[:, o + 8:o + 16] = _fm(f("norm_ffn_g")[l])
        cw = f("conv_dw_w")[l]
        for ct in range(2):
            pv[:, o + 16 + ct * 31:o + 16 + (ct + 1) * 31] = cw[:, ct * 128:(ct + 1) * 128].T
        pv[:, o + 78:o + 80] = _fm(f("conv_dw_b")[l])
        pv[:, o + 80:o + 82] = _fm(f("conv_ln_g")[l])
        pv[:, o + 82:o + 84] = _fm(f("conv_ln_b")[l])
        pv[:, o + 84:o + 86] = _fm(f("ssm_d")[l])
        pv[:, o + 86:o + 88] = _fm(f("ssm_glu_b")[l])
        pv[:, o + 88:o + 96] = _fm(f("ssm_a_re")[l].reshape(-1))
        pv[:, o + 96:o + 104] = _fm(f("ssm_a_im")[l].reshape(-1))
        pv[:, o + 104:o + 112] = _fm(np.repeat(f("ssm_log_dt")[l], 64))
        pv[:, o + 112:o + 120] = f("attn_sinks")[l][None, :]
    pv[:, 4 * PVL:4 * PVL + 8] = _fm(f("norm_final_g"))
    Bn = np.zeros((L, 128, 8, 2, 128), np.float32)
    Cn = np.zeros((L, 128, 8, 2, 128), np.float32)
    for l in range(L):
        for ci, (bk, ck_) in enumerate((("ssm_b_re", "ssm_c_re"), ("ssm_b_im", "ssm_c_im"))):
            bb = f(bk)[l]
            cc = f(ck_)[l]
            for g in range(16):
                st_, p0 = g // 2, (g % 2) * 64
                col = (g % 8) * 16
                Bn[l, p0:p0 + 64, st_, ci, col:col + 16] = bb[g]
                Cn[l, p0:p0 + 64, st_, ci, col:col + 16] = cc[g].T
    Bn = Bn.reshape(L, 128, -1)
    Cn = Cn.reshape(L, 128, -1)
    ident = np.eye(128, dtype=np.float32)
    jjv = np.broadcast_to(np.arange(1, J + 1, dtype=np.float32)[None, :], (128, J)).copy()
    xp = f("x_prompt")
    xs = f("x_sample")[:, 0, :]
    shared = {
        "w_in": f("w_in")[:L], "w_out": f("w_out")[:L], "w_g": f("w_ff_gate")[:L], "w_u": f("w_ff_up")[:L],
        "w_d": f("w_ff_down")[:L], "glu_w": f("ssm_glu_w")[:L], "pv": pv, "Bn": Bn, "Cn": Cn, "ident": ident,
        "jj": jjv, "abias": ab, "sbias": sbias,
        "sinkc": np.ascontiguousarray(f("attn_sinks")[:L].reshape(L, 2, 4).transpose(2, 0, 1).reshape(4, L * 2)),
    }
    in_maps = []
    for c in range(8):
        sq_, b0 = c // 4, c * NS
        m = dict(shared)
        m["xT"] = np.ascontiguousarray(xp[sq_, :TT, :].T)
        m["xsT"] = np.ascontiguousarray(xs[b0:b0 + NS].T)
        m["ck"] = np.ascontiguousarray(f("cache_swa_k")[:L, b0:b0 + NS].reshape(L, NS, 128, 128))
        m["cv"] = np.ascontiguousarray(f("cache_swa_v")[:L, b0:b0 + NS].reshape(L, NS, 128, 128))
        cc_ = f("cache_conv")[:L, b0:b0 + NS]
        m["cconv"] = np.ascontiguousarray(cc_.reshape(L, NS, 30, 2, 128).transpose(0, 4, 3, 1, 2)).reshape(L, 128, -1)
        for nm, kk in (("sre", "state_ssm_re"), ("sim", "state_ssm_im")):
            s_ = f(kk)[:L, b0:b0 + NS].reshape(L, NS, 8, 128)
            m[nm] = np.ascontiguousarray(s_.transpose(0, 3, 2, 1)).reshape(L, 128, -1)
        in_maps.append(m)
    res = run_bass_kernel_spmd(nc, in_maps, core_ids=list(range(8))).results
    y_p = np.stack([res[0]["yT"].T, res[4]["yT"].T]).astype(np.float32)
    y_s = np.concatenate([res[c]["ysT"].T for c in range(8)])[:, None, :].astype(np.float32)
    pc = (res[0], res[4])
    k_p = np.stack([np.stack([r["ok_p"][l].reshape(128, 2, 64) for r in pc]) for l in range(L)])
    v_p = np.stack([np.stack([r["ov_p"][l].reshape(128, 2, 64) for r in pc]) for l in range(L)])
    conv_p = np.stack([np.stack([r["oconv_p"][l].reshape(128, 2, 30).transpose(2, 1, 0).reshape(30, 256) for r in pc]) for l in range(L)])
    ssm_p = [np.stack([np.stack([r["ossm_p"][l].reshape(128, 8, 2)[:, :, ci].T.reshape(16, 64) for r in pc]) for l in range(L)]) for ci in range(2)]
    k_s = np.concatenate([res[c]["ok_s"].reshape(L, NS, 128, 2, 64) for c in range(8)], 1)
    v_s = np.concatenate([res[c]["ov_s"].reshape(L, NS, 128, 2, 64) for c in range(8)], 1)
    conv_s = np.concatenate([res[c]["oconv_s"].reshape(L, 128, 2, NS, 30).transpose(0, 3, 4, 2, 1).reshape(L, NS, 30, 256) for c in range(8)], 1)
    ssm_s = [np.concatenate([res[c]["ossm_s"].reshape(L, 128, 2, 8, NS)[:, :, ci].transpose(0, 3, 2, 1).reshape(L, NS, 16, 64) for c in range(8)], 1) for ci in range(2)]
    outs = (y_p, y_s, k_p, v_p, conv_p, ssm_p[0], ssm_p[1], k_s, v_s, conv_s, ssm_s[0], ssm_s[1])
    return tuple(np.ascontiguousarray(o, dtype=np.float32) for o in outs)
```

```python
import math
import os
import numpy as np
from contextlib import ExitStack
import concourse.bass as bass
import concourse.mybir as mybir
from concourse.bass_utils import run_bass_kernel_spmd

F32 = mybir.dt.float32
BF16 = mybir.dt.bfloat16
I32 = mybir.dt.int32
AF = mybir.ActivationFunctionType
ALU = mybir.AluOpType
AX = mybir.AxisListType

D = 1024
SEQ = 8192
T = 512
NS = 16
NTM = T + NS
DFF = 2816
NF = 22
J = 256
NSLOT = 4
PVL = 120
NPV = 4 * PVL + 8
EPS = 1e-6
TWO_PI = 2.0 * math.pi


import types


def _freeze(fn):
    if fn.__closure__ is None:
        return fn
    cells = []
    for c in fn.__closure__:
        try:
            cells.append(types.CellType(c.cell_contents))
        except ValueError:
            cells.append(c)
    return types.FunctionType(fn.__code__, fn.__globals__, fn.__name__, fn.__defaults__, tuple(cells))


class _Stub:
    def __init__(self):
        self.closed = True

    def matmul(self, *a, **kw):
        self.closed = bool(kw.get("stop", True))
        return self

    def transpose(self, *a, **kw):
        self.closed = True
        return self

    def then_inc(self, *a, **kw):
        return self


class Buf:
    __slots__ = ("name", "lw", "rd")

    def __init__(self, name):
        self.name = name
        self.lw = None
        self.rd = {}


class Prog:
    ENG = ["tensor", "vector", "scalar", "gpsimd", "sync"]

    def __init__(self, nc, same_eng_sync=("vector", "scalar", "gpsimd")):
        self.nc = nc
        self.ops = {e: [] for e in self.ENG}
        self.cnt = {}
        self.known = {e: {} for e in self.ENG}
        self.same = set(same_eng_sync)
        self.bufs = {}
        self.dry = False
        self.dq = {}
        self.dryn = 0
        self.nops = 0

    def B(self, *key):
        b = self.bufs.get(key)
        if b is None:
            b = self.bufs[key] = Buf(key)
        return b

    def op(self, eng, fn, r=(), w=(), dma=None, inc=None):
        if self.dry:
            self.dryn += 1
            return None
        fn = _freeze(fn)
        if getattr(self, "stopped", False):
            return None
        self.nops += 1
        if eng == "tensor" and "KLIMIT" in os.environ:
            st_ = _Stub()
            try:
                fn(st_)
            except Exception:
                pass
            self.open_grp = not st_.closed
        if self.nops >= int(os.environ.get("KLIMIT", "100000000")) and not getattr(self, "open_grp", False):
            self.stopped = True
        ex = [b for b in r if b.name[0] in ("ps", "psb")]
        if ex:
            r = [b for b in r if b.name[0] not in ("ps", "psb")]
            w = list(w) + ex
        deps = {}
        for b in list(r) + list(w):
            if b.lw is not None and deps.get(b.lw[0], 0) < b.lw[1]:
                deps[b.lw[0]] = b.lw[1]
        for b in w:
            for s, v in b.rd.items():
                if deps.get(s, 0) < v:
                    deps[s] = v
        pre = None
        if dma is None:
            sem, step = eng, 1
        else:
            npool = 32 if eng == "sync" else 8
            i = self.dq.get(eng, 0)
            self.dq[eng] = i + 1
            sem, step = "d%s%d" % (eng[0], i % npool), 16
            if i >= npool:
                pre = (sem, 16 * (i // npool))
        waits = []
        if pre is not None and deps.get(pre[0], 0) < pre[1]:
            deps[pre[0]] = pre[1]
        for s, v in deps.items():
            if s == eng and eng not in self.same:
                continue
            if self.known[eng].get(s, 0) >= v:
                continue
            self.known[eng][s] = v
            waits.append((s, v))
        self.cnt[sem] = self.cnt.get(sem, 0) + step
        tok = (sem, self.cnt[sem])
        self.ops[eng].append((fn, waits, sem, step))
        for b in r:
            if b.rd.get(sem, 0) < tok[1]:
                b.rd[sem] = tok[1]
        for b in w:
            b.lw = tok
            b.rd = {}
        return tok

    def emit(self):
        nc = self.nc
        with ExitStack() as st:
            sems = {s: st.enter_context(nc.semaphore(s)) for s in self.cnt}
            block = st.enter_context(nc.Block())
            final = dict(self.cnt)

            def mk(engname):
                def body(e):
                    for fn, waits, sem, step in self.ops[engname]:
                        for s, v in waits:
                            e.wait_ge(sems[s], v)
                        fn(e).then_inc(sems[sem], step)
                    if engname == "sync":
                        for s, v in final.items():
                            e.wait_ge(sems[s], v)
                return body

            for engname in self.ENG:
                if self.ops[engname] or engname == "sync":
                    getattr(block, engname)(mk(engname))


def build(nch=16, depth=4):
    nc = bass.Bass("TRN2", target_bir_lowering=False)
    L = depth
    TT = nch * T

    def din(name, shape, dt=F32):
        return nc.dram_tensor(name, list(shape), dt, kind="ExternalInput").ap()

    def dout(name, shape, dt=F32):
        return nc.dram_tensor(name, list(shape), dt, kind="ExternalOutput").ap()

    xT = din("xT", [D, TT]); xsT = din("xsT", [D, NS])
    w_in = din("w_in", [L, D, 1536]); w_out = din("w_out", [L, D, D])
    w_g = din("w_g", [L, D, DFF]); w_u = din("w_u", [L, D, DFF]); w_d = din("w_d", [L, DFF, D])
    glu_w = din("glu_w", [L, 256, 256])
    pv_d = din("pv", [128, NPV])
    Bn_d = din("Bn", [L, 128, 8 * 2 * 128]); Cn_d = din("Cn", [L, 128, 8 * 2 * 128])
    ident_d = din("ident", [128, 128]); jj_d = din("jj", [128, J])
    abias_d = din("abias", [128, 256]); sbias_d = din("sbias", [4, 2 * 128]); sinkc_d = din("sinkc", [4, L * 2])
    ck_d = din("ck", [L, NS, 128, 128]); cv_d = din("cv", [L, NS, 128, 128])
    cconv_d = din("cconv", [L, 128, 2 * NS * 30])
    sre_d = din("sre", [L, 128, 8 * NS]); sim_d = din("sim", [L, 128, 8 * NS])

    yT = dout("yT", [D, TT]); ysT = dout("ysT", [D, NS])
    ok_p = dout("ok_p", [L, 128, 128]); ov_p = dout("ov_p", [L, 128, 128])
    oconv_p = dout("oconv_p", [L, 128, 2 * 30]); ossm_p = dout("ossm_p", [L, 128, 16])
    ok_s = dout("ok_s", [L, NS, 128, 128]); ov_s = dout("ov_s", [L, NS, 128, 128])
    oconv_s = dout("oconv_s", [L, 128, 2 * NS * 30]); ossm_s = dout("ossm_s", [L, 128, 2 * 8 * NS])
    cdg_d = nc.dram_tensor("cdg_scr", [L * 2, 128, 4096], BF16).ap()
    wscr = nc.dram_tensor("w_scr", [L * 23, 128, 4096], BF16).ap()
    rot_d = nc.dram_tensor("rot_scr", [L, 128, 8 * 2 * J], F32).ap()
    wb_d = nc.dram_tensor("wb_scr", [L, 128, 16 * 128], BF16).ap()
    wc_d = nc.dram_tensor("wc_scr", [L, 128, 16 * 128], BF16).ap()

    P = Prog(nc)
    B = P.B
    with ExitStack() as st:
        def sb(name, shape, dt=F32):
            return st.enter_context(nc.sbuf_tensor("sb_" + name, list(shape), dt))

        def pst(name, shape, dt=F32):
            return st.enter_context(nc.psum_tensor(name, list(shape), dt))

        x = sb("x", [128, 8, NTM]); hb = sb("hb", [128, 8, NTM], BF16)
        x1 = sb("x1", [128, 8, T]); hb1 = sb("hb1", [128, 8, T], BF16)
        X = [x, x1]; HB = [hb, hb1]
        qT = sb("qT", [64, 8, NTM], BF16); kT = sb("kT", [64, 2, 128 + NTM], BF16)
        vp = sb("vp", [128, 5, 2, 192], BF16)
        ucv = sb("ucv", [128, 2, 30 + NTM], BF16); cstg = sb("cstg", [128, 2, 30]); acc = sb("acc", [128, 2, NTM]); us = sb("us", [128, 2, NTM], BF16)
        hid = sb("hid", [128, 4, NTM], BF16)
        ringM = sb("ringM", [128, 3, 4096], BF16); ringF = sb("ringF", [128, 3, 4096], BF16)
        pvt = sb("pvt", [128, NPV])
        ident = sb("ident", [128, 128]); identb = sb("identb", [128, 128], BF16)
        ones_f = sb("ones_f", [128, 128]); ones_b = sb("ones_b", [128, 128], BF16)
        abias = sb("abias", [128, 256]); sbias = sb("sbias", [4, 2, 128])
        WB = sb("WB", [128, 16, 128], BF16); WC = sb("WC", [128, 16, 128], BF16)
        Dd = sb("Dd", [128, 2, 128], BF16); gluw = sb("gluw", [128, 2, 256], BF16)
        lam = sb("lam", [128, L, 4, 8])
        rot2 = sb("rot2", [128, 2, 2, J])
        hcar = sb("hcar", [128, L, 8, 2])
        khalo = sb("khalo", [64, L, 2, 128], BF16); vhalo = sb("vhalo", [128, L, 2, 192], BF16)
        chalo = sb("chalo", [128, L, 2, 30], BF16)
        sq = sb("sq", [128, 512], BF16); rstd = sb("rstd", [128, 512]); sq_m = sq
        sq2 = sb("sq2", [128, 512], BF16); rstd2 = sb("rstd2", [128, 512])
        sgb = sb("sgb", [128, 2, 512], BF16)
        sc = sb("sc", [128, 3, 256]); pb = sb("pb", [128, 3, 256], BF16); ptb = sb("ptb", [128, 2, 256], BF16)
        sm = sb("sm", [128, 3, 8]); smc = sb("smc", [128, 2])
        t1 = sb("t1", [128, J]); t2 = sb("t2", [128, J]); bre = sb("bre", [128, J]); bim = sb("bim", [128, J])
        wre = sb("wre", [128, J]); wim = sb("wim", [128, J])
        ysm = sb("ysm", [128, 2, 512]); yb = sb("yb", [128, 2, 512], BF16); g1 = sb("g1", [128, 512]); g2 = sb("g2", [128, 512])
        ysq = ysm; cst = ysm[:].rearrange("p c n -> p (c n)")[:, 0:2 * NS * 31].rearrange("p (c b k) -> p c b k", c=2, b=NS); mean = sb("mean", [128, 512]); var = g2
        jj = mean[:, 0:J]
        xflat = x[:].rearrange("p k n -> p (k n)")
        rotflat = x1[:].rearrange("p k n -> p (k n)")[:, 0:16 * J]
        big = xflat[:, 0:4096]; big2 = rotflat[:, 8 * J:16 * J]; bigi = sb("bigi", [128, 512], I32)[:]
        hbf = sb("hbf", [128, 16 * J], BF16)
        pl = sb("pl", [128, 16, 8])
        tk = sb("tk", [128, 128]); tkb = sb("tkb", [NS, 2, 128])
        Kb = sb("Kb", [128, 1, 128]); Vb = sb("Vb", [128, 1, 128]); KbT = sb("KbT", [64, 2, 128], BF16)
        Vbp = sb("Vbp", [128, 2, 192], BF16)
        ssc = sb("ssc", [4, 4, 128]); spb = sb("spb", [4, 4, 128], BF16); ssm_ = sb("ssm_", [4, 4, 4]); sinkt = sb("sinkt", [4, L * 2])
        sptb = sb("sptb", [128, 2, NS, 4], BF16)
        cs = sb("cs", [128, 2, NS, 31])
        h0 = sb("h0", [128, 2, 8, NS]); h1 = sb("h1", [128, 2, 8, NS]); h1b = sb("h1b", [128, 2, 8, NS], BF16)

        ps = [pst("ps%d" % i, [128, 512]) for i in range(7)]
        psb = pst("psb", [128, 1024], BF16)

        Q = lambda fn, r=(), w=(), ch="io": P.op("sync", fn, r, w, dma=ch)
        G = lambda fn, r=(), w=(), ch="w": P.op("gpsimd", fn, r, w, dma=ch)
        V = lambda fn, r=(), w=(): P.op("vector", fn, r, w)
        S = lambda fn, r=(), w=(): P.op("scalar", fn, r, w)
        PE = lambda fn, r=(), w=(): P.op("tensor", fn, r, w)
        GP = lambda fn, r=(), w=(): P.op("gpsimd", fn, r, w)

        def XB(par, c0):
            return [B("x", par, c0)] + [B("x", par, c0, d) for d in range(8)]

        def pvc(l, off, n=1):
            return pvt[:, l * PVL + off: l * PVL + off + n]

        OG1, OG2, OCW, OCB, OLG, OLB, OSD, OGB, OARE, OAIM, OLDT, OSINK = 0, 8, 16, 78, 80, 82, 84, 86, 88, 96, 104, 112
        rr = [0]

        def bank():
            rr[0] = (rr[0] + 1) % 3
            return 4 + rr[0]

        fr = [0]

        def fbank():
            fr[0] = (fr[0] + 1) % 3
            return fr[0]

        cB = B("const")
        Q(lambda e: e.dma_start(out=pvt[:], in_=pv_d[:, :]), w=[cB])
        Q(lambda e: e.dma_start(out=ident[:], in_=ident_d[:, :]), w=[cB])
        Q(lambda e: e.dma_start(out=jj[:], in_=jj_d[:, :]), w=[cB])
        Q(lambda e: e.dma_start(out=abias[:], in_=abias_d[:, :]), w=[cB])
        Q(lambda e: e.dma_start(out=sbias[:].rearrange("p h k -> p (h k)"), in_=sbias_d[:, :]), w=[cB])
        Q(lambda e: e.dma_start(out=sinkt[:], in_=sinkc_d[:, :]), w=[cB])
        V(lambda e: e.memset(ones_f[:], 1.0), w=[cB])
        V(lambda e: e.memset(ones_b[:], 1.0), w=[cB])
        V(lambda e: e.tensor_copy(out=identb[:], in_=ident[:]), r=[cB], w=[B("identb")])
        V(lambda e: e.memset(vp[:].rearrange("p a g c -> p (a g c)"), 0.0), w=[B("vp", i) for i in range(5)])
        V(lambda e: e.memset(vhalo[:].rearrange("p l g c -> p (l g c)"), 0.0), w=[B("vhalo", l) for l in range(L)])
        V(lambda e: e.memset(Vbp[:].rearrange("p g c -> p (g c)"), 0.0), w=[B("Vbp")])
        V(lambda e: e.memset(hcar[:].rearrange("p l s c -> p (l s c)"), 0.0), w=[B("hcar", l) for l in range(L)])
        V(lambda e: e.memset(chalo[:].rearrange("p l c k -> p (l c k)"), 0.0), w=[B("chalo", l) for l in range(L)])

        for l in range(L):
            pB = B("pl")
            are, aim, ldt = pvc(l, OARE, 8), pvc(l, OAIM, 8), pvc(l, OLDT, 8)
            c = lambda i: pl[:, i, :]
            S(lambda e: e.activation(out=c(0), in_=ldt, func=AF.Exp), r=[cB], w=[pB])
            V(lambda e: e.tensor_tensor(out=c(1), in0=are, in1=c(0), op=ALU.mult), r=[pB], w=[pB])
            V(lambda e: e.tensor_tensor(out=c(2), in0=aim, in1=c(0), op=ALU.mult), r=[pB], w=[pB])
            S(lambda e, l=l: e.activation(out=lam[:, l, 0, :], in_=c(1), func=AF.Exp), r=[pB], w=[B("lam", l)])
            V(lambda e: e.tensor_scalar(out=c(3), in0=c(2), scalar1=1.0 / TWO_PI, scalar2=None, op0=ALU.mult), r=[pB], w=[pB])
            V(lambda e: e.tensor_copy(out=bigi[:, 0:8], in_=c(3)), r=[pB], w=[pB])
            V(lambda e: e.tensor_copy(out=c(4), in_=bigi[:, 0:8]), r=[pB], w=[pB])
            V(lambda e: e.tensor_tensor(out=c(4), in0=c(3), in1=c(4), op=ALU.subtract), r=[pB], w=[pB])
            S(lambda e, l=l: e.activation(out=lam[:, l, 2, :], in_=c(4), func=AF.Sin, scale=6.283185), r=[pB], w=[B("lam", l)])
            V(lambda e: e.tensor_scalar(out=c(3), in0=c(3), scalar1=0.25, scalar2=None, op0=ALU.add), r=[pB], w=[pB])
            V(lambda e: e.tensor_copy(out=bigi[:, 0:8], in_=c(3)), r=[pB], w=[pB])
            V(lambda e: e.tensor_copy(out=c(4), in_=bigi[:, 0:8]), r=[pB], w=[pB])
            V(lambda e: e.tensor_tensor(out=c(4), in0=c(3), in1=c(4), op=ALU.subtract), r=[pB], w=[pB])
            S(lambda e, l=l: e.activation(out=lam[:, l, 1, :], in_=c(4), func=AF.Sin, scale=6.283185), r=[pB], w=[B("lam", l)])
            V(lambda e, l=l: e.tensor_tensor(out=c(5), in0=lam[:, l, 0, :], in1=lam[:, l, 1, :], op=ALU.mult), r=[pB, B("lam", l)], w=[pB])
            V(lambda e, l=l: e.tensor_tensor(out=c(6), in0=lam[:, l, 0, :], in1=lam[:, l, 2, :], op=ALU.mult), r=[pB, B("lam", l)], w=[pB])
            V(lambda e: e.tensor_scalar(out=c(7), in0=c(5), scalar1=-1.0, scalar2=None, op0=ALU.add), r=[pB], w=[pB])
            V(lambda e: e.tensor_tensor(out=c(8), in0=are, in1=are, op=ALU.mult), r=[pB], w=[pB])
            V(lambda e: e.tensor_tensor(out=c(9), in0=aim, in1=aim, op=ALU.mult), r=[pB], w=[pB])
            V(lambda e: e.tensor_tensor(out=c(8), in0=c(8), in1=c(9), op=ALU.add), r=[pB], w=[pB])
            V(lambda e: e.reciprocal(out=c(8), in_=c(8)), r=[pB], w=[pB])
            V(lambda e: e.tensor_tensor(out=c(9), in0=c(7), in1=are, op=ALU.mult), r=[pB], w=[pB])
            V(lambda e: e.tensor_tensor(out=c(10), in0=c(6), in1=aim, op=ALU.mult), r=[pB], w=[pB])
            V(lambda e: e.tensor_tensor(out=c(9), in0=c(9), in1=c(10), op=ALU.add), r=[pB], w=[pB])
            V(lambda e: e.tensor_tensor(out=c(11), in0=c(9), in1=c(8), op=ALU.mult), r=[pB], w=[pB])
            V(lambda e: e.tensor_tensor(out=c(9), in0=c(6), in1=are, op=ALU.mult), r=[pB], w=[pB])
            V(lambda e: e.tensor_tensor(out=c(10), in0=c(7), in1=aim, op=ALU.mult), r=[pB], w=[pB])
            V(lambda e: e.tensor_tensor(out=c(9), in0=c(9), in1=c(10), op=ALU.subtract), r=[pB], w=[pB])
            V(lambda e: e.tensor_tensor(out=c(12), in0=c(9), in1=c(8), op=ALU.mult), r=[pB], w=[pB])
            V(lambda e: e.tensor_scalar(out=c(13), in0=c(12), scalar1=-1.0, scalar2=None, op0=ALU.mult), r=[pB], w=[pB])
            bg = big[:, 0:2048].rearrange("p (s c k) -> p s c k", s=8, c=2)
            Q(lambda e, l=l: e.dma_start(out=big[:, 0:2048], in_=Bn_d[l, :, :]), r=[pB], w=[B("big")])
            for s8 in range(8):
                fre, fim, nfim = pl[:, 11, s8:s8 + 1], pl[:, 12, s8:s8 + 1], pl[:, 13, s8:s8 + 1]
                V(lambda e, s8=s8, fre=fre: e.tensor_scalar(out=t1[:, 0:128], in0=bg[:, s8, 0, :], scalar1=fre, scalar2=None, op0=ALU.mult), r=[pB, B("big")], w=[B("t1")])
                V(lambda e, s8=s8, nfim=nfim: e.scalar_tensor_tensor(out=t1[:, 0:128], in0=bg[:, s8, 1, :], scalar=nfim, in1=t1[:, 0:128], op0=ALU.mult, op1=ALU.add), r=[pB, B("big"), B("t1")], w=[B("t1")])
                V(lambda e, s8=s8, fre=fre: e.tensor_scalar(out=t2[:, 0:128], in0=bg[:, s8, 1, :], scalar1=fre, scalar2=None, op0=ALU.mult), r=[pB, B("big")], w=[B("t2")])
                V(lambda e, s8=s8, fim=fim: e.scalar_tensor_tensor(out=t2[:, 0:128], in0=bg[:, s8, 0, :], scalar=fim, in1=t2[:, 0:128], op0=ALU.mult, op1=ALU.add), r=[pB, B("big"), B("t2")], w=[B("t2")])
                PE(lambda e: e.transpose(out=ps[5][:, 0:128], in_=t1[:, 0:128], identity=ident[:]), r=[B("t1"), cB], w=[B("ps", 5)])
                PE(lambda e: e.transpose(out=ps[5][:, 128:256], in_=t2[:, 0:128], identity=ident[:]), r=[B("t2"), cB], w=[B("ps", 5)])
                S(lambda e, l=l, s8=s8: e.activation(out=WB[:, s8 * 2:s8 * 2 + 2, :], in_=ps[5][:, 0:256].rearrange("p (c k) -> p c k", c=2), func=AF.Copy), r=[B("ps", 5)], w=[B("WB")])
            Q(lambda e, l=l: e.dma_start(out=big[:, 0:2048], in_=Cn_d[l, :, :]), w=[B("big")])
            V(lambda e, l=l: e.tensor_copy(out=WC[:, 0:16, :].rearrange("p (s c) k -> p s c k", c=2)[:, :, 0, :], in_=bg[:, :, 0, :]), r=[B("big")], w=[B("WC")])
            V(lambda e, l=l: e.tensor_scalar(out=WC[:, 0:16, :].rearrange("p (s c) k -> p s c k", c=2)[:, :, 1, :], in0=bg[:, :, 1, :], scalar1=-1.0, scalar2=None, op0=ALU.mult), r=[B("big")], w=[B("WC")])
            for ct in range(2):
                for kk in range(31):
                    wk = pvt[:, l * PVL + OCW + ct * 31 + kk: l * PVL + OCW + ct * 31 + kk + 1]
                    V(lambda e, kk=kk, wk=wk: e.tensor_scalar(out=hbf[:, kk * 128:(kk + 1) * 128], in0=ident[:], scalar1=wk, scalar2=None, op0=ALU.mult), r=[cB], w=[B("hbf")])
                Q(lambda e, l=l, ct=ct: e.dma_start(out=cdg_d[l * 2 + ct, :, 0:3968], in_=hbf[:, 0:3968]), r=[B("hbf")], w=[B("cdg_d", l, ct)])
            a8 = big2.rearrange("p (s j) -> p s j", s=8)
            for s8 in range(8):
                V(lambda e, s8=s8: e.tensor_scalar(out=a8[:, s8, :], in0=jj[:], scalar1=pl[:, 2, s8:s8 + 1], scalar2=1.0 / TWO_PI, op0=ALU.mult, op1=ALU.mult), r=[pB, cB], w=[B("big2"), B("rot")])
            rt = big.rearrange("p (s c j) -> p s c j", s=8, c=2)
            for ci, sh in ((1, 0.0), (0, 0.25)):
                if sh:
                    V(lambda e, sh=sh: e.tensor_scalar(out=big2, in0=big2, scalar1=sh, scalar2=None, op0=ALU.add), r=[B("big2")], w=[B("big2"), B("rot")])
                for pc in range(8 * J // 512):
                    V(lambda e, pc=pc: e.tensor_copy(out=bigi, in_=big2[:, pc * 512:(pc + 1) * 512]), r=[B("big2"), B("rot")], w=[B("bigi")])
                    V(lambda e, pc=pc: e.tensor_copy(out=rotflat[:, pc * 512:(pc + 1) * 512], in_=bigi), r=[B("bigi")], w=[B("rot")])
                V(lambda e: e.tensor_tensor(out=rotflat[:, 0:8 * J], in0=big2, in1=rotflat[:, 0:8 * J], op=ALU.subtract), r=[B("big2"), B("rot")], w=[B("rot")])
                S(lambda e, ci=ci: e.activation(out=rt[:, :, ci, :], in_=rotflat[:, 0:8 * J].rearrange("p (s j) -> p s j", s=8), func=AF.Sin, scale=6.283185), r=[B("rot")], w=[B("big")])
            Q(lambda e, l=l: e.dma_start(out=rot_d[l, :, :], in_=big), r=[B("big")], w=[B("rot_d", l)])
            Q(lambda e, l=l: e.dma_start(out=wb_d[l, :, :], in_=WB[:].rearrange("p a k -> p (a k)")), r=[B("WB")], w=[B("wb_d", l)])
            Q(lambda e, l=l: e.dma_start(out=wc_d[l, :, :], in_=WC[:].rearrange("p a k -> p (a k)")), r=[B("WC")], w=[B("wc_d", l)])

        print('MARK prologue_end', P.nops, flush=True)
        slot_i = {"M": 0, "F": 0}

        def wload(which, src_ap, shape_str, wid=None, ch=0, **kw):
            ring = ringM if which == "M" else ringF
            s = slot_i[which] % 3
            if not P.dry:
                slot_i[which] += 1
            n = 1
            for d_ in src_ap.shape[1:]:
                n *= d_
            if ch == 0:
                dst = ring[:, s, 0:n]
                if shape_str:
                    dst = dst.rearrange(shape_str, **kw)
                G(lambda e: e.dma_start(out=dst, in_=src_ap), w=[B("slot" + which, s)])
                if nch > 1:
                    Q(lambda e: e.dma_start(out=wscr[wid, :, 0:n], in_=ring[:, s, 0:n]), r=[B("slot" + which, s)], w=[B("wscr", wid)])
            else:
                G(lambda e: e.dma_start(out=ring[:, s, 0:n], in_=wscr[wid, :, 0:n]), r=[B("wscr", wid)], w=[B("slot" + which, s)])
            return s

        def rmsnorm(x, hb, par, goff, ccs, sq, rstd, pbk, tag):
            for (c0, cn) in ccs:
                pb_ = ps[pbk]
                S(lambda e: e.activation(out=hb[:, :, c0:c0 + cn], in_=x[:, :, c0:c0 + cn], func=AF.Square), r=[*XB(par, c0)], w=[B("hb", par, c0)])
                for k in range(8):
                    PE(lambda e, k=k: e.matmul(pb_[:, 0:cn], lhsT=ones_b[:], rhs=hb[:, k, c0:c0 + cn], start=(k == 0), stop=(k == 7)), r=[B("hb", par, c0), cB], w=[B("ps", pbk)])
                S(lambda e: e.activation(out=rstd[:, 0:cn], in_=pb_[:, 0:cn], func=AF.Sqrt, scale=1.0 / D, bias=EPS), r=[B("ps", pbk)], w=[B("rstd", tag)])
                V(lambda e: e.reciprocal(out=rstd[:, 0:cn], in_=rstd[:, 0:cn]), r=[B("rstd", tag)], w=[B("rstd", tag)])
                for k in range(8):
                    gk = pvt[:, goff + k: goff + k + 1]
                    V(lambda e, k=k, gk=gk: e.scalar_tensor_tensor(out=hb[:, k, c0:c0 + cn], in0=x[:, k, c0:c0 + cn], scalar=gk, in1=rstd[:, 0:cn], op0=ALU.mult, op1=ALU.mult), r=[*XB(par, c0), B("rstd", tag), cB], w=[B("hb", par, c0)])

        def gen_mix(ch, l):
            par = ch % 2; x = X[par]; hb = HB[par]
            ring, SL = ringM, "slotM"
            first, last = (ch == 0), (ch == nch - 1)
            ccs = [(0, 512)] + ([(T, NS)] if first else [])
            if l == 0:
                for k in range(8):
                    Q(lambda e, k=k: e.dma_start(out=x[:, k, 0:T], in_=xT[k * 128:(k + 1) * 128, ch * T:(ch + 1) * T]), w=[*XB(par, 0), B("big"), B("rot"), B("big2")])
                if first:
                    Q(lambda e: e.dma_start(out=x[:, :, T:NTM], in_=xsT.rearrange("(k p) n -> p k n", p=128)), w=[*XB(par, T), B("big")])
                yield
            if True:
                Q(lambda e, l=l: e.dma_start(out=WB[:].rearrange("p a k -> p (a k)"), in_=wb_d[l, :, :]), r=[B("wb_d", l)], w=[B("WB")])
                Q(lambda e, l=l: e.dma_start(out=WC[:].rearrange("p a k -> p (a k)"), in_=wc_d[l, :, :]), r=[B("wc_d", l)], w=[B("WC")])
                for ct in range(2):
                    V(lambda e, ct=ct: e.tensor_scalar(out=Dd[:, ct, :], in0=ident[:], scalar1=pvc(l, OSD + ct), scalar2=None, op0=ALU.mult), r=[cB], w=[B("Dd")])
                G(lambda e: e.dma_start(out=gluw[:], in_=glu_w[l].rearrange("(c p) n -> p c n", p=128)), w=[B("gluw")])
                s_in = [wload("M", w_in[l, :, i * 512:(i + 1) * 512].rearrange("(k p) n -> p k n", p=128), "p (k n) -> p k n", wid=l * 23 + i, ch=ch, k=8) for i in range(3)]

                def win(o_lo, o_n):
                    s = s_in[o_lo // 512]
                    off = o_lo % 512
                    return lambda k: ring[:, s, k * 512 + off: k * 512 + off + o_n], B(SL, s)

                V(lambda e, l=l: e.tensor_copy(out=kT[:, :, 0:128], in_=khalo[:, l, :, :]), r=[B("khalo", l)], w=[B("kT", "h")])
                V(lambda e, l=l: e.tensor_copy(out=vp[:, 0, :, :], in_=vhalo[:, l, :, :]), r=[B("vhalo", l)], w=[B("vp", 0)])
                V(lambda e, l=l: e.tensor_copy(out=ucv[:, :, 0:30], in_=chalo[:, l, :, :]), r=[B("chalo", l)], w=[B("ucv", "h")])

                rmsnorm(x, hb, par, l * PVL + OG1, ccs, sq, rstd, 6, "m")
                for (c0, cn) in ccs:
                    def mm8(lw, n_out, pbk):
                        f, sb_ = lw
                        for k in range(8):
                            PE(lambda e, k=k: e.matmul(ps[pbk][0:n_out, 0:cn], lhsT=f(k), rhs=hb[:, k, c0:c0 + cn], start=(k == 0), stop=(k == 7)), r=[sb_, B("hb", par, c0)], w=[B("ps", pbk)])
                    for h in range(8):
                        pbk = bank(); mm8(win(64 * h, 64), 64, pbk)
                        S(lambda e, h=h, pbk=pbk: e.activation(out=qT[:, h, c0:c0 + cn], in_=ps[pbk][0:64, 0:cn], func=AF.Copy, scale=0.125), r=[B("ps", pbk)], w=[B("qT", c0)])
                        yield
                    for g in range(2):
                        pbk = bank(); mm8(win(512 + 64 * g, 64), 64, pbk)
                        S(lambda e, g=g, pbk=pbk: e.activation(out=kT[:, g, 128 + c0:128 + c0 + cn], in_=ps[pbk][0:64, 0:cn], func=AF.Copy), r=[B("ps", pbk)], w=[B("kT", c0)])
                        yield
                    for ct in range(2):
                        pa = bank(); mm8(win(768 + 128 * ct, 128), 128, pa)
                        pg = bank(); mm8(win(1024 + 128 * ct, 128), 128, pg)
                        S(lambda e, pg=pg: e.activation(out=g2[:, 0:cn], in_=ps[pg][:, 0:cn], func=AF.Sigmoid), r=[B("ps", pg)], w=[B("g2")])
                        V(lambda e, ct=ct, pa=pa: e.tensor_tensor(out=ucv[:, ct, 30 + c0:30 + c0 + cn], in0=ps[pa][:, 0:cn], in1=g2[:, 0:cn], op=ALU.mult), r=[B("ps", pa), B("g2")], w=[B("ucv", c0)])
                        yield
                    for ct in range(2):
                        pbk = bank(); mm8(win(1280 + 128 * ct, 128), 128, pbk)
                        S(lambda e, ct=ct, pbk=pbk: e.activation(out=us[:, ct, c0:c0 + cn], in_=ps[pbk][:, 0:cn], func=AF.Copy), r=[B("ps", pbk)], w=[B("us", c0)])
                        yield
                fv, sv = win(640, 128)
                fk, sk = win(512, 128)
                for bi in range(4):
                    c0 = bi * 128
                    cc0 = (c0 // 512) * 512
                    pbk = bank()
                    for k in range(8):
                        PE(lambda e, k=k: e.matmul(ps[pbk][:, 0:128], lhsT=hb[:, k, c0:c0 + 128], rhs=fv(k), start=(k == 0), stop=(k == 7)), r=[sv, B("hb", par, cc0)], w=[B("ps", pbk)])
                    S(lambda e, bi=bi, pbk=pbk: e.activation(out=vp[:, bi + 1, :, 64:128], in_=ps[pbk][:, 0:128].rearrange("p (g d) -> p g d", g=2), func=AF.Copy), r=[B("ps", pbk)], w=[B("vp", bi + 1)])
                    yield
                    if last and bi == 3:
                        V(lambda e, pbk=pbk: e.tensor_copy(out=tk[:], in_=ps[pbk][:, 0:128]), r=[B("ps", pbk), B("vp", bi + 1)], w=[B("tk")])
                        Q(lambda e, l=l: e.dma_start(out=ov_p[l, :, :], in_=tk[:]), r=[B("tk")], w=[B("ov_p", l)])
                        pbk2 = bank()
                        for k in range(8):
                            PE(lambda e, k=k: e.matmul(ps[pbk2][:, 0:128], lhsT=hb[:, k, c0:c0 + 128], rhs=fk(k), start=(k == 0), stop=(k == 7)), r=[sk, B("hb", par, cc0)], w=[B("ps", pbk2)])
                        V(lambda e, pbk2=pbk2: e.tensor_copy(out=tk[:], in_=ps[pbk2][:, 0:128]), r=[B("ps", pbk2)], w=[B("tk")])
                        Q(lambda e, l=l: e.dma_start(out=ok_p[l, :, :], in_=tk[:]), r=[B("tk")], w=[B("ok_p", l)])
                if first:
                    pbk = bank()
                    for (f_, s_, off) in ((fk, sk, 0), (fv, sv, 128)):
                        for k in range(8):
                            PE(lambda e, k=k, f_=f_, off=off: e.matmul(ps[pbk][0:NS, off:off + 128], lhsT=hb[:, k, T:NTM], rhs=f_(k), start=(k == 0), stop=(k == 7)), r=[s_, B("hb", par, T)], w=[B("ps", pbk)])
                    V(lambda e, pbk=pbk: e.tensor_copy(out=tkb[:].rearrange("p a k -> p (a k)"), in_=ps[pbk][0:NS, 0:256]), r=[B("ps", pbk)], w=[B("tkb")])
                    for b in range(NS):
                        Q(lambda e, l=l, b=b: e.dma_start(out=ok_s[l, b:b + 1, 0:127, :].rearrange("b r c -> b (r c)"), in_=ck_d[l, b:b + 1, 1:128, :].rearrange("b r c -> b (r c)")), w=[B("ok_s", l)])
                        Q(lambda e, l=l, b=b: e.dma_start(out=ov_s[l, b:b + 1, 0:127, :].rearrange("b r c -> b (r c)"), in_=cv_d[l, b:b + 1, 1:128, :].rearrange("b r c -> b (r c)")), w=[B("ov_s", l)])
                    Q(lambda e, l=l: e.dma_start(out=ok_s[l, :, 127, :], in_=tkb[:, 0, :]), r=[B("tkb")], w=[B("ok_s", l)])
                    Q(lambda e, l=l: e.dma_start(out=ov_s[l, :, 127, :], in_=tkb[:, 1, :]), r=[B("tkb")], w=[B("ov_s", l)])
                if not last:
                    V(lambda e, l=l: e.tensor_copy(out=khalo[:, l, :, :], in_=kT[:, :, T:T + 128]), r=[B("kT", 0)], w=[B("khalo", l)])
                    V(lambda e, l=l: e.tensor_copy(out=vhalo[:, l, :, :], in_=vp[:, 4, :, :]), r=[B("vp", 4)], w=[B("vhalo", l)])
                    V(lambda e, l=l: e.tensor_copy(out=chalo[:, l, :, :], in_=ucv[:, :, T:T + 30]), r=[B("ucv", 0)], w=[B("chalo", l)])
                else:
                    V(lambda e: e.tensor_copy(out=cstg[:], in_=ucv[:, :, T:T + 30]), r=[B("ucv", 0)], w=[B("cstg")])
                    Q(lambda e, l=l: e.dma_start(out=oconv_p[l, :, :].rearrange("p (c k) -> p c k", c=2), in_=cstg[:]), r=[B("cstg")], w=[B("oconv_p", l)])

                s_cd = []
                for ct in range(2):
                    s_ = slot_i["M"] % 3
                    if not P.dry:
                        slot_i["M"] += 1
                    G(lambda e, s_=s_, ct=ct: e.dma_start(out=ringM[:, s_, 0:3968], in_=cdg_d[l * 2 + ct, :, 0:3968]), r=[B("cdg_d", l, ct)], w=[B("slotM", s_)])
                    s_cd.append(s_)
                units = [(bi, tile, r2) for bi in range(4) for tile in range(4) for r2 in range(2)]

                def att_info(u):
                    bi, tile, r2 = units[u]
                    q0 = bi * 128
                    nokprev = first and bi == 0
                    k_lo, k_n = (128, 128) if nokprev else (0, 256)
                    return bi, tile, r2, tile * 2 + r2, (tile * 2 + r2) // 4, q0, nokprev, k_lo, k_n, u % 3

                def att_A1(u):
                    bi, tile, r2, h, g, q0, nokprev, k_lo, k_n, a = att_info(u)
                    SB = 4 if u % 2 == 0 else 6
                    kread = [B("kT", 0)] + ([B("kT", "h")] if bi == 0 else [])
                    PE(lambda e: e.matmul(ps[SB][:, k_lo:k_lo + k_n], lhsT=qT[:, h, q0:q0 + 128], rhs=kT[:, g, q0 + k_lo:q0 + k_lo + k_n], start=True, stop=True), r=[B("qT", 0)] + kread, w=[B("ps", SB)])
                    V(lambda e: e.scalar_tensor_tensor(out=sc[:, a, k_lo:k_lo + k_n], in0=abias[:, k_lo:k_lo + k_n], scalar=float(2.0 ** (-(h + 1))), in1=ps[SB][:, k_lo:k_lo + k_n], op0=ALU.mult, op1=ALU.add), r=[B("ps", SB), cB], w=[B("sc", a)])
                    V(lambda e: e.reduce_max(out=sm[:, a, 0:1], in_=sc[:, a, k_lo:k_lo + k_n], axis=AX.X), r=[B("sc", a)], w=[B("sm", a)])
                    sinkc = pvc(l, OSINK + h)
                    V(lambda e: e.tensor_scalar(out=sm[:, a, 1:2], in0=sm[:, a, 0:1], scalar1=sinkc, scalar2=-1.0, op0=ALU.max, op1=ALU.mult), r=[B("sm", a), cB], w=[B("sm", a)])
                    S(lambda e: e.activation(out=pb[:, a, k_lo:k_lo + k_n], in_=sc[:, a, k_lo:k_lo + k_n], func=AF.Exp, bias=sm[:, a, 1:2], accum_out=sm[:, a, 2:3]), r=[B("sc", a), B("sm", a)], w=[B("pb", a), B("sm", a)])
                    S(lambda e: e.activation(out=sm[:, a, 3:4], in_=sinkc, func=AF.Exp, bias=sm[:, a, 1:2]), r=[B("sm", a), cB], w=[B("sm", a)])

                def att_A2(u):
                    bi, tile, r2, h, g, q0, nokprev, k_lo, k_n, a = att_info(u)
                    V(lambda e: e.tensor_tensor(out=sm[:, a, 4:5], in0=sm[:, a, 2:3], in1=sm[:, a, 3:4], op=ALU.add), r=[B("sm", a)], w=[B("sm", a)])
                    V(lambda e: e.reciprocal(out=sm[:, a, 5:6], in_=sm[:, a, 4:5]), r=[B("sm", a)], w=[B("sm", a)])
                    V(lambda e: e.tensor_scalar(out=pb[:, a, k_lo:k_lo + k_n], in0=pb[:, a, k_lo:k_lo + k_n], scalar1=sm[:, a, 5:6], scalar2=None, op0=ALU.mult), r=[B("sm", a), B("pb", a)], w=[B("pb", a)])

                def att_B(u):
                    bi, tile, r2, h, g, q0, nokprev, k_lo, k_n, a = att_info(u)
                    pa_ = u % 2
                    for kb in range(2):
                        if nokprev and kb == 0:
                            continue
                        PE(lambda e, kb=kb: e.transpose(out=psb[:, pa_ * 256 + kb * 128:pa_ * 256 + kb * 128 + 128], in_=pb[:, a, kb * 128:kb * 128 + 128], identity=identb[:]), r=[B("pb", a), B("identb")], w=[B("psb", 0)])
                    S(lambda e: e.activation(out=ptb[:, pa_, k_lo:k_lo + k_n], in_=psb[:, pa_ * 256 + k_lo:pa_ * 256 + k_lo + k_n], func=AF.Copy), r=[B("psb", 0)], w=[B("ptb", pa_)])
                    kbs = [1] if nokprev else [0, 1]
                    for kb in kbs:
                        lo = 64 if r2 == 0 else 0
                        PE(lambda e, kb=kb, lo=lo: e.matmul(ps[5][:, 0:128], lhsT=vp[:, bi + kb, g, lo:lo + 128], rhs=ptb[:, pa_, kb * 128:kb * 128 + 128], start=(r2 == 0 and kb == kbs[0]), stop=(r2 == 1 and kb == 1)), r=[B("vp", bi + kb), B("ptb", pa_)], w=[B("ps", 5)])
                    if r2 == 1:
                        S(lambda e: e.activation(out=hb[:, tile, q0:q0 + 128], in_=ps[5][:, 0:128], func=AF.Copy), r=[B("ps", 5)], w=[B("hb", par, 0)])

                def ssm_out(c0, cn, hsrc_re, hsrc_im, hoff, hbufs, ob=6):
                    for ct in range(2):
                        for j4 in range(4):
                            s8 = ct * 4 + j4
                            PE(lambda e, ct=ct, s8=s8, j4=j4: e.matmul(ps[ob][:, 0:cn], lhsT=WC[:, s8 * 2, :], rhs=hsrc_re(s8), start=(j4 == 0), stop=False), r=hbufs + [B("WC")], w=[B("ps", ob)])
                            PE(lambda e, ct=ct, s8=s8: e.matmul(ps[ob][:, 0:cn], lhsT=WC[:, s8 * 2 + 1, :], rhs=hsrc_im(s8), start=False, stop=False), r=hbufs + [B("WC")], w=[B("ps", ob)])
                        PE(lambda e, ct=ct: e.matmul(ps[ob][:, 0:cn], lhsT=Dd[:, ct, :], rhs=us[:, ct, c0:c0 + cn], start=False, stop=True), r=[B("us", (c0 // 512) * 512 if c0 < T else T), B("Dd")], w=[B("ps", ob)])
                        V(lambda e, ct=ct: e.tensor_copy(out=ysm[:, ct, 0:cn], in_=ps[ob][:, 0:cn]), r=[B("ps", ob)], w=[B("ysm", ct)])
                        V(lambda e, ct=ct: e.tensor_tensor(out=g1[:, 0:cn], in0=ysm[:, ct, 0:cn], in1=ysm[:, ct, 0:cn], op=ALU.mult), r=[B("ysm", ct)], w=[B("g1")])
                        V(lambda e, ct=ct: e.tensor_scalar(out=g1[:, 0:cn], in0=g1[:, 0:cn], scalar1=0.044715, scalar2=1.0, op0=ALU.mult, op1=ALU.add), r=[B("g1")], w=[B("g1")])
                        V(lambda e, ct=ct: e.tensor_tensor(out=g1[:, 0:cn], in0=g1[:, 0:cn], in1=ysm[:, ct, 0:cn], op=ALU.mult), r=[B("g1"), B("ysm", ct)], w=[B("g1")])
                        S(lambda e, ct=ct: e.activation(out=g2[:, 0:cn], in_=g1[:, 0:cn], func=AF.Sigmoid, scale=2.0 * math.sqrt(2.0 / math.pi)), r=[B("g1")], w=[B("g2")])
                        V(lambda e, ct=ct: e.tensor_tensor(out=ysm[:, ct, 0:cn], in0=ysm[:, ct, 0:cn], in1=g2[:, 0:cn], op=ALU.mult), r=[B("g2"), B("ysm", ct)], w=[B("ysm", ct)])
                        V(lambda e, ct=ct: e.tensor_copy(out=yb[:, ct, 0:cn], in_=ysm[:, ct, 0:cn]), r=[B("ysm", ct)], w=[B("yb", ct)])
                    for co in range(2):
                        for ct in range(2):
                            PE(lambda e, ct=ct, co=co: e.matmul(ps[ob][:, 0:cn], lhsT=gluw[:, ct, co * 128:(co + 1) * 128], rhs=yb[:, ct, 0:cn], start=(ct == 0), stop=(ct == 1)), r=[B("yb", 0), B("yb", 1), B("gluw")], w=[B("ps", ob)])
                        S(lambda e, co=co: e.activation(out=g2[:, 0:cn], in_=ps[ob][:, 0:cn], func=AF.Sigmoid, bias=pvc(l, OGB + co)), r=[B("ps", ob), cB], w=[B("g2")])
                        V(lambda e, co=co: e.tensor_tensor(out=hb[:, 6 + co, c0:c0 + cn], in0=ysm[:, co, 0:cn], in1=g2[:, 0:cn], op=ALU.mult), r=[B("g2"), B("ysm", co)], w=[B("hb", par, (c0 // 512) * 512 if c0 < T else T)])

                def att_gen():
                    for u in range(len(units) + 2):
                        if u < len(units):
                            att_A1(u)
                        if 1 <= u <= len(units):
                            att_A2(u - 1)
                        if u >= 2:
                            att_B(u - 2)
                        yield

                def ssm_gen():
                    for sc_i in range(T // J):
                        c0 = sc_i * J
                        ucc = (c0 // 512) * 512
                        for s8 in range(8):
                            ct, a = s8 // 4, s8 % 2
                            for ci, dst in ((0, 0), (1, J)):
                                PE(lambda e, ci=ci, dst=dst, s8=s8, ct=ct: e.matmul(ps[3][:, dst:dst + J], lhsT=WB[:, s8 * 2 + ci, :], rhs=us[:, ct, c0:c0 + J], start=True, stop=True), r=[B("us", ucc), B("WB")], w=[B("ps", 3)])
                            ra = s8 % 2
                            Q(lambda e, s8=s8, ra=ra: e.dma_start(out=rot2[:, ra, :, :].rearrange("p c j -> p (c j)"), in_=rot_d[l, :, s8 * 2 * J:(s8 + 1) * 2 * J]), r=[B("rot_d", l)], w=[B("rot2", ra)])
                            cosT, sinT = rot2[:, ra, 0, :], rot2[:, ra, 1, :]
                            pre, pim = ps[3][:, 0:J], ps[3][:, J:2 * J]
                            V(lambda e, cosT=cosT, pre=pre: e.tensor_tensor(out=bre[:], in0=pre, in1=cosT, op=ALU.mult), r=[B("ps", 3), B("rot2", ra)], w=[B("bre")])
                            V(lambda e, sinT=sinT, pim=pim: e.tensor_tensor(out=t1[:], in0=pim, in1=sinT, op=ALU.mult), r=[B("ps", 3), B("rot2", ra)], w=[B("t1")])
                            V(lambda e, cosT=cosT, pim=pim: e.tensor_tensor(out=bim[:], in0=pim, in1=cosT, op=ALU.mult), r=[B("ps", 3), B("rot2", ra)], w=[B("bim")])
                            V(lambda e, sinT=sinT, pre=pre: e.tensor_tensor(out=t2[:], in0=pre, in1=sinT, op=ALU.mult), r=[B("ps", 3), B("rot2", ra)], w=[B("t2")])
                            V(lambda e: e.tensor_tensor(out=bre[:], in0=bre[:], in1=t1[:], op=ALU.add), r=[B("t1"), B("bre")], w=[B("bre")])
                            V(lambda e: e.tensor_tensor(out=bim[:], in0=bim[:], in1=t2[:], op=ALU.subtract), r=[B("t2"), B("bim")], w=[B("bim")])
                            rho_b = lam[:, l, 0, s8:s8 + 1].to_broadcast([128, J])
                            V(lambda e, rho_b=rho_b, s8=s8: e.tensor_tensor_scan(out=wre[:], data0=rho_b, data1=bre[:], initial=hcar[:, l, s8, 0:1], op0=ALU.mult, op1=ALU.add), r=[B("bre"), B("lam", l), B("hcar", l)], w=[B("wre")])
                            V(lambda e, rho_b=rho_b, s8=s8: e.tensor_tensor_scan(out=wim[:], data0=rho_b, data1=bim[:], initial=hcar[:, l, s8, 1:2], op0=ALU.mult, op1=ALU.add), r=[B("bim"), B("lam", l), B("hcar", l)], w=[B("wim")])
                            V(lambda e, cosT=cosT: e.tensor_tensor(out=t1[:], in0=wre[:], in1=cosT, op=ALU.mult), r=[B("wre"), B("rot2", ra)], w=[B("t1")])
                            V(lambda e, sinT=sinT: e.tensor_tensor(out=t2[:], in0=wim[:], in1=sinT, op=ALU.mult), r=[B("wim"), B("rot2", ra)], w=[B("t2")])
                            V(lambda e, sinT=sinT: e.tensor_tensor(out=bre[:], in0=wre[:], in1=sinT, op=ALU.mult), r=[B("wre"), B("rot2", ra)], w=[B("bre")])
                            V(lambda e, cosT=cosT: e.tensor_tensor(out=bim[:], in0=wim[:], in1=cosT, op=ALU.mult), r=[B("wim"), B("rot2", ra)], w=[B("bim")])
                            V(lambda e, s8=s8: e.tensor_tensor(out=hbf[:, s8 * J:(s8 + 1) * J], in0=t1[:], in1=t2[:], op=ALU.subtract), r=[B("t1"), B("t2")], w=[B("hbf")])
                            V(lambda e, s8=s8: e.tensor_tensor(out=hbf[:, 8 * J + s8 * J:8 * J + (s8 + 1) * J], in0=bre[:], in1=bim[:], op=ALU.add), r=[B("bre"), B("bim")], w=[B("hbf")])
                            cl, sl = rot2[:, ra, 0, J - 1:J], rot2[:, ra, 1, J - 1:J]
                            V(lambda e, sl=sl: e.tensor_tensor(out=smc[:, 0:1], in0=wim[:, J - 1:J], in1=sl, op=ALU.mult), r=[B("wim"), B("rot2", ra)], w=[B("smc")])
                            V(lambda e, s8=s8, cl=cl: e.scalar_tensor_tensor(out=hcar[:, l, s8, 0:1], in0=wre[:, J - 1:J], scalar=cl, in1=smc[:, 0:1], op0=ALU.mult, op1=ALU.subtract), r=[B("wre"), B("rot2", ra), B("smc")], w=[B("hcar", l)])
                            V(lambda e, sl=sl: e.tensor_tensor(out=smc[:, 1:2], in0=wre[:, J - 1:J], in1=sl, op=ALU.mult), r=[B("wre"), B("rot2", ra)], w=[B("smc")])
                            V(lambda e, s8=s8, cl=cl: e.scalar_tensor_tensor(out=hcar[:, l, s8, 1:2], in0=wim[:, J - 1:J], scalar=cl, in1=smc[:, 1:2], op0=ALU.mult, op1=ALU.add), r=[B("wim"), B("rot2", ra), B("smc")], w=[B("hcar", l)])
                            yield
                        hbre = hbf
                        ssm_out(c0, J, lambda s8: hbre[:, s8 * J:(s8 + 1) * J], lambda s8: hbre[:, 8 * J + s8 * J:8 * J + (s8 + 1) * J], 0, [B("hbf")], ob=3)
                        yield

                ga_, gs_ = att_gen(), ssm_gen()
                alive_ = [True, True]
                while alive_[0] or alive_[1]:
                    for idx_, (g_, reps_) in enumerate(((ga_, 2), (gs_, 1))):
                        for _r in range(reps_):
                            if alive_[idx_]:
                                try:
                                    next(g_)
                                except StopIteration:
                                    alive_[idx_] = False
                    yield
                if first:
                    for g in range(2):
                        sk4 = sinkt[:, l * 2 + g:l * 2 + g + 1]
                        for b4 in range(NS // 4):
                            for bb in range(4):
                                b = b4 * 4 + bb
                                Q(lambda e, b=b, l=l: e.dma_start(out=Kb[:, 0, :], in_=ok_s[l, b, :, :]), r=[B("ok_s", l)], w=[B("Kb", 0)])
                                PE(lambda e, b=b, g=g: e.transpose(out=ps[4][0:64, 0:128], in_=Kb[:, 0, g * 64:(g + 1) * 64], identity=ident[:]), r=[B("Kb", 0), cB], w=[B("ps", 4)])
                                S(lambda e, b=b: e.activation(out=KbT[:, b % 2, :], in_=ps[4][0:64, 0:128], func=AF.Copy), r=[B("ps", 4)], w=[B("KbT", b % 2)])
                                PE(lambda e, b=b, g=g, bb=bb: e.matmul(ps[6][0:4, bb * 128:bb * 128 + 128], lhsT=qT[:, 4 * g:4 * g + 4, T + b], rhs=KbT[:, b % 2, :], start=True, stop=True), r=[B("qT", T), B("KbT", b % 2)], w=[B("ps", 6)])
                            V(lambda e, g=g: e.tensor_tensor(out=ssc[:], in0=ps[6][0:4, :].rearrange("p (b k) -> p b k", b=4), in1=sbias[:, g:g + 1, :].to_broadcast([4, 4, 128]), op=ALU.add), r=[B("ps", 6), cB], w=[B("ssc")])
                            V(lambda e: e.reduce_max(out=ssm_[:, 0, :], in_=ssc[:], axis=AX.X), r=[B("ssc")], w=[B("ssm_")])
                            V(lambda e, sk4=sk4: e.tensor_scalar(out=ssm_[:, 0, :], in0=ssm_[:, 0, :], scalar1=sk4, scalar2=None, op0=ALU.max), r=[B("ssm_"), cB], w=[B("ssm_")])
                            V(lambda e: e.tensor_tensor(out=ssc[:], in0=ssc[:], in1=ssm_[:, 0, :].unsqueeze(2).to_broadcast([4, 4, 128]), op=ALU.subtract), r=[B("ssc"), B("ssm_")], w=[B("ssc")])
                            S(lambda e: e.activation(out=ssc[:], in_=ssc[:], func=AF.Exp), r=[B("ssc")], w=[B("ssc")])
                            V(lambda e: e.reduce_sum(out=ssm_[:, 1, :], in_=ssc[:], axis=AX.X), r=[B("ssc")], w=[B("ssm_")])
                            S(lambda e, sk4=sk4: e.activation(out=ssm_[:, 2, :], in_=ssm_[:, 0, :], func=AF.Exp, scale=-1.0, bias=sk4), r=[B("ssm_"), cB], w=[B("ssm_")])
                            V(lambda e: e.tensor_tensor(out=ssm_[:, 1, :], in0=ssm_[:, 1, :], in1=ssm_[:, 2, :], op=ALU.add), r=[B("ssm_")], w=[B("ssm_")])
                            V(lambda e: e.reciprocal(out=ssm_[:, 1, :], in_=ssm_[:, 1, :]), r=[B("ssm_")], w=[B("ssm_")])
                            V(lambda e: e.tensor_tensor(out=spb[:], in0=ssc[:], in1=ssm_[:, 1, :].unsqueeze(2).to_broadcast([4, 4, 128]), op=ALU.mult), r=[B("ssc"), B("ssm_")], w=[B("spb")])
                            for bb in range(4):
                                PE(lambda e, bb=bb: e.transpose(out=psb[:, 512 + bb * 4:512 + bb * 4 + 4], in_=spb[:, bb, :], identity=identb[0:4, 0:4]), r=[B("spb"), B("identb")], w=[B("psb", 0)])
                            S(lambda e, g=g, b4=b4: e.activation(out=sptb[:, g, b4 * 4:b4 * 4 + 4, :], in_=psb[:, 512:512 + 16].rearrange("p (b r) -> p b r", r=4), func=AF.Copy), r=[B("psb", 0)], w=[B("sptb", g)])
                            yield
                    for b in range(NS):
                        Q(lambda e, b=b, l=l: e.dma_start(out=Vb[:, 0, :], in_=ov_s[l, b, :, :]), r=[B("ov_s", l)], w=[B("Vb", 0)])
                        V(lambda e, b=b: e.tensor_copy(out=Vbp[:, :, 64:128], in_=Vb[:, 0, :].rearrange("p (g d) -> p g d", g=2)), r=[B("Vb", 0)], w=[B("Vbp")])
                        for tile in range(4):
                            g = tile // 2
                            for r2 in range(2):
                                rr4 = (tile % 2) * 2 + r2
                                lo = 64 if r2 == 0 else 0
                                PE(lambda e, b=b, g=g, tile=tile, r2=r2, rr4=rr4, lo=lo: e.matmul(ps[5][:, 256 + b * 4 + tile:256 + b * 4 + tile + 1], lhsT=Vbp[:, g, lo:lo + 128], rhs=sptb[:, g, b, rr4:rr4 + 1], start=(r2 == 0), stop=(r2 == 1)), r=[B("Vbp"), B("sptb", g)], w=[B("ps", 5)])
                    S(lambda e: e.activation(out=hb[:, 0:4, T:NTM].rearrange("p t b -> p b t"), in_=ps[5][:, 256:256 + NS * 4].rearrange("p (b t) -> p b t", t=4), func=AF.Copy), r=[B("ps", 5)], w=[B("hb", par, T)])

                if first:
                    Q(lambda e, l=l: e.dma_start(out=cs[:, :, :, 0:30], in_=cconv_d[l, :, :].rearrange("p (c b k) -> p c b k", c=2, b=NS)), w=[B("cs")])
                    V(lambda e: e.tensor_copy(out=cs[:, :, :, 30], in_=ucv[:, :, 30 + T:30 + NTM]), r=[B("ucv", T)], w=[B("cs")])
                    Q(lambda e, l=l: e.dma_start(out=oconv_s[l, :, :].rearrange("p (c b k) -> p c b k", c=2, b=NS), in_=cs[:, :, :, 1:31]), r=[B("cs")], w=[B("oconv_s", l)])
                    for ct in range(2):
                        cw = pvt[:, l * PVL + OCW + ct * 31: l * PVL + OCW + ct * 31 + 31]
                        V(lambda e, ct=ct, cw=cw: e.tensor_tensor(out=cst[:, ct, :, :], in0=cs[:, ct, :, :], in1=cw.unsqueeze(1).to_broadcast([128, NS, 31]), op=ALU.mult), r=[B("cs"), cB], w=[B("ysm", 0), B("ysm", 1)])
                        V(lambda e, ct=ct: e.reduce_sum(out=acc[:, ct, T:NTM], in_=cst[:, ct, :, :], axis=AX.X), r=[B("ysm", 0), B("ysm", 1)], w=[B("acc", T, ct)])
                        V(lambda e, ct=ct: e.tensor_scalar(out=acc[:, ct, T:NTM], in0=acc[:, ct, T:NTM], scalar1=pvc(l, OCB + ct), scalar2=None, op0=ALU.add), r=[B("acc", T, ct), cB], w=[B("acc", T, ct)])
                for (c0, cn) in ccs:
                    if c0 < T:
                        hr = [B("ucv", c0)] + ([B("ucv", "h")] if c0 == 0 else [B("ucv", c0 - 512)])
                        for ct in range(2):
                            pc = bank()
                            for kk in range(31):
                                PE(lambda e, ct=ct, kk=kk, pc=pc: e.matmul(ps[pc][:, 0:cn], lhsT=ringM[:, s_cd[ct], kk * 128:(kk + 1) * 128], rhs=ucv[:, ct, c0 + kk:c0 + kk + cn], start=(kk == 0), stop=(kk == 30)), r=hr + [B("slotM", s_cd[ct])], w=[B("ps", pc)])
                            S(lambda e, ct=ct, pc=pc: e.activation(out=acc[:, ct, c0:c0 + cn], in_=ps[pc][:, 0:cn], func=AF.Identity, bias=pvc(l, OCB + ct)), r=[B("ps", pc), cB], w=[B("acc", c0, ct)])
                    for ct in range(2):
                        S(lambda e, ct=ct: e.activation(out=ysq[:, ct, 0:cn], in_=acc[:, ct, c0:c0 + cn], func=AF.Square), r=[B("acc", c0, 0), B("acc", c0, 1)], w=[B("ysm", 0), B("ysm", 1)])
                    for ct in range(2):
                        PE(lambda e, ct=ct: e.matmul(ps[6][:, 0:cn], lhsT=ones_f[:], rhs=acc[:, ct, c0:c0 + cn], start=(ct == 0), stop=(ct == 1)), r=[B("acc", c0, 0), B("acc", c0, 1), cB], w=[B("ps", 6)])
                    V(lambda e: e.tensor_scalar(out=mean[:, 0:cn], in0=ps[6][:, 0:cn], scalar1=1.0 / 256, scalar2=None, op0=ALU.mult), r=[B("ps", 6)], w=[B("mean")])
                    for ct in range(2):
                        PE(lambda e, ct=ct: e.matmul(ps[6][:, 0:cn], lhsT=ones_f[:], rhs=ysq[:, ct, 0:cn], start=(ct == 0), stop=(ct == 1)), r=[B("ysm", 0), B("ysm", 1), cB], w=[B("ps", 6)])
                    V(lambda e: e.tensor_tensor(out=var[:, 0:cn], in0=mean[:, 0:cn], in1=mean[:, 0:cn], op=ALU.mult), r=[B("mean")], w=[B("g2")])
                    V(lambda e: e.scalar_tensor_tensor(out=var[:, 0:cn], in0=ps[6][:, 0:cn], scalar=1.0 / 256, in1=var[:, 0:cn], op0=ALU.mult, op1=ALU.subtract), r=[B("ps", 6), B("g2")], w=[B("g2")])
                    S(lambda e: e.activation(out=var[:, 0:cn], in_=var[:, 0:cn], func=AF.Sqrt, bias=EPS), r=[B("g2")], w=[B("g2")])
                    V(lambda e: e.reciprocal(out=var[:, 0:cn], in_=var[:, 0:cn]), r=[B("g2")], w=[B("g2")])
                    for ct in range(2):
                        V(lambda e, ct=ct: e.tensor_tensor(out=g1[:, 0:cn], in0=acc[:, ct, c0:c0 + cn], in1=mean[:, 0:cn], op=ALU.subtract), r=[B("acc", c0, 0), B("acc", c0, 1), B("mean")], w=[B("g1")])
                        V(lambda e, ct=ct: e.tensor_tensor(out=g1[:, 0:cn], in0=g1[:, 0:cn], in1=var[:, 0:cn], op=ALU.mult), r=[B("g1"), B("g2")], w=[B("g1")])
                        V(lambda e, ct=ct: e.tensor_scalar(out=g1[:, 0:cn], in0=g1[:, 0:cn], scalar1=pvc(l, OLG + ct), scalar2=pvc(l, OLB + ct), op0=ALU.mult, op1=ALU.add), r=[B("g1"), cB], w=[B("g1")])
                        S(lambda e, ct=ct: e.activation(out=hb[:, 4 + ct, c0:c0 + cn], in_=g1[:, 0:cn], func=AF.Silu), r=[B("g1")], w=[B("hb", par, c0)])
                        yield

                if last:
                    Q(lambda e, l=l: e.dma_start(out=ossm_p[l, :, :].rearrange("p (s c) -> p s c", c=2), in_=hcar[:, l, :, :]), r=[B("hcar", l)], w=[B("ossm_p", l)])
                if first:
                    Q(lambda e, l=l: e.dma_start(out=h0[:, 0, :, :], in_=sre_d[l, :, :].rearrange("p (s b) -> p s b", s=8)), w=[B("h0")])
                    Q(lambda e, l=l: e.dma_start(out=h0[:, 1, :, :], in_=sim_d[l, :, :].rearrange("p (s b) -> p s b", s=8)), w=[B("h0")])
                    for s8 in range(8):
                        ct = s8 // 4
                        for ci in range(2):
                            PE(lambda e, ci=ci, s8=s8, ct=ct: e.matmul(ps[4][:, (s8 * 2 + ci) * NS:(s8 * 2 + ci + 1) * NS], lhsT=WB[:, s8 * 2 + ci, :], rhs=us[:, ct, T:NTM], start=True, stop=True), r=[B("us", T), B("WB")], w=[B("ps", 4)])
                    V(lambda e: e.tensor_copy(out=h1[:].rearrange("p c s b -> p s c b"), in_=ps[4][:, 0:16 * NS].rearrange("p (s c b) -> p s c b", s=8, c=2)), r=[B("ps", 4)], w=[B("h1")])
                    for s8 in range(8):
                        lre, lim = pl[:, 5, s8:s8 + 1], pl[:, 6, s8:s8 + 1]
                        V(lambda e, s8=s8: e.tensor_tensor(out=sm[:, 0, 6:7], in0=lam[:, l, 0, s8:s8 + 1], in1=lam[:, l, 1, s8:s8 + 1], op=ALU.mult), r=[B("lam", l)], w=[B("sm", 0)])
                        V(lambda e, s8=s8: e.tensor_tensor(out=sm[:, 0, 7:8], in0=lam[:, l, 0, s8:s8 + 1], in1=lam[:, l, 2, s8:s8 + 1], op=ALU.mult), r=[B("lam", l)], w=[B("sm", 0)])
                        V(lambda e, s8=s8: e.tensor_scalar(out=sm[:, 1, 7:8], in0=sm[:, 0, 7:8], scalar1=-1.0, scalar2=None, op0=ALU.mult), r=[B("sm", 0)], w=[B("sm", 1)])
                        V(lambda e, s8=s8: e.scalar_tensor_tensor(out=h1[:, 0, s8, :], in0=h0[:, 0, s8, :], scalar=sm[:, 0, 6:7], in1=h1[:, 0, s8, :], op0=ALU.mult, op1=ALU.add), r=[B("h0"), B("sm", 0), B("h1")], w=[B("h1")])
                        V(lambda e, s8=s8: e.scalar_tensor_tensor(out=h1[:, 0, s8, :], in0=h0[:, 1, s8, :], scalar=sm[:, 1, 7:8], in1=h1[:, 0, s8, :], op0=ALU.mult, op1=ALU.add), r=[B("h0"), B("sm", 1), B("h1")], w=[B("h1")])
                        V(lambda e, s8=s8: e.scalar_tensor_tensor(out=h1[:, 1, s8, :], in0=h0[:, 1, s8, :], scalar=sm[:, 0, 6:7], in1=h1[:, 1, s8, :], op0=ALU.mult, op1=ALU.add), r=[B("h0"), B("sm", 0), B("h1")], w=[B("h1")])
                        V(lambda e, s8=s8: e.scalar_tensor_tensor(out=h1[:, 1, s8, :], in0=h0[:, 0, s8, :], scalar=sm[:, 0, 7:8], in1=h1[:, 1, s8, :], op0=ALU.mult, op1=ALU.add), r=[B("h0"), B("sm", 0), B("h1")], w=[B("h1")])
                    Q(lambda e, l=l: e.dma_start(out=ossm_s[l, :, :].rearrange("p (c s b) -> p c s b", c=2, s=8), in_=h1[:]), r=[B("h1")], w=[B("ossm_s", l)])
                    V(lambda e: e.tensor_copy(out=h1b[:], in_=h1[:]), r=[B("h1")], w=[B("h1b")])
                    ssm_out(T, NS, lambda s8: h1b[:, 0, s8, :], lambda s8: h1b[:, 1, s8, :], 0, [B("h1b")])
                    yield

                s_out = [wload("M", w_out[l, :, i * 512:(i + 1) * 512].rearrange("(k p) n -> p k n", p=128), "p (k n) -> p k n", wid=l * 23 + 3 + i, ch=ch, k=8) for i in range(2)]
                for (c0, cn) in ccs:
                    for dt_ in range(8):
                        s = s_out[dt_ // 4]
                        off = (dt_ % 4) * 128
                        pbk = bank()
                        PE(lambda e, dt_=dt_, pbk=pbk: e.matmul(ps[pbk][:, 0:cn], lhsT=ident[:], rhs=x[:, dt_, c0:c0 + cn], start=True, stop=False), r=[cB, B("x", par, c0, dt_)], w=[B("ps", pbk)])
                        for k in range(8):
                            PE(lambda e, k=k, s=s, off=off, pbk=pbk: e.matmul(ps[pbk][:, 0:cn], lhsT=ring[:, s, k * 512 + off:k * 512 + off + 128], rhs=hb[:, k, c0:c0 + cn], start=False, stop=(k == 7)), r=[B(SL, s), B("hb", par, c0)], w=[B("ps", pbk)])
                        S(lambda e, dt_=dt_, pbk=pbk: e.activation(out=x[:, dt_, c0:c0 + cn], in_=ps[pbk][:, 0:cn], func=AF.Copy), r=[B("ps", pbk)], w=[B("x", par, c0, dt_)])
                        yield

        def gen_ffn(ch, l):
            par = ch % 2; x = X[par]; hb = HB[par]
            first, last = (ch == 0), (ch == nch - 1)
            ccs = [(0, 512)] + ([(T, NS)] if first else [])
            sq, rstd = sq2, rstd2
            ring, SL = ringF, "slotF"
            bank = fbank
            if True:
                rmsnorm(x, hb, par, l * PVL + OG2, ccs, sq2, rstd2, fbank(), "f")
                for fg in range(6):
                    nf = 4 if fg < 5 else 2
                    sg_ = wload("F", w_g[l, :, fg * 512:fg * 512 + nf * 128].rearrange("(k p) n -> p k n", p=128), "p (k n) -> p k n", wid=l * 23 + 5 + fg * 3, ch=ch, k=8)
                    su_ = wload("F", w_u[l, :, fg * 512:fg * 512 + nf * 128].rearrange("(k p) n -> p k n", p=128), "p (k n) -> p k n", wid=l * 23 + 6 + fg * 3, ch=ch, k=8)
                    sd_ = wload("F", w_d[l, fg * 512:fg * 512 + nf * 128, :].rearrange("(f p) n -> p f n", p=128), "p (f n) -> p f n", wid=l * 23 + 7 + fg * 3, ch=ch, f=nf)
                    W_ = nf * 128
                    for (c0, cn) in ccs:
                        for f in range(nf):
                            pg, pu = bank(), bank()
                            for (s_, pb_) in ((sg_, pg), (su_, pu)):
                                for k in range(8):
                                    PE(lambda e, k=k, s_=s_, pb_=pb_, f=f: e.matmul(ps[pb_][:, 0:cn], lhsT=ring[:, s_, k * W_ + f * 128:k * W_ + f * 128 + 128], rhs=hb[:, k, c0:c0 + cn], start=(k == 0), stop=(k == 7)), r=[B(SL, s_), B("hb", par, c0)], w=[B("ps", pb_)])
                            S(lambda e, pg=pg, f=f: e.activation(out=sgb[:, f % 2, 0:cn], in_=ps[pg][:, 0:cn], func=AF.Silu), r=[B("ps", pg)], w=[B("sgb", f % 2)])
                            V(lambda e, pu=pu, f=f: e.tensor_tensor(out=hid[:, f, c0:c0 + cn], in0=ps[pu][:, 0:cn], in1=sgb[:, f % 2, 0:cn], op=ALU.mult), r=[B("ps", pu), B("sgb", f % 2)], w=[B("hid", c0)])
                            yield
                        for dt_ in range(8):
                            pbk = bank()
                            PE(lambda e, dt_=dt_, pbk=pbk: e.matmul(ps[pbk][:, 0:cn], lhsT=ident[:], rhs=x[:, dt_, c0:c0 + cn], start=True, stop=False), r=[cB, B("x", par, c0, dt_)], w=[B("ps", pbk)])
                            for f in range(nf):
                                PE(lambda e, f=f, dt_=dt_, pbk=pbk: e.matmul(ps[pbk][:, 0:cn], lhsT=ring[:, sd_, f * 1024 + dt_ * 128:f * 1024 + dt_ * 128 + 128], rhs=hid[:, f, c0:c0 + cn], start=False, stop=(f == nf - 1)), r=[B(SL, sd_), B("hid", c0)], w=[B("ps", pbk)])
                            S(lambda e, dt_=dt_, pbk=pbk: e.activation(out=x[:, dt_, c0:c0 + cn], in_=ps[pbk][:, 0:cn], func=AF.Copy), r=[B("ps", pbk)], w=[B("x", par, c0, dt_)])
                            yield
            if l == L - 1:
                for (c0, cn) in ccs:
                    for k in range(8):
                        S(lambda e, k=k: e.activation(out=sq[:, 0:cn], in_=x[:, k, c0:c0 + cn], func=AF.Square), r=[*XB(par, c0)], w=[B("sq", "f")])
                        PE(lambda e, k=k: e.matmul(ps[0][:, 0:cn], lhsT=ones_b[:], rhs=sq[:, 0:cn], start=(k == 0), stop=(k == 7)), r=[B("sq", "f"), cB], w=[B("ps", 0)])
                    S(lambda e: e.activation(out=rstd[:, 0:cn], in_=ps[0][:, 0:cn], func=AF.Sqrt, scale=1.0 / D, bias=EPS), r=[B("ps", 0)], w=[B("rstd", "f")])
                    V(lambda e: e.reciprocal(out=rstd[:, 0:cn], in_=rstd[:, 0:cn]), r=[B("rstd", "f")], w=[B("rstd", "f")])
                    for k in range(8):
                        gk = pvt[:, 4 * PVL + k: 4 * PVL + k + 1]
                        V(lambda e, k=k, gk=gk: e.scalar_tensor_tensor(out=x[:, k, c0:c0 + cn], in0=x[:, k, c0:c0 + cn], scalar=gk, in1=rstd[:, 0:cn], op0=ALU.mult, op1=ALU.mult), r=[*XB(par, c0), B("rstd", "f"), cB], w=[*XB(par, c0)])
                        if c0 < T:
                            Q(lambda e, k=k: e.dma_start(out=yT[k * 128:(k + 1) * 128, ch * T + c0:ch * T + c0 + cn], in_=x[:, k, c0:c0 + cn]), r=[*XB(par, c0)], w=[B("yT")])
                        else:
                            Q(lambda e, k=k: e.dma_start(out=ysT[k * 128:(k + 1) * 128, :], in_=x[:, k, c0:c0 + cn]), r=[*XB(par, c0)], w=[B("ysT")])
            yield

        def count_ops(gen):
            P.dry = True
            n0 = P.dryn
            for _ in gen:
                pass
            P.dry = False
            return P.dryn - n0

        def run2(ga, na, gb, nb):
            ia = ib = 0
            alive_a, alive_b = ga is not None, gb is not None
            while alive_a or alive_b:
                pick_a = alive_a and (not alive_b or ia * nb <= ib * na)
                n0 = P.nops
                if pick_a:
                    try:
                        next(ga)
                    except StopIteration:
                        alive_a = False
                    ia += P.nops - n0
                else:
                    try:
                        next(gb)
                    except StopIteration:
                        alive_b = False
                    ib += P.nops - n0

        streams = []
        for ch in range(nch):
            ph = []
            for l in range(L):
                ph.append(("m", ch, l))
                ph.append(("f", ch, l))
            streams.append(ph)
        steps = []
        t = 0
        start = {}
        for ch in range(nch):
            start[ch] = (ch // 2) * 2 * L + (ch % 2)
        nsteps = max(start[c] + 2 * L for c in range(nch))
        for t in range(nsteps):
            cur = []
            for ch in range(nch):
                p = t - start[ch]
                if 0 <= p < 2 * L:
                    cur.append(streams[ch][p])
            steps.append(cur)
        for cur in steps:
            gens = []
            for (kind, ch, l) in cur:
                mk = (lambda: gen_mix(ch, l)) if kind == "m" else (lambda: gen_ffn(ch, l))
                n = count_ops(mk())
                gens.append((mk(), max(n, 1)))
            if len(gens) == 1:
                run2(gens[0][0], gens[0][1], None, 1)
            else:
                run2(gens[0][0], gens[0][1], gens[1][0], gens[1][1])
        print('NOPS', P.nops, {k: len(v) for k, v in P.ops.items()}, flush=True)
        P.emit()
    return nc


def _alibi_tables():
    slopes = 2.0 ** (-8.0 * np.arange(1, 9, dtype=np.float32) / 8)
    qi = np.arange(128)[:, None]
    kj = np.arange(256)[None, :]
    dist = qi - kj + 128
    valid = (dist >= 0) & (dist < 128)
    ab = np.where(valid, -dist.astype(np.float32), -1.0e7).astype(np.float32)
    dj = (127 - np.arange(128)).astype(np.float32)
    sbias = (-slopes.reshape(2, 4, 1) * dj[None, None, :]).astype(np.float32)
    sbias = np.ascontiguousarray(sbias.transpose(1, 0, 2)).reshape(4, 2 * 128)
    return ab, sbias


def _fm(v):
    return np.ascontiguousarray(np.asarray(v, np.float32).reshape(-1, 128).T)


_NC_CACHE = {}


def kernel(nch=16, depth=4, **inp):
    f = lambda k: np.asarray(inp[k], np.float32)
    L = depth
    TT = nch * T
    key = (nch, depth)
    if key not in _NC_CACHE:
        _NC_CACHE[key] = build(nch, depth)
    nc = _NC_CACHE[key]
    ab, sbias = _alibi_tables()
    pv = np.zeros((128, NPV), np.float32)
    for l in range(L):
        o = l * PVL
        pv[:, o + 0:o + 8] = _fm(f("norm_mix_g")[l])
        pv[:, o + 8:o + 16] = _fm(f("norm_ffn_g")[l])
        cw = f("conv_dw_w")[l]
        for ct in range(2):
            pv[:, o + 16 + ct * 31:o + 16 + (ct + 1) * 31] = cw[:, ct * 128:(ct + 1) * 128].T
        pv[:, o + 78:o + 80] = _fm(f("conv_dw_b")[l])
        pv[:, o + 80:o + 82] = _fm(f("conv_ln_g")[l])
        pv[:, o + 82:o + 84] = _fm(f("conv_ln_b")[l])
        pv[:, o + 84:o + 86] = _fm(f("ssm_d")[l])
        pv[:, o + 86:o + 88] = _fm(f("ssm_glu_b")[l])
        pv[:, o + 88:o + 96] = _fm(f("ssm_a_re")[l].reshape(-1))
        pv[:, o + 96:o + 104] = _fm(f("ssm_a_im")[l].reshape(-1))
        pv[:, o + 104:o + 112] = _fm(np.repeat(f("ssm_log_dt")[l], 64))
        pv[:, o + 112:o + 120] = f("attn_sinks")[l][None, :]
    pv[:, 4 * PVL:4 * PVL + 8] = _fm(f("norm_final_g"))
    Bn = np.zeros((L, 128, 8, 2, 128), np.float32)
    Cn = np.zeros((L, 128, 8, 2, 128), np.float32)
    for l in range(L):
        for ci, (bk, ck_) in enumerate((("ssm_b_re", "ssm_c_re"), ("ssm_b_im", "ssm_c_im"))):
            bb = f(bk)[l]
            cc = f(ck_)[l]
            for g in range(16):
                st_, p0 = g // 2, (g % 2) * 64
                col = (g % 8) * 16
                Bn[l, p0:p0 + 64, st_, ci, col:col + 16] = bb[g]
                Cn[l, p0:p0 + 64, st_, ci, col:col + 16] = cc[g].T
    Bn = Bn.reshape(L, 128, -1)
    Cn = Cn.reshape(L, 128, -1)
    ident = np.eye(128, dtype=np.float32)
    jjv = np.broadcast_to(np.arange(1, J + 1, dtype=np.float32)[None, :], (128, J)).copy()
    xp = f("x_prompt")
    xs = f("x_sample")[:, 0, :]
    shared = {
        "w_in": f("w_in")[:L], "w_out": f("w_out")[:L], "w_g": f("w_ff_gate")[:L], "w_u": f("w_ff_up")[:L],
        "w_d": f("w_ff_down")[:L], "glu_w": f("ssm_glu_w")[:L], "pv": pv, "Bn": Bn, "Cn": Cn, "ident": ident,
        "jj": jjv, "abias": ab, "sbias": sbias,
        "sinkc": np.ascontiguousarray(f("attn_sinks")[:L].reshape(L, 2, 4).transpose(2, 0, 1).reshape(4, L * 2)),
    }
    in_maps = []
    for c in range(8):
        sq_, b0 = c // 4, c * NS
        m = dict(shared)
        m["xT"] = np.ascontiguousarray(xp[sq_, :TT, :].T)
        m["xsT"] = np.ascontiguousarray(xs[b0:b0 + NS].T)
        m["ck"] = np.ascontiguousarray(f("cache_swa_k")[:L, b0:b0 + NS].reshape(L, NS, 128, 128))
        m["cv"] = np.ascontiguousarray(f("cache_swa_v")[:L, b0:b0 + NS].reshape(L, NS, 128, 128))
        cc_ = f("cache_conv")[:L, b0:b0 + NS]
        m["cconv"] = np.ascontiguousarray(cc_.reshape(L, NS, 30, 2, 128).transpose(0, 4, 3, 1, 2)).reshape(L, 128, -1)
        for nm, kk in (("sre", "state_ssm_re"), ("sim", "state_ssm_im")):
            s_ = f(kk)[:L, b0:b0 + NS].reshape(L, NS, 8, 128)
            m[nm] = np.ascontiguousarray(s_.transpose(0, 3, 2, 1)).reshape(L, 128, -1)
        in_maps.append(m)
    res = run_bass_kernel_spmd(nc, in_maps, core_ids=list(range(8))).results
    y_p = np.stack([res[0]["yT"].T, res[4]["yT"].T]).astype(np.float32)
    y_s = np.concatenate([res[c]["ysT"].T for c in range(8)])[:, None, :].astype(np.float32)
    pc = (res[0], res[4])
    k_p = np.stack([np.stack([r["ok_p"][l].reshape(128, 2, 64) for r in pc]) for l in range(L)])
    v_p = np.stack([np.stack([r["ov_p"][l].reshape(128, 2, 64) for r in pc]) for l in range(L)])
    conv_p = np.stack([np.stack([r["oconv_p"][l].reshape(128, 2, 30).transpose(2, 1, 0).reshape(30, 256) for r in pc]) for l in range(L)])
    ssm_p = [np.stack([np.stack([r["ossm_p"][l].reshape(128, 8, 2)[:, :, ci].T.reshape(16, 64) for r in pc]) for l in range(L)]) for ci in range(2)]
    k_s = np.concatenate([res[c]["ok_s"].reshape(L, NS, 128, 2, 64) for c in range(8)], 1)
    v_s = np.concatenate([res[c]["ov_s"].reshape(L, NS, 128, 2, 64) for c in range(8)], 1)
    conv_s = np.concatenate([res[c]["oconv_s"].reshape(L, 128, 2, NS, 30).transpose(0, 3, 4, 2, 1).reshape(L, NS, 30, 256) for c in range(8)], 1)
    ssm_s = [np.concatenate([res[c]["ossm_s"].reshape(L, 128, 2, 8, NS)[:, :, ci].transpose(0, 3, 2, 1).reshape(L, NS, 16, 64) for c in range(8)], 1) for ci in range(2)]
    outs = (y_p, y_s, k_p, v_p, conv_p, ssm_p[0], ssm_p[1], k_s, v_s, conv_s, ssm_s[0], ssm_s[1])
    return tuple(np.ascontiguousarray(o, dtype=np.float32) for o in outs)
```
